# Optimizing a Trainium2 kernel written in Bass

```python
import math
import jax, jax.numpy as jnp
from jax import lax
import numpy as np

D_MODEL = 1024
BATCH = 4
SEQ = 4096
DEPTH = 2

GRID_W = 64
CTX_LEN = 256
EPS = 1e-6
F32 = jnp.float32
HEAD_DIM = 64
ROPE_BASE = 10000.0

WA_HEADS = 4
WA_KV_HEADS = 2
WA_WINDOW = 128
WA_BLOCK = 128
NA_HEADS = 4
NA_MAX_KH = 8
NA_KW = 16
NA_QBW = 16
NA_KBW = NA_QBW + NA_KW
SSM_HEADS = 8
SSM_HEAD_DIM = 64
SSM_INNER = SSM_HEADS * SSM_HEAD_DIM
SSM_GROUPS = 2
SSM_STATE = 128
SSM_CONV = 7
SSM_CHUNK = 128
D_FF = ((8 * D_MODEL + 3 * 256 - 1) // (3 * 256)) * 256

QA_COLS = WA_HEADS * HEAD_DIM
QB_COLS = NA_HEADS * HEAD_DIM
Z_COLS = SSM_INNER
Q_SIDE = QA_COLS + QB_COLS + Z_COLS
KA_COLS = WA_KV_HEADS * HEAD_DIM
KB_COLS = NA_HEADS * HEAD_DIM
XBC_COLS = SSM_INNER + 2 * SSM_GROUPS * SSM_STATE
DT_COLS = 2 * SSM_HEADS
IN_COLS = Q_SIDE + 2 * KA_COLS + 2 * KB_COLS + XBC_COLS + DT_COLS
MIX_WIDTH = QA_COLS + QB_COLS + SSM_INNER

kernel_name = 'hymba_style_window_natten_ssd_prefix_dit'


def rms_norm(x, g):
    xf = x.astype(F32)
    y = xf * lax.rsqrt(jnp.mean(xf * xf, axis=-1, keepdims=True) + EPS)
    return (y * g.astype(F32)).astype(x.dtype)


def modulate(h, shift, scale):
    return h * (1 + scale) + shift


def split_cols(p, sizes):
    out, off = [], 0
    for s in sizes:
        out.append(p[..., off:off + s])
        off += s
    return out


def rope_2d(x, rows, cols):
    d = x.shape[-1]
    half = d // 2
    quarter = half // 2
    inv_freq = ROPE_BASE ** (-jnp.arange(quarter, dtype=F32) / quarter)
    xf = x.astype(F32)

    def rot(xp, pos):
        ang = pos.astype(F32)[:, None] * inv_freq[None, :]
        cos = jnp.cos(ang)[None, :, None, :]
        sin = jnp.sin(ang)[None, :, None, :]
        x1, x2 = xp[..., :quarter], xp[..., quarter:]
        return jnp.concatenate([x1 * cos - x2 * sin, x2 * cos + x1 * sin], axis=-1)

    return jnp.concatenate([rot(xf[..., :half], rows), rot(xf[..., half:], cols)], axis=-1).astype(x.dtype)


def window_attention(q, k, v, k_ctx, v_ctx, sink):
    b, L, H, d = q.shape
    G = k.shape[2]
    rep = H // G
    blk = WA_BLOCK
    nb = L // blk
    Lc = k_ctx.shape[1]
    scale = d ** -0.5
    qb = q.reshape(b, nb, blk, G, rep, d)

    def band(t):
        tp = jnp.pad(t, ((0, 0), (blk, blk), (0, 0), (0, 0))).reshape(b, nb + 2, blk, G, d)
        return jnp.concatenate([tp[:, 0:nb], tp[:, 1:nb + 1], tp[:, 2:nb + 2]], axis=2)

    kb, vb = band(k), band(v)
    qpos = jnp.arange(nb)[:, None] * blk + jnp.arange(blk)[None, :]
    kpos = (jnp.arange(nb)[:, None] - 1) * blk + jnp.arange(3 * blk)[None, :]
    mask = ((jnp.abs(qpos[:, :, None] - kpos[:, None, :]) <= WA_WINDOW)
            & (kpos[:, None, :] >= 0) & (kpos[:, None, :] < L))
    s_loc = jnp.einsum('bnqgrd,bnkgd->bngrqk', qb, kb).astype(F32) * scale
    s_loc = jnp.where(mask[None, :, None, None], s_loc, -jnp.inf)
    s_ctx = jnp.einsum('bnqgrd,bcgd->bngrqc', qb, k_ctx).astype(F32) * scale
    s_sink = jnp.broadcast_to(sink.astype(F32).reshape(G, rep)[None, None, :, :, None, None],
                              s_loc.shape[:-1] + (1,))
    p = jax.nn.softmax(jnp.concatenate([s_loc, s_ctx, s_sink], axis=-1), axis=-1).astype(v.dtype)
    nk = 3 * blk
    o = (jnp.einsum('bngrqk,bnkgd->bnqgrd', p[..., :nk], vb)
         + jnp.einsum('bngrqc,bcgd->bnqgrd', p[..., nk:nk + Lc], v_ctx))
    return o.reshape(b, L, H * d)


def context_attention(q, k, v, sink):
    b, Lc, H, d = q.shape
    G = k.shape[2]
    rep = H // G
    qg = q.reshape(b, Lc, G, rep, d)
    s = jnp.einsum('bqgrd,bkgd->bgrqk', qg, k).astype(F32) * d ** -0.5
    if sink is not None:
        s_sink = jnp.broadcast_to(sink.astype(F32).reshape(G, rep)[None, :, :, None, None], s.shape[:-1] + (1,))
        s = jnp.concatenate([s, s_sink], axis=-1)
    p = jax.nn.softmax(s, axis=-1)[..., :Lc].astype(v.dtype)
    o = jnp.einsum('bgrqk,bkgd->bqgrd', p, v)
    return o.reshape(b, Lc, H * d)


def neighbourhood_attention(q, k, v, k_ctx, v_ctx, rpb, grid_rows):
    b, L, H, d = q.shape
    kh = min(NA_MAX_KH, grid_rows)
    ncb = GRID_W // NA_QBW
    scale = d ** -0.5
    r = jnp.arange(grid_rows)
    row_idx = jnp.clip(r - kh // 2, 0, grid_rows - kh)[:, None] + jnp.arange(kh)[None, :]
    cb = jnp.arange(ncb)
    col_idx = jnp.clip(cb * NA_QBW - NA_KW // 2, 0, GRID_W - NA_KBW)[:, None] + jnp.arange(NA_KBW)[None, :]
    qcol = cb[:, None] * NA_QBW + jnp.arange(NA_QBW)[None, :]
    cstart = jnp.clip(qcol - NA_KW // 2, 0, GRID_W - NA_KW)
    cmask = (col_idx[:, None, :] >= cstart[:, :, None]) & (col_idx[:, None, :] < cstart[:, :, None] + NA_KW)
    dy = row_idx - r[:, None] + (NA_MAX_KH - 1)
    dx = jnp.clip(col_idx[:, None, :] - qcol[:, :, None], -(NA_KW - 1), NA_KW - 1) + (NA_KW - 1)
    bias = rpb.astype(F32)[:, dy[:, None, None, :, None], dx[None, :, :, None, :]]
    bias = jnp.moveaxis(bias, 0, 2)
    qg = q.reshape(b, grid_rows, ncb, NA_QBW, H, d)
    kg = k.reshape(b, grid_rows, GRID_W, H, d)
    vg = v.reshape(b, grid_rows, GRID_W, H, d)
    ri = row_idx[:, None, :, None]
    ci = col_idx[None, :, None, :]
    kwin = kg[:, ri, ci]
    vwin = vg[:, ri, ci]
    s = jnp.einsum('brcqhd,brcyxhd->brchqyx', qg, kwin).astype(F32) * scale + bias[None]
    s = jnp.where(cmask[None, None, :, None, :, None, :], s, -jnp.inf)
    nloc = kh * NA_KBW
    s = s.reshape(b, grid_rows, ncb, H, NA_QBW, nloc)
    s_ctx = jnp.einsum('brcqhd,bkhd->brchqk', qg, k_ctx).astype(F32) * scale
    p = jax.nn.softmax(jnp.concatenate([s, s_ctx], axis=-1), axis=-1).astype(v.dtype)
    p_loc = p[..., :nloc].reshape(b, grid_rows, ncb, H, NA_QBW, kh, NA_KBW)
    o = (jnp.einsum('brchqyx,brcyxhd->brcqhd', p_loc, vwin)
         + jnp.einsum('brchqk,bkhd->brcqhd', p[..., nloc:], v_ctx))
    return o.reshape(b, L, H * d)


def depthwise_conv(x, w, bias):
    k = w.shape[0]
    y = lax.conv_general_dilated(x, w[:, None, :].astype(x.dtype), window_strides=(1,),
                                 padding=[(k // 2, k // 2)], dimension_numbers=('NWC', 'WIO', 'NWC'),
                                 feature_group_count=x.shape[-1])
    return y + bias


def ssm_prepare(xbc_raw, dt_raw, conv_w, conv_b, dt_bias):
    xbc = jax.nn.silu(depthwise_conv(xbc_raw, conv_w, conv_b))
    xs, bm, cm = split_cols(xbc, (SSM_INNER, SSM_GROUPS * SSM_STATE, SSM_GROUPS * SSM_STATE))
    b, L = xs.shape[:2]
    rep = SSM_HEADS // SSM_GROUPS
    xs = xs.reshape(b, L, SSM_HEADS, SSM_HEAD_DIM)
    bm = jnp.repeat(bm.reshape(b, L, SSM_GROUPS, SSM_STATE), rep, axis=2)
    cm = jnp.repeat(cm.reshape(b, L, SSM_GROUPS, SSM_STATE), rep, axis=2)
    dt = jax.nn.softplus(dt_raw.astype(F32) + dt_bias.astype(F32).reshape(2 * SSM_HEADS))
    return xs, bm, cm, dt.reshape(b, L, 2, SSM_HEADS)


def ssd_scan(x, dt, A, Bm, Cm, h0, with_output):
    b, L, H, P = x.shape
    N = Bm.shape[-1]
    Q = SSM_CHUNK
    nc = L // Q
    xc = x.astype(F32).reshape(b, nc, Q, H, P)
    dtc = dt.astype(F32).reshape(b, nc, Q, H)
    bc = Bm.astype(F32).reshape(b, nc, Q, H, N)
    cc = Cm.astype(F32).reshape(b, nc, Q, H, N)
    acum = jnp.cumsum(dtc * A, axis=2)
    decay_to_end = jnp.exp(acum[:, :, -1:, :] - acum)
    states = jnp.einsum('bcjhn,bcjh,bcjhp->bchpn', bc, decay_to_end * dtc, xc)
    chunk_decay = jnp.exp(acum[:, :, -1, :])

    def step(h, inp):
        st, dec = inp
        return h * dec[:, :, None, None] + st, h

    h_final, h_enter = lax.scan(step, h0, (jnp.moveaxis(states, 1, 0), jnp.moveaxis(chunk_decay, 1, 0)))
    if not with_output:
        return h_final
    h_enter = jnp.moveaxis(h_enter, 0, 1)
    seg = acum[:, :, :, None, :] - acum[:, :, None, :, :]
    lower = jnp.tril(jnp.ones((Q, Q), dtype=bool))
    decay_ij = jnp.exp(jnp.where(lower[None, None, :, :, None], seg, -jnp.inf))
    w = jnp.einsum('bcihn,bcjhn->bcijh', cc, bc) * decay_ij * dtc[:, :, None, :, :]
    y = (jnp.einsum('bcijh,bcjhp->bcihp', w, xc)
         + jnp.einsum('bcihn,bchpn->bcihp', cc, h_enter) * jnp.exp(acum)[..., None])
    return y.reshape(b, L, H, P), h_final


def ssd_bidir(xs, bm, cm, dt, A, h0_f, h0_b, with_output):
    rev = lambda t: jnp.flip(t, axis=1)
    fwd = ssd_scan(xs, dt[:, :, 0], A[0], bm, cm, h0_f, with_output)
    bwd = ssd_scan(rev(xs), rev(dt[:, :, 1]), A[1], rev(bm), rev(cm), h0_b, with_output)
    if not with_output:
        return fwd, bwd
    (y_f, h_f), (y_b, h_b) = fwd, bwd
    return y_f + rev(y_b), h_f, h_b


def ssm_output(y, xs, z, d_skip, g):
    b, L = y.shape[:2]
    y = y + d_skip.astype(F32)[:, None] * xs.astype(F32)
    y = y.reshape(b, L, SSM_INNER) * jax.nn.silu(z.astype(F32))
    return rms_norm(y, g).astype(z.dtype)


def swiglu(h, w_in, w_out):
    gate, up = jnp.split(h @ w_in, 2, axis=-1)
    return (jax.nn.silu(gate) * up) @ w_out


def hybrid_layer(xl, xc, sc, scc, rows, cols, grid_rows, w_mod, b_mod, g_mix, w_in, wa_sink, na_rpb,
                 conv_w, conv_b, dt_bias, a_log, d_skip, ssm_g, w_out, g_ffn, w_ffn_in, w_ffn_out, ctx_out):
    D = D_MODEL
    b, L, _ = xl.shape
    Lc = xc.shape[1]
    hd = HEAD_DIM
    mod_l = sc @ w_mod + b_mod
    sh1, sc1, gt1, sh2, sc2, gt2 = [m[:, None, :] for m in jnp.split(mod_l, 6, axis=-1)]
    n_ctx_mod = 6 if ctx_out else 2
    mods_c = jnp.split(scc @ w_mod[:, :n_ctx_mod * D] + b_mod[:n_ctx_mod * D], n_ctx_mod)

    hl = modulate(rms_norm(xl, g_mix), sh1, sc1)
    hc = modulate(rms_norm(xc, g_mix), mods_c[0], mods_c[1])
    kv_sizes = (KA_COLS, KA_COLS, KB_COLS, KB_COLS, XBC_COLS, DT_COLS)
    qa, qb, z, ka, va, kb, vb, xbc, dtr = split_cols(hl @ w_in, (QA_COLS, QB_COLS, Z_COLS) + kv_sizes)
    if ctx_out:
        qa_c, qb_c, z_c, ka_c, va_c, kb_c, vb_c, xbc_c, dtr_c = split_cols(hc @ w_in, (QA_COLS, QB_COLS, Z_COLS) + kv_sizes)
    else:
        ka_c, va_c, kb_c, vb_c, xbc_c, dtr_c = split_cols(hc @ w_in[:, Q_SIDE:], kv_sizes)

    ka_c = ka_c.reshape(b, Lc, WA_KV_HEADS, hd)
    va_c = va_c.reshape(b, Lc, WA_KV_HEADS, hd)
    o_a = window_attention(rope_2d(qa.reshape(b, L, WA_HEADS, hd), rows, cols),
                           rope_2d(ka.reshape(b, L, WA_KV_HEADS, hd), rows, cols),
                           va.reshape(b, L, WA_KV_HEADS, hd), ka_c, va_c, wa_sink)
    kb_c = kb_c.reshape(b, Lc, NA_HEADS, hd)
    vb_c = vb_c.reshape(b, Lc, NA_HEADS, hd)
    o_b = neighbourhood_attention(qb.reshape(b, L, NA_HEADS, hd), kb.reshape(b, L, NA_HEADS, hd),
                                  vb.reshape(b, L, NA_HEADS, hd), kb_c, vb_c, na_rpb, grid_rows)
    A = -jnp.exp(a_log.astype(F32))
    xs_c, bm_c, cm_c, dt_c = ssm_prepare(xbc_c, dtr_c, conv_w, conv_b, dt_bias)
    h0 = jnp.zeros((b, SSM_HEADS, SSM_HEAD_DIM, SSM_STATE), F32)
    if ctx_out:
        y_c, h_f, h_b = ssd_bidir(xs_c, bm_c, cm_c, dt_c, A, h0, h0, True)
    else:
        h_f, h_b = ssd_bidir(xs_c, bm_c, cm_c, dt_c, A, h0, h0, False)
    xs, bm, cm, dt = ssm_prepare(xbc, dtr, conv_w, conv_b, dt_bias)
    y_l, _, _ = ssd_bidir(xs, bm, cm, dt, A, h_f, h_b, True)
    o_c = ssm_output(y_l, xs, z, d_skip, ssm_g)

    mix = jnp.concatenate([o_a, o_b, o_c.astype(o_a.dtype)], axis=-1) @ w_out
    xl = xl + gt1 * mix
    xl = xl + gt2 * swiglu(modulate(rms_norm(xl, g_ffn), sh2, sc2), w_ffn_in, w_ffn_out)
    if not ctx_out:
        return xl, None

    o_ac = context_attention(qa_c.reshape(b, Lc, WA_HEADS, hd), ka_c, va_c, wa_sink)
    o_bc = context_attention(qb_c.reshape(b, Lc, NA_HEADS, hd), kb_c, vb_c, None)
    o_cc = ssm_output(y_c, xs_c, z_c, d_skip, ssm_g)
    mix_c = jnp.concatenate([o_ac, o_bc, o_cc.astype(o_ac.dtype)], axis=-1) @ w_out
    xc = xc + mods_c[2] * mix_c
    xc = xc + mods_c[5] * swiglu(modulate(rms_norm(xc, g_ffn), mods_c[3], mods_c[4]), w_ffn_in, w_ffn_out)
    return xl, xc


def setup_inputs(seed: int = 0) -> dict:
    key = jax.random.key(seed)
    ks = jax.random.split(key, 24)
    nrm = jax.random.normal
    D = D_MODEL
    dt0 = jnp.exp(jax.random.uniform(ks[12], (DEPTH, 2, SSM_HEADS), minval=math.log(1e-3), maxval=math.log(0.1)))
    return {
        'x': nrm(ks[0], (BATCH, SEQ, D), F32),
        'c': nrm(ks[1], (BATCH, D), F32),
        'ctx': nrm(ks[2], (BATCH, CTX_LEN, D), F32),
        'c_ctx': nrm(ks[3], (D,), F32),
        'w_mod': nrm(ks[4], (DEPTH, D, 6 * D), F32) * (0.5 * D ** -0.5),
        'b_mod': nrm(ks[5], (DEPTH, 6 * D), F32) * 0.01,
        'g_mix': 1.0 + 0.05 * nrm(ks[6], (DEPTH, D), F32),
        'w_in': nrm(ks[7], (DEPTH, D, IN_COLS), F32) * D ** -0.5,
        'wa_sink': nrm(ks[8], (DEPTH, WA_HEADS), F32) * 0.5,
        'na_rpb': nrm(ks[9], (DEPTH, NA_HEADS, 2 * NA_MAX_KH - 1, 2 * NA_KW - 1), F32) * 0.1,
        'ssm_conv_w': nrm(ks[10], (DEPTH, SSM_CONV, XBC_COLS), F32) * SSM_CONV ** -0.5,
        'ssm_conv_b': nrm(ks[11], (DEPTH, XBC_COLS), F32) * 0.01,
        'ssm_dt_bias': dt0 + jnp.log(-jnp.expm1(-dt0)),
        'ssm_a_log': jnp.log(jax.random.uniform(ks[13], (DEPTH, 2, SSM_HEADS), minval=1.0, maxval=16.0)),
        'ssm_d': 1.0 + 0.1 * nrm(ks[14], (DEPTH, SSM_HEADS), F32),
        'ssm_norm_g': 1.0 + 0.05 * nrm(ks[15], (DEPTH, SSM_INNER), F32),
        'w_out': nrm(ks[16], (DEPTH, MIX_WIDTH, D), F32) * MIX_WIDTH ** -0.5,
        'g_ffn': 1.0 + 0.05 * nrm(ks[17], (DEPTH, D), F32),
        'w_ffn_in': nrm(ks[18], (DEPTH, D, 2 * D_FF), F32) * D ** -0.5,
        'w_ffn_out': nrm(ks[19], (DEPTH, D_FF, D), F32) * D_FF ** -0.5,
        'g_final': 1.0 + 0.05 * nrm(ks[20], (D,), F32),
    }


def reference(x, c, ctx, c_ctx, w_mod, b_mod, g_mix, w_in, wa_sink, na_rpb, ssm_conv_w, ssm_conv_b,
              ssm_dt_bias, ssm_a_log, ssm_d, ssm_norm_g, w_out, g_ffn, w_ffn_in, w_ffn_out, g_final):
    L = x.shape[1]
    grid_rows = L // GRID_W
    t = jnp.arange(L)
    rows, cols = t // GRID_W, t % GRID_W
    sc = jax.nn.silu(c)
    scc = jax.nn.silu(c_ctx)
    xl, xc = x, ctx
    for i in range(DEPTH):
        xl, xc = hybrid_layer(xl, xc, sc, scc, rows, cols, grid_rows, w_mod[i], b_mod[i], g_mix[i], w_in[i],
                              wa_sink[i], na_rpb[i], ssm_conv_w[i], ssm_conv_b[i], ssm_dt_bias[i], ssm_a_log[i],
                              ssm_d[i], ssm_norm_g[i], w_out[i], g_ffn[i], w_ffn_in[i], w_ffn_out[i],
                              ctx_out=(i < DEPTH - 1))
    return rms_norm(xl, g_final)
```

```python
import numpy as np
from contextlib import ExitStack
import concourse.bass as bass
import concourse.mybir as mybir
from concourse.bass_utils import run_bass_kernel_spmd

F32 = mybir.dt.float32
BF16 = mybir.dt.bfloat16
AF = mybir.ActivationFunctionType
ALU = mybir.AluOpType
AX = mybir.AxisListType

D = 1024
SEQ = 4096
LC = 256
DEPTH = 2
NT = SEQ // 128
NTC = LC // 128
EPS = 1e-6
DFF = 2816
NFM = 18
NFMO = 15
TMC = 912
WCOLS = NFM * 128 + TMC
NEG = -30000.0
VS = 66


class Res:
    __slots__ = ("name", "w", "rs")

    def __init__(self, name=""):
        self.name = name
        self.w = None
        self.rs = []


class Sched:
    ENGS = ("pe", "act", "dve", "pool", "sp")
    NDMA = 40
    NSDMA = 16

    def __init__(self, nc):
        self.nc = nc
        self.prog = {e: [] for e in self.ENGS}
        self.count = {}
        self.known = {e: {} for e in self.ENGS}
        self.dma_i = 0
        self.sdma_i = 0

    def _deps(self, eng, reads, writes):
        waits = {}

        def add(sv):
            if sv is None:
                return
            s, v = sv
            if eng == "pe" and s == "pe":
                return
            if waits.get(s, 0) < v:
                waits[s] = v
        for r in reads:
            add(r.w)
        for w in writes:
            add(w.w)
            for x in w.rs:
                add(x)
        out = []
        kn = self.known[eng]
        for s, v in waits.items():
            if kn.get(s, 0) < v:
                kn[s] = v
                out.append((s, v))
        return out

    def _mark(self, tag, reads, writes):
        for r in reads:
            r.rs.append(tag)
        for w in writes:
            w.w = tag
            w.rs = []

    def op(self, eng, fn, reads=(), writes=()):
        waits = self._deps(eng, reads, writes)
        c = self.count.get(eng, 0) + 1
        self.count[eng] = c
        self.prog[eng].append((waits, fn, (eng, 1)))
        self._mark((eng, c), reads, writes)

    def dma(self, q, out, in_, reads=(), writes=(), **kw):
        if q == "pool":
            slot = "sdma%d" % (self.sdma_i % self.NSDMA)
            self.sdma_i += 1
        else:
            slot = "dma%d" % (self.dma_i % self.NDMA)
            self.dma_i += 1
        waits = self._deps(q, reads, writes)
        prev = self.count.get(slot, 0)
        kn = self.known[q]
        if prev and kn.get(slot, 0) < prev:
            kn[slot] = prev
            waits.append((slot, prev))
        c = prev + 16
        self.count[slot] = c

        def fn(e, out=out, in_=in_, kw=kw):
            return e.dma_start(out=out, in_=in_, **kw)
        self.prog[q].append((waits, fn, (slot, 16)))
        self._mark((slot, c), reads, writes)

    def barrier_all(self):
        allv = list(self.count.items())
        for e in self.ENGS:
            kn = self.known[e]
            waits = []
            for s, v in allv:
                if s == e:
                    continue
                if kn.get(s, 0) < v:
                    kn[s] = v
                    waits.append((s, v))
            if waits:
                self.prog[e].append((waits, None, None))

    def final_wait(self, eng="sp"):
        waits = []
        kn = self.known[eng]
        for s, v in self.count.items():
            if s != eng and kn.get(s, 0) < v:
                kn[s] = v
                waits.append((s, v))
        self.prog[eng].append((waits, None, None))

    def emit(self):
        nc = self.nc
        with ExitStack() as es:
            sems = {}
            for s in self.count:
                sems[s] = es.enter_context(nc.semaphore(s))
            block = es.enter_context(nc.Block())

            def replay(name, e):
                for waits, fn, inc in self.prog[name]:
                    for s, v in waits:
                        e.wait_ge(sems[s], v)
                    if fn is not None:
                        ins = fn(e)
                        if inc is not None:
                            ins.then_inc(sems[inc[0]], inc[1])

            @block.tensor
            def _(e):
                replay("pe", e)

            @block.scalar
            def _(e):
                replay("act", e)

            @block.vector
            def _(e):
                replay("dve", e)

            @block.gpsimd
            def _(e):
                replay("pool", e)

            @block.sync
            def _(e):
                replay("sp", e)


class Ring:
    def __init__(self, aps, name=""):
        self.items = [(a, Res("%s%d" % (name, i))) for i, a in enumerate(aps)]
        self.i = 0

    def next(self):
        it = self.items[self.i % len(self.items)]
        self.i += 1
        return it


class Builder:
    def __init__(self, debug=(), stop_after=None, opts=None):
        self.opts = opts or {}
        self.debug = set(debug)
        self.stop_after = stop_after
        self.nc = bass.Bass("TRN2", target_bir_lowering=False)
        self.S = Sched(self.nc)
        self.es = ExitStack()
        self.resmap = {}

    def din(self, name, shape, dt=F32):
        return self.nc.dram_tensor(name, list(shape), dt, kind="ExternalInput").ap()

    def dscr(self, name, shape, dt=F32):
        kind = "ExternalOutput" if name in self.debug else "Internal"
        if name in self.opts.get("inject", ()):
            kind = "ExternalInput"
        return self.nc.dram_tensor(name, list(shape), dt, kind=kind).ap()

    def res(self, *key):
        r = self.resmap.get(key)
        if r is None:
            r = Res(str(key))
            self.resmap[key] = r
        return r

    def sb(self, st, name, shape, dt=F32):
        self.uid = getattr(self, "uid", 0) + 1
        return st.enter_context(self.nc.sbuf_tensor("%s_%d" % (name, self.uid), list(shape), dt))

    def declare(self):
        self.x_in = self.din("x", [SEQ, D])
        self.ctx_in = self.din("ctx", [LC, D])
        self.cvec = self.din("cvec", [128, 16])
        self.w_mod = self.din("w_mod", [DEPTH, D, 6 * D])
        self.b_mod = self.din("b_mod", [DEPTH, 6 * D])
        self.g_mix = self.din("g_mix", [DEPTH, D])
        self.g_ffn = self.din("g_ffn", [DEPTH, D])
        self.w_in = self.din("w_in_ext", [DEPTH, D, WCOLS])
        self.rope = self.din("rope", [4, 128, SEQ])
        self.w_out = self.din("w_out", [DEPTH, D, D])
        self.w_ffn_in = self.din("w_ffn_in", [DEPTH, D, 2 * DFF])
        self.w_ffn_out = self.din("w_ffn_out", [DEPTH, DFF, D])
        self.g_final = self.din("g_final", [D])
        self.wa_sink = self.din("wa_sink", [DEPTH, 4])
        self.conv_w = self.din("conv_w_l", [DEPTH, 128, 8, 7])
        self.conv_b = self.din("conv_b_l", [DEPTH, 128, 8])
        self.conv_brow = self.din("conv_brow", [DEPTH, 1, 1024])
        self.dt_bias = self.din("dt_bias", [DEPTH, 16])
        self.a_log = self.din("a_log", [DEPTH, 16])
        self.ssm_d = self.din("ssm_d", [DEPTH, 8])
        self.ssm_g = self.din("ssm_g", [DEPTH, 512])
        self.bm_tab = self.din("bm_tab", [DEPTH, 128, 84 * 128])
        self.out = self.nc.dram_tensor("out", [SEQ, D], F32, kind="ExternalOutput").ap()
        self.MOD = self.dscr("MOD", [2, 6, 128, D])
        self.FM = [self.dscr("FM_l", [NFMO, 128, SEQ], BF16), self.dscr("FM_c", [NFMO, 128, LC], BF16)]
        self.ZS = [self.dscr("ZS_l", [SEQ, 512], BF16), self.dscr("ZS_c", [LC, 512], BF16)]
        self.VA = [self.dscr("VA_l", [SEQ, 2 * VS], BF16), self.dscr("VA_c", [LC, 2 * VS], BF16)]
        self.VB = [self.dscr("VB_l", [SEQ, 4 * VS], BF16), self.dscr("VB_c", [LC, 4 * VS], BF16)]
        self.DT = [self.dscr("DT_l", [SEQ, 16]), self.dscr("DT_c", [LC, 16])]
        self.MIX = [self.dscr("MIX_l", [SEQ, D], BF16), self.dscr("MIX_c", [LC, D], BF16)]
        self.XSS = [self.dscr("XSS_l", [SEQ, 768], BF16), self.dscr("XSS_c", [LC, 768], BF16)]
        self.XM = [self.dscr("XM_l", [SEQ, D]), self.dscr("XM_c", [LC, D])]
        self.XL = [self.dscr("XL_l", [SEQ, D]), self.dscr("XL_c", [LC, D])]

    def setup_common(self, st):
        nc, S = self.nc, self.S
        self.ps = st.enter_context(nc.psum_tensor("ps", [128, 4096], F32))
        self.psb = [(self.ps[:, b * 512:(b + 1) * 512], Res("bank%d" % b)) for b in range(8)]
        self.ident = self.sb(st, "ident", [128, 128], BF16)
        self.r_ident = Res("ident")
        ident = self.ident

        S.op("pool", lambda e: e.memset(ident[:], 0.0), writes=[self.r_ident])
        S.op("pool", lambda e: e.affine_select(out=ident[:], in_=ident[:], pattern=[[-1, 128]], compare_op=ALU.not_equal,
                                               fill=1.0, base=0, channel_multiplier=1),
             reads=[self.r_ident], writes=[self.r_ident])
        self.maskP = self.sb(st, "maskP", [128, 128], BF16)
        self.maskN = self.sb(st, "maskN", [128, 128], BF16)
        self.r_mask = Res("mask")
        mP, mN = self.maskP, self.maskN
        r1, r2 = Res(), Res()
        S.op("pool", lambda e: e.memset(mP[:], 0.0), writes=[r1])
        S.op("pool", lambda e: e.memset(mN[:], 0.0), writes=[r2])
        S.op("pool", lambda e: e.affine_select(out=mP[:], in_=mP[:], pattern=[[-1, 128]], compare_op=ALU.is_ge,
                                               fill=NEG, base=0, channel_multiplier=1), reads=[r1], writes=[r1])
        S.op("pool", lambda e: e.affine_select(out=mN[:], in_=mN[:], pattern=[[1, 128]], compare_op=ALU.is_ge,
                                               fill=NEG, base=0, channel_multiplier=-1), reads=[r2], writes=[r2])
        S.op("pool", lambda e: e.memset(self.ident[0:1, 0:1], 1.0), reads=[r1, r2, self.r_ident], writes=[self.r_mask, self.r_ident])
        self.U32 = self.sb(st, "U32", [128, 128])
        self.L32 = self.sb(st, "L32", [128, 128])
        self.ones32 = self.sb(st, "ones32", [128, 128])
        self.r_tri = Res("tri")
        U32, L32, ones32 = self.U32, self.L32, self.ones32
        r3, r4 = Res(), Res()
        S.op("pool", lambda e: e.memset(U32[:], 1.0), writes=[r3])
        S.op("pool", lambda e: e.memset(L32[:], 1.0), writes=[r4])
        S.op("pool", lambda e: e.memset(ones32[:], 1.0), writes=[self.r_tri])
        S.op("pool", lambda e: e.affine_select(out=U32[:], in_=U32[:], pattern=[[1, 128]], compare_op=ALU.is_ge,
                                               fill=0.0, base=0, channel_multiplier=-1), reads=[r3], writes=[r3])
        S.op("pool", lambda e: e.affine_select(out=L32[:], in_=L32[:], pattern=[[-1, 128]], compare_op=ALU.is_ge,
                                               fill=0.0, base=0, channel_multiplier=1), reads=[r4], writes=[r4])
        S.op("pool", lambda e: e.memset(ones32[0:1, 0:1], 1.0), reads=[r3, r4, self.r_tri], writes=[self.r_tri])
        self.bank_i = 0

    def bank(self):
        b = self.psb[self.bank_i % 8]
        self.bank_i += 1
        return b

    def p0(self, l):
        nc, S = self.nc, self.S
        with ExitStack() as st:
            cv = self.sb(st, "p0_cv", [128, 16])
            scv = self.sb(st, "p0_scv", [128, 16])
            scbc = self.sb(st, "p0_scbc", [128, 16, 128])
            gbc = self.sb(st, "p0_gbc", [128, 2, D])
            wb = [self.sb(st, "p0_w%d" % i, [128, 8, 512]) for i in range(2)]
            bb = [self.sb(st, "p0_b%d" % i, [128, 512]) for i in range(2)]
            mt = [self.sb(st, "p0_m%d" % i, [128, D]) for i in range(4)]
            r_cv, r_scv, r_scbc, r_g = Res(), Res(), Res(), Res()
            wring = Ring(wb, "p0w")
            bring = Ring(bb, "p0b")
            mring = Ring(mt, "p0m")
            S.dma("sp", cv[:], self.cvec, writes=[r_cv])
            S.dma("sp", gbc[:, 0, :], self.g_mix[l].partition_broadcast(128), writes=[r_g])
            S.dma("sp", gbc[:, 1, :], self.g_ffn[l].partition_broadcast(128), writes=[r_g])
            S.op("act", lambda e: e.activation(out=scv[:], in_=cv[:], func=AF.Silu), reads=[r_cv], writes=[r_scv])
            S.op("dve", lambda e: e.tensor_copy(out=scbc[:], in_=scv[:].unsqueeze(2).to_broadcast([128, 16, 128])),
                 reads=[r_scv], writes=[r_scbc])
            wv = self.w_mod[l].rearrange("(k p) n -> p k n", p=128)
            cur = {}
            for blk in range(12):
                j, half = blk // 2, blk % 2
                w_t, r_w = wring.next()
                b_t, r_b = bring.next()
                S.dma("sp", w_t[:], wv[:, :, blk * 512:(blk + 1) * 512], writes=[r_w])
                S.dma("sp", b_t[:], self.b_mod[l, blk * 512:(blk + 1) * 512].partition_broadcast(128), writes=[r_b])
                for s in range(2):
                    if half == 0:
                        cur[s] = mring.next()
                    m_t, r_m = cur[s]
                    pb, r_pb = self.bank()

                    def mm(e, pb=pb, w_t=w_t, s=s):
                        for k in range(8):
                            ins = e.matmul(pb, lhsT=scbc[:, s * 8 + k, :], rhs=w_t[:, k, :], start=(k == 0), stop=(k == 7))
                        return ins
                    S.op("pe", mm, reads=[r_scbc, r_w], writes=[r_pb])
                    dst = m_t[:, half * 512:(half + 1) * 512]
                    if j in (1, 4):
                        gsl = gbc[:, 0 if j == 1 else 1, half * 512:(half + 1) * 512]
                        tmp_r = Res()

                        def ev(e, dst=dst, pb=pb, b_t=b_t, gsl=gsl):
                            e.tensor_tensor(out=dst, in0=pb, in1=b_t[:], op=ALU.add)
                            return e.scalar_tensor_tensor(out=dst, in0=dst, scalar=1.0, in1=gsl, op0=ALU.add, op1=ALU.mult)
                        S.op("dve", lambda e, dst=dst, pb=pb, b_t=b_t: e.tensor_tensor(out=dst, in0=pb, in1=b_t[:], op=ALU.add),
                             reads=[r_pb, r_b], writes=[r_m])
                        S.op("dve", lambda e, dst=dst, gsl=gsl: e.scalar_tensor_tensor(out=dst, in0=dst, scalar=1.0, in1=gsl,
                                                                                         op0=ALU.add, op1=ALU.mult),
                             reads=[r_m, r_g], writes=[r_m])
                    else:
                        S.op("dve", lambda e, dst=dst, pb=pb, b_t=b_t: e.tensor_tensor(out=dst, in0=pb, in1=b_t[:], op=ALU.add),
                             reads=[r_pb, r_b], writes=[r_m])
                    if half == 1:
                        S.dma("sp", self.MOD[s, j], m_t[:], reads=[r_m], writes=[self.res("MOD", s, j)])

    def p1(self, l, src, do_ctx_q):
        nc, S = self.nc, self.S
        with ExitStack() as st:
            W = self.sb(st, "p1_W", [128, 8, WCOLS], BF16)
            r_Wall = []
            wv = self.w_in[l].rearrange("(k p) n -> p k n", p=128)
            for k in range(8):
                for c0 in range(0, WCOLS, 1608):
                    r = Res()
                    r_Wall.append(r)
                    S.dma("pool", W[:, k, c0:c0 + 1608], wv[:, k, c0:c0 + 1608], writes=[r])
            modt = self.sb(st, "p1_mod", [128, 2, 2, D])
            r_mod = Res("p1mod")
            for s in range(2):
                for jj, j in enumerate((0, 1)):
                    S.dma("sp", modt[:, s, jj, :], self.MOD[s, j], reads=[self.res("MOD", s, j)], writes=[r_mod])
            xr = Ring([self.sb(st, "p1_x%d" % i, [128, D]) for i in range(3)], "p1x")
            junk = self.sb(st, "p1_junk", [128, D], BF16)
            r_junk = Res()
            stat = Ring([self.sb(st, "p1_st%d" % i, [128, 4]) for i in range(3)], "p1st")
            tmpr = Ring([self.sb(st, "p1_t%d" % i, [128, D]) for i in range(2)], "p1t")
            hr = Ring([self.sb(st, "p1_h%d" % i, [128, D], BF16) for i in range(8)], "p1h")
            hTr = Ring([self.sb(st, "p1_hT%d" % i, [128, 8, 512], BF16) for i in range(2)], "p1hT")
            ropr = Ring([self.sb(st, "p1_rp%d" % i, [128, 4, 512]) for i in range(2)], "p1rp")
            rtr = Ring([self.sb(st, "p1_rt%d" % i, [128, 2, 512]) for i in range(3)], "p1rt")
            fmr = Ring([self.sb(st, "p1_fm%d" % i, [128, NFMO, 512], BF16) for i in range(2)], "p1fm")
            zr = Ring([self.sb(st, "p1_z%d" % i, [128, 4, 512], BF16) for i in range(2)], "p1z")
            var_ = [self.sb(st, "p1_va%d" % i, [128, 4, 2, VS], BF16) for i in range(2)]
            vbr_ = [self.sb(st, "p1_vb%d" % i, [128, 4, 4, VS], BF16) for i in range(2)]
            dtr = Ring([self.sb(st, "p1_dt%d" % i, [128, 4, 16]) for i in range(2)], "p1dt")
            var = Ring(var_, "p1va")
            vbr = Ring(vbr_, "p1vb")
            for (t_, r_) in var.items + vbr.items:
                S.op("pool", lambda e, t_=t_: e.memset(t_[:], 1.0), writes=[r_])

            groups = [(1, 0, NTC)] + [(0, g * 4, 4) for g in range(NT // 4)]
            lim = self.opts.get('p1_lim', 9)
            groups = groups[:self.opts.get('p1_groups', 99)]
            if lim == 0:
                groups = []
            def norm_group(grp):
                (s, t0, ntile) = grp
                TG = ntile * 128
                tok0 = t0 * 128
                hT, r_hT = hTr.next()
                fins = []
                G1 = modt[:, s, 1, :]
                SH1 = modt[:, s, 0, :]
                for ti in range(ntile):
                    x_t, r_x = xr.next()
                    S.dma("sp", x_t[:], src[s][(t0 + ti) * 128:(t0 + ti + 1) * 128, :],
                          reads=[self.res("XL", s, t0 + ti)], writes=[r_x])
                    st_t, r_st = stat.next()
                    S.op("act", lambda e, x_t=x_t, st_t=st_t: e.activation(out=junk[:], in_=x_t[:], func=AF.Square,
                                                                            accum_out=st_t[:, 0:1]),
                         reads=[r_x], writes=[r_junk, r_st])
                    S.op("act", lambda e, st_t=st_t: e.activation(out=st_t[:, 1:2], in_=st_t[:, 0:1], func=AF.Sqrt,
                                                                  scale=1.0 / D, bias=EPS),
                         reads=[r_st], writes=[r_st])
                    S.op("dve", lambda e, st_t=st_t: e.reciprocal(out=st_t[:, 2:3], in_=st_t[:, 1:2]), reads=[r_st], writes=[r_st])
                    tm, r_tm = tmpr.next()
                    S.op("dve", lambda e, tm=tm, x_t=x_t, st_t=st_t, G1=G1: e.scalar_tensor_tensor(
                        out=tm[:], in0=x_t[:], scalar=st_t[:, 2:3], in1=G1, op0=ALU.mult, op1=ALU.mult),
                        reads=[r_x, r_st, r_mod], writes=[r_tm])
                    h_t, r_h = hr.next()
                    S.op("pool", lambda e, h_t=h_t, tm=tm, SH1=SH1: e.tensor_tensor(out=h_t[:], in0=tm[:], in1=SH1, op=ALU.add),
                         reads=[r_tm, r_mod], writes=[r_h])
                    def fin(h_t=h_t, r_h=r_h, hT=hT, r_hT=r_hT, ti=ti):
                        pb, r_pb = self.bank()
                        pbT = pb.bitcast(BF16)

                        def tr(e, pbT=pbT, h_t=h_t):
                            for k in range(8):
                                ins = e.transpose(out=pbT[:, k * 128:(k + 1) * 128], in_=h_t[:, k * 128:(k + 1) * 128],
                                                  identity=self.ident[:])
                            return ins
                        S.op("pe", tr, reads=[r_h, self.r_ident], writes=[r_pb])
                        S.op("act", lambda e, hT=hT, ti=ti, pbT=pbT: e.copy(out=hT[:, :, ti * 128:(ti + 1) * 128],
                                                                             in_=pbT.rearrange("p (k t) -> p k t", k=8)),
                             reads=[], writes=[r_hT, r_pb])
                    fins.append(fin)
                return hT, r_hT, fins

            pend = norm_group(groups[0]) if groups else None
            if pend:
                for f_ in pend[2]:
                    f_()
            for gi, (s, t0, ntile) in enumerate(groups):
                TG = ntile * 128
                tok0 = t0 * 128
                hT, r_hT, _ = pend
                pend = norm_group(groups[gi + 1]) if gi + 1 < len(groups) else None
                defer = list(pend[2]) if pend else []
                if lim <= 1:
                    continue
                fm, r_fm = fmr.next()
                if s == 0:
                    rp, r_rp = ropr.next()
                    S.dma("sp", rp[:], self.rope[:, :, tok0:tok0 + TG].rearrange("c p t -> p c t"), writes=[r_rp])

                def fm_mm(ct, pb, TG=TG, hT=hT):
                    def f(e):
                        for k in range(8):
                            ins = e.matmul(pb[:, 0:TG], lhsT=W[:, k, ct * 128:(ct + 1) * 128], rhs=hT[:, k, 0:TG],
                                           start=(k == 0), stop=(k == 7))
                        return ins
                    return f
                for (ct, slot, ci) in ((0, 0, 0), (2, 1, 0), (4, 2, 2)):
                    pq, r_pq = self.bank()
                    S.op("pe", fm_mm(ct, pq), reads=r_Wall + [r_hT], writes=[r_pq])
                    if s == 0:
                        psw, r_psw = self.bank()
                        S.op("pe", fm_mm(ct + 1, psw), reads=r_Wall + [r_hT], writes=[r_psw])
                        rt, r_rt = rtr.next()
                        S.op("dve", lambda e, rt=rt, pq=pq, rp=rp, ci=ci, TG=TG: e.tensor_tensor(
                            out=rt[:, 0, 0:TG], in0=pq[:, 0:TG], in1=rp[:, ci, 0:TG], op=ALU.mult),
                            reads=[r_pq, r_rp], writes=[r_rt])
                        S.op("dve", lambda e, rt=rt, psw=psw, rp=rp, ci=ci, TG=TG: e.tensor_tensor(
                            out=rt[:, 1, 0:TG], in0=psw[:, 0:TG], in1=rp[:, ci + 1, 0:TG], op=ALU.mult),
                            reads=[r_psw, r_rp], writes=[r_rt])
                        S.op("pool", lambda e, rt=rt, fm=fm, slot=slot, TG=TG: e.tensor_tensor(
                            out=fm[:, slot, 0:TG], in0=rt[:, 0, 0:TG], in1=rt[:, 1, 0:TG], op=ALU.add),
                            reads=[r_rt], writes=[r_fm])
                    else:
                        sc_ = 0.125 if ct < 4 else 1.0
                        S.op("act", lambda e, fm=fm, slot=slot, pq=pq, TG=TG, sc_=sc_: e.activation(
                            out=fm[:, slot, 0:TG], in_=pq[:, 0:TG], func=AF.Copy, scale=sc_),
                            reads=[r_pq], writes=[r_fm])
                for ct in range(6, NFM):
                    if defer and ct in (8, 11, 14, 17):
                        defer.pop(0)()
                    slot = ct - 3
                    pq, r_pq = self.bank()
                    S.op("pe", fm_mm(ct, pq), reads=r_Wall + [r_hT], writes=[r_pq])
                    sc_ = 0.125 if ct < 8 else 1.0
                    if ct % 2 == 0:
                        S.op("act", lambda e, fm=fm, slot=slot, pq=pq, TG=TG, sc_=sc_: e.activation(
                            out=fm[:, slot, 0:TG], in_=pq[:, 0:TG], func=AF.Copy, scale=sc_),
                            reads=[r_pq], writes=[r_fm])
                    else:
                        S.op("dve", lambda e, fm=fm, slot=slot, pq=pq, TG=TG, sc_=sc_: e.tensor_scalar(
                            out=fm[:, slot, 0:TG], in0=pq[:, 0:TG], scalar1=sc_, scalar2=None, op0=ALU.mult),
                            reads=[r_pq], writes=[r_fm])
                for c0 in range(0, NFMO, 5):
                    S.dma("sp", self.FM[s][c0:c0 + 5, :, tok0:tok0 + TG].rearrange("c p t -> p c t"), fm[:, c0:c0 + 5, 0:TG],
                          reads=[r_fm], writes=[self.res("FM", s, t0 // 4, c0)])
                while defer:
                    defer.pop(0)()
                if lim <= 2:
                    continue
                z_t, r_z = zr.next()
                va_t, r_va = var.next()
                vb_t, r_vb = vbr.next()
                dt_t, r_dt = dtr.next()
                tmm = self.opts.get('tm_mask', 7)
                for ti in range(ntile):
                    def tm_mm(pb, c0, n, hT=hT, ti=ti):
                        def f(e):
                            for k in range(8):
                                ins = e.matmul(pb[:, 0:n], lhsT=hT[:, k, ti * 128:(ti + 1) * 128],
                                               rhs=W[:, k, c0:c0 + n], start=(k == 0), stop=(k == 7))
                            return ins
                        return f
                    if not (tmm & 1):
                        continue
                    pz, r_pz = self.bank()
                    S.op("pe", tm_mm(pz, NFM * 128, 512), reads=r_Wall + [r_hT], writes=[r_pz])
                    S.op("act", lambda e, z_t=z_t, ti=ti, pz=pz: e.activation(out=z_t[:, ti, :], in_=pz, func=AF.Silu),
                         reads=[r_pz], writes=[r_z])
                    if not (tmm & 2):
                        continue
                    pv, r_pv = self.bank()
                    S.op("pe", tm_mm(pv, NFM * 128 + 512, 400), reads=r_Wall + [r_hT], writes=[r_pv])
                    if not (tmm & 8):
                      S.op("dve", lambda e, va_t=va_t, ti=ti, pv=pv: e.tensor_copy(
                        out=va_t[:, ti, :, 0:64], in_=pv[:, 0:128].rearrange("p (g d) -> p g d", g=2)),
                        reads=[r_pv], writes=[r_va, r_pv])
                    if not (tmm & 16):
                      S.op("dve", lambda e, vb_t=vb_t, ti=ti, pv=pv: e.tensor_copy(
                        out=vb_t[:, ti, :, 0:64], in_=pv[:, 128:384].rearrange("p (g d) -> p g d", g=4)),
                        reads=[r_pv], writes=[r_vb, r_pv])
                    if not (tmm & 32):
                      S.op("act", lambda e, dt_t=dt_t, ti=ti, pv=pv: e.copy(out=dt_t[:, ti, :], in_=pv[:, 384:400]),
                         reads=[r_pv], writes=[r_dt, r_pv])
                if not (tmm & 4):
                    continue
                rows = slice(tok0, tok0 + TG)
                gk = t0 // 4
                S.dma("sp", self.ZS[s][rows, :].rearrange("(t p) c -> p t c", p=128), z_t[:, 0:ntile, :],
                      reads=[r_z], writes=[self.res("ZS", s, gk)])
                S.dma("sp", self.VA[s][rows, :].rearrange("(t p) c -> p t c", p=128),
                      va_t[:, 0:ntile].rearrange("p t g d -> p t (g d)"), reads=[r_va], writes=[self.res("VA", s, gk)])
                S.dma("sp", self.VB[s][rows, :].rearrange("(t p) c -> p t c", p=128),
                      vb_t[:, 0:ntile].rearrange("p t g d -> p t (g d)"), reads=[r_vb], writes=[self.res("VB", s, gk)])
                S.dma("sp", self.DT[s][rows, :].rearrange("(t p) c -> p t c", p=128), dt_t[:, 0:ntile, :],
                      reads=[r_dt], writes=[self.res("DT", s, gk)])

    @staticmethod
    def nb_window(i):
        s0 = min(max(2 * i - 4, 0), 56)
        s1 = min(max(2 * i + 1 - 4, 0), 56)
        return list(range(s0 // 2, (s1 + 7) // 2 + 1))

    @staticmethod
    def bm_index(i, j):
        if 2 <= i <= 29:
            return j - i + 2
        if i == 0:
            return 5 + j
        if i == 1:
            return 9 + j
        if i == 30:
            return 13 + (j - 28)
        return 17 + (j - 28)

    def p_attn(self, l, kind, streams):
        nc, S = self.nc, self.S
        isA = kind == "A"
        nkt = 1 if isA else 2
        ng = 2 if isA else 4
        qs0 = 0 if isA else 3
        ks0 = 2 if isA else 5
        VSRC = self.VA if isA else self.VB
        col0 = 0 if isA else 256
        with ExitStack() as st:
            QT = self.sb(st, "at_Q", [128, 2, 2, SEQ], BF16)
            QTc = self.sb(st, "at_Qc", [128, 2, 2, LC], BF16)
            KT = self.sb(st, "at_K", [128, nkt, SEQ + LC], BF16)
            V = self.sb(st, "at_V", [128, NT + NTC, ng * VS], BF16)
            r_in = Res()
            for T in range(2):
                for hh in range(2):
                    lo, zl = hh * 64, (1 - hh) * 64
                    S.op("pool", lambda e, T=T, hh=hh, zl=zl: e.memset(QT[zl:zl + 64, T, hh, :], 0.0), writes=[r_in])
                    S.op("pool", lambda e, T=T, hh=hh, zl=zl: e.memset(QTc[zl:zl + 64, T, hh, :], 0.0), writes=[r_in])
                    S.dma("sp", QT[lo:lo + 64, T, hh, :], self.FM[0][qs0 + T][lo:lo + 64, :], writes=[r_in])
                    S.dma("sp", QTc[lo:lo + 64, T, hh, :], self.FM[1][qs0 + T][lo:lo + 64, :], writes=[r_in])
            for kt in range(nkt):
                S.dma("sp", KT[:, kt, 0:SEQ], self.FM[0][ks0 + kt], writes=[r_in])
                S.dma("sp", KT[:, kt, SEQ:SEQ + LC], self.FM[1][ks0 + kt], writes=[r_in])
            for t0 in range(0, NT, 8):
                S.dma("sp", V[:, t0:t0 + 8, :], VSRC[0][t0 * 128:(t0 + 8) * 128, :].rearrange("(t p) c -> p t c", p=128), writes=[r_in])
            S.dma("sp", V[:, NT:NT + NTC, :], VSRC[1].rearrange("(t p) c -> p t c", p=128), writes=[r_in])
            r_c = Res()
            if isA:
                esk = self.sb(st, "at_esk", [128, 4])
                S.dma("sp", esk[:], self.wa_sink[l].partition_broadcast(128), writes=[r_c])
                S.op("act", lambda e: e.activation(out=esk[:], in_=esk[:], func=AF.Exp), reads=[r_c], writes=[r_c])
            else:
                BM = self.sb(st, "at_BM", [128, 84 * 128], BF16)
                for c0 in range(0, 84 * 128, 1792):
                    S.dma("pool", BM[:, c0:c0 + 1792], self.bm_tab[l][:, c0:c0 + 1792], writes=[r_c])
            Sring = Ring([self.ps[:, i * 1024:(i + 1) * 1024] for i in range(3)])
            Oring = Ring([self.ps[:, 3072 + i * 512:3072 + (i + 1) * 512] for i in range(2)])
            PTr = Ring([self.sb(st, "at_PT%d" % i, [128, 7 * 128], BF16) for i in range(3)])
            mor = Ring([self.sb(st, "at_mo%d" % i, [128, 4, 64], BF16) for i in range(3)])
            dnr = Ring([self.sb(st, "at_dn%d" % i, [128, 8]) for i in range(3)])
            for s in streams:
                ntl = min(NT if s == 0 else NTC, self.opts.get("at_nt", 99))
                for n in range(ntl):
                    O, r_O = Oring.next()
                    pend_pv = []
                    for T in range(2):
                        for hh in range(2):
                            h = (2 * hh + T) if isA else (2 * T + hh)
                            g = hh if isA else h
                            kt = 0 if isA else T
                            ps_ = slice(hh * 64, (hh + 1) * 64)
                            chunks = []
                            if s == 0:
                                if isA:
                                    if n > 0:
                                        chunks.append(((n - 1) * 128, self.maskP[:], n - 1))
                                    chunks.append((n * 128, None, n))
                                    if n < NT - 1:
                                        chunks.append(((n + 1) * 128, self.maskN[:], n + 1))
                                else:
                                    for j in self.nb_window(n):
                                        bi = h * 21 + self.bm_index(n, j)
                                        chunks.append((j * 128, BM[:, bi * 128:(bi + 1) * 128], j))
                            chunks.append((SEQ, None, NT))
                            chunks.append((SEQ + 128, None, NT + 1))
                            nch = len(chunks)
                            q_ap = (QT if s == 0 else QTc)[:, T, hh, n * 128:(n + 1) * 128]
                            Sp, r_S = Sring.next()

                            def smm(e, Sp=Sp, chunks=chunks, q_ap=q_ap, kt=kt, ps_=ps_):
                                for c, (kc, bias, vt) in enumerate(chunks):
                                    ins = e.matmul(Sp[:, c * 128:(c + 1) * 128], lhsT=KT[:, kt, kc:kc + 128], rhs=q_ap,
                                                   start=True, stop=(bias is None))
                                    if bias is not None:
                                        ins = e.matmul(Sp[:, c * 128:(c + 1) * 128], lhsT=self.ident[:], rhs=bias,
                                                       start=False, stop=True)
                                return ins
                            S.op("pe", smm, reads=[r_in, r_c, self.r_ident, self.r_mask], writes=[r_S])
                            PT, r_PT = PTr.next()
                            S.op("act", lambda e, PT=PT, Sp=Sp, nch=nch: e.activation(out=PT[:, 0:nch * 128], in_=Sp[:, 0:nch * 128], func=AF.Exp),
                                 reads=[], writes=[r_PT, r_S])

                            def pv(e, O=O, PT=PT, chunks=chunks, h=h, g=g):
                                for c, (kc, bias, vt) in enumerate(chunks):
                                    ins = e.matmul(O[:, h * VS:h * VS + 65], lhsT=PT[:, c * 128:(c + 1) * 128],
                                                   rhs=V[:, vt, g * VS:g * VS + 65], start=(c == 0), stop=(c == len(chunks) - 1))
                                return ins
                            pend_pv.append((pv, [r_PT, r_in], [r_O]))
                            if len(pend_pv) > 1:
                                f_, rd_, wr_ = pend_pv.pop(0)
                                S.op("pe", f_, reads=rd_, writes=wr_)
                    while pend_pv:
                        f_, rd_, wr_ = pend_pv.pop(0)
                        S.op("pe", f_, reads=rd_, writes=wr_)
                    dn, r_dn = dnr.next()
                    O3 = O[:, 0:4 * VS].rearrange("p (h c) -> p h c", c=VS)
                    if isA:
                        S.op("dve", lambda e, dn=dn, O3=O3: e.tensor_tensor(out=dn[:, 0:4].unsqueeze(2), in0=O3[:, :, 64:65],
                                                                             in1=esk[:].unsqueeze(2), op=ALU.add),
                             reads=[r_c], writes=[r_dn, r_O])
                    else:
                        S.op("dve", lambda e, dn=dn, O3=O3: e.tensor_copy(out=dn[:, 0:4].unsqueeze(2), in_=O3[:, :, 64:65]),
                             reads=[], writes=[r_dn, r_O])
                    S.op("dve", lambda e, dn=dn: e.reciprocal(out=dn[:, 4:8], in_=dn[:, 0:4]), reads=[r_dn], writes=[r_dn])
                    mo, r_mo = mor.next()
                    S.op("dve", lambda e, mo=mo, O3=O3, dn=dn: e.tensor_tensor(
                        out=mo[:], in0=O3[:, :, 0:64], in1=dn[:, 4:8].unsqueeze(2).to_broadcast([128, 4, 64]), op=ALU.mult),
                        reads=[r_dn], writes=[r_mo, r_O])
                    S.dma("sp", self.MIX[s][n * 128:(n + 1) * 128, col0:col0 + 256], mo[:].rearrange("p h d -> p (h d)"),
                          reads=[r_mo], writes=[self.res("MIX", s, n, kind)])

    def p2(self, l, streams):
        self.p_attn(l, "A", streams)

    def p3(self, l, streams):
        self.p_attn(l, "B", streams)

    def p4(self, l, streams):
        nc, S = self.nc, self.S
        last = (l == DEPTH - 1)
        PB = self.psb
        with ExitStack() as st:
            cw = self.sb(st, "s_cw", [128, 8, 7])
            cb = self.sb(st, "s_cb", [128, 8])
            cbrow = self.sb(st, "s_cbrow", [128, 1024], BF16)
            ones1 = self.sb(st, "s_ones1", [128, 128], BF16)
            diag = self.sb(st, "s_diag", [128, 8, 7, 128], BF16)
            Dh = self.sb(st, "s_Dh", [128, 8, 128], BF16)
            sm = self.sb(st, "s_sm", [128, 64])
            gn = self.sb(st, "s_gn", [128, 512])
            r_p = Res("ssm_params")
            r_diag = Res("diag")
            S.dma("sp", cw[:], self.conv_w[l], writes=[r_p])
            S.dma("sp", cb[:], self.conv_b[l], writes=[r_p])
            r_cb0 = Res()
            S.op("pool", lambda e: e.memset(cbrow[:], 0.0), writes=[r_cb0])
            S.dma("pool", cbrow[0:1, :], self.conv_brow[l], reads=[r_cb0], writes=[r_p, r_cb0])
            S.dma("sp", sm[:, 0:16], self.dt_bias[l].partition_broadcast(128), writes=[r_p])
            S.dma("sp", sm[:, 16:32], self.a_log[l].partition_broadcast(128), writes=[r_p])
            S.dma("sp", sm[:, 32:40], self.ssm_d[l].partition_broadcast(128), writes=[r_p])
            S.dma("sp", gn[:], self.ssm_g[l].partition_broadcast(128), writes=[r_p])
            S.op("pool", lambda e: e.memset(ones1[:], 0.0), writes=[r_diag])
            S.op("pool", lambda e: e.memset(ones1[0:1, :], 1.0), reads=[r_diag], writes=[r_diag])
            S.op("act", lambda e: e.activation(out=sm[:, 40:56], in_=sm[:, 16:32], func=AF.Exp), reads=[r_p], writes=[r_p])
            S.op("dve", lambda e: e.tensor_scalar(out=sm[:, 16:32], in0=sm[:, 40:56], scalar1=-1.0, scalar2=None, op0=ALU.mult),
                 reads=[r_p], writes=[r_p])
            for c in range(8):
                for j in range(7):
                    S.op("dve", lambda e, c=c, j=j: e.tensor_scalar(out=diag[:, c, j, :], in0=self.ident[:], scalar1=cw[:, c, j:j + 1],
                                                                    scalar2=None, op0=ALU.mult),
                         reads=[r_p, self.r_ident], writes=[r_diag])
            for h in range(8):
                S.op("dve", lambda e, h=h: e.tensor_scalar(out=Dh[:, h, :], in0=self.ident[:], scalar1=sm[:, 32 + h:33 + h],
                                                            scalar2=None, op0=ALU.mult),
                     reads=[r_p, self.r_ident], writes=[r_diag])
            ULb = self.sb(st, "s_ULb", [128, 16, 128], BF16)
            MK4 = self.sb(st, "s_MK4", [128, 2, 512], BF16)
            onesb = self.sb(st, "s_onesb", [128, 128], BF16)
            r_ul = Res("ULb")
            S.op("dve", lambda e: e.tensor_copy(out=ULb[:, 0:8, :], in_=self.U32[:].unsqueeze(1).to_broadcast([128, 8, 128])),
                 reads=[self.r_tri], writes=[r_ul])
            S.op("dve", lambda e: e.tensor_copy(out=ULb[:, 8:16, :], in_=self.L32[:].unsqueeze(1).to_broadcast([128, 8, 128])),
                 reads=[self.r_tri, r_ul], writes=[r_ul])
            S.op("dve", lambda e: e.tensor_copy(out=MK4[:, 0, :].rearrange("p (a b) -> p a b", a=4), in_=self.maskN[:].unsqueeze(1).to_broadcast([128, 4, 128])),
                 reads=[self.r_mask, r_ul], writes=[r_ul])
            S.op("dve", lambda e: e.tensor_copy(out=MK4[:, 1, :].rearrange("p (a b) -> p a b", a=4), in_=self.maskP[:].unsqueeze(1).to_broadcast([128, 4, 128])),
                 reads=[self.r_mask, r_ul], writes=[r_ul])
            S.op("dve", lambda e: e.tensor_copy(out=onesb[:], in_=self.ones32[:]), reads=[self.r_tri, r_ul], writes=[r_ul])
            Hf = self.sb(st, "s_Hf", [128, 512])
            Hb = self.sb(st, "s_Hb", [128, 512])
            HFb = self.sb(st, "s_HFb", [128, 512], BF16)
            r_Hf, r_Hb, r_HFb = Res("Hf"), Res("Hb"), Res("HFb")
            S.op("pool", lambda e: e.memset(Hf[:], 0.0), writes=[r_Hf])
            S.op("pool", lambda e: e.memset(Hb[:], 0.0), writes=[r_Hb])
            XB = self.sb(st, "s_XB", [128, 8, SEQ + 6], BF16)
            HB = self.sb(st, "s_HB", [128, NT, 512], BF16)
            r_HB = [Res("HB%d" % c) for c in range(NT)]
            NB = 9
            big = [self.sb(st, "s_big%d" % i, [128, NT * 16]) for i in range(NB)]
            xsr = Ring([self.sb(st, "s_xs%d" % i, [128, 512], BF16) for i in range(4)])
            btr = Ring([self.sb(st, "s_bt%d" % i, [128, 256], BF16) for i in range(4)])
            bcr = Ring([self.sb(st, "s_bc%d" % i, [128, 4, 128], BF16) for i in range(4)])
            xwr = Ring([self.sb(st, "s_xw%d" % i, [128, 512], BF16) for i in range(2)])
            mr = Ring([self.sb(st, "s_m%d" % i, [128, 128]) for i in range(6)])
            wr = Ring([self.sb(st, "s_w%d" % i, [128, 128], BF16) for i in range(6)])
            t1r = Ring([self.sb(st, "s_t1%d" % i, [128, 512]) for i in range(1)])
            t2r = Ring([self.sb(st, "s_t2%d" % i, [128, 512]) for i in range(1)])
            yr = Ring([self.sb(st, "s_y%d" % i, [128, 512]) for i in range(2)])
            y1r = Ring([self.sb(st, "s_y1%d" % i, [128, 512]) for i in range(2)])
            zr = Ring([self.sb(st, "s_z%d" % i, [128, 512], BF16) for i in range(2)])
            ocr = Ring([self.sb(st, "s_oc%d" % i, [128, 512], BF16) for i in range(2)])
            str_ = Ring([self.sb(st, "s_st%d" % i, [128, 4]) for i in range(3)])
            junk = self.sb(st, "s_junk", [128, 512], BF16)
            r_junk = Res()
            tmpH = self.sb(st, "s_tmpH", [128, 512])
            r_tmpH = Res()

            r_XB = Res("XB")
            _rb = [Res("big%d" % i) for i in range(NB)]
            r_b = {0: _rb[0], 1: _rb[1], 2: _rb[2], 3: _rb[3], 4: _rb[4], 5: _rb[5], 6: _rb[6], 7: _rb[6], 8: _rb[1], 9: _rb[7], 10: _rb[8], 11: _rb[3]}
            ahi = self.sb(st, "s_ahi", [128, NT * 16], BF16)
            alo = self.sb(st, "s_alo", [128, NT * 16], BF16)
            r_ahl = Res("ahl")
            rhr = Ring([self.sb(st, "s_rh%d" % i, [128, 2, 16, 128], BF16) for i in range(2)])
            for s in streams if 1 in streams else (1,) + tuple(streams):
                with_out = not (s == 1 and last)
                nch = NTC if s == 1 else NT
                nch = min(nch, self.opts.get("ssm_nch", 99))
                T = nch * 128
                W16 = nch * 16
                S.op("pool", lambda e: e.memset(XB[:, :, 0:3], 0.0), writes=[r_XB])
                S.op("pool", lambda e, T=T: e.memset(XB[:, :, 3 + T:6 + T], 0.0), writes=[r_XB])
                for c in range(8):
                    S.dma("sp", XB[:, c, 3:3 + T], self.FM[s][7 + c][:, 0:T], writes=[r_XB])
                DTr, Ev, DTv, LN, Av, AC, TOT, DEC, EE = [b[:, 0:W16] for b in big]
                DTE, SDT, MB = TOT, Ev, LN
                v3 = lambda ap: ap.rearrange("p (c k) -> p c k", k=16)
                for c0 in range(0, nch, 8):
                    c1 = min(c0 + 8, nch)
                    S.dma("sp", v3(DTr)[:, c0:c1, :], self.DT[s][c0 * 128:c1 * 128, :].rearrange("(c p) k -> p c k", p=128), writes=[r_b[0]])
                S.op("dve", lambda e, DTr=DTr, nch=nch: e.tensor_tensor(out=v3(DTr), in0=v3(DTr), in1=sm[:, 0:16].unsqueeze(1).to_broadcast([128, nch, 16]), op=ALU.add),
                     reads=[r_p], writes=[r_b[0]])
                S.op("act", lambda e, Ev=Ev, DTr=DTr: e.activation(out=Ev, in_=DTr, func=AF.Exp), reads=[r_b[0]], writes=[r_b[1]])
                S.op("act", lambda e, Ev=Ev, DTv=DTv: e.activation(out=DTv, in_=Ev, func=AF.Ln, bias=1.0), reads=[r_b[1]], writes=[r_b[2]])
                S.op("act", lambda e, LN=LN, DTv=DTv: e.activation(out=LN, in_=DTv, func=AF.Ln), reads=[r_b[2]], writes=[r_b[3]])
                S.op("dve", lambda e, Av=Av, DTv=DTv, nch=nch: e.tensor_tensor(out=v3(Av), in0=v3(DTv), in1=sm[:, 16:32].unsqueeze(1).to_broadcast([128, nch, 16]), op=ALU.mult),
                     reads=[r_b[2], r_p], writes=[r_b[4]])
                AHI = ahi[:, 0:W16]
                ALO = alo[:, 0:W16]
                S.op("dve", lambda e, AHI=AHI, Av=Av: e.tensor_copy(out=AHI, in_=Av), reads=[r_b[4]], writes=[r_ahl])
                S.op("dve", lambda e, ALO=ALO, Av=Av, AHI=AHI: e.tensor_tensor(out=ALO, in0=Av, in1=AHI, op=ALU.subtract),
                     reads=[r_b[4], r_ahl], writes=[r_ahl])
                for (mat, tgt, lo) in ((self.U32, AC, 0), (self.L32, AC, 8), (self.ones32, TOT, None)):
                    pb, r_pb = self.bank()
                    S.op("pe", lambda e, pb=pb, mat=mat, Av=Av, W16=W16: e.matmul(pb[:, 0:W16], lhsT=mat[:], rhs=Av, start=True, stop=True),
                         reads=[r_b[4], self.r_tri], writes=[r_pb])
                    if lo is None:
                        S.op("dve", lambda e, pb=pb, tgt=tgt, W16=W16: e.tensor_copy(out=tgt, in_=pb[:, 0:W16]), reads=[], writes=[r_b[6], r_pb])
                    else:
                        S.op("dve", lambda e, pb=pb, tgt=tgt, lo=lo, W16=W16: e.tensor_copy(out=v3(tgt)[:, :, lo:lo + 8], in_=v3(pb[:, 0:W16])[:, :, lo:lo + 8]),
                             reads=[], writes=[r_b[5], r_pb])
                S.op("act", lambda e, DEC=DEC, TOT=TOT: e.activation(out=DEC, in_=TOT, func=AF.Exp), reads=[r_b[6]], writes=[r_b[9]])
                S.op("dve", lambda e, DTE=DTE, TOT=TOT, AC=AC: e.tensor_tensor(out=DTE, in0=TOT, in1=AC, op=ALU.subtract), reads=[r_b[5]], writes=[r_b[7]])
                S.op("act", lambda e, DTE=DTE: e.activation(out=DTE, in_=DTE, func=AF.Exp), reads=[], writes=[r_b[7]])
                S.op("dve", lambda e, SDT=SDT, DTE=DTE, DTv=DTv: e.tensor_tensor(out=SDT, in0=DTE, in1=DTv, op=ALU.mult), reads=[r_b[7], r_b[2]], writes=[r_b[8]])
                S.op("act", lambda e, EE=EE, AC=AC: e.activation(out=EE, in_=AC, func=AF.Exp), reads=[r_b[5]], writes=[r_b[10]])
                S.op("dve", lambda e, MB=MB, LN=LN, AC=AC: e.tensor_tensor(out=MB, in0=LN, in1=AC, op=ALU.subtract), reads=[r_b[3], r_b[5]], writes=[r_b[11]])

                def conv_chunk(c, want_fm, load=False):
                    if load:
                        xs, r_xs = xsr.next()
                        bt, r_bt = btr.next()
                        rd = [self.res("XSS", s, c)]
                        S.dma("sp", xs[:], self.XSS[s][c * 128:(c + 1) * 128, 0:512], reads=rd, writes=[r_xs])
                        S.dma("sp", bt[:], self.XSS[s][c * 128:(c + 1) * 128, 512:768], reads=rd, writes=[r_bt])
                        bct, r_bct = None, None
                        if want_fm:
                            pb2, r_pb2 = PB[2]

                            def cfm(e, pb2=pb2, c=c):
                                for q, ct in enumerate((4, 5, 6, 7)):
                                    for j in range(7):
                                        ins = e.matmul(pb2[:, q * 128:(q + 1) * 128], lhsT=diag[:, ct, j, :], rhs=XB[:, ct, c * 128 + j:c * 128 + j + 128],
                                                       start=(j == 0), stop=(j == 6))
                                return ins
                            S.op("pe", cfm, reads=[r_XB, r_diag], writes=[r_pb2])
                            bct, r_bct = bcr.next()
                            for q, ct in enumerate((4, 5, 6, 7)):
                                S.op("act", lambda e, bct=bct, q=q, ct=ct, pb2=pb2: e.activation(out=bct[:, q, :], in_=pb2[:, q * 128:(q + 1) * 128], func=AF.Silu,
                                                                                                  bias=cb[:, ct:ct + 1]),
                                     reads=[r_p], writes=[r_bct, r_pb2])
                        return xs, r_xs, bt, r_bt, bct, r_bct
                    pb, r_pb = PB[0]

                    def cx(e, pb=pb, c=c):
                        for ct in range(4):
                            for j in range(7):
                                e.matmul(pb[:, ct * 128:(ct + 1) * 128], lhsT=XB[:, ct, c * 128 + j:c * 128 + j + 128], rhs=diag[:, ct, j, :],
                                         start=(j == 0), stop=False)
                            ins = e.matmul(pb[:, ct * 128:(ct + 1) * 128], lhsT=ones1[:], rhs=cbrow[:, ct * 128:(ct + 1) * 128], start=False, stop=True)
                        return ins
                    S.op("pe", cx, reads=[r_XB, r_diag, r_p], writes=[r_pb])
                    xs, r_xs = xsr.next()
                    S.op("act", lambda e, xs=xs, pb=pb: e.activation(out=xs[:], in_=pb, func=AF.Silu), reads=[], writes=[r_xs, r_pb])
                    pb1, r_pb1 = PB[1]

                    def cbt(e, pb1=pb1, c=c):
                        for ct in range(4, 6):
                            o = pb1[:, (ct - 4) * 128:(ct - 3) * 128]
                            for j in range(7):
                                e.matmul(o, lhsT=XB[:, ct, c * 128 + j:c * 128 + j + 128], rhs=diag[:, ct, j, :], start=(j == 0), stop=False)
                            ins = e.matmul(o, lhsT=ones1[:], rhs=cbrow[:, ct * 128:(ct + 1) * 128], start=False, stop=True)
                        return ins
                    S.op("pe", cbt, reads=[r_XB, r_diag, r_p], writes=[r_pb1])
                    bt, r_bt = btr.next()
                    S.op("act", lambda e, bt=bt, pb1=pb1: e.activation(out=bt[:], in_=pb1[:, 0:256], func=AF.Silu), reads=[], writes=[r_bt, r_pb1])
                    bct, r_bct = None, None
                    if want_fm:
                        pb2, r_pb2 = PB[2]

                        def cfm(e, pb2=pb2, c=c):
                            for q, ct in enumerate((4, 5, 6, 7)):
                                for j in range(7):
                                    ins = e.matmul(pb2[:, q * 128:(q + 1) * 128], lhsT=diag[:, ct, j, :], rhs=XB[:, ct, c * 128 + j:c * 128 + j + 128],
                                                   start=(j == 0), stop=(j == 6))
                            return ins
                        S.op("pe", cfm, reads=[r_XB, r_diag], writes=[r_pb2])
                        bct, r_bct = bcr.next()
                        for q, ct in enumerate((4, 5, 6, 7)):
                            S.op("act", lambda e, bct=bct, q=q, ct=ct, pb2=pb2: e.activation(out=bct[:, q, :], in_=pb2[:, q * 128:(q + 1) * 128], func=AF.Silu,
                                                                                              bias=cb[:, ct:ct + 1]),
                                 reads=[r_p], writes=[r_bct, r_pb2])
                    return xs, r_xs, bt, r_bt, bct, r_bct

                def state_mm(c, xs, r_xs, bt, r_bt, lo):
                    xw, r_xw = xwr.next()
                    S.op("dve", lambda e, xw=xw, xs=xs, c=c, lo=lo: e.tensor_tensor(
                        out=xw[:].rearrange("p (h d) -> p h d", h=8), in0=xs[:].rearrange("p (h d) -> p h d", h=8),
                        in1=v3(SDT)[:, c, lo:lo + 8].unsqueeze(2).to_broadcast([128, 8, 64]), op=ALU.mult),
                        reads=[r_xs, r_b[8]], writes=[r_xw])
                    pb3, r_pb3 = PB[3]

                    def smm(e, pb3=pb3, bt=bt, xw=xw):
                        for g in range(2):
                            ins = e.matmul(pb3[:, g * 256:(g + 1) * 256], lhsT=bt[:, g * 128:(g + 1) * 128], rhs=xw[:, g * 256:(g + 1) * 256],
                                           start=True, stop=True)
                        return ins
                    S.op("pe", smm, reads=[r_bt, r_xw], writes=[r_pb3])
                    return pb3, r_pb3

                def scan_step(H, r_H, c, lo, pb3, r_pb3):
                    S.op("dve", lambda e, H=H, c=c, lo=lo: e.tensor_tensor(
                        out=tmpH[:].rearrange("p (h d) -> p h d", h=8), in0=H[:].rearrange("p (h d) -> p h d", h=8),
                        in1=v3(DEC)[:, c, lo:lo + 8].unsqueeze(2).to_broadcast([128, 8, 64]), op=ALU.mult),
                        reads=[r_H, r_b[9]], writes=[r_tmpH])
                    S.op("dve", lambda e, H=H, pb3=pb3: e.tensor_tensor(out=H[:], in0=pb3, in1=tmpH[:], op=ALU.add),
                         reads=[r_tmpH], writes=[r_H, r_pb3])

                nxt = conv_chunk(nch - 1, False)
                for c in range(nch - 1, -1, -1):
                    xs, r_xs, bt, r_bt, _, _ = nxt
                    S.dma("sp", self.XSS[s][c * 128:(c + 1) * 128, 0:512], xs[:], reads=[r_xs], writes=[self.res("XSS", s, c)])
                    S.dma("sp", self.XSS[s][c * 128:(c + 1) * 128, 512:768], bt[:], reads=[r_bt], writes=[self.res("XSS", s, c)])
                    if c > 0:
                        nxt = conv_chunk(c - 1, False)
                    S.op("pool", lambda e, c=c: e.tensor_copy(out=HB[:, c, :], in_=Hb[:]), reads=[r_Hb], writes=[r_HB[c]])
                    pb3, r_pb3 = state_mm(c, xs, r_xs, bt, r_bt, 8)
                    scan_step(Hb, r_Hb, c, 8, pb3, r_pb3)
                h3 = lambda ap: ap.rearrange("p (h d) -> p h d", h=8)
                a3 = lambda ap: ap.rearrange("p (c k) -> p c k", k=16)
                convs = {0: conv_chunk(0, with_out, load=True)}

                rhs_ = {}

                def build_rh(c):
                    rh, r_rh = rhr.next()
                    r_rl = Res()
                    S.op("dve", lambda e, rh=rh, c=c: e.tensor_tensor(out=rh[:, 0], in0=ULb[:], in1=a3(AHI)[:, c, :].unsqueeze(2).to_broadcast([128, 16, 128]), op=ALU.mult),
                         reads=[r_ahl, r_ul], writes=[r_rh, r_rl])
                    S.op("pool", lambda e, rh=rh, c=c: e.tensor_tensor(out=rh[:, 1], in0=ULb[:], in1=a3(ALO)[:, c, :].unsqueeze(2).to_broadcast([128, 16, 128]), op=ALU.mult),
                         reads=[r_ahl, r_ul], writes=[r_rl])
                    rhs_[c] = (rh, r_rh, r_rl)

                def head(c):
                    xs, r_xs, bt, r_bt, bct, r_bct = convs[c]
                    pG, r_pG = PB[4]

                    def gmm(e, pG=pG, bct=bct):
                        for g in range(2):
                            ins = e.matmul(pG[:, g * 128:(g + 1) * 128], lhsT=bct[:, g, :], rhs=bct[:, 2 + g, :], start=True, stop=True)
                        return ins
                    S.op("pe", gmm, reads=[r_bct], writes=[r_pG])
                    rh, r_rh, r_rl = rhs_.pop(c)
                    pY1, r_pY1 = PB[7]
                    for rnd in range(2):
                        pDs = []
                        for d_ in range(2):
                            pD, r_pD = PB[5 + d_]

                            def dmm(e, pD=pD, rh=rh, d_=d_, rnd=rnd):
                                hs = slice(d_ * 8 + rnd * 4, d_ * 8 + rnd * 4 + 4)
                                e.matmul(pD, lhsT=onesb[:], rhs=rh[:, 0, hs, :], start=True, stop=False)
                                e.matmul(pD, lhsT=onesb[:], rhs=rh[:, 1, hs, :], start=False, stop=False)
                                return e.matmul(pD, lhsT=self.ident[:], rhs=MK4[:, d_, :], start=False, stop=True)
                            S.op("pe", dmm, reads=[r_rh, r_rl, r_ul, self.r_ident], writes=[r_pD])
                            pDs.append((pD, r_pD))
                        for hq in range(4):
                            h = rnd * 4 + hq
                            g = h // 4
                            ws = []
                            for d_, lo in ((0, 0), (1, 8)):
                                pD, r_pD = pDs[d_]
                                m_t, r_m = mr.next()
                                S.op("act", lambda e, m_t=m_t, pD=pD, hq=hq, c=c, lo=lo, h=h: e.activation(
                                    out=m_t[:], in_=pD[:, hq * 128:(hq + 1) * 128], func=AF.Exp, bias=MB[:, c * 16 + lo + h:c * 16 + lo + h + 1]),
                                    reads=[r_b[11]], writes=[r_m, r_pD])
                                w_t, r_w = wr.next()
                                S.op("dve", lambda e, w_t=w_t, pG=pG, g=g, m_t=m_t: e.tensor_tensor(out=w_t[:], in0=pG[:, g * 128:(g + 1) * 128], in1=m_t[:], op=ALU.mult),
                                     reads=[r_m], writes=[r_w, r_pG])
                                ws.append((w_t, r_w))

                            def ymm(e, pY1=pY1, ws=ws, xs=xs, h=h):
                                o = pY1[:, h * 64:(h + 1) * 64]
                                e.matmul(o, lhsT=ws[0][0][:], rhs=xs[:, h * 64:(h + 1) * 64], start=True, stop=False)
                                e.matmul(o, lhsT=ws[1][0][:], rhs=xs[:, h * 64:(h + 1) * 64], start=False, stop=False)
                                return e.matmul(o, lhsT=Dh[:, h, :], rhs=xs[:, h * 64:(h + 1) * 64], start=False, stop=True)
                            S.op("pe", ymm, reads=[ws[0][1], ws[1][1], r_xs, r_diag], writes=[r_pY1])
                    y1, r_y1 = y1r.next()
                    S.op("act", lambda e, y1=y1, pY1=pY1: e.copy(out=y1[:], in_=pY1), reads=[], writes=[r_y1, r_pY1])
                    if c + 1 < nch:
                        build_rh(c + 1)
                    return y1, r_y1

                def tail(c, y1, r_y1):
                    xs, r_xs, bt, r_bt, bct, r_bct = convs.pop(c)
                    S.op("pool", lambda e: e.tensor_copy(out=HFb[:], in_=Hf[:]), reads=[r_Hf], writes=[r_HFb])
                    pb3, r_pb3 = state_mm(c, xs, r_xs, bt, r_bt, 0)
                    scan_step(Hf, r_Hf, c, 0, pb3, r_pb3)
                    pY2, r_pY2 = PB[5]
                    pY3, r_pY3 = PB[6]

                    def y2mm(e, pY2=pY2, bct=bct):
                        for g in range(2):
                            ins = e.matmul(pY2[:, g * 256:(g + 1) * 256], lhsT=bct[:, 2 + g, :], rhs=HFb[:, g * 256:(g + 1) * 256], start=True, stop=True)
                        return ins
                    S.op("pe", y2mm, reads=[r_bct, r_HFb], writes=[r_pY2])

                    def y3mm(e, pY3=pY3, bct=bct, c=c):
                        for g in range(2):
                            ins = e.matmul(pY3[:, g * 256:(g + 1) * 256], lhsT=bct[:, 2 + g, :], rhs=HB[:, c, g * 256:(g + 1) * 256], start=True, stop=True)
                        return ins
                    S.op("pe", y3mm, reads=[r_bct, r_HB[c]], writes=[r_pY3])
                    t1, r_t1 = t1r.next()
                    t2, r_t2 = t2r.next()
                    S.op("dve", lambda e, t1=t1, pY2=pY2, c=c: e.tensor_tensor(out=h3(t1[:]), in0=h3(pY2), in1=v3(EE)[:, c, 0:8].unsqueeze(2).to_broadcast([128, 8, 64]), op=ALU.mult),
                         reads=[r_b[10]], writes=[r_t1, r_pY2])
                    S.op("dve", lambda e, t2=t2, pY3=pY3, c=c: e.tensor_tensor(out=h3(t2[:]), in0=h3(pY3), in1=v3(EE)[:, c, 8:16].unsqueeze(2).to_broadcast([128, 8, 64]), op=ALU.mult),
                         reads=[r_b[10]], writes=[r_t2, r_pY3])
                    S.op("pool", lambda e, t1=t1, t2=t2: e.tensor_tensor(out=t1[:], in0=t1[:], in1=t2[:], op=ALU.add), reads=[r_t2], writes=[r_t1])
                    y, r_y = yr.next()
                    S.op("dve", lambda e, y=y, y1=y1, t1=t1: e.tensor_tensor(out=y[:], in0=y1[:], in1=t1[:], op=ALU.add), reads=[r_t1, r_y1], writes=[r_y])
                    z_t, r_z = zr.next()
                    S.dma("sp", z_t[:], self.ZS[s][c * 128:(c + 1) * 128, :], writes=[r_z])
                    S.op("pool", lambda e, y=y, z_t=z_t: e.tensor_tensor(out=y[:], in0=y[:], in1=z_t[:], op=ALU.mult), reads=[r_z], writes=[r_y])
                    st_t, r_st = str_.next()
                    S.op("act", lambda e, y=y, st_t=st_t: e.activation(out=junk[:], in_=y[:], func=AF.Square, accum_out=st_t[:, 0:1]),
                         reads=[r_y], writes=[r_junk, r_st])
                    S.op("act", lambda e, st_t=st_t: e.activation(out=st_t[:, 1:2], in_=st_t[:, 0:1], func=AF.Sqrt, scale=1.0 / 512, bias=EPS),
                         reads=[r_st], writes=[r_st])
                    S.op("dve", lambda e, st_t=st_t: e.reciprocal(out=st_t[:, 2:3], in_=st_t[:, 1:2]), reads=[r_st], writes=[r_st])
                    oc, r_oc = ocr.next()
                    S.op("dve", lambda e, oc=oc, y=y, st_t=st_t: e.scalar_tensor_tensor(out=oc[:], in0=y[:], scalar=st_t[:, 2:3], in1=gn[:], op0=ALU.mult, op1=ALU.mult),
                         reads=[r_y, r_st, r_p], writes=[r_oc])
                    S.dma("sp", self.MIX[s][c * 128:(c + 1) * 128, 512:1024], oc[:], reads=[r_oc], writes=[self.res("MIX", s, c, "C")])

                if with_out:
                    build_rh(0)
                    if nch > 1:
                        convs[1] = conv_chunk(1, with_out, load=True)
                    hd = head(0)
                    for c in range(nch):
                        nh = head(c + 1) if c + 1 < nch else None
                        tail(c, *hd)
                        if c + 2 < nch:
                            convs[c + 2] = conv_chunk(c + 2, with_out, load=True)
                        hd = nh
                else:
                    for c in range(nch):
                        xs, r_xs, bt, r_bt, _, _ = convs.pop(c)
                        if c + 1 < nch:
                            convs[c + 1] = conv_chunk(c + 1, with_out, load=True)
                        pb3, r_pb3 = state_mm(c, xs, r_xs, bt, r_bt, 0)
                        scan_step(Hf, r_Hf, c, 0, pb3, r_pb3)

    def _p4_end(self):
        pass

    def p5(self, l, src, streams, final):
        nc, S = self.nc, self.S
        NK2 = DFF // 128
        with ExitStack() as stw:
            W1 = self.sb(stw, "p5b_W1", [128, 8, 2 * DFF], BF16)
            W2 = self.sb(stw, "p5b_W2", [128, NK2, D], BF16)
            r_W1, r_W2 = [], []
            with ExitStack() as st:
                Wo = self.sb(st, "p5a_W", [128, 8, D], BF16)
                r_W = []
                wv = self.w_out[l].rearrange("(k p) n -> p k n", p=128)
                for k in range(8):
                    r = Res()
                    r_W.append(r)
                    S.dma("pool", Wo[:, k, :], wv[:, k, :], writes=[r])
                w1v = self.w_ffn_in[l].rearrange("(k p) n -> p k n", p=128)
                w2v = self.w_ffn_out[l].rearrange("(k p) n -> p k n", p=128)
                for k in range(8):
                    for c0 in range(0, 2 * DFF, 1408):
                        r = Res()
                        r_W1.append(r)
                        S.dma("pool", W1[:, k, c0:c0 + 1408], w1v[:, k, c0:c0 + 1408], writes=[r])
                for k in range(NK2):
                    r = Res()
                    r_W2.append(r)
                    S.dma("pool", W2[:, k, :], w2v[:, k, :], writes=[r])
                gt = self.sb(st, "p5a_gt", [128, 2, D])
                r_gt = Res()
                for s in streams:
                    S.dma("sp", gt[:, s, :], self.MOD[s, 2], writes=[r_gt])
                xr = Ring([self.sb(st, "p5a_x%d" % i, [128, D]) for i in range(3)])
                mr = Ring([self.sb(st, "p5a_m%d" % i, [128, D], BF16) for i in range(3)])
                mTr = Ring([self.sb(st, "p5a_mT%d" % i, [128, 8, 128], BF16) for i in range(3)])
                tr_ = Ring([self.sb(st, "p5a_t%d" % i, [128, D]) for i in range(2)])
                orr = Ring([self.sb(st, "p5a_o%d" % i, [128, D]) for i in range(2)])
                tiles = [(s, t) for s in streams for t in range(min(NT if s == 0 else NTC, self.opts.get('p5_nt', 99)))]

                def prep(s, t):
                    rows = slice(t * 128, (t + 1) * 128)
                    x_t, r_x = xr.next()
                    S.dma("sp", x_t[:], src[s][rows, :], writes=[r_x])
                    m_t, r_m = mr.next()
                    S.dma("sp", m_t[:], self.MIX[s][rows, :], writes=[r_m])
                    mT, r_mT = mTr.next()

                    def fin(m_t=m_t, r_m=r_m, mT=mT, r_mT=r_mT):
                        pb, r_pb = self.bank()
                        pbT = pb.bitcast(BF16)

                        def tr(e, pbT=pbT, m_t=m_t):
                            for k in range(8):
                                ins = e.transpose(out=pbT[:, k * 128:(k + 1) * 128], in_=m_t[:, k * 128:(k + 1) * 128],
                                                  identity=self.ident[:])
                            return ins
                        S.op("pe", tr, reads=[r_m, self.r_ident], writes=[r_pb])
                        S.op("act", lambda e, mT=mT, pbT=pbT: e.copy(out=mT[:], in_=pbT.rearrange("p (k t) -> p k t", k=8)),
                             reads=[], writes=[r_mT, r_pb])
                    return x_t, r_x, mT, r_mT, fin

                pend = prep(*tiles[0]) if tiles else None
                if pend:
                    pend[4]()
                for i, (s, t) in enumerate(tiles):
                    rows = slice(t * 128, (t + 1) * 128)
                    x_t, r_x, mT, r_mT, _ = pend
                    pend = prep(*tiles[i + 1]) if i + 1 < len(tiles) else None
                    t_t, r_t = tr_.next()
                    for half in range(2):
                        if half == 1 and pend:
                            pend[4]()
                        po, r_po = self.bank()

                        def mm(e, po=po, mT=mT, half=half):
                            for k in range(8):
                                ins = e.matmul(po, lhsT=mT[:, k, :], rhs=Wo[:, k, half * 512:(half + 1) * 512],
                                               start=(k == 0), stop=(k == 7))
                            return ins
                        S.op("pe", mm, reads=r_W + [r_mT], writes=[r_po])
                        S.op("dve", lambda e, t_t=t_t, po=po, half=half, s=s: e.tensor_tensor(
                            out=t_t[:, half * 512:(half + 1) * 512], in0=po, in1=gt[:, s, half * 512:(half + 1) * 512], op=ALU.mult),
                            reads=[r_gt], writes=[r_t, r_po])
                    o_t, r_o = orr.next()
                    S.op("dve", lambda e, o_t=o_t, t_t=t_t, x_t=x_t: e.tensor_tensor(out=o_t[:], in0=t_t[:], in1=x_t[:], op=ALU.add),
                         reads=[r_t, r_x], writes=[r_o])
                    S.dma("sp", self.XM[s][rows, :], o_t[:], reads=[r_o], writes=[self.res("XM", s, t)])
            S.barrier_all()
            with ExitStack() as st:
                modt = self.sb(st, "p5b_mod", [128, 3, D])
                r_mod = Res()
                gfin = None
                if final:
                    gfin = self.sb(st, "p5b_gf", [128, D])
                    r_gf = Res()
                    S.dma("sp", gfin[:], self.g_final.partition_broadcast(128), writes=[r_gf])
                xr = Ring([self.sb(st, "p5b_x%d" % i, [128, 2, D]) for i in range(2)])
                junk = self.sb(st, "p5b_junk", [128, D], BF16)
                r_junk = Res()
                stat = Ring([self.sb(st, "p5b_st%d" % i, [128, 8]) for i in range(6)])
                tmpr = Ring([self.sb(st, "p5b_t%d" % i, [128, D]) for i in range(2)])
                hr = Ring([self.sb(st, "p5b_h%d" % i, [128, D], BF16) for i in range(3)])
                hTr = Ring([self.sb(st, "p5b_hT%d" % i, [128, 8, 256], BF16) for i in range(2)])
                sgr = Ring([self.sb(st, "p5b_sg%d" % i, [128, 256]) for i in range(2)])
                actr = Ring([self.sb(st, "p5b_a%d" % i, [128, NK2, 256], BF16) for i in range(1)])
                orr = Ring([self.sb(st, "p5b_o%d" % i, [128, D]) for i in range(1)])
                groups = [(s, g0) for s in streams for g0 in range(0, min(NT if s == 0 else NTC, self.opts.get('p5_nt', 99)), 2)]
                cur_mod = [None]

                def normg(s, g0):
                    if cur_mod[0] != s:
                        cur_mod[0] = s
                        for jj, j in enumerate((3, 4, 5)):
                            S.dma("sp", modt[:, jj, :], self.MOD[s, j], writes=[r_mod])
                    x_t, r_x = xr.next()
                    S.dma("sp", x_t[:], self.XM[s][g0 * 128:(g0 + 2) * 128, :].rearrange("(t p) c -> p t c", p=128), writes=[r_x])
                    hT, r_hT = hTr.next()
                    fins = []
                    for ti in range(2):
                        st_t, r_st = stat.next()
                        S.op("act", lambda e, x_t=x_t, ti=ti, st_t=st_t: e.activation(out=junk[:], in_=x_t[:, ti, :], func=AF.Square,
                                                                                       accum_out=st_t[:, 0:1]),
                             reads=[r_x], writes=[r_junk, r_st])
                        S.op("act", lambda e, st_t=st_t: e.activation(out=st_t[:, 1:2], in_=st_t[:, 0:1], func=AF.Sqrt,
                                                                      scale=1.0 / D, bias=EPS), reads=[r_st], writes=[r_st])
                        S.op("dve", lambda e, st_t=st_t: e.reciprocal(out=st_t[:, 2:3], in_=st_t[:, 1:2]), reads=[r_st], writes=[r_st])
                        tm, r_tm = tmpr.next()
                        S.op("dve", lambda e, tm=tm, x_t=x_t, ti=ti, st_t=st_t: e.scalar_tensor_tensor(
                            out=tm[:], in0=x_t[:, ti, :], scalar=st_t[:, 2:3], in1=modt[:, 1, :], op0=ALU.mult, op1=ALU.mult),
                            reads=[r_x, r_st, r_mod], writes=[r_tm])
                        h_t, r_h = hr.next()
                        S.op("pool", lambda e, h_t=h_t, tm=tm: e.tensor_tensor(out=h_t[:], in0=tm[:], in1=modt[:, 0, :], op=ALU.add),
                             reads=[r_tm, r_mod], writes=[r_h])
                        def fin(h_t=h_t, r_h=r_h, hT=hT, r_hT=r_hT, ti=ti):
                            pb, r_pb = self.bank()
                            pbT = pb.bitcast(BF16)

                            def tr(e, pbT=pbT, h_t=h_t):
                                for k in range(8):
                                    ins = e.transpose(out=pbT[:, k * 128:(k + 1) * 128], in_=h_t[:, k * 128:(k + 1) * 128],
                                                      identity=self.ident[:])
                                return ins
                            S.op("pe", tr, reads=[r_h, self.r_ident], writes=[r_pb])
                            S.op("act", lambda e, hT=hT, ti=ti, pbT=pbT: e.copy(out=hT[:, :, ti * 128:(ti + 1) * 128],
                                                                                 in_=pbT.rearrange("p (k t) -> p k t", k=8)),
                                 reads=[], writes=[r_hT, r_pb])
                        fins.append(fin)
                    return x_t, r_x, hT, r_hT, fins

                pend = normg(*groups[0]) if groups else None
                if pend:
                    for f_ in pend[4]:
                        f_()
                for gi, (s, g0) in enumerate(groups):
                    x_t, r_x, hT, r_hT, _ = pend
                    defer = []
                    if gi + 1 < len(groups) and groups[gi + 1][0] == s:
                        pend = normg(*groups[gi + 1])
                        defer = list(pend[4])
                        late = False
                    else:
                        late = True
                    a_t, r_a = actr.next()
                    for ct in range(NK2):
                        if defer and ct in (8, 15):
                            defer.pop(0)()
                        pg, r_pg = self.bank()
                        pu, r_pu = self.bank()

                        def mm1(pb_, c0, hT=hT):
                            def f(e):
                                for k in range(8):
                                    ins = e.matmul(pb_[:, 0:256], lhsT=W1[:, k, c0:c0 + 128], rhs=hT[:, k, :],
                                                   start=(k == 0), stop=(k == 7))
                                return ins
                            return f
                        S.op("pe", mm1(pg, ct * 128), reads=r_W1 + [r_hT], writes=[r_pg])
                        S.op("pe", mm1(pu, DFF + ct * 128), reads=r_W1 + [r_hT], writes=[r_pu])
                        sg, r_sg = sgr.next()
                        S.op("act", lambda e, sg=sg, pg=pg: e.activation(out=sg[:], in_=pg[:, 0:256], func=AF.Silu),
                             reads=[], writes=[r_sg, r_pg])
                        S.op("dve", lambda e, a_t=a_t, ct=ct, pu=pu, sg=sg: e.tensor_tensor(
                            out=a_t[:, ct, :], in0=pu[:, 0:256], in1=sg[:], op=ALU.mult),
                            reads=[r_sg], writes=[r_a, r_pu])
                    for ti in range(2):
                        t = g0 + ti
                        tm, r_tm = tmpr.next()
                        for half in range(2):
                            po, r_po = self.bank()

                            def mm2(e, po=po, a_t=a_t, ti=ti, half=half):
                                for k in range(NK2):
                                    ins = e.matmul(po, lhsT=a_t[:, k, ti * 128:(ti + 1) * 128], rhs=W2[:, k, half * 512:(half + 1) * 512],
                                                   start=(k == 0), stop=(k == NK2 - 1))
                                return ins
                            S.op("pe", mm2, reads=r_W2 + [r_a], writes=[r_po])
                            S.op("dve", lambda e, tm=tm, po=po, half=half: e.tensor_tensor(
                                out=tm[:, half * 512:(half + 1) * 512], in0=po, in1=modt[:, 2, half * 512:(half + 1) * 512], op=ALU.mult),
                                reads=[r_mod], writes=[r_tm, r_po])
                        o_t, r_o = orr.next()
                        S.op("pool", lambda e, o_t=o_t, tm=tm, x_t=x_t, ti=ti: e.tensor_tensor(out=o_t[:], in0=tm[:], in1=x_t[:, ti, :], op=ALU.add),
                             reads=[r_tm, r_x], writes=[r_o])
                        rows = slice(t * 128, (t + 1) * 128)
                        if not final:
                            S.dma("sp", self.XL[s][rows, :], o_t[:], reads=[r_o], writes=[self.res("XL", s, t)])
                        else:
                            st_t, r_st = stat.next()
                            S.op("act", lambda e, o_t=o_t, st_t=st_t: e.activation(out=junk[:], in_=o_t[:], func=AF.Square,
                                                                                    accum_out=st_t[:, 0:1]),
                                 reads=[r_o], writes=[r_junk, r_st])
                            S.op("act", lambda e, st_t=st_t: e.activation(out=st_t[:, 1:2], in_=st_t[:, 0:1], func=AF.Sqrt,
                                                                          scale=1.0 / D, bias=EPS), reads=[r_st], writes=[r_st])
                            S.op("dve", lambda e, st_t=st_t: e.reciprocal(out=st_t[:, 2:3], in_=st_t[:, 1:2]), reads=[r_st], writes=[r_st])
                            f_t, r_f = tmpr.next()
                            S.op("dve", lambda e, f_t=f_t, o_t=o_t, st_t=st_t: e.scalar_tensor_tensor(
                                out=f_t[:], in0=o_t[:], scalar=st_t[:, 2:3], in1=gfin[:], op0=ALU.mult, op1=ALU.mult),
                                reads=[r_o, r_st, r_gf], writes=[r_f])
                            S.dma("sp", self.out[rows, :], f_t[:], reads=[r_f], writes=[self.res("OUT", t)])
                    while defer:
                        defer.pop(0)()
                    if late and gi + 1 < len(groups):
                        pend = normg(*groups[gi + 1])
                        for f_ in pend[4]:
                            f_()

    def build(self):
        S = self.S
        self.declare()
        phases = self.opts.get("phases")
        with ExitStack() as st:
            self.setup_common(st)
            for l in range(DEPTH):
                src = [self.x_in, self.ctx_in] if l == 0 else self.XL
                last = (l == DEPTH - 1)
                streams = (0,) if last else (1, 0)

                def run(name, fn):
                    if phases is None or (name, l) in phases:
                        fn()
                        S.barrier_all()
                run("p0", lambda: self.p0(l))
                run("p1", lambda: self.p1(l, src, do_ctx_q=not last))
                run("p2", lambda: self.p2(l, streams))
                run("p3", lambda: self.p3(l, streams))
                run("p4", lambda: self.p4(l, streams))
                run("p5", lambda: self.p5(l, src, streams, final=last))
            S.final_wait("sp")
            S.emit()
        return self.nc


def _rope_tables():
    t = np.arange(SEQ)
    rows, cols = t // 64, t % 64
    inv = (10000.0 ** (-np.arange(16, dtype=np.float32) / 16)).astype(np.float32)
    C = np.zeros((64, SEQ), np.float32)
    Sg = np.zeros((64, SEQ), np.float32)
    for blk, pos in ((0, rows), (1, cols)):
        ang = pos.astype(np.float32)[None, :] * inv[:, None]
        cs, sn = np.cos(ang).astype(np.float32), np.sin(ang).astype(np.float32)
        C[blk * 32:blk * 32 + 16] = cs
        C[blk * 32 + 16:blk * 32 + 32] = cs
        Sg[blk * 32:blk * 32 + 16] = -sn
        Sg[blk * 32 + 16:blk * 32 + 32] = sn
    C2 = np.concatenate([C, C], 0)
    S2 = np.concatenate([Sg, Sg], 0)
    return np.stack([C2 * 0.125, S2 * 0.125, C2, S2]).astype(np.float32)


def _swap_idx():
    d = np.arange(64)
    return np.where(d % 32 < 16, d + 16, d - 16)


def _w_in_ext(w_in):
    qa = np.arange(0, 256)
    qb = np.arange(256, 512)
    z = np.arange(512, 1024)
    ka = np.arange(1024, 1152)
    va = np.arange(1152, 1280)
    kb = np.arange(1280, 1536)
    vb = np.arange(1536, 1792)
    xbc = np.arange(1792, 2816)
    dt = np.arange(2816, 2832)
    sw = _swap_idx()

    def heads(base, hs):
        return np.concatenate([base[h * 64:(h + 1) * 64] for h in hs])

    def heads_sw(base, hs):
        return np.concatenate([base[h * 64:(h + 1) * 64][sw] for h in hs])
    cols = [heads(qa, (0, 2)), heads_sw(qa, (0, 2)), heads(qa, (1, 3)), heads_sw(qa, (1, 3)),
            heads(ka, (0, 1)), heads_sw(ka, (0, 1)), qb, kb, xbc, z, va, vb, dt]
    idx = np.concatenate(cols)
    assert idx.shape[0] == WCOLS
    return np.ascontiguousarray(w_in[:, :, idx])


def _bm_table(rpb):
    L = rpb.shape[0]
    krl, kc = np.divmod(np.arange(128), 64)
    qrl, qc = np.divmod(np.arange(128), 64)
    cases = [(i, j) for (i, js) in ((2, range(0, 5)), (0, range(0, 4)), (1, range(0, 4)), (30, range(28, 32)), (31, range(28, 32))) for j in js]
    out = np.full((L, 4, 21, 128, 128), NEG, np.float32)
    for ci, (i, j) in enumerate(cases):
        kr = (2 * j + krl)[:, None]
        qr = (2 * i + qrl)[None, :]
        s_ = np.clip(qr - 4, 0, 56)
        vrow = (kr >= s_) & (kr <= s_ + 7)
        cst = np.clip(qc - 8, 0, 48)[None, :]
        vcol = (kc[:, None] >= cst) & (kc[:, None] < cst + 16)
        valid = vrow & vcol
        dy = np.clip(kr - qr + 7, 0, 14)
        dx = np.clip(kc[:, None] - qc[None, :] + 15, 0, 30)
        dyb, dxb = np.broadcast_arrays(dy, dx)
        g = rpb[:, :, dyb, dxb]
        out[:, :, ci] = np.where(valid[None, None], g, np.float32(NEG))
    return np.ascontiguousarray(out.transpose(0, 3, 1, 2, 4).reshape(L, 128, 84 * 128))


def prep_inputs(inputs, n_cores):
    f = lambda a: np.ascontiguousarray(np.asarray(a, dtype=np.float32))
    x, c, ctx, c_ctx = f(inputs["x"]), f(inputs["c"]), f(inputs["ctx"]), f(inputs["c_ctx"])
    shared = {
        "w_mod": f(inputs["w_mod"]), "b_mod": f(inputs["b_mod"]), "g_mix": f(inputs["g_mix"]), "g_ffn": f(inputs["g_ffn"]),
        "w_in_ext": _w_in_ext(f(inputs["w_in"])), "rope": _rope_tables(),
        "w_out": f(inputs["w_out"]), "w_ffn_in": f(inputs["w_ffn_in"]), "w_ffn_out": f(inputs["w_ffn_out"]),
        "g_final": f(inputs["g_final"]), "wa_sink": f(inputs["wa_sink"]), "bm_tab": _bm_table(f(inputs["na_rpb"])),
        "conv_w_l": np.ascontiguousarray(f(inputs["ssm_conv_w"]).reshape(DEPTH, 7, 8, 128).transpose(0, 3, 2, 1)),
        "conv_b_l": np.ascontiguousarray(f(inputs["ssm_conv_b"]).reshape(DEPTH, 8, 128).transpose(0, 2, 1)),
        "conv_brow": f(inputs["ssm_conv_b"]).reshape(DEPTH, 1, 1024),
        "dt_bias": f(inputs["ssm_dt_bias"]).reshape(DEPTH, 16), "a_log": f(inputs["ssm_a_log"]).reshape(DEPTH, 16),
        "ssm_d": f(inputs["ssm_d"]), "ssm_g": f(inputs["ssm_norm_g"]),
    }
    maps = []
    for i in range(n_cores):
        b = i % 4
        cvec = np.concatenate([c[b].reshape(8, 128).T, c_ctx.reshape(8, 128).T], 1)
        m = dict(shared)
        m.update({"x": x[b], "ctx": ctx[b], "cvec": np.ascontiguousarray(cvec)})
        maps.append(m)
    return maps


N_CORES = 4


def kernel(**inputs):
    nc = Builder().build()
    maps = prep_inputs(inputs, N_CORES)
    res = run_bass_kernel_spmd(nc, maps, core_ids=list(range(N_CORES)))
    out = np.stack([res.results[b]["out"] for b in range(4)], 0)
    return out.astype(np.float32)
```

```python
import numpy as np
from contextlib import ExitStack
import concourse.bass as bass
import concourse.mybir as mybir
from concourse.bass_utils import run_bass_kernel_spmd

F32 = mybir.dt.float32
BF16 = mybir.dt.bfloat16
AF = mybir.ActivationFunctionType
ALU = mybir.AluOpType
AX = mybir.AxisListType

D = 1024
SEQ = 4096
LC = 256
DEPTH = 2
NT = SEQ // 128
NTC = LC // 128
EPS = 1e-6
DFF = 2816
NFM = 18
NFMO = 15
TMC = 912
WCOLS = NFM * 128 + TMC
NEG = -30000.0
VS = 66


class Res:
    __slots__ = ("name", "w", "rs")

    def __init__(self, name=""):
        self.name = name
        self.w = None
        self.rs = []


class Sched:
    ENGS = ("pe", "act", "dve", "pool", "sp")
    NDMA = 40
    NSDMA = 16

    def __init__(self, nc):
        self.nc = nc
        self.prog = {e: [] for e in self.ENGS}
        self.count = {}
        self.known = {e: {} for e in self.ENGS}
        self.dma_i = 0
        self.sdma_i = 0

    def _deps(self, eng, reads, writes):
        waits = {}

        def add(sv):
            if sv is None:
                return
            s, v = sv
            if eng == "pe" and s == "pe":
                return
            if waits.get(s, 0) < v:
                waits[s] = v
        for r in reads:
            add(r.w)
        for w in writes:
            add(w.w)
            for x in w.rs:
                add(x)
        out = []
        kn = self.known[eng]
        for s, v in waits.items():
            if kn.get(s, 0) < v:
                kn[s] = v
                out.append((s, v))
        return out

    def _mark(self, tag, reads, writes):
        for r in reads:
            r.rs.append(tag)
        for w in writes:
            w.w = tag
            w.rs = []

    def op(self, eng, fn, reads=(), writes=()):
        waits = self._deps(eng, reads, writes)
        c = self.count.get(eng, 0) + 1
        self.count[eng] = c
        self.prog[eng].append((waits, fn, (eng, 1)))
        self._mark((eng, c), reads, writes)

    def dma(self, q, out, in_, reads=(), writes=(), **kw):
        if q == "pool":
            slot = "sdma%d" % (self.sdma_i % self.NSDMA)
            self.sdma_i += 1
        else:
            slot = "dma%d" % (self.dma_i % self.NDMA)
            self.dma_i += 1
        waits = self._deps(q, reads, writes)
        prev = self.count.get(slot, 0)
        kn = self.known[q]
        if prev and kn.get(slot, 0) < prev:
            kn[slot] = prev
            waits.append((slot, prev))
        c = prev + 16
        self.count[slot] = c

        def fn(e, out=out, in_=in_, kw=kw):
            return e.dma_start(out=out, in_=in_, **kw)
        self.prog[q].append((waits, fn, (slot, 16)))
        self._mark((slot, c), reads, writes)

    def barrier_all(self):
        allv = list(self.count.items())
        for e in self.ENGS:
            kn = self.known[e]
            waits = []
            for s, v in allv:
                if s == e:
                    continue
                if kn.get(s, 0) < v:
                    kn[s] = v
                    waits.append((s, v))
            if waits:
                self.prog[e].append((waits, None, None))

    def final_wait(self, eng="sp"):
        waits = []
        kn = self.known[eng]
        for s, v in self.count.items():
            if s != eng and kn.get(s, 0) < v:
                kn[s] = v
                waits.append((s, v))
        self.prog[eng].append((waits, None, None))

    def emit(self):
        nc = self.nc
        with ExitStack() as es:
            sems = {}
            for s in self.count:
                sems[s] = es.enter_context(nc.semaphore(s))
            block = es.enter_context(nc.Block())

            def replay(name, e):
                for waits, fn, inc in self.prog[name]:
                    for s, v in waits:
                        e.wait_ge(sems[s], v)
                    if fn is not None:
                        ins = fn(e)
                        if inc is not None:
                            ins.then_inc(sems[inc[0]], inc[1])

            @block.tensor
            def _(e):
                replay("pe", e)

            @block.scalar
            def _(e):
                replay("act", e)

            @block.vector
            def _(e):
                replay("dve", e)

            @block.gpsimd
            def _(e):
                replay("pool", e)

            @block.sync
            def _(e):
                replay("sp", e)


class Ring:
    def __init__(self, aps, name=""):
        self.items = [(a, Res("%s%d" % (name, i))) for i, a in enumerate(aps)]
        self.i = 0

    def next(self):
        it = self.items[self.i % len(self.items)]
        self.i += 1
        return it


class Builder:
    def __init__(self, debug=(), stop_after=None, opts=None):
        self.opts = opts or {}
        self.debug = set(debug)
        self.stop_after = stop_after
        self.nc = bass.Bass("TRN2", target_bir_lowering=False)
        self.S = Sched(self.nc)
        self.es = ExitStack()
        self.resmap = {}

    def din(self, name, shape, dt=F32):
        return self.nc.dram_tensor(name, list(shape), dt, kind="ExternalInput").ap()

    def dscr(self, name, shape, dt=F32):
        kind = "ExternalOutput" if name in self.debug else "Internal"
        if name in self.opts.get("inject", ()):
            kind = "ExternalInput"
        return self.nc.dram_tensor(name, list(shape), dt, kind=kind).ap()

    def res(self, *key):
        r = self.resmap.get(key)
        if r is None:
            r = Res(str(key))
            self.resmap[key] = r
        return r

    def sb(self, st, name, shape, dt=F32):
        self.uid = getattr(self, "uid", 0) + 1
        return st.enter_context(self.nc.sbuf_tensor("%s_%d" % (name, self.uid), list(shape), dt))

    def declare(self):
        self.x_in = self.din("x", [SEQ, D])
        self.ctx_in = self.din("ctx", [LC, D])
        self.cvec = self.din("cvec", [128, 16])
        self.w_mod = self.din("w_mod", [DEPTH, D, 6 * D])
        self.b_mod = self.din("b_mod", [DEPTH, 6 * D])
        self.g_mix = self.din("g_mix", [DEPTH, D])
        self.g_ffn = self.din("g_ffn", [DEPTH, D])
        self.w_in = self.din("w_in_ext", [DEPTH, D, WCOLS])
        self.rope = self.din("rope", [4, 128, SEQ])
        self.w_out = self.din("w_out", [DEPTH, D, D])
        self.w_ffn_in = self.din("w_ffn_in", [DEPTH, D, 2 * DFF])
        self.w_ffn_out = self.din("w_ffn_out", [DEPTH, DFF, D])
        self.g_final = self.din("g_final", [D])
        self.wa_sink = self.din("wa_sink", [DEPTH, 4])
        self.conv_w = self.din("conv_w_l", [DEPTH, 128, 8, 7])
        self.conv_b = self.din("conv_b_l", [DEPTH, 128, 8])
        self.conv_brow = self.din("conv_brow", [DEPTH, 1, 1024])
        self.dt_bias = self.din("dt_bias", [DEPTH, 16])
        self.a_log = self.din("a_log", [DEPTH, 16])
        self.ssm_d = self.din("ssm_d", [DEPTH, 8])
        self.ssm_g = self.din("ssm_g", [DEPTH, 512])
        self.bm_tab = self.din("bm_tab", [DEPTH, 128, 84 * 128])
        self.out = self.nc.dram_tensor("out", [SEQ, D], F32, kind="ExternalOutput").ap()
        self.MOD = self.dscr("MOD", [2, 6, 128, D])
        self.FM = [self.dscr("FM_l", [NFMO, 128, SEQ], BF16), self.dscr("FM_c", [NFMO, 128, LC], BF16)]
        self.ZS = [self.dscr("ZS_l", [SEQ, 512], BF16), self.dscr("ZS_c", [LC, 512], BF16)]
        self.VA = [self.dscr("VA_l", [SEQ, 2 * VS], BF16), self.dscr("VA_c", [LC, 2 * VS], BF16)]
        self.VB = [self.dscr("VB_l", [SEQ, 4 * VS], BF16), self.dscr("VB_c", [LC, 4 * VS], BF16)]
        self.DT = [self.dscr("DT_l", [SEQ, 16]), self.dscr("DT_c", [LC, 16])]
        self.MIX = [self.dscr("MIX_l", [SEQ, D], BF16), self.dscr("MIX_c", [LC, D], BF16)]
        self.XSS = [self.dscr("XSS_l", [SEQ, 768], BF16), self.dscr("XSS_c", [LC, 768], BF16)]
        self.XM = [self.dscr("XM_l", [SEQ, D]), self.dscr("XM_c", [LC, D])]
        self.XL = [self.dscr("XL_l", [SEQ, D]), self.dscr("XL_c", [LC, D])]

    def setup_common(self, st):
        nc, S = self.nc, self.S
        self.ps = st.enter_context(nc.psum_tensor("ps", [128, 4096], F32))
        self.psb = [(self.ps[:, b * 512:(b + 1) * 512], Res("bank%d" % b)) for b in range(8)]
        self.ident = self.sb(st, "ident", [128, 128], BF16)
        self.r_ident = Res("ident")
        ident = self.ident

        S.op("pool", lambda e: e.memset(ident[:], 0.0), writes=[self.r_ident])
        S.op("pool", lambda e: e.affine_select(out=ident[:], in_=ident[:], pattern=[[-1, 128]], compare_op=ALU.not_equal,
                                               fill=1.0, base=0, channel_multiplier=1),
             reads=[self.r_ident], writes=[self.r_ident])
        self.maskP = self.sb(st, "maskP", [128, 128], BF16)
        self.maskN = self.sb(st, "maskN", [128, 128], BF16)
        self.r_mask = Res("mask")
        mP, mN = self.maskP, self.maskN
        r1, r2 = Res(), Res()
        S.op("pool", lambda e: e.memset(mP[:], 0.0), writes=[r1])
        S.op("pool", lambda e: e.memset(mN[:], 0.0), writes=[r2])
        S.op("pool", lambda e: e.affine_select(out=mP[:], in_=mP[:], pattern=[[-1, 128]], compare_op=ALU.is_ge,
                                               fill=NEG, base=0, channel_multiplier=1), reads=[r1], writes=[r1])
        S.op("pool", lambda e: e.affine_select(out=mN[:], in_=mN[:], pattern=[[1, 128]], compare_op=ALU.is_ge,
                                               fill=NEG, base=0, channel_multiplier=-1), reads=[r2], writes=[r2])
        S.op("pool", lambda e: e.memset(self.ident[0:1, 0:1], 1.0), reads=[r1, r2, self.r_ident], writes=[self.r_mask, self.r_ident])
        self.U32 = self.sb(st, "U32", [128, 128])
        self.L32 = self.sb(st, "L32", [128, 128])
        self.ones32 = self.sb(st, "ones32", [128, 128])
        self.r_tri = Res("tri")
        U32, L32, ones32 = self.U32, self.L32, self.ones32
        r3, r4 = Res(), Res()
        S.op("pool", lambda e: e.memset(U32[:], 1.0), writes=[r3])
        S.op("pool", lambda e: e.memset(L32[:], 1.0), writes=[r4])
        S.op("pool", lambda e: e.memset(ones32[:], 1.0), writes=[self.r_tri])
        S.op("pool", lambda e: e.affine_select(out=U32[:], in_=U32[:], pattern=[[1, 128]], compare_op=ALU.is_ge,
                                               fill=0.0, base=0, channel_multiplier=-1), reads=[r3], writes=[r3])
        S.op("pool", lambda e: e.affine_select(out=L32[:], in_=L32[:], pattern=[[-1, 128]], compare_op=ALU.is_ge,
                                               fill=0.0, base=0, channel_multiplier=1), reads=[r4], writes=[r4])
        S.op("pool", lambda e: e.memset(ones32[0:1, 0:1], 1.0), reads=[r3, r4, self.r_tri], writes=[self.r_tri])
        self.bank_i = 0

    def bank(self):
        b = self.psb[self.bank_i % 8]
        self.bank_i += 1
        return b

    def p0(self, l):
        nc, S = self.nc, self.S
        with ExitStack() as st:
            cv = self.sb(st, "p0_cv", [128, 16])
            scv = self.sb(st, "p0_scv", [128, 16])
            scbc = self.sb(st, "p0_scbc", [128, 16, 128])
            gbc = self.sb(st, "p0_gbc", [128, 2, D])
            wb = [self.sb(st, "p0_w%d" % i, [128, 8, 512]) for i in range(2)]
            bb = [self.sb(st, "p0_b%d" % i, [128, 512]) for i in range(2)]
            mt = [self.sb(st, "p0_m%d" % i, [128, D]) for i in range(4)]
            r_cv, r_scv, r_scbc, r_g = Res(), Res(), Res(), Res()
            wring = Ring(wb, "p0w")
            bring = Ring(bb, "p0b")
            mring = Ring(mt, "p0m")
            S.dma("sp", cv[:], self.cvec, writes=[r_cv])
            S.dma("sp", gbc[:, 0, :], self.g_mix[l].partition_broadcast(128), writes=[r_g])
            S.dma("sp", gbc[:, 1, :], self.g_ffn[l].partition_broadcast(128), writes=[r_g])
            S.op("act", lambda e: e.activation(out=scv[:], in_=cv[:], func=AF.Silu), reads=[r_cv], writes=[r_scv])
            S.op("dve", lambda e: e.tensor_copy(out=scbc[:], in_=scv[:].unsqueeze(2).to_broadcast([128, 16, 128])),
                 reads=[r_scv], writes=[r_scbc])
            wv = self.w_mod[l].rearrange("(k p) n -> p k n", p=128)
            cur = {}
            for blk in range(12):
                j, half = blk // 2, blk % 2
                w_t, r_w = wring.next()
                b_t, r_b = bring.next()
                S.dma("sp", w_t[:], wv[:, :, blk * 512:(blk + 1) * 512], writes=[r_w])
                S.dma("sp", b_t[:], self.b_mod[l, blk * 512:(blk + 1) * 512].partition_broadcast(128), writes=[r_b])
                for s in range(2):
                    if half == 0:
                        cur[s] = mring.next()
                    m_t, r_m = cur[s]
                    pb, r_pb = self.bank()

                    def mm(e, pb=pb, w_t=w_t, s=s):
                        for k in range(8):
                            ins = e.matmul(pb, lhsT=scbc[:, s * 8 + k, :], rhs=w_t[:, k, :], start=(k == 0), stop=(k == 7))
                        return ins
                    S.op("pe", mm, reads=[r_scbc, r_w], writes=[r_pb])
                    dst = m_t[:, half * 512:(half + 1) * 512]
                    if j in (1, 4):
                        gsl = gbc[:, 0 if j == 1 else 1, half * 512:(half + 1) * 512]
                        tmp_r = Res()

                        def ev(e, dst=dst, pb=pb, b_t=b_t, gsl=gsl):
                            e.tensor_tensor(out=dst, in0=pb, in1=b_t[:], op=ALU.add)
                            return e.scalar_tensor_tensor(out=dst, in0=dst, scalar=1.0, in1=gsl, op0=ALU.add, op1=ALU.mult)
                        S.op("dve", lambda e, dst=dst, pb=pb, b_t=b_t: e.tensor_tensor(out=dst, in0=pb, in1=b_t[:], op=ALU.add),
                             reads=[r_pb, r_b], writes=[r_m])
                        S.op("dve", lambda e, dst=dst, gsl=gsl: e.scalar_tensor_tensor(out=dst, in0=dst, scalar=1.0, in1=gsl,
                                                                                         op0=ALU.add, op1=ALU.mult),
                             reads=[r_m, r_g], writes=[r_m])
                    else:
                        S.op("dve", lambda e, dst=dst, pb=pb, b_t=b_t: e.tensor_tensor(out=dst, in0=pb, in1=b_t[:], op=ALU.add),
                             reads=[r_pb, r_b], writes=[r_m])
                    if half == 1:
                        S.dma("sp", self.MOD[s, j], m_t[:], reads=[r_m], writes=[self.res("MOD", s, j)])

    def p1(self, l, src, do_ctx_q):
        nc, S = self.nc, self.S
        with ExitStack() as st:
            W = self.sb(st, "p1_W", [128, 8, WCOLS], BF16)
            r_Wall = []
            wv = self.w_in[l].rearrange("(k p) n -> p k n", p=128)
            for k in range(8):
                for c0 in range(0, WCOLS, 1608):
                    r = Res()
                    r_Wall.append(r)
                    S.dma("pool", W[:, k, c0:c0 + 1608], wv[:, k, c0:c0 + 1608], writes=[r])
            modt = self.sb(st, "p1_mod", [128, 2, 2, D])
            r_mod = Res("p1mod")
            for s in range(2):
                for jj, j in enumerate((0, 1)):
                    S.dma("sp", modt[:, s, jj, :], self.MOD[s, j], reads=[self.res("MOD", s, j)], writes=[r_mod])
            xr = Ring([self.sb(st, "p1_x%d" % i, [128, D]) for i in range(3)], "p1x")
            junk = self.sb(st, "p1_junk", [128, D], BF16)
            r_junk = Res()
            stat = Ring([self.sb(st, "p1_st%d" % i, [128, 4]) for i in range(3)], "p1st")
            tmpr = Ring([self.sb(st, "p1_t%d" % i, [128, D]) for i in range(2)], "p1t")
            hr = Ring([self.sb(st, "p1_h%d" % i, [128, D], BF16) for i in range(8)], "p1h")
            hTr = Ring([self.sb(st, "p1_hT%d" % i, [128, 8, 512], BF16) for i in range(2)], "p1hT")
            ropr = Ring([self.sb(st, "p1_rp%d" % i, [128, 4, 512]) for i in range(2)], "p1rp")
            rtr = Ring([self.sb(st, "p1_rt%d" % i, [128, 2, 512]) for i in range(3)], "p1rt")
            fmr = Ring([self.sb(st, "p1_fm%d" % i, [128, NFMO, 512], BF16) for i in range(2)], "p1fm")
            zr = Ring([self.sb(st, "p1_z%d" % i, [128, 4, 512], BF16) for i in range(2)], "p1z")
            var_ = [self.sb(st, "p1_va%d" % i, [128, 4, 2, VS], BF16) for i in range(2)]
            vbr_ = [self.sb(st, "p1_vb%d" % i, [128, 4, 4, VS], BF16) for i in range(2)]
            dtr = Ring([self.sb(st, "p1_dt%d" % i, [128, 4, 16]) for i in range(2)], "p1dt")
            var = Ring(var_, "p1va")
            vbr = Ring(vbr_, "p1vb")
            for (t_, r_) in var.items + vbr.items:
                S.op("pool", lambda e, t_=t_: e.memset(t_[:], 1.0), writes=[r_])

            groups = [(1, 0, NTC)] + [(0, g * 4, 4) for g in range(NT // 4)]
            lim = self.opts.get('p1_lim', 9)
            groups = groups[:self.opts.get('p1_groups', 99)]
            if lim == 0:
                groups = []
            def norm_group(grp):
                (s, t0, ntile) = grp
                TG = ntile * 128
                tok0 = t0 * 128
                hT, r_hT = hTr.next()
                fins = []
                G1 = modt[:, s, 1, :]
                SH1 = modt[:, s, 0, :]
                for ti in range(ntile):
                    x_t, r_x = xr.next()
                    S.dma("sp", x_t[:], src[s][(t0 + ti) * 128:(t0 + ti + 1) * 128, :],
                          reads=[self.res("XL", s, t0 + ti)], writes=[r_x])
                    st_t, r_st = stat.next()
                    S.op("act", lambda e, x_t=x_t, st_t=st_t: e.activation(out=junk[:], in_=x_t[:], func=AF.Square,
                                                                            accum_out=st_t[:, 0:1]),
                         reads=[r_x], writes=[r_junk, r_st])
                    S.op("act", lambda e, st_t=st_t: e.activation(out=st_t[:, 1:2], in_=st_t[:, 0:1], func=AF.Sqrt,
                                                                  scale=1.0 / D, bias=EPS),
                         reads=[r_st], writes=[r_st])
                    S.op("dve", lambda e, st_t=st_t: e.reciprocal(out=st_t[:, 2:3], in_=st_t[:, 1:2]), reads=[r_st], writes=[r_st])
                    tm, r_tm = tmpr.next()
                    S.op("dve", lambda e, tm=tm, x_t=x_t, st_t=st_t, G1=G1: e.scalar_tensor_tensor(
                        out=tm[:], in0=x_t[:], scalar=st_t[:, 2:3], in1=G1, op0=ALU.mult, op1=ALU.mult),
                        reads=[r_x, r_st, r_mod], writes=[r_tm])
                    h_t, r_h = hr.next()
                    S.op("pool", lambda e, h_t=h_t, tm=tm, SH1=SH1: e.tensor_tensor(out=h_t[:], in0=tm[:], in1=SH1, op=ALU.add),
                         reads=[r_tm, r_mod], writes=[r_h])
                    def fin(h_t=h_t, r_h=r_h, hT=hT, r_hT=r_hT, ti=ti):
                        pb, r_pb = self.bank()
                        pbT = pb.bitcast(BF16)

                        def tr(e, pbT=pbT, h_t=h_t):
                            for k in range(8):
                                ins = e.transpose(out=pbT[:, k * 128:(k + 1) * 128], in_=h_t[:, k * 128:(k + 1) * 128],
                                                  identity=self.ident[:])
                            return ins
                        S.op("pe", tr, reads=[r_h, self.r_ident], writes=[r_pb])
                        S.op("act", lambda e, hT=hT, ti=ti, pbT=pbT: e.copy(out=hT[:, :, ti * 128:(ti + 1) * 128],
                                                                             in_=pbT.rearrange("p (k t) -> p k t", k=8)),
                             reads=[], writes=[r_hT, r_pb])
                    fins.append(fin)
                return hT, r_hT, fins

            pend = norm_group(groups[0]) if groups else None
            if pend:
                for f_ in pend[2]:
                    f_()
            for gi, (s, t0, ntile) in enumerate(groups):
                TG = ntile * 128
                tok0 = t0 * 128
                hT, r_hT, _ = pend
                pend = norm_group(groups[gi + 1]) if gi + 1 < len(groups) else None
                defer = list(pend[2]) if pend else []
                if lim <= 1:
                    continue
                fm, r_fm = fmr.next()
                if s == 0:
                    rp, r_rp = ropr.next()
                    S.dma("sp", rp[:], self.rope[:, :, tok0:tok0 + TG].rearrange("c p t -> p c t"), writes=[r_rp])

                def fm_mm(ct, pb, TG=TG, hT=hT):
                    def f(e):
                        for k in range(8):
                            ins = e.matmul(pb[:, 0:TG], lhsT=W[:, k, ct * 128:(ct + 1) * 128], rhs=hT[:, k, 0:TG],
                                           start=(k == 0), stop=(k == 7))
                        return ins
                    return f
                for (ct, slot, ci) in ((0, 0, 0), (2, 1, 0), (4, 2, 2)):
                    pq, r_pq = self.bank()
                    S.op("pe", fm_mm(ct, pq), reads=r_Wall + [r_hT], writes=[r_pq])
                    if s == 0:
                        psw, r_psw = self.bank()
                        S.op("pe", fm_mm(ct + 1, psw), reads=r_Wall + [r_hT], writes=[r_psw])
                        rt, r_rt = rtr.next()
                        S.op("dve", lambda e, rt=rt, pq=pq, rp=rp, ci=ci, TG=TG: e.tensor_tensor(
                            out=rt[:, 0, 0:TG], in0=pq[:, 0:TG], in1=rp[:, ci, 0:TG], op=ALU.mult),
                            reads=[r_pq, r_rp], writes=[r_rt])
                        S.op("dve", lambda e, rt=rt, psw=psw, rp=rp, ci=ci, TG=TG: e.tensor_tensor(
                            out=rt[:, 1, 0:TG], in0=psw[:, 0:TG], in1=rp[:, ci + 1, 0:TG], op=ALU.mult),
                            reads=[r_psw, r_rp], writes=[r_rt])
                        S.op("pool", lambda e, rt=rt, fm=fm, slot=slot, TG=TG: e.tensor_tensor(
                            out=fm[:, slot, 0:TG], in0=rt[:, 0, 0:TG], in1=rt[:, 1, 0:TG], op=ALU.add),
                            reads=[r_rt], writes=[r_fm])
                    else:
                        sc_ = 0.125 if ct < 4 else 1.0
                        S.op("act", lambda e, fm=fm, slot=slot, pq=pq, TG=TG, sc_=sc_: e.activation(
                            out=fm[:, slot, 0:TG], in_=pq[:, 0:TG], func=AF.Copy, scale=sc_),
                            reads=[r_pq], writes=[r_fm])
                for ct in range(6, NFM):
                    if defer and ct in (8, 11, 14, 17):
                        defer.pop(0)()
                    slot = ct - 3
                    pq, r_pq = self.bank()
                    S.op("pe", fm_mm(ct, pq), reads=r_Wall + [r_hT], writes=[r_pq])
                    sc_ = 0.125 if ct < 8 else 1.0
                    if ct % 2 == 0:
                        S.op("act", lambda e, fm=fm, slot=slot, pq=pq, TG=TG, sc_=sc_: e.activation(
                            out=fm[:, slot, 0:TG], in_=pq[:, 0:TG], func=AF.Copy, scale=sc_),
                            reads=[r_pq], writes=[r_fm])
                    else:
                        S.op("dve", lambda e, fm=fm, slot=slot, pq=pq, TG=TG, sc_=sc_: e.tensor_scalar(
                            out=fm[:, slot, 0:TG], in0=pq[:, 0:TG], scalar1=sc_, scalar2=None, op0=ALU.mult),
                            reads=[r_pq], writes=[r_fm])
                for c0 in range(0, NFMO, 5):
                    S.dma("sp", self.FM[s][c0:c0 + 5, :, tok0:tok0 + TG].rearrange("c p t -> p c t"), fm[:, c0:c0 + 5, 0:TG],
                          reads=[r_fm], writes=[self.res("FM", s, t0 // 4, c0)])
                while defer:
                    defer.pop(0)()
                if lim <= 2:
                    continue
                z_t, r_z = zr.next()
                va_t, r_va = var.next()
                vb_t, r_vb = vbr.next()
                dt_t, r_dt = dtr.next()
                tmm = self.opts.get('tm_mask', 7)
                for ti in range(ntile):
                    def tm_mm(pb, c0, n, hT=hT, ti=ti):
                        def f(e):
                            for k in range(8):
                                ins = e.matmul(pb[:, 0:n], lhsT=hT[:, k, ti * 128:(ti + 1) * 128],
                                               rhs=W[:, k, c0:c0 + n], start=(k == 0), stop=(k == 7))
                            return ins
                        return f
                    if not (tmm & 1):
                        continue
                    pz, r_pz = self.bank()
                    S.op("pe", tm_mm(pz, NFM * 128, 512), reads=r_Wall + [r_hT], writes=[r_pz])
                    S.op("act", lambda e, z_t=z_t, ti=ti, pz=pz: e.activation(out=z_t[:, ti, :], in_=pz, func=AF.Silu),
                         reads=[r_pz], writes=[r_z])
                    if not (tmm & 2):
                        continue
                    pv, r_pv = self.bank()
                    S.op("pe", tm_mm(pv, NFM * 128 + 512, 400), reads=r_Wall + [r_hT], writes=[r_pv])
                    if not (tmm & 8):
                      S.op("dve", lambda e, va_t=va_t, ti=ti, pv=pv: e.tensor_copy(
                        out=va_t[:, ti, :, 0:64], in_=pv[:, 0:128].rearrange("p (g d) -> p g d", g=2)),
                        reads=[r_pv], writes=[r_va, r_pv])
                    if not (tmm & 16):
                      S.op("dve", lambda e, vb_t=vb_t, ti=ti, pv=pv: e.tensor_copy(
                        out=vb_t[:, ti, :, 0:64], in_=pv[:, 128:384].rearrange("p (g d) -> p g d", g=4)),
                        reads=[r_pv], writes=[r_vb, r_pv])
                    if not (tmm & 32):
                      S.op("act", lambda e, dt_t=dt_t, ti=ti, pv=pv: e.copy(out=dt_t[:, ti, :], in_=pv[:, 384:400]),
                         reads=[r_pv], writes=[r_dt, r_pv])
                if not (tmm & 4):
                    continue
                rows = slice(tok0, tok0 + TG)
                gk = t0 // 4
                S.dma("sp", self.ZS[s][rows, :].rearrange("(t p) c -> p t c", p=128), z_t[:, 0:ntile, :],
                      reads=[r_z], writes=[self.res("ZS", s, gk)])
                S.dma("sp", self.VA[s][rows, :].rearrange("(t p) c -> p t c", p=128),
                      va_t[:, 0:ntile].rearrange("p t g d -> p t (g d)"), reads=[r_va], writes=[self.res("VA", s, gk)])
                S.dma("sp", self.VB[s][rows, :].rearrange("(t p) c -> p t c", p=128),
                      vb_t[:, 0:ntile].rearrange("p t g d -> p t (g d)"), reads=[r_vb], writes=[self.res("VB", s, gk)])
                S.dma("sp", self.DT[s][rows, :].rearrange("(t p) c -> p t c", p=128), dt_t[:, 0:ntile, :],
                      reads=[r_dt], writes=[self.res("DT", s, gk)])

    @staticmethod
    def nb_window(i):
        s0 = min(max(2 * i - 4, 0), 56)
        s1 = min(max(2 * i + 1 - 4, 0), 56)
        return list(range(s0 // 2, (s1 + 7) // 2 + 1))

    @staticmethod
    def bm_index(i, j):
        if 2 <= i <= 29:
            return j - i + 2
        if i == 0:
            return 5 + j
        if i == 1:
            return 9 + j
        if i == 30:
            return 13 + (j - 28)
        return 17 + (j - 28)

    def p_attn(self, l, kind, streams):
        nc, S = self.nc, self.S
        isA = kind == "A"
        nkt = 1 if isA else 2
        ng = 2 if isA else 4
        qs0 = 0 if isA else 3
        ks0 = 2 if isA else 5
        VSRC = self.VA if isA else self.VB
        col0 = 0 if isA else 256
        with ExitStack() as st:
            QT = self.sb(st, "at_Q", [128, 2, 2, SEQ], BF16)
            QTc = self.sb(st, "at_Qc", [128, 2, 2, LC], BF16)
            KT = self.sb(st, "at_K", [128, nkt, SEQ + LC], BF16)
            V = self.sb(st, "at_V", [128, NT + NTC, ng * VS], BF16)
            r_in = Res()
            for T in range(2):
                for hh in range(2):
                    lo, zl = hh * 64, (1 - hh) * 64
                    S.op("pool", lambda e, T=T, hh=hh, zl=zl: e.memset(QT[zl:zl + 64, T, hh, :], 0.0), writes=[r_in])
                    S.op("pool", lambda e, T=T, hh=hh, zl=zl: e.memset(QTc[zl:zl + 64, T, hh, :], 0.0), writes=[r_in])
                    S.dma("sp", QT[lo:lo + 64, T, hh, :], self.FM[0][qs0 + T][lo:lo + 64, :], writes=[r_in])
                    S.dma("sp", QTc[lo:lo + 64, T, hh, :], self.FM[1][qs0 + T][lo:lo + 64, :], writes=[r_in])
            for kt in range(nkt):
                S.dma("sp", KT[:, kt, 0:SEQ], self.FM[0][ks0 + kt], writes=[r_in])
                S.dma("sp", KT[:, kt, SEQ:SEQ + LC], self.FM[1][ks0 + kt], writes=[r_in])
            for t0 in range(0, NT, 8):
                S.dma("sp", V[:, t0:t0 + 8, :], VSRC[0][t0 * 128:(t0 + 8) * 128, :].rearrange("(t p) c -> p t c", p=128), writes=[r_in])
            S.dma("sp", V[:, NT:NT + NTC, :], VSRC[1].rearrange("(t p) c -> p t c", p=128), writes=[r_in])
            r_c = Res()
            if isA:
                esk = self.sb(st, "at_esk", [128, 4])
                S.dma("sp", esk[:], self.wa_sink[l].partition_broadcast(128), writes=[r_c])
                S.op("act", lambda e: e.activation(out=esk[:], in_=esk[:], func=AF.Exp), reads=[r_c], writes=[r_c])
            else:
                BM = self.sb(st, "at_BM", [128, 84 * 128], BF16)
                for c0 in range(0, 84 * 128, 1792):
                    S.dma("pool", BM[:, c0:c0 + 1792], self.bm_tab[l][:, c0:c0 + 1792], writes=[r_c])
            Sring = Ring([self.ps[:, i * 1024:(i + 1) * 1024] for i in range(3)])
            Oring = Ring([self.ps[:, 3072 + i * 512:3072 + (i + 1) * 512] for i in range(2)])
            PTr = Ring([self.sb(st, "at_PT%d" % i, [128, 7 * 128], BF16) for i in range(3)])
            mor = Ring([self.sb(st, "at_mo%d" % i, [128, 4, 64], BF16) for i in range(3)])
            dnr = Ring([self.sb(st, "at_dn%d" % i, [128, 8]) for i in range(3)])
            for s in streams:
                ntl = min(NT if s == 0 else NTC, self.opts.get("at_nt", 99))
                for n in range(ntl):
                    O, r_O = Oring.next()
                    pend_pv = []
                    for T in range(2):
                        for hh in range(2):
                            h = (2 * hh + T) if isA else (2 * T + hh)
                            g = hh if isA else h
                            kt = 0 if isA else T
                            ps_ = slice(hh * 64, (hh + 1) * 64)
                            chunks = []
                            if s == 0:
                                if isA:
                                    if n > 0:
                                        chunks.append(((n - 1) * 128, self.maskP[:], n - 1))
                                    chunks.append((n * 128, None, n))
                                    if n < NT - 1:
                                        chunks.append(((n + 1) * 128, self.maskN[:], n + 1))
                                else:
                                    for j in self.nb_window(n):
                                        bi = h * 21 + self.bm_index(n, j)
                                        chunks.append((j * 128, BM[:, bi * 128:(bi + 1) * 128], j))
                            chunks.append((SEQ, None, NT))
                            chunks.append((SEQ + 128, None, NT + 1))
                            nch = len(chunks)
                            q_ap = (QT if s == 0 else QTc)[:, T, hh, n * 128:(n + 1) * 128]
                            Sp, r_S = Sring.next()

                            def smm(e, Sp=Sp, chunks=chunks, q_ap=q_ap, kt=kt, ps_=ps_):
                                for c, (kc, bias, vt) in enumerate(chunks):
                                    ins = e.matmul(Sp[:, c * 128:(c + 1) * 128], lhsT=KT[:, kt, kc:kc + 128], rhs=q_ap,
                                                   start=True, stop=(bias is None))
                                    if bias is not None:
                                        ins = e.matmul(Sp[:, c * 128:(c + 1) * 128], lhsT=self.ident[:], rhs=bias,
                                                       start=False, stop=True)
                                return ins
                            S.op("pe", smm, reads=[r_in, r_c, self.r_ident, self.r_mask], writes=[r_S])
                            PT, r_PT = PTr.next()
                            S.op("act", lambda e, PT=PT, Sp=Sp, nch=nch: e.activation(out=PT[:, 0:nch * 128], in_=Sp[:, 0:nch * 128], func=AF.Exp),
                                 reads=[], writes=[r_PT, r_S])

                            def pv(e, O=O, PT=PT, chunks=chunks, h=h, g=g):
                                for c, (kc, bias, vt) in enumerate(chunks):
                                    ins = e.matmul(O[:, h * VS:h * VS + 65], lhsT=PT[:, c * 128:(c + 1) * 128],
                                                   rhs=V[:, vt, g * VS:g * VS + 65], start=(c == 0), stop=(c == len(chunks) - 1))
                                return ins
                            pend_pv.append((pv, [r_PT, r_in], [r_O]))
                            if len(pend_pv) > 1:
                                f_, rd_, wr_ = pend_pv.pop(0)
                                S.op("pe", f_, reads=rd_, writes=wr_)
                    while pend_pv:
                        f_, rd_, wr_ = pend_pv.pop(0)
                        S.op("pe", f_, reads=rd_, writes=wr_)
                    dn, r_dn = dnr.next()
                    O3 = O[:, 0:4 * VS].rearrange("p (h c) -> p h c", c=VS)
                    if isA:
                        S.op("dve", lambda e, dn=dn, O3=O3: e.tensor_tensor(out=dn[:, 0:4].unsqueeze(2), in0=O3[:, :, 64:65],
                                                                             in1=esk[:].unsqueeze(2), op=ALU.add),
                             reads=[r_c], writes=[r_dn, r_O])
                    else:
                        S.op("dve", lambda e, dn=dn, O3=O3: e.tensor_copy(out=dn[:, 0:4].unsqueeze(2), in_=O3[:, :, 64:65]),
                             reads=[], writes=[r_dn, r_O])
                    S.op("dve", lambda e, dn=dn: e.reciprocal(out=dn[:, 4:8], in_=dn[:, 0:4]), reads=[r_dn], writes=[r_dn])
                    mo, r_mo = mor.next()
                    S.op("dve", lambda e, mo=mo, O3=O3, dn=dn: e.tensor_tensor(
                        out=mo[:], in0=O3[:, :, 0:64], in1=dn[:, 4:8].unsqueeze(2).to_broadcast([128, 4, 64]), op=ALU.mult),
                        reads=[r_dn], writes=[r_mo, r_O])
                    S.dma("sp", self.MIX[s][n * 128:(n + 1) * 128, col0:col0 + 256], mo[:].rearrange("p h d -> p (h d)"),
                          reads=[r_mo], writes=[self.res("MIX", s, n, kind)])

    def p2(self, l, streams):
        self.p_attn(l, "A", streams)

    def p3(self, l, streams):
        self.p_attn(l, "B", streams)

    def p4(self, l, streams):
        nc, S = self.nc, self.S
        last = (l == DEPTH - 1)
        PB = self.psb
        with ExitStack() as st:
            cw = self.sb(st, "s_cw", [128, 8, 7])
            cb = self.sb(st, "s_cb", [128, 8])
            cbrow = self.sb(st, "s_cbrow", [128, 1024], BF16)
            ones1 = self.sb(st, "s_ones1", [128, 128], BF16)
            diag = self.sb(st, "s_diag", [128, 8, 7, 128], BF16)
            Dh = self.sb(st, "s_Dh", [128, 8, 128], BF16)
            sm = self.sb(st, "s_sm", [128, 64])
            gn = self.sb(st, "s_gn", [128, 512])
            r_p = Res("ssm_params")
            r_diag = Res("diag")
            S.dma("sp", cw[:], self.conv_w[l], writes=[r_p])
            S.dma("sp", cb[:], self.conv_b[l], writes=[r_p])
            r_cb0 = Res()
            S.op("pool", lambda e: e.memset(cbrow[:], 0.0), writes=[r_cb0])
            S.dma("pool", cbrow[0:1, :], self.conv_brow[l], reads=[r_cb0], writes=[r_p, r_cb0])
            S.dma("sp", sm[:, 0:16], self.dt_bias[l].partition_broadcast(128), writes=[r_p])
            S.dma("sp", sm[:, 16:32], self.a_log[l].partition_broadcast(128), writes=[r_p])
            S.dma("sp", sm[:, 32:40], self.ssm_d[l].partition_broadcast(128), writes=[r_p])
            S.dma("sp", gn[:], self.ssm_g[l].partition_broadcast(128), writes=[r_p])
            S.op("pool", lambda e: e.memset(ones1[:], 0.0), writes=[r_diag])
            S.op("pool", lambda e: e.memset(ones1[0:1, :], 1.0), reads=[r_diag], writes=[r_diag])
            S.op("act", lambda e: e.activation(out=sm[:, 40:56], in_=sm[:, 16:32], func=AF.Exp), reads=[r_p], writes=[r_p])
            S.op("dve", lambda e: e.tensor_scalar(out=sm[:, 16:32], in0=sm[:, 40:56], scalar1=-1.0, scalar2=None, op0=ALU.mult),
                 reads=[r_p], writes=[r_p])
            for c in range(8):
                for j in range(7):
                    S.op("dve", lambda e, c=c, j=j: e.tensor_scalar(out=diag[:, c, j, :], in0=self.ident[:], scalar1=cw[:, c, j:j + 1],
                                                                    scalar2=None, op0=ALU.mult),
                         reads=[r_p, self.r_ident], writes=[r_diag])
            for h in range(8):
                S.op("dve", lambda e, h=h: e.tensor_scalar(out=Dh[:, h, :], in0=self.ident[:], scalar1=sm[:, 32 + h:33 + h],
                                                            scalar2=None, op0=ALU.mult),
                     reads=[r_p, self.r_ident], writes=[r_diag])
            ULb = self.sb(st, "s_ULb", [128, 16, 128], BF16)
            MK4 = self.sb(st, "s_MK4", [128, 2, 512], BF16)
            onesb = self.sb(st, "s_onesb", [128, 128], BF16)
            r_ul = Res("ULb")
            S.op("dve", lambda e: e.tensor_copy(out=ULb[:, 0:8, :], in_=self.U32[:].unsqueeze(1).to_broadcast([128, 8, 128])),
                 reads=[self.r_tri], writes=[r_ul])
            S.op("dve", lambda e: e.tensor_copy(out=ULb[:, 8:16, :], in_=self.L32[:].unsqueeze(1).to_broadcast([128, 8, 128])),
                 reads=[self.r_tri, r_ul], writes=[r_ul])
            S.op("dve", lambda e: e.tensor_copy(out=MK4[:, 0, :].rearrange("p (a b) -> p a b", a=4), in_=self.maskN[:].unsqueeze(1).to_broadcast([128, 4, 128])),
                 reads=[self.r_mask, r_ul], writes=[r_ul])
            S.op("dve", lambda e: e.tensor_copy(out=MK4[:, 1, :].rearrange("p (a b) -> p a b", a=4), in_=self.maskP[:].unsqueeze(1).to_broadcast([128, 4, 128])),
                 reads=[self.r_mask, r_ul], writes=[r_ul])
            S.op("dve", lambda e: e.tensor_copy(out=onesb[:], in_=self.ones32[:]), reads=[self.r_tri, r_ul], writes=[r_ul])
            Hf = self.sb(st, "s_Hf", [128, 512])
            Hb = self.sb(st, "s_Hb", [128, 512])
            HFb = self.sb(st, "s_HFb", [128, 512], BF16)
            r_Hf, r_Hb, r_HFb = Res("Hf"), Res("Hb"), Res("HFb")
            S.op("pool", lambda e: e.memset(Hf[:], 0.0), writes=[r_Hf])
            S.op("pool", lambda e: e.memset(Hb[:], 0.0), writes=[r_Hb])
            XB = self.sb(st, "s_XB", [128, 8, SEQ + 6], BF16)
            HB = self.sb(st, "s_HB", [128, NT, 512], BF16)
            r_HB = [Res("HB%d" % c) for c in range(NT)]
            NB = 9
            big = [self.sb(st, "s_big%d" % i, [128, NT * 16]) for i in range(NB)]
            xsr = Ring([self.sb(st, "s_xs%d" % i, [128, 512], BF16) for i in range(4)])
            btr = Ring([self.sb(st, "s_bt%d" % i, [128, 256], BF16) for i in range(4)])
            bcr = Ring([self.sb(st, "s_bc%d" % i, [128, 4, 128], BF16) for i in range(4)])
            xwr = Ring([self.sb(st, "s_xw%d" % i, [128, 512], BF16) for i in range(2)])
            mr = Ring([self.sb(st, "s_m%d" % i, [128, 128]) for i in range(6)])
            wr = Ring([self.sb(st, "s_w%d" % i, [128, 128], BF16) for i in range(6)])
            t1r = Ring([self.sb(st, "s_t1%d" % i, [128, 512]) for i in range(1)])
            t2r = Ring([self.sb(st, "s_t2%d" % i, [128, 512]) for i in range(1)])
            yr = Ring([self.sb(st, "s_y%d" % i, [128, 512]) for i in range(2)])
            y1r = Ring([self.sb(st, "s_y1%d" % i, [128, 512]) for i in range(2)])
            zr = Ring([self.sb(st, "s_z%d" % i, [128, 512], BF16) for i in range(2)])
            ocr = Ring([self.sb(st, "s_oc%d" % i, [128, 512], BF16) for i in range(2)])
            str_ = Ring([self.sb(st, "s_st%d" % i, [128, 4]) for i in range(3)])
            junk = self.sb(st, "s_junk", [128, 512], BF16)
            r_junk = Res()
            tmpH = self.sb(st, "s_tmpH", [128, 512])
            r_tmpH = Res()

            r_XB = Res("XB")
            _rb = [Res("big%d" % i) for i in range(NB)]
            r_b = {0: _rb[0], 1: _rb[1], 2: _rb[2], 3: _rb[3], 4: _rb[4], 5: _rb[5], 6: _rb[6], 7: _rb[6], 8: _rb[1], 9: _rb[7], 10: _rb[8], 11: _rb[3]}
            ahi = self.sb(st, "s_ahi", [128, NT * 16], BF16)
            alo = self.sb(st, "s_alo", [128, NT * 16], BF16)
            r_ahl = Res("ahl")
            rhr = Ring([self.sb(st, "s_rh%d" % i, [128, 2, 16, 128], BF16) for i in range(2)])
            for s in streams if 1 in streams else (1,) + tuple(streams):
                with_out = not (s == 1 and last)
                nch = NTC if s == 1 else NT
                nch = min(nch, self.opts.get("ssm_nch", 99))
                T = nch * 128
                W16 = nch * 16
                S.op("pool", lambda e: e.memset(XB[:, :, 0:3], 0.0), writes=[r_XB])
                S.op("pool", lambda e, T=T: e.memset(XB[:, :, 3 + T:6 + T], 0.0), writes=[r_XB])
                for c in range(8):
                    S.dma("sp", XB[:, c, 3:3 + T], self.FM[s][7 + c][:, 0:T], writes=[r_XB])
                DTr, Ev, DTv, LN, Av, AC, TOT, DEC, EE = [b[:, 0:W16] for b in big]
                DTE, SDT, MB = TOT, Ev, LN
                v3 = lambda ap: ap.rearrange("p (c k) -> p c k", k=16)
                for c0 in range(0, nch, 8):
                    c1 = min(c0 + 8, nch)
                    S.dma("sp", v3(DTr)[:, c0:c1, :], self.DT[s][c0 * 128:c1 * 128, :].rearrange("(c p) k -> p c k", p=128), writes=[r_b[0]])
                S.op("dve", lambda e, DTr=DTr, nch=nch: e.tensor_tensor(out=v3(DTr), in0=v3(DTr), in1=sm[:, 0:16].unsqueeze(1).to_broadcast([128, nch, 16]), op=ALU.add),
                     reads=[r_p], writes=[r_b[0]])
                S.op("act", lambda e, Ev=Ev, DTr=DTr: e.activation(out=Ev, in_=DTr, func=AF.Exp), reads=[r_b[0]], writes=[r_b[1]])
                S.op("act", lambda e, Ev=Ev, DTv=DTv: e.activation(out=DTv, in_=Ev, func=AF.Ln, bias=1.0), reads=[r_b[1]], writes=[r_b[2]])
                S.op("act", lambda e, LN=LN, DTv=DTv: e.activation(out=LN, in_=DTv, func=AF.Ln), reads=[r_b[2]], writes=[r_b[3]])
                S.op("dve", lambda e, Av=Av, DTv=DTv, nch=nch: e.tensor_tensor(out=v3(Av), in0=v3(DTv), in1=sm[:, 16:32].unsqueeze(1).to_broadcast([128, nch, 16]), op=ALU.mult),
                     reads=[r_b[2], r_p], writes=[r_b[4]])
                AHI = ahi[:, 0:W16]
                ALO = alo[:, 0:W16]
                S.op("dve", lambda e, AHI=AHI, Av=Av: e.tensor_copy(out=AHI, in_=Av), reads=[r_b[4]], writes=[r_ahl])
                S.op("dve", lambda e, ALO=ALO, Av=Av, AHI=AHI: e.tensor_tensor(out=ALO, in0=Av, in1=AHI, op=ALU.subtract),
                     reads=[r_b[4], r_ahl], writes=[r_ahl])
                for (mat, tgt, lo) in ((self.U32, AC, 0), (self.L32, AC, 8), (self.ones32, TOT, None)):
                    pb, r_pb = self.bank()
                    S.op("pe", lambda e, pb=pb, mat=mat, Av=Av, W16=W16: e.matmul(pb[:, 0:W16], lhsT=mat[:], rhs=Av, start=True, stop=True),
                         reads=[r_b[4], self.r_tri], writes=[r_pb])
                    if lo is None:
                        S.op("dve", lambda e, pb=pb, tgt=tgt, W16=W16: e.tensor_copy(out=tgt, in_=pb[:, 0:W16]), reads=[], writes=[r_b[6], r_pb])
                    else:
                        S.op("dve", lambda e, pb=pb, tgt=tgt, lo=lo, W16=W16: e.tensor_copy(out=v3(tgt)[:, :, lo:lo + 8], in_=v3(pb[:, 0:W16])[:, :, lo:lo + 8]),
                             reads=[], writes=[r_b[5], r_pb])
                S.op("act", lambda e, DEC=DEC, TOT=TOT: e.activation(out=DEC, in_=TOT, func=AF.Exp), reads=[r_b[6]], writes=[r_b[9]])
                S.op("dve", lambda e, DTE=DTE, TOT=TOT, AC=AC: e.tensor_tensor(out=DTE, in0=TOT, in1=AC, op=ALU.subtract), reads=[r_b[5]], writes=[r_b[7]])
                S.op("act", lambda e, DTE=DTE: e.activation(out=DTE, in_=DTE, func=AF.Exp), reads=[], writes=[r_b[7]])
                S.op("dve", lambda e, SDT=SDT, DTE=DTE, DTv=DTv: e.tensor_tensor(out=SDT, in0=DTE, in1=DTv, op=ALU.mult), reads=[r_b[7], r_b[2]], writes=[r_b[8]])
                S.op("act", lambda e, EE=EE, AC=AC: e.activation(out=EE, in_=AC, func=AF.Exp), reads=[r_b[5]], writes=[r_b[10]])
                S.op("dve", lambda e, MB=MB, LN=LN, AC=AC: e.tensor_tensor(out=MB, in0=LN, in1=AC, op=ALU.subtract), reads=[r_b[3], r_b[5]], writes=[r_b[11]])

                def conv_chunk(c, want_fm, load=False):
                    if load:
                        xs, r_xs = xsr.next()
                        bt, r_bt = btr.next()
                        rd = [self.res("XSS", s, c)]
                        S.dma("sp", xs[:], self.XSS[s][c * 128:(c + 1) * 128, 0:512], reads=rd, writes=[r_xs])
                        S.dma("sp", bt[:], self.XSS[s][c * 128:(c + 1) * 128, 512:768], reads=rd, writes=[r_bt])
                        bct, r_bct = None, None
                        if want_fm:
                            pb2, r_pb2 = PB[2]

                            def cfm(e, pb2=pb2, c=c):
                                for q, ct in enumerate((4, 5, 6, 7)):
                                    for j in range(7):
                                        ins = e.matmul(pb2[:, q * 128:(q + 1) * 128], lhsT=diag[:, ct, j, :], rhs=XB[:, ct, c * 128 + j:c * 128 + j + 128],
                                                       start=(j == 0), stop=(j == 6))
                                return ins
                            S.op("pe", cfm, reads=[r_XB, r_diag], writes=[r_pb2])
                            bct, r_bct = bcr.next()
                            for q, ct in enumerate((4, 5, 6, 7)):
                                S.op("act", lambda e, bct=bct, q=q, ct=ct, pb2=pb2: e.activation(out=bct[:, q, :], in_=pb2[:, q * 128:(q + 1) * 128], func=AF.Silu,
                                                                                                  bias=cb[:, ct:ct + 1]),
                                     reads=[r_p], writes=[r_bct, r_pb2])
                        return xs, r_xs, bt, r_bt, bct, r_bct
                    pb, r_pb = PB[0]

                    def cx(e, pb=pb, c=c):
                        for ct in range(4):
                            for j in range(7):
                                e.matmul(pb[:, ct * 128:(ct + 1) * 128], lhsT=XB[:, ct, c * 128 + j:c * 128 + j + 128], rhs=diag[:, ct, j, :],
                                         start=(j == 0), stop=False)
                            ins = e.matmul(pb[:, ct * 128:(ct + 1) * 128], lhsT=ones1[:], rhs=cbrow[:, ct * 128:(ct + 1) * 128], start=False, stop=True)
                        return ins
                    S.op("pe", cx, reads=[r_XB, r_diag, r_p], writes=[r_pb])
                    xs, r_xs = xsr.next()
                    S.op("act", lambda e, xs=xs, pb=pb: e.activation(out=xs[:], in_=pb, func=AF.Silu), reads=[], writes=[r_xs, r_pb])
                    pb1, r_pb1 = PB[1]

                    def cbt(e, pb1=pb1, c=c):
                        for ct in range(4, 6):
                            o = pb1[:, (ct - 4) * 128:(ct - 3) * 128]
                            for j in range(7):
                                e.matmul(o, lhsT=XB[:, ct, c * 128 + j:c * 128 + j + 128], rhs=diag[:, ct, j, :], start=(j == 0), stop=False)
                            ins = e.matmul(o, lhsT=ones1[:], rhs=cbrow[:, ct * 128:(ct + 1) * 128], start=False, stop=True)
                        return ins
                    S.op("pe", cbt, reads=[r_XB, r_diag, r_p], writes=[r_pb1])
                    bt, r_bt = btr.next()
                    S.op("act", lambda e, bt=bt, pb1=pb1: e.activation(out=bt[:], in_=pb1[:, 0:256], func=AF.Silu), reads=[], writes=[r_bt, r_pb1])
                    bct, r_bct = None, None
                    if want_fm:
                        pb2, r_pb2 = PB[2]

                        def cfm(e, pb2=pb2, c=c):
                            for q, ct in enumerate((4, 5, 6, 7)):
                                for j in range(7):
                                    ins = e.matmul(pb2[:, q * 128:(q + 1) * 128], lhsT=diag[:, ct, j, :], rhs=XB[:, ct, c * 128 + j:c * 128 + j + 128],
                                                   start=(j == 0), stop=(j == 6))
                            return ins
                        S.op("pe", cfm, reads=[r_XB, r_diag], writes=[r_pb2])
                        bct, r_bct = bcr.next()
                        for q, ct in enumerate((4, 5, 6, 7)):
                            S.op("act", lambda e, bct=bct, q=q, ct=ct, pb2=pb2: e.activation(out=bct[:, q, :], in_=pb2[:, q * 128:(q + 1) * 128], func=AF.Silu,
                                                                                              bias=cb[:, ct:ct + 1]),
                                 reads=[r_p], writes=[r_bct, r_pb2])
                    return xs, r_xs, bt, r_bt, bct, r_bct

                def state_mm(c, xs, r_xs, bt, r_bt, lo):
                    xw, r_xw = xwr.next()
                    S.op("dve", lambda e, xw=xw, xs=xs, c=c, lo=lo: e.tensor_tensor(
                        out=xw[:].rearrange("p (h d) -> p h d", h=8), in0=xs[:].rearrange("p (h d) -> p h d", h=8),
                        in1=v3(SDT)[:, c, lo:lo + 8].unsqueeze(2).to_broadcast([128, 8, 64]), op=ALU.mult),
                        reads=[r_xs, r_b[8]], writes=[r_xw])
                    pb3, r_pb3 = PB[3]

                    def smm(e, pb3=pb3, bt=bt, xw=xw):
                        for g in range(2):
                            ins = e.matmul(pb3[:, g * 256:(g + 1) * 256], lhsT=bt[:, g * 128:(g + 1) * 128], rhs=xw[:, g * 256:(g + 1) * 256],
                                           start=True, stop=True)
                        return ins
                    S.op("pe", smm, reads=[r_bt, r_xw], writes=[r_pb3])
                    return pb3, r_pb3

                def scan_step(H, r_H, c, lo, pb3, r_pb3):
                    S.op("dve", lambda e, H=H, c=c, lo=lo: e.tensor_tensor(
                        out=tmpH[:].rearrange("p (h d) -> p h d", h=8), in0=H[:].rearrange("p (h d) -> p h d", h=8),
                        in1=v3(DEC)[:, c, lo:lo + 8].unsqueeze(2).to_broadcast([128, 8, 64]), op=ALU.mult),
                        reads=[r_H, r_b[9]], writes=[r_tmpH])
                    S.op("dve", lambda e, H=H, pb3=pb3: e.tensor_tensor(out=H[:], in0=pb3, in1=tmpH[:], op=ALU.add),
                         reads=[r_tmpH], writes=[r_H, r_pb3])

                nxt = conv_chunk(nch - 1, False)
                for c in range(nch - 1, -1, -1):
                    xs, r_xs, bt, r_bt, _, _ = nxt
                    S.dma("sp", self.XSS[s][c * 128:(c + 1) * 128, 0:512], xs[:], reads=[r_xs], writes=[self.res("XSS", s, c)])
                    S.dma("sp", self.XSS[s][c * 128:(c + 1) * 128, 512:768], bt[:], reads=[r_bt], writes=[self.res("XSS", s, c)])
                    if c > 0:
                        nxt = conv_chunk(c - 1, False)
                    S.op("pool", lambda e, c=c: e.tensor_copy(out=HB[:, c, :], in_=Hb[:]), reads=[r_Hb], writes=[r_HB[c]])
                    pb3, r_pb3 = state_mm(c, xs, r_xs, bt, r_bt, 8)
                    scan_step(Hb, r_Hb, c, 8, pb3, r_pb3)
                h3 = lambda ap: ap.rearrange("p (h d) -> p h d", h=8)
                a3 = lambda ap: ap.rearrange("p (c k) -> p c k", k=16)
                convs = {0: conv_chunk(0, with_out, load=True)}

                rhs_ = {}

                def build_rh(c):
                    rh, r_rh = rhr.next()
                    r_rl = Res()
                    S.op("dve", lambda e, rh=rh, c=c: e.tensor_tensor(out=rh[:, 0], in0=ULb[:], in1=a3(AHI)[:, c, :].unsqueeze(2).to_broadcast([128, 16, 128]), op=ALU.mult),
                         reads=[r_ahl, r_ul], writes=[r_rh, r_rl])
                    S.op("pool", lambda e, rh=rh, c=c: e.tensor_tensor(out=rh[:, 1], in0=ULb[:], in1=a3(ALO)[:, c, :].unsqueeze(2).to_broadcast([128, 16, 128]), op=ALU.mult),
                         reads=[r_ahl, r_ul], writes=[r_rl])
                    rhs_[c] = (rh, r_rh, r_rl)

                def head(c):
                    xs, r_xs, bt, r_bt, bct, r_bct = convs[c]
                    pG, r_pG = PB[4]

                    def gmm(e, pG=pG, bct=bct):
                        for g in range(2):
                            ins = e.matmul(pG[:, g * 128:(g + 1) * 128], lhsT=bct[:, g, :], rhs=bct[:, 2 + g, :], start=True, stop=True)
                        return ins
                    S.op("pe", gmm, reads=[r_bct], writes=[r_pG])
                    rh, r_rh, r_rl = rhs_.pop(c)
                    pY1, r_pY1 = PB[7]
                    for rnd in range(2):
                        pDs = []
                        for d_ in range(2):
                            pD, r_pD = PB[(5 + d_) if rnd == 0 else d_]

                            def dmm(e, pD=pD, rh=rh, d_=d_, rnd=rnd):
                                hs = slice(d_ * 8 + rnd * 4, d_ * 8 + rnd * 4 + 4)
                                e.matmul(pD, lhsT=onesb[:], rhs=rh[:, 0, hs, :], start=True, stop=False)
                                e.matmul(pD, lhsT=onesb[:], rhs=rh[:, 1, hs, :], start=False, stop=False)
                                return e.matmul(pD, lhsT=self.ident[:], rhs=MK4[:, d_, :], start=False, stop=True)
                            S.op("pe", dmm, reads=[r_rh, r_rl, r_ul, self.r_ident], writes=[r_pD])
                            pDs.append((pD, r_pD))
                        for hq in range(4):
                            h = rnd * 4 + hq
                            g = h // 4
                            ws = []
                            for d_, lo in ((0, 0), (1, 8)):
                                pD, r_pD = pDs[d_]
                                m_t, r_m = mr.next()
                                S.op("act", lambda e, m_t=m_t, pD=pD, hq=hq, c=c, lo=lo, h=h: e.activation(
                                    out=m_t[:], in_=pD[:, hq * 128:(hq + 1) * 128], func=AF.Exp, bias=MB[:, c * 16 + lo + h:c * 16 + lo + h + 1]),
                                    reads=[r_b[11]], writes=[r_m, r_pD])
                                w_t, r_w = wr.next()
                                S.op("dve", lambda e, w_t=w_t, pG=pG, g=g, m_t=m_t: e.tensor_tensor(out=w_t[:], in0=pG[:, g * 128:(g + 1) * 128], in1=m_t[:], op=ALU.mult),
                                     reads=[r_m], writes=[r_w, r_pG])
                                ws.append((w_t, r_w))

                            def ymm(e, pY1=pY1, ws=ws, xs=xs, h=h):
                                o = pY1[:, h * 64:(h + 1) * 64]
                                e.matmul(o, lhsT=ws[0][0][:], rhs=xs[:, h * 64:(h + 1) * 64], start=True, stop=False)
                                e.matmul(o, lhsT=ws[1][0][:], rhs=xs[:, h * 64:(h + 1) * 64], start=False, stop=False)
                                return e.matmul(o, lhsT=Dh[:, h, :], rhs=xs[:, h * 64:(h + 1) * 64], start=False, stop=True)
                            S.op("pe", ymm, reads=[ws[0][1], ws[1][1], r_xs, r_diag], writes=[r_pY1])
                    y1, r_y1 = y1r.next()
                    S.op("act", lambda e, y1=y1, pY1=pY1: e.copy(out=y1[:], in_=pY1), reads=[], writes=[r_y1, r_pY1])
                    if c + 1 < nch:
                        build_rh(c + 1)
                    return y1, r_y1

                def tail(c, y1, r_y1):
                    xs, r_xs, bt, r_bt, bct, r_bct = convs.pop(c)
                    S.op("pool", lambda e: e.tensor_copy(out=HFb[:], in_=Hf[:]), reads=[r_Hf], writes=[r_HFb])
                    pb3, r_pb3 = state_mm(c, xs, r_xs, bt, r_bt, 0)
                    scan_step(Hf, r_Hf, c, 0, pb3, r_pb3)
                    pY2, r_pY2 = PB[5]
                    pY3, r_pY3 = PB[6]

                    def y2mm(e, pY2=pY2, bct=bct):
                        for g in range(2):
                            ins = e.matmul(pY2[:, g * 256:(g + 1) * 256], lhsT=bct[:, 2 + g, :], rhs=HFb[:, g * 256:(g + 1) * 256], start=True, stop=True)
                        return ins
                    S.op("pe", y2mm, reads=[r_bct, r_HFb], writes=[r_pY2])

                    def y3mm(e, pY3=pY3, bct=bct, c=c):
                        for g in range(2):
                            ins = e.matmul(pY3[:, g * 256:(g + 1) * 256], lhsT=bct[:, 2 + g, :], rhs=HB[:, c, g * 256:(g + 1) * 256], start=True, stop=True)
                        return ins
                    S.op("pe", y3mm, reads=[r_bct, r_HB[c]], writes=[r_pY3])
                    t1, r_t1 = t1r.next()
                    t2, r_t2 = t2r.next()
                    S.op("dve", lambda e, t1=t1, pY2=pY2, c=c: e.tensor_tensor(out=h3(t1[:]), in0=h3(pY2), in1=v3(EE)[:, c, 0:8].unsqueeze(2).to_broadcast([128, 8, 64]), op=ALU.mult),
                         reads=[r_b[10]], writes=[r_t1, r_pY2])
                    S.op("dve", lambda e, t2=t2, pY3=pY3, c=c: e.tensor_tensor(out=h3(t2[:]), in0=h3(pY3), in1=v3(EE)[:, c, 8:16].unsqueeze(2).to_broadcast([128, 8, 64]), op=ALU.mult),
                         reads=[r_b[10]], writes=[r_t2, r_pY3])
                    S.op("pool", lambda e, t1=t1, t2=t2: e.tensor_tensor(out=t1[:], in0=t1[:], in1=t2[:], op=ALU.add), reads=[r_t2], writes=[r_t1])
                    y, r_y = yr.next()
                    S.op("dve", lambda e, y=y, y1=y1, t1=t1: e.tensor_tensor(out=y[:], in0=y1[:], in1=t1[:], op=ALU.add), reads=[r_t1, r_y1], writes=[r_y])
                    z_t, r_z = zr.next()
                    S.dma("sp", z_t[:], self.ZS[s][c * 128:(c + 1) * 128, :], writes=[r_z])
                    S.op("pool", lambda e, y=y, z_t=z_t: e.tensor_tensor(out=y[:], in0=y[:], in1=z_t[:], op=ALU.mult), reads=[r_z], writes=[r_y])
                    st_t, r_st = str_.next()
                    S.op("act", lambda e, y=y, st_t=st_t: e.activation(out=junk[:], in_=y[:], func=AF.Square, accum_out=st_t[:, 0:1]),
                         reads=[r_y], writes=[r_junk, r_st])
                    S.op("act", lambda e, st_t=st_t: e.activation(out=st_t[:, 1:2], in_=st_t[:, 0:1], func=AF.Sqrt, scale=1.0 / 512, bias=EPS),
                         reads=[r_st], writes=[r_st])
                    S.op("dve", lambda e, st_t=st_t: e.reciprocal(out=st_t[:, 2:3], in_=st_t[:, 1:2]), reads=[r_st], writes=[r_st])
                    oc, r_oc = ocr.next()
                    S.op("dve", lambda e, oc=oc, y=y, st_t=st_t: e.scalar_tensor_tensor(out=oc[:], in0=y[:], scalar=st_t[:, 2:3], in1=gn[:], op0=ALU.mult, op1=ALU.mult),
                         reads=[r_y, r_st, r_p], writes=[r_oc])
                    S.dma("sp", self.MIX[s][c * 128:(c + 1) * 128, 512:1024], oc[:], reads=[r_oc], writes=[self.res("MIX", s, c, "C")])

                if with_out:
                    build_rh(0)
                    if nch > 1:
                        convs[1] = conv_chunk(1, with_out, load=True)
                    hd = head(0)
                    for c in range(nch):
                        nh = head(c + 1) if c + 1 < nch else None
                        tail(c, *hd)
                        if c + 2 < nch:
                            convs[c + 2] = conv_chunk(c + 2, with_out, load=True)
                        hd = nh
                else:
                    for c in range(nch):
                        xs, r_xs, bt, r_bt, _, _ = convs.pop(c)
                        if c + 1 < nch:
                            convs[c + 1] = conv_chunk(c + 1, with_out, load=True)
                        pb3, r_pb3 = state_mm(c, xs, r_xs, bt, r_bt, 0)
                        scan_step(Hf, r_Hf, c, 0, pb3, r_pb3)

    def _p4_end(self):
        pass

    def p5(self, l, src, streams, final):
        nc, S = self.nc, self.S
        NK2 = DFF // 128
        with ExitStack() as stw:
            W1 = self.sb(stw, "p5b_W1", [128, 8, 2 * DFF], BF16)
            W2 = self.sb(stw, "p5b_W2", [128, NK2, D], BF16)
            r_W1, r_W2 = [], []
            with ExitStack() as st:
                Wo = self.sb(st, "p5a_W", [128, 8, D], BF16)
                r_W = []
                wv = self.w_out[l].rearrange("(k p) n -> p k n", p=128)
                for k in range(8):
                    r = Res()
                    r_W.append(r)
                    S.dma("pool", Wo[:, k, :], wv[:, k, :], writes=[r])
                w1v = self.w_ffn_in[l].rearrange("(k p) n -> p k n", p=128)
                w2v = self.w_ffn_out[l].rearrange("(k p) n -> p k n", p=128)
                for k in range(8):
                    for c0 in range(0, 2 * DFF, 1408):
                        r = Res()
                        r_W1.append(r)
                        S.dma("pool", W1[:, k, c0:c0 + 1408], w1v[:, k, c0:c0 + 1408], writes=[r])
                for k in range(NK2):
                    r = Res()
                    r_W2.append(r)
                    S.dma("pool", W2[:, k, :], w2v[:, k, :], writes=[r])
                gt = self.sb(st, "p5a_gt", [128, 2, D])
                r_gt = Res()
                for s in streams:
                    S.dma("sp", gt[:, s, :], self.MOD[s, 2], writes=[r_gt])
                xr = Ring([self.sb(st, "p5a_x%d" % i, [128, D]) for i in range(3)])
                mr = Ring([self.sb(st, "p5a_m%d" % i, [128, D], BF16) for i in range(3)])
                mTr = Ring([self.sb(st, "p5a_mT%d" % i, [128, 8, 128], BF16) for i in range(3)])
                tr_ = Ring([self.sb(st, "p5a_t%d" % i, [128, D]) for i in range(2)])
                orr = Ring([self.sb(st, "p5a_o%d" % i, [128, D]) for i in range(2)])
                tiles = [(s, t) for s in streams for t in range(min(NT if s == 0 else NTC, self.opts.get('p5_nt', 99)))]

                def prep(s, t):
                    rows = slice(t * 128, (t + 1) * 128)
                    x_t, r_x = xr.next()
                    S.dma("sp", x_t[:], src[s][rows, :], writes=[r_x])
                    m_t, r_m = mr.next()
                    S.dma("sp", m_t[:], self.MIX[s][rows, :], writes=[r_m])
                    mT, r_mT = mTr.next()

                    def fin(m_t=m_t, r_m=r_m, mT=mT, r_mT=r_mT):
                        pb, r_pb = self.bank()
                        pbT = pb.bitcast(BF16)

                        def tr(e, pbT=pbT, m_t=m_t):
                            for k in range(8):
                                ins = e.transpose(out=pbT[:, k * 128:(k + 1) * 128], in_=m_t[:, k * 128:(k + 1) * 128],
                                                  identity=self.ident[:])
                            return ins
                        S.op("pe", tr, reads=[r_m, self.r_ident], writes=[r_pb])
                        S.op("act", lambda e, mT=mT, pbT=pbT: e.copy(out=mT[:], in_=pbT.rearrange("p (k t) -> p k t", k=8)),
                             reads=[], writes=[r_mT, r_pb])
                    return x_t, r_x, mT, r_mT, fin

                pend = prep(*tiles[0]) if tiles else None
                if pend:
                    pend[4]()
                for i, (s, t) in enumerate(tiles):
                    rows = slice(t * 128, (t + 1) * 128)
                    x_t, r_x, mT, r_mT, _ = pend
                    pend = prep(*tiles[i + 1]) if i + 1 < len(tiles) else None
                    t_t, r_t = tr_.next()
                    for half in range(2):
                        if half == 1 and pend:
                            pend[4]()
                        po, r_po = self.bank()

                        def mm(e, po=po, mT=mT, half=half):
                            for k in range(8):
                                ins = e.matmul(po, lhsT=mT[:, k, :], rhs=Wo[:, k, half * 512:(half + 1) * 512],
                                               start=(k == 0), stop=(k == 7))
                            return ins
                        S.op("pe", mm, reads=r_W + [r_mT], writes=[r_po])
                        S.op("dve", lambda e, t_t=t_t, po=po, half=half, s=s: e.tensor_tensor(
                            out=t_t[:, half * 512:(half + 1) * 512], in0=po, in1=gt[:, s, half * 512:(half + 1) * 512], op=ALU.mult),
                            reads=[r_gt], writes=[r_t, r_po])
                    o_t, r_o = orr.next()
                    S.op("dve", lambda e, o_t=o_t, t_t=t_t, x_t=x_t: e.tensor_tensor(out=o_t[:], in0=t_t[:], in1=x_t[:], op=ALU.add),
                         reads=[r_t, r_x], writes=[r_o])
                    S.dma("sp", self.XM[s][rows, :], o_t[:], reads=[r_o], writes=[self.res("XM", s, t)])
            S.barrier_all()
            with ExitStack() as st:
                modt = self.sb(st, "p5b_mod", [128, 3, D])
                r_mod = Res()
                gfin = None
                if final:
                    gfin = self.sb(st, "p5b_gf", [128, D])
                    r_gf = Res()
                    S.dma("sp", gfin[:], self.g_final.partition_broadcast(128), writes=[r_gf])
                xr = Ring([self.sb(st, "p5b_x%d" % i, [128, 2, D]) for i in range(2)])
                junk = self.sb(st, "p5b_junk", [128, D], BF16)
                r_junk = Res()
                stat = Ring([self.sb(st, "p5b_st%d" % i, [128, 8]) for i in range(6)])
                tmpr = Ring([self.sb(st, "p5b_t%d" % i, [128, D]) for i in range(2)])
                hr = Ring([self.sb(st, "p5b_h%d" % i, [128, D], BF16) for i in range(3)])
                hTr = Ring([self.sb(st, "p5b_hT%d" % i, [128, 8, 256], BF16) for i in range(2)])
                sgr = Ring([self.sb(st, "p5b_sg%d" % i, [128, 256]) for i in range(2)])
                actr = Ring([self.sb(st, "p5b_a%d" % i, [128, NK2, 256], BF16) for i in range(1)])
                orr = Ring([self.sb(st, "p5b_o%d" % i, [128, D]) for i in range(1)])
                groups = [(s, g0) for s in streams for g0 in range(0, min(NT if s == 0 else NTC, self.opts.get('p5_nt', 99)), 2)]
                cur_mod = [None]

                def normg(s, g0):
                    if cur_mod[0] != s:
                        cur_mod[0] = s
                        for jj, j in enumerate((3, 4, 5)):
                            S.dma("sp", modt[:, jj, :], self.MOD[s, j], writes=[r_mod])
                    x_t, r_x = xr.next()
                    S.dma("sp", x_t[:], self.XM[s][g0 * 128:(g0 + 2) * 128, :].rearrange("(t p) c -> p t c", p=128), writes=[r_x])
                    hT, r_hT = hTr.next()
                    fins = []
                    for ti in range(2):
                        st_t, r_st = stat.next()
                        S.op("act", lambda e, x_t=x_t, ti=ti, st_t=st_t: e.activation(out=junk[:], in_=x_t[:, ti, :], func=AF.Square,
                                                                                       accum_out=st_t[:, 0:1]),
                             reads=[r_x], writes=[r_junk, r_st])
                        S.op("act", lambda e, st_t=st_t: e.activation(out=st_t[:, 1:2], in_=st_t[:, 0:1], func=AF.Sqrt,
                                                                      scale=1.0 / D, bias=EPS), reads=[r_st], writes=[r_st])
                        S.op("dve", lambda e, st_t=st_t: e.reciprocal(out=st_t[:, 2:3], in_=st_t[:, 1:2]), reads=[r_st], writes=[r_st])
                        tm, r_tm = tmpr.next()
                        S.op("dve", lambda e, tm=tm, x_t=x_t, ti=ti, st_t=st_t: e.scalar_tensor_tensor(
                            out=tm[:], in0=x_t[:, ti, :], scalar=st_t[:, 2:3], in1=modt[:, 1, :], op0=ALU.mult, op1=ALU.mult),
                            reads=[r_x, r_st, r_mod], writes=[r_tm])
                        h_t, r_h = hr.next()
                        S.op("pool", lambda e, h_t=h_t, tm=tm: e.tensor_tensor(out=h_t[:], in0=tm[:], in1=modt[:, 0, :], op=ALU.add),
                             reads=[r_tm, r_mod], writes=[r_h])
                        def fin(h_t=h_t, r_h=r_h, hT=hT, r_hT=r_hT, ti=ti):
                            pb, r_pb = self.bank()
                            pbT = pb.bitcast(BF16)

                            def tr(e, pbT=pbT, h_t=h_t):
                                for k in range(8):
                                    ins = e.transpose(out=pbT[:, k * 128:(k + 1) * 128], in_=h_t[:, k * 128:(k + 1) * 128],
                                                      identity=self.ident[:])
                                return ins
                            S.op("pe", tr, reads=[r_h, self.r_ident], writes=[r_pb])
                            S.op("act", lambda e, hT=hT, ti=ti, pbT=pbT: e.copy(out=hT[:, :, ti * 128:(ti + 1) * 128],
                                                                                 in_=pbT.rearrange("p (k t) -> p k t", k=8)),
                                 reads=[], writes=[r_hT, r_pb])
                        fins.append(fin)
                    return x_t, r_x, hT, r_hT, fins

                pend = normg(*groups[0]) if groups else None
                if pend:
                    for f_ in pend[4]:
                        f_()
                for gi, (s, g0) in enumerate(groups):
                    x_t, r_x, hT, r_hT, _ = pend
                    defer = []
                    if gi + 1 < len(groups) and groups[gi + 1][0] == s:
                        pend = normg(*groups[gi + 1])
                        defer = list(pend[4])
                        late = False
                    else:
                        late = True
                    a_t, r_a = actr.next()
                    for ct in range(NK2):
                        if defer and ct in (8, 15):
                            defer.pop(0)()
                        pg, r_pg = self.bank()
                        pu, r_pu = self.bank()

                        def mm1(pb_, c0, hT=hT):
                            def f(e):
                                for k in range(8):
                                    ins = e.matmul(pb_[:, 0:256], lhsT=W1[:, k, c0:c0 + 128], rhs=hT[:, k, :],
                                                   start=(k == 0), stop=(k == 7))
                                return ins
                            return f
                        S.op("pe", mm1(pg, ct * 128), reads=r_W1 + [r_hT], writes=[r_pg])
                        S.op("pe", mm1(pu, DFF + ct * 128), reads=r_W1 + [r_hT], writes=[r_pu])
                        sg, r_sg = sgr.next()
                        S.op("act", lambda e, sg=sg, pg=pg: e.activation(out=sg[:], in_=pg[:, 0:256], func=AF.Silu),
                             reads=[], writes=[r_sg, r_pg])
                        S.op("dve", lambda e, a_t=a_t, ct=ct, pu=pu, sg=sg: e.tensor_tensor(
                            out=a_t[:, ct, :], in0=pu[:, 0:256], in1=sg[:], op=ALU.mult),
                            reads=[r_sg], writes=[r_a, r_pu])
                    for ti in range(2):
                        t = g0 + ti
                        tm, r_tm = tmpr.next()
                        for half in range(2):
                            po, r_po = self.bank()

                            def mm2(e, po=po, a_t=a_t, ti=ti, half=half):
                                for k in range(NK2):
                                    ins = e.matmul(po, lhsT=a_t[:, k, ti * 128:(ti + 1) * 128], rhs=W2[:, k, half * 512:(half + 1) * 512],
                                                   start=(k == 0), stop=(k == NK2 - 1))
                                return ins
                            S.op("pe", mm2, reads=r_W2 + [r_a], writes=[r_po])
                            S.op("dve", lambda e, tm=tm, po=po, half=half: e.tensor_tensor(
                                out=tm[:, half * 512:(half + 1) * 512], in0=po, in1=modt[:, 2, half * 512:(half + 1) * 512], op=ALU.mult),
                                reads=[r_mod], writes=[r_tm, r_po])
                        o_t, r_o = orr.next()
                        S.op("pool", lambda e, o_t=o_t, tm=tm, x_t=x_t, ti=ti: e.tensor_tensor(out=o_t[:], in0=tm[:], in1=x_t[:, ti, :], op=ALU.add),
                             reads=[r_tm, r_x], writes=[r_o])
                        rows = slice(t * 128, (t + 1) * 128)
                        if not final:
                            S.dma("sp", self.XL[s][rows, :], o_t[:], reads=[r_o], writes=[self.res("XL", s, t)])
                        else:
                            st_t, r_st = stat.next()
                            S.op("act", lambda e, o_t=o_t, st_t=st_t: e.activation(out=junk[:], in_=o_t[:], func=AF.Square,
                                                                                    accum_out=st_t[:, 0:1]),
                                 reads=[r_o], writes=[r_junk, r_st])
                            S.op("act", lambda e, st_t=st_t: e.activation(out=st_t[:, 1:2], in_=st_t[:, 0:1], func=AF.Sqrt,
                                                                          scale=1.0 / D, bias=EPS), reads=[r_st], writes=[r_st])
                            S.op("dve", lambda e, st_t=st_t: e.reciprocal(out=st_t[:, 2:3], in_=st_t[:, 1:2]), reads=[r_st], writes=[r_st])
                            f_t, r_f = tmpr.next()
                            S.op("dve", lambda e, f_t=f_t, o_t=o_t, st_t=st_t: e.scalar_tensor_tensor(
                                out=f_t[:], in0=o_t[:], scalar=st_t[:, 2:3], in1=gfin[:], op0=ALU.mult, op1=ALU.mult),
                                reads=[r_o, r_st, r_gf], writes=[r_f])
                            S.dma("sp", self.out[rows, :], f_t[:], reads=[r_f], writes=[self.res("OUT", t)])
                    while defer:
                        defer.pop(0)()
                    if late and gi + 1 < len(groups):
                        pend = normg(*groups[gi + 1])
                        for f_ in pend[4]:
                            f_()

    def build(self):
        S = self.S
        self.declare()
        phases = self.opts.get("phases")
        with ExitStack() as st:
            self.setup_common(st)
            for l in range(DEPTH):
                src = [self.x_in, self.ctx_in] if l == 0 else self.XL
                last = (l == DEPTH - 1)
                streams = (0,) if last else (1, 0)

                def run(name, fn):
                    if phases is None or (name, l) in phases:
                        fn()
                        S.barrier_all()
                run("p0", lambda: self.p0(l))
                run("p1", lambda: self.p1(l, src, do_ctx_q=not last))
                run("p2", lambda: self.p2(l, streams))
                run("p3", lambda: self.p3(l, streams))
                run("p4", lambda: self.p4(l, streams))
                run("p5", lambda: self.p5(l, src, streams, final=last))
            S.final_wait("sp")
            S.emit()
        return self.nc


def _rope_tables():
    t = np.arange(SEQ)
    rows, cols = t // 64, t % 64
    inv = (10000.0 ** (-np.arange(16, dtype=np.float32) / 16)).astype(np.float32)
    C = np.zeros((64, SEQ), np.float32)
    Sg = np.zeros((64, SEQ), np.float32)
    for blk, pos in ((0, rows), (1, cols)):
        ang = pos.astype(np.float32)[None, :] * inv[:, None]
        cs, sn = np.cos(ang).astype(np.float32), np.sin(ang).astype(np.float32)
        C[blk * 32:blk * 32 + 16] = cs
        C[blk * 32 + 16:blk * 32 + 32] = cs
        Sg[blk * 32:blk * 32 + 16] = -sn
        Sg[blk * 32 + 16:blk * 32 + 32] = sn
    C2 = np.concatenate([C, C], 0)
    S2 = np.concatenate([Sg, Sg], 0)
    return np.stack([C2 * 0.125, S2 * 0.125, C2, S2]).astype(np.float32)


def _swap_idx():
    d = np.arange(64)
    return np.where(d % 32 < 16, d + 16, d - 16)


def _w_in_ext(w_in):
    qa = np.arange(0, 256)
    qb = np.arange(256, 512)
    z = np.arange(512, 1024)
    ka = np.arange(1024, 1152)
    va = np.arange(1152, 1280)
    kb = np.arange(1280, 1536)
    vb = np.arange(1536, 1792)
    xbc = np.arange(1792, 2816)
    dt = np.arange(2816, 2832)
    sw = _swap_idx()

    def heads(base, hs):
        return np.concatenate([base[h * 64:(h + 1) * 64] for h in hs])

    def heads_sw(base, hs):
        return np.concatenate([base[h * 64:(h + 1) * 64][sw] for h in hs])
    cols = [heads(qa, (0, 2)), heads_sw(qa, (0, 2)), heads(qa, (1, 3)), heads_sw(qa, (1, 3)),
            heads(ka, (0, 1)), heads_sw(ka, (0, 1)), qb, kb, xbc, z, va, vb, dt]
    idx = np.concatenate(cols)
    assert idx.shape[0] == WCOLS
    return np.ascontiguousarray(w_in[:, :, idx])


def _bm_table(rpb):
    L = rpb.shape[0]
    krl, kc = np.divmod(np.arange(128), 64)
    qrl, qc = np.divmod(np.arange(128), 64)
    cases = [(i, j) for (i, js) in ((2, range(0, 5)), (0, range(0, 4)), (1, range(0, 4)), (30, range(28, 32)), (31, range(28, 32))) for j in js]
    out = np.full((L, 4, 21, 128, 128), NEG, np.float32)
    for ci, (i, j) in enumerate(cases):
        kr = (2 * j + krl)[:, None]
        qr = (2 * i + qrl)[None, :]
        s_ = np.clip(qr - 4, 0, 56)
        vrow = (kr >= s_) & (kr <= s_ + 7)
        cst = np.clip(qc - 8, 0, 48)[None, :]
        vcol = (kc[:, None] >= cst) & (kc[:, None] < cst + 16)
        valid = vrow & vcol
        dy = np.clip(kr - qr + 7, 0, 14)
        dx = np.clip(kc[:, None] - qc[None, :] + 15, 0, 30)
        dyb, dxb = np.broadcast_arrays(dy, dx)
        g = rpb[:, :, dyb, dxb]
        out[:, :, ci] = np.where(valid[None, None], g, np.float32(NEG))
    return np.ascontiguousarray(out.transpose(0, 3, 1, 2, 4).reshape(L, 128, 84 * 128))


def prep_inputs(inputs, n_cores):
    f = lambda a: np.ascontiguousarray(np.asarray(a, dtype=np.float32))
    x, c, ctx, c_ctx = f(inputs["x"]), f(inputs["c"]), f(inputs["ctx"]), f(inputs["c_ctx"])
    shared = {
        "w_mod": f(inputs["w_mod"]), "b_mod": f(inputs["b_mod"]), "g_mix": f(inputs["g_mix"]), "g_ffn": f(inputs["g_ffn"]),
        "w_in_ext": _w_in_ext(f(inputs["w_in"])), "rope": _rope_tables(),
        "w_out": f(inputs["w_out"]), "w_ffn_in": f(inputs["w_ffn_in"]), "w_ffn_out": f(inputs["w_ffn_out"]),
        "g_final": f(inputs["g_final"]), "wa_sink": f(inputs["wa_sink"]), "bm_tab": _bm_table(f(inputs["na_rpb"])),
        "conv_w_l": np.ascontiguousarray(f(inputs["ssm_conv_w"]).reshape(DEPTH, 7, 8, 128).transpose(0, 3, 2, 1)),
        "conv_b_l": np.ascontiguousarray(f(inputs["ssm_conv_b"]).reshape(DEPTH, 8, 128).transpose(0, 2, 1)),
        "conv_brow": f(inputs["ssm_conv_b"]).reshape(DEPTH, 1, 1024),
        "dt_bias": f(inputs["ssm_dt_bias"]).reshape(DEPTH, 16), "a_log": f(inputs["ssm_a_log"]).reshape(DEPTH, 16),
        "ssm_d": f(inputs["ssm_d"]), "ssm_g": f(inputs["ssm_norm_g"]),
    }
    maps = []
    for i in range(n_cores):
        b = i % 4
        cvec = np.concatenate([c[b].reshape(8, 128).T, c_ctx.reshape(8, 128).T], 1)
        m = dict(shared)
        m.update({"x": x[b], "ctx": ctx[b], "cvec": np.ascontiguousarray(cvec)})
        maps.append(m)
    return maps


N_CORES = 4


def kernel(**inputs):
    nc = Builder().build()
    maps = prep_inputs(inputs, N_CORES)
    res = run_bass_kernel_spmd(nc, maps, core_ids=list(range(N_CORES)))
    out = np.stack([res.results[b]["out"] for b in range(4)], 0)
    return out.astype(np.float32)
```

```python
import numpy as np
from contextlib import ExitStack
import concourse.bass as bass
import concourse.mybir as mybir
from concourse.bass_utils import run_bass_kernel_spmd

F32 = mybir.dt.float32
BF16 = mybir.dt.bfloat16
AF = mybir.ActivationFunctionType
ALU = mybir.AluOpType
AX = mybir.AxisListType

D = 1024
SEQ = 4096
LC = 256
DEPTH = 2
NT = SEQ // 128
NTC = LC // 128
EPS = 1e-6
DFF = 2816
NFM = 18
NFMO = 15
TMC = 912
WCOLS = NFM * 128 + TMC
NEG = -30000.0
VS = 66


class Res:
    __slots__ = ("name", "w", "rs")

    def __init__(self, name=""):
        self.name = name
        self.w = None
        self.rs = []


class Sched:
    ENGS = ("pe", "act", "dve", "pool", "sp")
    NDMA = 40
    NSDMA = 16

    def __init__(self, nc):
        self.nc = nc
        self.prog = {e: [] for e in self.ENGS}
        self.count = {}
        self.known = {e: {} for e in self.ENGS}
        self.dma_i = 0
        self.sdma_i = 0

    def _deps(self, eng, reads, writes):
        waits = {}

        def add(sv):
            if sv is None:
                return
            s, v = sv
            if eng == "pe" and s == "pe":
                return
            if waits.get(s, 0) < v:
                waits[s] = v
        for r in reads:
            add(r.w)
        for w in writes:
            add(w.w)
            for x in w.rs:
                add(x)
        out = []
        kn = self.known[eng]
        for s, v in waits.items():
            if kn.get(s, 0) < v:
                kn[s] = v
                out.append((s, v))
        return out

    def _mark(self, tag, reads, writes):
        for r in reads:
            r.rs.append(tag)
        for w in writes:
            w.w = tag
            w.rs = []

    def op(self, eng, fn, reads=(), writes=()):
        waits = self._deps(eng, reads, writes)
        c = self.count.get(eng, 0) + 1
        self.count[eng] = c
        self.prog[eng].append((waits, fn, (eng, 1)))
        self._mark((eng, c), reads, writes)

    def dma(self, q, out, in_, reads=(), writes=(), **kw):
        if q == "pool":
            slot = "sdma%d" % (self.sdma_i % self.NSDMA)
            self.sdma_i += 1
        else:
            slot = "dma%d" % (self.dma_i % self.NDMA)
            self.dma_i += 1
        waits = self._deps(q, reads, writes)
        prev = self.count.get(slot, 0)
        kn = self.known[q]
        if prev and kn.get(slot, 0) < prev:
            kn[slot] = prev
            waits.append((slot, prev))
        c = prev + 16
        self.count[slot] = c

        def fn(e, out=out, in_=in_, kw=kw):
            return e.dma_start(out=out, in_=in_, **kw)
        self.prog[q].append((waits, fn, (slot, 16)))
        self._mark((slot, c), reads, writes)

    def barrier_all(self):
        allv = list(self.count.items())
        for e in self.ENGS:
            kn = self.known[e]
            waits = []
            for s, v in allv:
                if s == e:
                    continue
                if kn.get(s, 0) < v:
                    kn[s] = v
                    waits.append((s, v))
            if waits:
                self.prog[e].append((waits, None, None))

    def final_wait(self, eng="sp"):
        waits = []
        kn = self.known[eng]
        for s, v in self.count.items():
            if s != eng and kn.get(s, 0) < v:
                kn[s] = v
                waits.append((s, v))
        self.prog[eng].append((waits, None, None))

    def emit(self):
        nc = self.nc
        with ExitStack() as es:
            sems = {}
            for s in self.count:
                sems[s] = es.enter_context(nc.semaphore(s))
            block = es.enter_context(nc.Block())

            def replay(name, e):
                for waits, fn, inc in self.prog[name]:
                    for s, v in waits:
                        e.wait_ge(sems[s], v)
                    if fn is not None:
                        ins = fn(e)
                        if inc is not None:
                            ins.then_inc(sems[inc[0]], inc[1])

            @block.tensor
            def _(e):
                replay("pe", e)

            @block.scalar
            def _(e):
                replay("act", e)

            @block.vector
            def _(e):
                replay("dve", e)

            @block.gpsimd
            def _(e):
                replay("pool", e)

            @block.sync
            def _(e):
                replay("sp", e)


class Ring:
    def __init__(self, aps, name=""):
        self.items = [(a, Res("%s%d" % (name, i))) for i, a in enumerate(aps)]
        self.i = 0

    def next(self):
        it = self.items[self.i % len(self.items)]
        self.i += 1
        return it


class Builder:
    def __init__(self, debug=(), stop_after=None, opts=None):
        self.opts = opts or {}
        self.debug = set(debug)
        self.stop_after = stop_after
        self.nc = bass.Bass("TRN2", target_bir_lowering=False)
        self.S = Sched(self.nc)
        self.es = ExitStack()
        self.resmap = {}

    def din(self, name, shape, dt=F32):
        return self.nc.dram_tensor(name, list(shape), dt, kind="ExternalInput").ap()

    def dscr(self, name, shape, dt=F32):
        kind = "ExternalOutput" if name in self.debug else "Internal"
        if name in self.opts.get("inject", ()):
            kind = "ExternalInput"
        return self.nc.dram_tensor(name, list(shape), dt, kind=kind).ap()

    def res(self, *key):
        r = self.resmap.get(key)
        if r is None:
            r = Res(str(key))
            self.resmap[key] = r
        return r

    def sb(self, st, name, shape, dt=F32):
        self.uid = getattr(self, "uid", 0) + 1
        return st.enter_context(self.nc.sbuf_tensor("%s_%d" % (name, self.uid), list(shape), dt))

    def declare(self):
        self.x_in = self.din("x", [SEQ, D])
        self.ctx_in = self.din("ctx", [LC, D])
        self.cvec = self.din("cvec", [128, 16])
        self.w_mod = self.din("w_mod", [DEPTH, D, 6 * D])
        self.b_mod = self.din("b_mod", [DEPTH, 6 * D])
        self.g_mix = self.din("g_mix", [DEPTH, D])
        self.g_ffn = self.din("g_ffn", [DEPTH, D])
        self.w_in = self.din("w_in_ext", [DEPTH, D, WCOLS])
        self.rope = self.din("rope", [4, 128, SEQ])
        self.w_out = self.din("w_out", [DEPTH, D, D])
        self.w_ffn_in = self.din("w_ffn_in", [DEPTH, D, 2 * DFF])
        self.w_ffn_out = self.din("w_ffn_out", [DEPTH, DFF, D])
        self.g_final = self.din("g_final", [D])
        self.wa_sink = self.din("wa_sink", [DEPTH, 4])
        self.conv_w = self.din("conv_w_l", [DEPTH, 128, 8, 7])
        self.conv_b = self.din("conv_b_l", [DEPTH, 128, 8])
        self.conv_brow = self.din("conv_brow", [DEPTH, 1, 1024])
        self.dt_bias = self.din("dt_bias", [DEPTH, 16])
        self.a_log = self.din("a_log", [DEPTH, 16])
        self.ssm_d = self.din("ssm_d", [DEPTH, 8])
        self.ssm_g = self.din("ssm_g", [DEPTH, 512])
        self.bm_tab = self.din("bm_tab", [DEPTH, 128, 84 * 128])
        self.out = self.nc.dram_tensor("out", [SEQ, D], F32, kind="ExternalOutput").ap()
        self.MOD = self.dscr("MOD", [2, 6, 128, D])
        self.FM = [self.dscr("FM_l", [NFMO, 128, SEQ], BF16), self.dscr("FM_c", [NFMO, 128, LC], BF16)]
        self.ZS = [self.dscr("ZS_l", [SEQ, 512], BF16), self.dscr("ZS_c", [LC, 512], BF16)]
        self.VA = [self.dscr("VA_l", [SEQ, 2 * VS], BF16), self.dscr("VA_c", [LC, 2 * VS], BF16)]
        self.VB = [self.dscr("VB_l", [SEQ, 4 * VS], BF16), self.dscr("VB_c", [LC, 4 * VS], BF16)]
        self.DT = [self.dscr("DT_l", [SEQ, 16]), self.dscr("DT_c", [LC, 16])]
        self.MIX = [self.dscr("MIX_l", [SEQ, D], BF16), self.dscr("MIX_c", [LC, D], BF16)]
        self.XSS = [self.dscr("XSS_l", [SEQ, 768], BF16), self.dscr("XSS_c", [LC, 768], BF16)]
        self.XM = [self.dscr("XM_l", [SEQ, D]), self.dscr("XM_c", [LC, D])]
        self.XL = [self.dscr("XL_l", [SEQ, D]), self.dscr("XL_c", [LC, D])]

    def setup_common(self, st):
        nc, S = self.nc, self.S
        self.ps = st.enter_context(nc.psum_tensor("ps", [128, 4096], F32))
        self.psb = [(self.ps[:, b * 512:(b + 1) * 512], Res("bank%d" % b)) for b in range(8)]
        self.ident = self.sb(st, "ident", [128, 128], BF16)
        self.r_ident = Res("ident")
        ident = self.ident

        S.op("pool", lambda e: e.memset(ident[:], 0.0), writes=[self.r_ident])
        S.op("pool", lambda e: e.affine_select(out=ident[:], in_=ident[:], pattern=[[-1, 128]], compare_op=ALU.not_equal,
                                               fill=1.0, base=0, channel_multiplier=1),
             reads=[self.r_ident], writes=[self.r_ident])
        self.maskP = self.sb(st, "maskP", [128, 128], BF16)
        self.maskN = self.sb(st, "maskN", [128, 128], BF16)
        self.r_mask = Res("mask")
        mP, mN = self.maskP, self.maskN
        r1, r2 = Res(), Res()
        S.op("pool", lambda e: e.memset(mP[:], 0.0), writes=[r1])
        S.op("pool", lambda e: e.memset(mN[:], 0.0), writes=[r2])
        S.op("pool", lambda e: e.affine_select(out=mP[:], in_=mP[:], pattern=[[-1, 128]], compare_op=ALU.is_ge,
                                               fill=NEG, base=0, channel_multiplier=1), reads=[r1], writes=[r1])
        S.op("pool", lambda e: e.affine_select(out=mN[:], in_=mN[:], pattern=[[1, 128]], compare_op=ALU.is_ge,
                                               fill=NEG, base=0, channel_multiplier=-1), reads=[r2], writes=[r2])
        S.op("pool", lambda e: e.memset(self.ident[0:1, 0:1], 1.0), reads=[r1, r2, self.r_ident], writes=[self.r_mask, self.r_ident])
        self.U32 = self.sb(st, "U32", [128, 128])
        self.L32 = self.sb(st, "L32", [128, 128])
        self.ones32 = self.sb(st, "ones32", [128, 128])
        self.r_tri = Res("tri")
        U32, L32, ones32 = self.U32, self.L32, self.ones32
        r3, r4 = Res(), Res()
        S.op("pool", lambda e: e.memset(U32[:], 1.0), writes=[r3])
        S.op("pool", lambda e: e.memset(L32[:], 1.0), writes=[r4])
        S.op("pool", lambda e: e.memset(ones32[:], 1.0), writes=[self.r_tri])
        S.op("pool", lambda e: e.affine_select(out=U32[:], in_=U32[:], pattern=[[1, 128]], compare_op=ALU.is_ge,
                                               fill=0.0, base=0, channel_multiplier=-1), reads=[r3], writes=[r3])
        S.op("pool", lambda e: e.affine_select(out=L32[:], in_=L32[:], pattern=[[-1, 128]], compare_op=ALU.is_ge,
                                               fill=0.0, base=0, channel_multiplier=1), reads=[r4], writes=[r4])
        S.op("pool", lambda e: e.memset(ones32[0:1, 0:1], 1.0), reads=[r3, r4, self.r_tri], writes=[self.r_tri])
        self.bank_i = 0

    def bank(self):
        b = self.psb[self.bank_i % 8]
        self.bank_i += 1
        return b

    def p0(self, l):
        nc, S = self.nc, self.S
        with ExitStack() as st:
            cv = self.sb(st, "p0_cv", [128, 16])
            scv = self.sb(st, "p0_scv", [128, 16])
            scbc = self.sb(st, "p0_scbc", [128, 16, 128])
            gbc = self.sb(st, "p0_gbc", [128, 2, D])
            wb = [self.sb(st, "p0_w%d" % i, [128, 8, 512]) for i in range(2)]
            bb = [self.sb(st, "p0_b%d" % i, [128, 512]) for i in range(2)]
            mt = [self.sb(st, "p0_m%d" % i, [128, D]) for i in range(4)]
            r_cv, r_scv, r_scbc, r_g = Res(), Res(), Res(), Res()
            wring = Ring(wb, "p0w")
            bring = Ring(bb, "p0b")
            mring = Ring(mt, "p0m")
            S.dma("sp", cv[:], self.cvec, writes=[r_cv])
            S.dma("sp", gbc[:, 0, :], self.g_mix[l].partition_broadcast(128), writes=[r_g])
            S.dma("sp", gbc[:, 1, :], self.g_ffn[l].partition_broadcast(128), writes=[r_g])
            S.op("act", lambda e: e.activation(out=scv[:], in_=cv[:], func=AF.Silu), reads=[r_cv], writes=[r_scv])
            S.op("dve", lambda e: e.tensor_copy(out=scbc[:], in_=scv[:].unsqueeze(2).to_broadcast([128, 16, 128])),
                 reads=[r_scv], writes=[r_scbc])
            wv = self.w_mod[l].rearrange("(k p) n -> p k n", p=128)
            cur = {}
            for blk in range(12):
                j, half = blk // 2, blk % 2
                w_t, r_w = wring.next()
                b_t, r_b = bring.next()
                S.dma("sp", w_t[:], wv[:, :, blk * 512:(blk + 1) * 512], writes=[r_w])
                S.dma("sp", b_t[:], self.b_mod[l, blk * 512:(blk + 1) * 512].partition_broadcast(128), writes=[r_b])
                for s in range(2):
                    if half == 0:
                        cur[s] = mring.next()
                    m_t, r_m = cur[s]
                    pb, r_pb = self.bank()

                    def mm(e, pb=pb, w_t=w_t, s=s):
                        for k in range(8):
                            ins = e.matmul(pb, lhsT=scbc[:, s * 8 + k, :], rhs=w_t[:, k, :], start=(k == 0), stop=(k == 7))
                        return ins
                    S.op("pe", mm, reads=[r_scbc, r_w], writes=[r_pb])
                    dst = m_t[:, half * 512:(half + 1) * 512]
                    if j in (1, 4):
                        gsl = gbc[:, 0 if j == 1 else 1, half * 512:(half + 1) * 512]
                        tmp_r = Res()

                        def ev(e, dst=dst, pb=pb, b_t=b_t, gsl=gsl):
                            e.tensor_tensor(out=dst, in0=pb, in1=b_t[:], op=ALU.add)
                            return e.scalar_tensor_tensor(out=dst, in0=dst, scalar=1.0, in1=gsl, op0=ALU.add, op1=ALU.mult)
                        S.op("dve", lambda e, dst=dst, pb=pb, b_t=b_t: e.tensor_tensor(out=dst, in0=pb, in1=b_t[:], op=ALU.add),
                             reads=[r_pb, r_b], writes=[r_m])
                        S.op("dve", lambda e, dst=dst, gsl=gsl: e.scalar_tensor_tensor(out=dst, in0=dst, scalar=1.0, in1=gsl,
                                                                                         op0=ALU.add, op1=ALU.mult),
                             reads=[r_m, r_g], writes=[r_m])
                    else:
                        S.op("dve", lambda e, dst=dst, pb=pb, b_t=b_t: e.tensor_tensor(out=dst, in0=pb, in1=b_t[:], op=ALU.add),
                             reads=[r_pb, r_b], writes=[r_m])
                    if half == 1:
                        S.dma("sp", self.MOD[s, j], m_t[:], reads=[r_m], writes=[self.res("MOD", s, j)])

    def p1(self, l, src, do_ctx_q):
        nc, S = self.nc, self.S
        with ExitStack() as st:
            W = self.sb(st, "p1_W", [128, 8, WCOLS], BF16)
            r_Wall = []
            wv = self.w_in[l].rearrange("(k p) n -> p k n", p=128)
            for k in range(8):
                for c0 in range(0, WCOLS, 1608):
                    r = Res()
                    r_Wall.append(r)
                    S.dma("pool", W[:, k, c0:c0 + 1608], wv[:, k, c0:c0 + 1608], writes=[r])
            modt = self.sb(st, "p1_mod", [128, 2, 2, D])
            r_mod = Res("p1mod")
            for s in range(2):
                for jj, j in enumerate((0, 1)):
                    S.dma("sp", modt[:, s, jj, :], self.MOD[s, j], reads=[self.res("MOD", s, j)], writes=[r_mod])
            xr = Ring([self.sb(st, "p1_x%d" % i, [128, D]) for i in range(3)], "p1x")
            junk = self.sb(st, "p1_junk", [128, D], BF16)
            r_junk = Res()
            stat = Ring([self.sb(st, "p1_st%d" % i, [128, 4]) for i in range(3)], "p1st")
            tmpr = Ring([self.sb(st, "p1_t%d" % i, [128, D]) for i in range(2)], "p1t")
            hr = Ring([self.sb(st, "p1_h%d" % i, [128, D], BF16) for i in range(8)], "p1h")
            hTr = Ring([self.sb(st, "p1_hT%d" % i, [128, 8, 512], BF16) for i in range(2)], "p1hT")
            ropr = Ring([self.sb(st, "p1_rp%d" % i, [128, 4, 512]) for i in range(2)], "p1rp")
            rtr = Ring([self.sb(st, "p1_rt%d" % i, [128, 2, 512]) for i in range(3)], "p1rt")
            fmr = Ring([self.sb(st, "p1_fm%d" % i, [128, NFMO, 512], BF16) for i in range(2)], "p1fm")
            zr = Ring([self.sb(st, "p1_z%d" % i, [128, 4, 512], BF16) for i in range(2)], "p1z")
            var_ = [self.sb(st, "p1_va%d" % i, [128, 4, 2, VS], BF16) for i in range(2)]
            vbr_ = [self.sb(st, "p1_vb%d" % i, [128, 4, 4, VS], BF16) for i in range(2)]
            dtr = Ring([self.sb(st, "p1_dt%d" % i, [128, 4, 16]) for i in range(2)], "p1dt")
            var = Ring(var_, "p1va")
            vbr = Ring(vbr_, "p1vb")
            for (t_, r_) in var.items + vbr.items:
                S.op("pool", lambda e, t_=t_: e.memset(t_[:], 1.0), writes=[r_])

            groups = [(1, 0, NTC)] + [(0, g * 4, 4) for g in range(NT // 4)]
            lim = self.opts.get('p1_lim', 9)
            groups = groups[:self.opts.get('p1_groups', 99)]
            if lim == 0:
                groups = []
            def norm_group(grp):
                (s, t0, ntile) = grp
                TG = ntile * 128
                tok0 = t0 * 128
                hT, r_hT = hTr.next()
                fins = []
                G1 = modt[:, s, 1, :]
                SH1 = modt[:, s, 0, :]
                for ti in range(ntile):
                    x_t, r_x = xr.next()
                    S.dma("sp", x_t[:], src[s][(t0 + ti) * 128:(t0 + ti + 1) * 128, :],
                          reads=[self.res("XL", s, t0 + ti)], writes=[r_x])
                    st_t, r_st = stat.next()
                    S.op("act", lambda e, x_t=x_t, st_t=st_t: e.activation(out=junk[:], in_=x_t[:], func=AF.Square,
                                                                            accum_out=st_t[:, 0:1]),
                         reads=[r_x], writes=[r_junk, r_st])
                    S.op("act", lambda e, st_t=st_t: e.activation(out=st_t[:, 1:2], in_=st_t[:, 0:1], func=AF.Sqrt,
                                                                  scale=1.0 / D, bias=EPS),
                         reads=[r_st], writes=[r_st])
                    S.op("dve", lambda e, st_t=st_t: e.reciprocal(out=st_t[:, 2:3], in_=st_t[:, 1:2]), reads=[r_st], writes=[r_st])
                    tm, r_tm = tmpr.next()
                    S.op("dve", lambda e, tm=tm, x_t=x_t, st_t=st_t, G1=G1: e.scalar_tensor_tensor(
                        out=tm[:], in0=x_t[:], scalar=st_t[:, 2:3], in1=G1, op0=ALU.mult, op1=ALU.mult),
                        reads=[r_x, r_st, r_mod], writes=[r_tm])
                    h_t, r_h = hr.next()
                    S.op("pool", lambda e, h_t=h_t, tm=tm, SH1=SH1: e.tensor_tensor(out=h_t[:], in0=tm[:], in1=SH1, op=ALU.add),
                         reads=[r_tm, r_mod], writes=[r_h])
                    def fin(h_t=h_t, r_h=r_h, hT=hT, r_hT=r_hT, ti=ti):
                        pb, r_pb = self.bank()
                        pbT = pb.bitcast(BF16)

                        def tr(e, pbT=pbT, h_t=h_t):
                            for k in range(8):
                                ins = e.transpose(out=pbT[:, k * 128:(k + 1) * 128], in_=h_t[:, k * 128:(k + 1) * 128],
                                                  identity=self.ident[:])
                            return ins
                        S.op("pe", tr, reads=[r_h, self.r_ident], writes=[r_pb])
                        S.op("act", lambda e, hT=hT, ti=ti, pbT=pbT: e.copy(out=hT[:, :, ti * 128:(ti + 1) * 128],
                                                                             in_=pbT.rearrange("p (k t) -> p k t", k=8)),
                             reads=[], writes=[r_hT, r_pb])
                    fins.append(fin)
                return hT, r_hT, fins

            pend = norm_group(groups[0]) if groups else None
            if pend:
                for f_ in pend[2]:
                    f_()
            for gi, (s, t0, ntile) in enumerate(groups):
                TG = ntile * 128
                tok0 = t0 * 128
                hT, r_hT, _ = pend
                pend = norm_group(groups[gi + 1]) if gi + 1 < len(groups) else None
                defer = list(pend[2]) if pend else []
                if lim <= 1:
                    continue
                fm, r_fm = fmr.next()
                if s == 0:
                    rp, r_rp = ropr.next()
                    S.dma("sp", rp[:], self.rope[:, :, tok0:tok0 + TG].rearrange("c p t -> p c t"), writes=[r_rp])

                def fm_mm(ct, pb, TG=TG, hT=hT):
                    def f(e):
                        for k in range(8):
                            ins = e.matmul(pb[:, 0:TG], lhsT=W[:, k, ct * 128:(ct + 1) * 128], rhs=hT[:, k, 0:TG],
                                           start=(k == 0), stop=(k == 7))
                        return ins
                    return f
                for (ct, slot, ci) in ((0, 0, 0), (2, 1, 0), (4, 2, 2)):
                    pq, r_pq = self.bank()
                    S.op("pe", fm_mm(ct, pq), reads=r_Wall + [r_hT], writes=[r_pq])
                    if s == 0:
                        psw, r_psw = self.bank()
                        S.op("pe", fm_mm(ct + 1, psw), reads=r_Wall + [r_hT], writes=[r_psw])
                        rt, r_rt = rtr.next()
                        S.op("dve", lambda e, rt=rt, pq=pq, rp=rp, ci=ci, TG=TG: e.tensor_tensor(
                            out=rt[:, 0, 0:TG], in0=pq[:, 0:TG], in1=rp[:, ci, 0:TG], op=ALU.mult),
                            reads=[r_pq, r_rp], writes=[r_rt])
                        S.op("dve", lambda e, rt=rt, psw=psw, rp=rp, ci=ci, TG=TG: e.tensor_tensor(
                            out=rt[:, 1, 0:TG], in0=psw[:, 0:TG], in1=rp[:, ci + 1, 0:TG], op=ALU.mult),
                            reads=[r_psw, r_rp], writes=[r_rt])
                        S.op("pool", lambda e, rt=rt, fm=fm, slot=slot, TG=TG: e.tensor_tensor(
                            out=fm[:, slot, 0:TG], in0=rt[:, 0, 0:TG], in1=rt[:, 1, 0:TG], op=ALU.add),
                            reads=[r_rt], writes=[r_fm])
                    else:
                        sc_ = 0.125 if ct < 4 else 1.0
                        S.op("act", lambda e, fm=fm, slot=slot, pq=pq, TG=TG, sc_=sc_: e.activation(
                            out=fm[:, slot, 0:TG], in_=pq[:, 0:TG], func=AF.Copy, scale=sc_),
                            reads=[r_pq], writes=[r_fm])
                for ct in range(6, NFM):
                    if defer and ct in (8, 11, 14, 17):
                        defer.pop(0)()
                    slot = ct - 3
                    pq, r_pq = self.bank()
                    S.op("pe", fm_mm(ct, pq), reads=r_Wall + [r_hT], writes=[r_pq])
                    sc_ = 0.125 if ct < 8 else 1.0
                    if ct % 2 == 0:
                        S.op("act", lambda e, fm=fm, slot=slot, pq=pq, TG=TG, sc_=sc_: e.activation(
                            out=fm[:, slot, 0:TG], in_=pq[:, 0:TG], func=AF.Copy, scale=sc_),
                            reads=[r_pq], writes=[r_fm])
                    else:
                        S.op("dve", lambda e, fm=fm, slot=slot, pq=pq, TG=TG, sc_=sc_: e.tensor_scalar(
                            out=fm[:, slot, 0:TG], in0=pq[:, 0:TG], scalar1=sc_, scalar2=None, op0=ALU.mult),
                            reads=[r_pq], writes=[r_fm])
                for c0 in range(0, NFMO, 5):
                    S.dma("sp", self.FM[s][c0:c0 + 5, :, tok0:tok0 + TG].rearrange("c p t -> p c t"), fm[:, c0:c0 + 5, 0:TG],
                          reads=[r_fm], writes=[self.res("FM", s, t0 // 4, c0)])
                while defer:
                    defer.pop(0)()
                if lim <= 2:
                    continue
                z_t, r_z = zr.next()
                va_t, r_va = var.next()
                vb_t, r_vb = vbr.next()
                dt_t, r_dt = dtr.next()
                tmm = self.opts.get('tm_mask', 7)
                for ti in range(ntile):
                    def tm_mm(pb, c0, n, hT=hT, ti=ti):
                        def f(e):
                            for k in range(8):
                                ins = e.matmul(pb[:, 0:n], lhsT=hT[:, k, ti * 128:(ti + 1) * 128],
                                               rhs=W[:, k, c0:c0 + n], start=(k == 0), stop=(k == 7))
                            return ins
                        return f
                    if not (tmm & 1):
                        continue
                    pz, r_pz = self.bank()
                    S.op("pe", tm_mm(pz, NFM * 128, 512), reads=r_Wall + [r_hT], writes=[r_pz])
                    S.op("act", lambda e, z_t=z_t, ti=ti, pz=pz: e.activation(out=z_t[:, ti, :], in_=pz, func=AF.Silu),
                         reads=[r_pz], writes=[r_z])
                    if not (tmm & 2):
                        continue
                    pv, r_pv = self.bank()
                    S.op("pe", tm_mm(pv, NFM * 128 + 512, 400), reads=r_Wall + [r_hT], writes=[r_pv])
                    if not (tmm & 8):
                      S.op("dve", lambda e, va_t=va_t, ti=ti, pv=pv: e.tensor_copy(
                        out=va_t[:, ti, :, 0:64], in_=pv[:, 0:128].rearrange("p (g d) -> p g d", g=2)),
                        reads=[r_pv], writes=[r_va, r_pv])
                    if not (tmm & 16):
                      S.op("dve", lambda e, vb_t=vb_t, ti=ti, pv=pv: e.tensor_copy(
                        out=vb_t[:, ti, :, 0:64], in_=pv[:, 128:384].rearrange("p (g d) -> p g d", g=4)),
                        reads=[r_pv], writes=[r_vb, r_pv])
                    if not (tmm & 32):
                      S.op("act", lambda e, dt_t=dt_t, ti=ti, pv=pv: e.copy(out=dt_t[:, ti, :], in_=pv[:, 384:400]),
                         reads=[r_pv], writes=[r_dt, r_pv])
                if not (tmm & 4):
                    continue
                rows = slice(tok0, tok0 + TG)
                gk = t0 // 4
                S.dma("sp", self.ZS[s][rows, :].rearrange("(t p) c -> p t c", p=128), z_t[:, 0:ntile, :],
                      reads=[r_z], writes=[self.res("ZS", s, gk)])
                S.dma("sp", self.VA[s][rows, :].rearrange("(t p) c -> p t c", p=128),
                      va_t[:, 0:ntile].rearrange("p t g d -> p t (g d)"), reads=[r_va], writes=[self.res("VA", s, gk)])
                S.dma("sp", self.VB[s][rows, :].rearrange("(t p) c -> p t c", p=128),
                      vb_t[:, 0:ntile].rearrange("p t g d -> p t (g d)"), reads=[r_vb], writes=[self.res("VB", s, gk)])
                S.dma("sp", self.DT[s][rows, :].rearrange("(t p) c -> p t c", p=128), dt_t[:, 0:ntile, :],
                      reads=[r_dt], writes=[self.res("DT", s, gk)])

    @staticmethod
    def nb_window(i):
        s0 = min(max(2 * i - 4, 0), 56)
        s1 = min(max(2 * i + 1 - 4, 0), 56)
        return list(range(s0 // 2, (s1 + 7) // 2 + 1))

    @staticmethod
    def bm_index(i, j):
        if 2 <= i <= 29:
            return j - i + 2
        if i == 0:
            return 5 + j
        if i == 1:
            return 9 + j
        if i == 30:
            return 13 + (j - 28)
        return 17 + (j - 28)

    def p_attn(self, l, kind, streams):
        nc, S = self.nc, self.S
        isA = kind == "A"
        nkt = 1 if isA else 2
        ng = 2 if isA else 4
        qs0 = 0 if isA else 3
        ks0 = 2 if isA else 5
        VSRC = self.VA if isA else self.VB
        col0 = 0 if isA else 256
        with ExitStack() as st:
            QT = self.sb(st, "at_Q", [128, 2, 2, SEQ], BF16)
            QTc = self.sb(st, "at_Qc", [128, 2, 2, LC], BF16)
            KT = self.sb(st, "at_K", [128, nkt, SEQ + LC], BF16)
            V = self.sb(st, "at_V", [128, NT + NTC, ng * VS], BF16)
            r_in = Res()
            for T in range(2):
                for hh in range(2):
                    lo, zl = hh * 64, (1 - hh) * 64
                    S.op("pool", lambda e, T=T, hh=hh, zl=zl: e.memset(QT[zl:zl + 64, T, hh, :], 0.0), writes=[r_in])
                    S.op("pool", lambda e, T=T, hh=hh, zl=zl: e.memset(QTc[zl:zl + 64, T, hh, :], 0.0), writes=[r_in])
                    S.dma("sp", QT[lo:lo + 64, T, hh, :], self.FM[0][qs0 + T][lo:lo + 64, :], writes=[r_in])
                    S.dma("sp", QTc[lo:lo + 64, T, hh, :], self.FM[1][qs0 + T][lo:lo + 64, :], writes=[r_in])
            for kt in range(nkt):
                S.dma("sp", KT[:, kt, 0:SEQ], self.FM[0][ks0 + kt], writes=[r_in])
                S.dma("sp", KT[:, kt, SEQ:SEQ + LC], self.FM[1][ks0 + kt], writes=[r_in])
            for t0 in range(0, NT, 8):
                S.dma("sp", V[:, t0:t0 + 8, :], VSRC[0][t0 * 128:(t0 + 8) * 128, :].rearrange("(t p) c -> p t c", p=128), writes=[r_in])
            S.dma("sp", V[:, NT:NT + NTC, :], VSRC[1].rearrange("(t p) c -> p t c", p=128), writes=[r_in])
            r_c = Res()
            if isA:
                esk = self.sb(st, "at_esk", [128, 4])
                S.dma("sp", esk[:], self.wa_sink[l].partition_broadcast(128), writes=[r_c])
                S.op("act", lambda e: e.activation(out=esk[:], in_=esk[:], func=AF.Exp), reads=[r_c], writes=[r_c])
            else:
                BM = self.sb(st, "at_BM", [128, 84 * 128], BF16)
                for c0 in range(0, 84 * 128, 1792):
                    S.dma("pool", BM[:, c0:c0 + 1792], self.bm_tab[l][:, c0:c0 + 1792], writes=[r_c])
            Sring = Ring([self.ps[:, i * 1024:(i + 1) * 1024] for i in range(3)])
            Oring = Ring([self.ps[:, 3072 + i * 512:3072 + (i + 1) * 512] for i in range(2)])
            PTr = Ring([self.sb(st, "at_PT%d" % i, [128, 7 * 128], BF16) for i in range(3)])
            mor = Ring([self.sb(st, "at_mo%d" % i, [128, 4, 64], BF16) for i in range(3)])
            dnr = Ring([self.sb(st, "at_dn%d" % i, [128, 8]) for i in range(3)])
            for s in streams:
                ntl = min(NT if s == 0 else NTC, self.opts.get("at_nt", 99))
                for n in range(ntl):
                    O, r_O = Oring.next()
                    pend_pv = []
                    for T in range(2):
                        for hh in range(2):
                            h = (2 * hh + T) if isA else (2 * T + hh)
                            g = hh if isA else h
                            kt = 0 if isA else T
                            ps_ = slice(hh * 64, (hh + 1) * 64)
                            chunks = []
                            if s == 0:
                                if isA:
                                    if n > 0:
                                        chunks.append(((n - 1) * 128, self.maskP[:], n - 1))
                                    chunks.append((n * 128, None, n))
                                    if n < NT - 1:
                                        chunks.append(((n + 1) * 128, self.maskN[:], n + 1))
                                else:
                                    for j in self.nb_window(n):
                                        bi = h * 21 + self.bm_index(n, j)
                                        chunks.append((j * 128, BM[:, bi * 128:(bi + 1) * 128], j))
                            chunks.append((SEQ, None, NT))
                            chunks.append((SEQ + 128, None, NT + 1))
                            nch = len(chunks)
                            q_ap = (QT if s == 0 else QTc)[:, T, hh, n * 128:(n + 1) * 128]
                            Sp, r_S = Sring.next()

                            def smm(e, Sp=Sp, chunks=chunks, q_ap=q_ap, kt=kt, ps_=ps_):
                                for c, (kc, bias, vt) in enumerate(chunks):
                                    ins = e.matmul(Sp[:, c * 128:(c + 1) * 128], lhsT=KT[:, kt, kc:kc + 128], rhs=q_ap,
                                                   start=True, stop=(bias is None))
                                    if bias is not None:
                                        ins = e.matmul(Sp[:, c * 128:(c + 1) * 128], lhsT=self.ident[:], rhs=bias,
                                                       start=False, stop=True)
                                return ins
                            S.op("pe", smm, reads=[r_in, r_c, self.r_ident, self.r_mask], writes=[r_S])
                            PT, r_PT = PTr.next()
                            S.op("act", lambda e, PT=PT, Sp=Sp, nch=nch: e.activation(out=PT[:, 0:nch * 128], in_=Sp[:, 0:nch * 128], func=AF.Exp),
                                 reads=[], writes=[r_PT, r_S])

                            def pv(e, O=O, PT=PT, chunks=chunks, h=h, g=g):
                                for c, (kc, bias, vt) in enumerate(chunks):
                                    ins = e.matmul(O[:, h * VS:h * VS + 65], lhsT=PT[:, c * 128:(c + 1) * 128],
                                                   rhs=V[:, vt, g * VS:g * VS + 65], start=(c == 0), stop=(c == len(chunks) - 1))
                                return ins
                            pend_pv.append((pv, [r_PT, r_in], [r_O]))
                            if len(pend_pv) > 1:
                                f_, rd_, wr_ = pend_pv.pop(0)
                                S.op("pe", f_, reads=rd_, writes=wr_)
                    while pend_pv:
                        f_, rd_, wr_ = pend_pv.pop(0)
                        S.op("pe", f_, reads=rd_, writes=wr_)
                    dn, r_dn = dnr.next()
                    O3 = O[:, 0:4 * VS].rearrange("p (h c) -> p h c", c=VS)
                    if isA:
                        S.op("dve", lambda e, dn=dn, O3=O3: e.tensor_tensor(out=dn[:, 0:4].unsqueeze(2), in0=O3[:, :, 64:65],
                                                                             in1=esk[:].unsqueeze(2), op=ALU.add),
                             reads=[r_c], writes=[r_dn, r_O])
                    else:
                        S.op("dve", lambda e, dn=dn, O3=O3: e.tensor_copy(out=dn[:, 0:4].unsqueeze(2), in_=O3[:, :, 64:65]),
                             reads=[], writes=[r_dn, r_O])
                    S.op("dve", lambda e, dn=dn: e.reciprocal(out=dn[:, 4:8], in_=dn[:, 0:4]), reads=[r_dn], writes=[r_dn])
                    mo, r_mo = mor.next()
                    S.op("dve", lambda e, mo=mo, O3=O3, dn=dn: e.tensor_tensor(
                        out=mo[:], in0=O3[:, :, 0:64], in1=dn[:, 4:8].unsqueeze(2).to_broadcast([128, 4, 64]), op=ALU.mult),
                        reads=[r_dn], writes=[r_mo, r_O])
                    S.dma("sp", self.MIX[s][n * 128:(n + 1) * 128, col0:col0 + 256], mo[:].rearrange("p h d -> p (h d)"),
                          reads=[r_mo], writes=[self.res("MIX", s, n, kind)])

    def p2(self, l, streams):
        self.p_attn(l, "A", streams)

    def p3(self, l, streams):
        self.p_attn(l, "B", streams)

    def p4(self, l, streams):
        nc, S = self.nc, self.S
        last = (l == DEPTH - 1)
        PB = self.psb
        with ExitStack() as st:
            cw = self.sb(st, "s_cw", [128, 8, 7])
            cb = self.sb(st, "s_cb", [128, 8])
            cbrow = self.sb(st, "s_cbrow", [128, 1024], BF16)
            ones1 = self.sb(st, "s_ones1", [128, 128], BF16)
            diag = self.sb(st, "s_diag", [128, 8, 7, 128], BF16)
            Dh = self.sb(st, "s_Dh", [128, 8, 128], BF16)
            sm = self.sb(st, "s_sm", [128, 64])
            gn = self.sb(st, "s_gn", [128, 512])
            r_p = Res("ssm_params")
            r_diag = Res("diag")
            S.dma("sp", cw[:], self.conv_w[l], writes=[r_p])
            S.dma("sp", cb[:], self.conv_b[l], writes=[r_p])
            r_cb0 = Res()
            S.op("pool", lambda e: e.memset(cbrow[:], 0.0), writes=[r_cb0])
            S.dma("pool", cbrow[0:1, :], self.conv_brow[l], reads=[r_cb0], writes=[r_p, r_cb0])
            S.dma("sp", sm[:, 0:16], self.dt_bias[l].partition_broadcast(128), writes=[r_p])
            S.dma("sp", sm[:, 16:32], self.a_log[l].partition_broadcast(128), writes=[r_p])
            S.dma("sp", sm[:, 32:40], self.ssm_d[l].partition_broadcast(128), writes=[r_p])
            S.dma("sp", gn[:], self.ssm_g[l].partition_broadcast(128), writes=[r_p])
            S.op("pool", lambda e: e.memset(ones1[:], 0.0), writes=[r_diag])
            S.op("pool", lambda e: e.memset(ones1[0:1, :], 1.0), reads=[r_diag], writes=[r_diag])
            S.op("act", lambda e: e.activation(out=sm[:, 40:56], in_=sm[:, 16:32], func=AF.Exp), reads=[r_p], writes=[r_p])
            S.op("dve", lambda e: e.tensor_scalar(out=sm[:, 16:32], in0=sm[:, 40:56], scalar1=-1.0, scalar2=None, op0=ALU.mult),
                 reads=[r_p], writes=[r_p])
            for c in range(8):
                for j in range(7):
                    S.op("dve", lambda e, c=c, j=j: e.tensor_scalar(out=diag[:, c, j, :], in0=self.ident[:], scalar1=cw[:, c, j:j + 1],
                                                                    scalar2=None, op0=ALU.mult),
                         reads=[r_p, self.r_ident], writes=[r_diag])
            for h in range(8):
                S.op("dve", lambda e, h=h: e.tensor_scalar(out=Dh[:, h, :], in0=self.ident[:], scalar1=sm[:, 32 + h:33 + h],
                                                            scalar2=None, op0=ALU.mult),
                     reads=[r_p, self.r_ident], writes=[r_diag])
            ULb = self.sb(st, "s_ULb", [128, 16, 128], BF16)
            MK4 = self.sb(st, "s_MK4", [128, 2, 512], BF16)
            onesb = self.sb(st, "s_onesb", [128, 128], BF16)
            r_ul = Res("ULb")
            S.op("dve", lambda e: e.tensor_copy(out=ULb[:, 0:8, :], in_=self.U32[:].unsqueeze(1).to_broadcast([128, 8, 128])),
                 reads=[self.r_tri], writes=[r_ul])
            S.op("dve", lambda e: e.tensor_copy(out=ULb[:, 8:16, :], in_=self.L32[:].unsqueeze(1).to_broadcast([128, 8, 128])),
                 reads=[self.r_tri, r_ul], writes=[r_ul])
            S.op("dve", lambda e: e.tensor_copy(out=MK4[:, 0, :].rearrange("p (a b) -> p a b", a=4), in_=self.maskN[:].unsqueeze(1).to_broadcast([128, 4, 128])),
                 reads=[self.r_mask, r_ul], writes=[r_ul])
            S.op("dve", lambda e: e.tensor_copy(out=MK4[:, 1, :].rearrange("p (a b) -> p a b", a=4), in_=self.maskP[:].unsqueeze(1).to_broadcast([128, 4, 128])),
                 reads=[self.r_mask, r_ul], writes=[r_ul])
            S.op("dve", lambda e: e.tensor_copy(out=onesb[:], in_=self.ones32[:]), reads=[self.r_tri, r_ul], writes=[r_ul])
            Hf = self.sb(st, "s_Hf", [128, 512])
            Hb = self.sb(st, "s_Hb", [128, 512])
            HFb = self.sb(st, "s_HFb", [128, 512], BF16)
            r_Hf, r_Hb, r_HFb = Res("Hf"), Res("Hb"), Res("HFb")
            S.op("pool", lambda e: e.memset(Hf[:], 0.0), writes=[r_Hf])
            S.op("pool", lambda e: e.memset(Hb[:], 0.0), writes=[r_Hb])
            XB = self.sb(st, "s_XB", [128, 8, SEQ + 6], BF16)
            HB = self.sb(st, "s_HB", [128, NT, 512], BF16)
            r_HB = [Res("HB%d" % c) for c in range(NT)]
            NB = 9
            big = [self.sb(st, "s_big%d" % i, [128, NT * 16]) for i in range(NB)]
            xsr = Ring([self.sb(st, "s_xs%d" % i, [128, 512], BF16) for i in range(4)])
            btr = Ring([self.sb(st, "s_bt%d" % i, [128, 256], BF16) for i in range(4)])
            bcr = Ring([self.sb(st, "s_bc%d" % i, [128, 4, 128], BF16) for i in range(4)])
            xwr = Ring([self.sb(st, "s_xw%d" % i, [128, 512], BF16) for i in range(2)])
            mr = Ring([self.sb(st, "s_m%d" % i, [128, 128]) for i in range(6)])
            wr = Ring([self.sb(st, "s_w%d" % i, [128, 128], BF16) for i in range(6)])
            t1r = Ring([self.sb(st, "s_t1%d" % i, [128, 512]) for i in range(1)])
            t2r = Ring([self.sb(st, "s_t2%d" % i, [128, 512]) for i in range(1)])
            yr = Ring([self.sb(st, "s_y%d" % i, [128, 512]) for i in range(2)])
            y1r = Ring([self.sb(st, "s_y1%d" % i, [128, 512]) for i in range(2)])
            zr = Ring([self.sb(st, "s_z%d" % i, [128, 512], BF16) for i in range(2)])
            ocr = Ring([self.sb(st, "s_oc%d" % i, [128, 512], BF16) for i in range(2)])
            str_ = Ring([self.sb(st, "s_st%d" % i, [128, 4]) for i in range(3)])
            junk = self.sb(st, "s_junk", [128, 512], BF16)
            r_junk = Res()
            tmpH = self.sb(st, "s_tmpH", [128, 512])
            r_tmpH = Res()

            r_XB = Res("XB")
            _rb = [Res("big%d" % i) for i in range(NB)]
            r_b = {0: _rb[0], 1: _rb[1], 2: _rb[2], 3: _rb[3], 4: _rb[4], 5: _rb[5], 6: _rb[6], 7: _rb[6], 8: _rb[1], 9: _rb[7], 10: _rb[8], 11: _rb[3]}
            ahi = self.sb(st, "s_ahi", [128, NT * 16], BF16)
            alo = self.sb(st, "s_alo", [128, NT * 16], BF16)
            r_ahl = Res("ahl")
            rhr = Ring([self.sb(st, "s_rh%d" % i, [128, 2, 16, 128], BF16) for i in range(2)])
            for s in streams if 1 in streams else (1,) + tuple(streams):
                with_out = not (s == 1 and last)
                nch = NTC if s == 1 else NT
                nch = min(nch, self.opts.get("ssm_nch", 99))
                T = nch * 128
                W16 = nch * 16
                S.op("pool", lambda e: e.memset(XB[:, :, 0:3], 0.0), writes=[r_XB])
                S.op("pool", lambda e, T=T: e.memset(XB[:, :, 3 + T:6 + T], 0.0), writes=[r_XB])
                for c in range(8):
                    S.dma("sp", XB[:, c, 3:3 + T], self.FM[s][7 + c][:, 0:T], writes=[r_XB])
                DTr, Ev, DTv, LN, Av, AC, TOT, DEC, EE = [b[:, 0:W16] for b in big]
                DTE, SDT, MB = TOT, Ev, LN
                v3 = lambda ap: ap.rearrange("p (c k) -> p c k", k=16)
                for c0 in range(0, nch, 8):
                    c1 = min(c0 + 8, nch)
                    S.dma("sp", v3(DTr)[:, c0:c1, :], self.DT[s][c0 * 128:c1 * 128, :].rearrange("(c p) k -> p c k", p=128), writes=[r_b[0]])
                S.op("dve", lambda e, DTr=DTr, nch=nch: e.tensor_tensor(out=v3(DTr), in0=v3(DTr), in1=sm[:, 0:16].unsqueeze(1).to_broadcast([128, nch, 16]), op=ALU.add),
                     reads=[r_p], writes=[r_b[0]])
                S.op("act", lambda e, Ev=Ev, DTr=DTr: e.activation(out=Ev, in_=DTr, func=AF.Exp), reads=[r_b[0]], writes=[r_b[1]])
                S.op("act", lambda e, Ev=Ev, DTv=DTv: e.activation(out=DTv, in_=Ev, func=AF.Ln, bias=1.0), reads=[r_b[1]], writes=[r_b[2]])
                S.op("act", lambda e, LN=LN, DTv=DTv: e.activation(out=LN, in_=DTv, func=AF.Ln), reads=[r_b[2]], writes=[r_b[3]])
                S.op("dve", lambda e, Av=Av, DTv=DTv, nch=nch: e.tensor_tensor(out=v3(Av), in0=v3(DTv), in1=sm[:, 16:32].unsqueeze(1).to_broadcast([128, nch, 16]), op=ALU.mult),
                     reads=[r_b[2], r_p], writes=[r_b[4]])
                AHI = ahi[:, 0:W16]
                ALO = alo[:, 0:W16]
                S.op("dve", lambda e, AHI=AHI, Av=Av: e.tensor_copy(out=AHI, in_=Av), reads=[r_b[4]], writes=[r_ahl])
                S.op("dve", lambda e, ALO=ALO, Av=Av, AHI=AHI: e.tensor_tensor(out=ALO, in0=Av, in1=AHI, op=ALU.subtract),
                     reads=[r_b[4], r_ahl], writes=[r_ahl])
                for (mat, tgt, lo) in ((self.U32, AC, 0), (self.L32, AC, 8), (self.ones32, TOT, None)):
                    pb, r_pb = self.bank()
                    S.op("pe", lambda e, pb=pb, mat=mat, Av=Av, W16=W16: e.matmul(pb[:, 0:W16], lhsT=mat[:], rhs=Av, start=True, stop=True),
                         reads=[r_b[4], self.r_tri], writes=[r_pb])
                    if lo is None:
                        S.op("dve", lambda e, pb=pb, tgt=tgt, W16=W16: e.tensor_copy(out=tgt, in_=pb[:, 0:W16]), reads=[], writes=[r_b[6], r_pb])
                    else:
                        S.op("dve", lambda e, pb=pb, tgt=tgt, lo=lo, W16=W16: e.tensor_copy(out=v3(tgt)[:, :, lo:lo + 8], in_=v3(pb[:, 0:W16])[:, :, lo:lo + 8]),
                             reads=[], writes=[r_b[5], r_pb])
                S.op("act", lambda e, DEC=DEC, TOT=TOT: e.activation(out=DEC, in_=TOT, func=AF.Exp), reads=[r_b[6]], writes=[r_b[9]])
                S.op("dve", lambda e, DTE=DTE, TOT=TOT, AC=AC: e.tensor_tensor(out=DTE, in0=TOT, in1=AC, op=ALU.subtract), reads=[r_b[5]], writes=[r_b[7]])
                S.op("act", lambda e, DTE=DTE: e.activation(out=DTE, in_=DTE, func=AF.Exp), reads=[], writes=[r_b[7]])
                S.op("dve", lambda e, SDT=SDT, DTE=DTE, DTv=DTv: e.tensor_tensor(out=SDT, in0=DTE, in1=DTv, op=ALU.mult), reads=[r_b[7], r_b[2]], writes=[r_b[8]])
                S.op("act", lambda e, EE=EE, AC=AC: e.activation(out=EE, in_=AC, func=AF.Exp), reads=[r_b[5]], writes=[r_b[10]])
                S.op("dve", lambda e, MB=MB, LN=LN, AC=AC: e.tensor_tensor(out=MB, in0=LN, in1=AC, op=ALU.subtract), reads=[r_b[3], r_b[5]], writes=[r_b[11]])

                def conv_chunk(c, want_fm, load=False):
                    if load:
                        xs, r_xs = xsr.next()
                        bt, r_bt = btr.next()
                        rd = [self.res("XSS", s, c)]
                        S.dma("sp", xs[:], self.XSS[s][c * 128:(c + 1) * 128, 0:512], reads=rd, writes=[r_xs])
                        S.dma("sp", bt[:], self.XSS[s][c * 128:(c + 1) * 128, 512:768], reads=rd, writes=[r_bt])
                        bct, r_bct = None, None
                        if want_fm:
                            pb2, r_pb2 = PB[2]

                            def cfm(e, pb2=pb2, c=c):
                                for q, ct in enumerate((4, 5, 6, 7)):
                                    for j in range(7):
                                        ins = e.matmul(pb2[:, q * 128:(q + 1) * 128], lhsT=diag[:, ct, j, :], rhs=XB[:, ct, c * 128 + j:c * 128 + j + 128],
                                                       start=(j == 0), stop=(j == 6))
                                return ins
                            S.op("pe", cfm, reads=[r_XB, r_diag], writes=[r_pb2])
                            bct, r_bct = bcr.next()
                            for q, ct in enumerate((4, 5, 6, 7)):
                                S.op("act", lambda e, bct=bct, q=q, ct=ct, pb2=pb2: e.activation(out=bct[:, q, :], in_=pb2[:, q * 128:(q + 1) * 128], func=AF.Silu,
                                                                                                  bias=cb[:, ct:ct + 1]),
                                     reads=[r_p], writes=[r_bct, r_pb2])
                        return xs, r_xs, bt, r_bt, bct, r_bct
                    pb, r_pb = PB[0]

                    def cx(e, pb=pb, c=c):
                        for ct in range(4):
                            for j in range(7):
                                e.matmul(pb[:, ct * 128:(ct + 1) * 128], lhsT=XB[:, ct, c * 128 + j:c * 128 + j + 128], rhs=diag[:, ct, j, :],
                                         start=(j == 0), stop=False)
                            ins = e.matmul(pb[:, ct * 128:(ct + 1) * 128], lhsT=ones1[:], rhs=cbrow[:, ct * 128:(ct + 1) * 128], start=False, stop=True)
                        return ins
                    S.op("pe", cx, reads=[r_XB, r_diag, r_p], writes=[r_pb])
                    xs, r_xs = xsr.next()
                    S.op("act", lambda e, xs=xs, pb=pb: e.activation(out=xs[:], in_=pb, func=AF.Silu), reads=[], writes=[r_xs, r_pb])
                    pb1, r_pb1 = PB[1]

                    def cbt(e, pb1=pb1, c=c):
                        for ct in range(4, 6):
                            o = pb1[:, (ct - 4) * 128:(ct - 3) * 128]
                            for j in range(7):
                                e.matmul(o, lhsT=XB[:, ct, c * 128 + j:c * 128 + j + 128], rhs=diag[:, ct, j, :], start=(j == 0), stop=False)
                            ins = e.matmul(o, lhsT=ones1[:], rhs=cbrow[:, ct * 128:(ct + 1) * 128], start=False, stop=True)
                        return ins
                    S.op("pe", cbt, reads=[r_XB, r_diag, r_p], writes=[r_pb1])
                    bt, r_bt = btr.next()
                    S.op("act", lambda e, bt=bt, pb1=pb1: e.activation(out=bt[:], in_=pb1[:, 0:256], func=AF.Silu), reads=[], writes=[r_bt, r_pb1])
                    bct, r_bct = None, None
                    if want_fm:
                        pb2, r_pb2 = PB[2]

                        def cfm(e, pb2=pb2, c=c):
                            for q, ct in enumerate((4, 5, 6, 7)):
                                for j in range(7):
                                    ins = e.matmul(pb2[:, q * 128:(q + 1) * 128], lhsT=diag[:, ct, j, :], rhs=XB[:, ct, c * 128 + j:c * 128 + j + 128],
                                                   start=(j == 0), stop=(j == 6))
                            return ins
                        S.op("pe", cfm, reads=[r_XB, r_diag], writes=[r_pb2])
                        bct, r_bct = bcr.next()
                        for q, ct in enumerate((4, 5, 6, 7)):
                            S.op("act", lambda e, bct=bct, q=q, ct=ct, pb2=pb2: e.activation(out=bct[:, q, :], in_=pb2[:, q * 128:(q + 1) * 128], func=AF.Silu,
                                                                                              bias=cb[:, ct:ct + 1]),
                                 reads=[r_p], writes=[r_bct, r_pb2])
                    return xs, r_xs, bt, r_bt, bct, r_bct

                def state_mm(c, xs, r_xs, bt, r_bt, lo):
                    xw, r_xw = xwr.next()
                    S.op("dve", lambda e, xw=xw, xs=xs, c=c, lo=lo: e.tensor_tensor(
                        out=xw[:].rearrange("p (h d) -> p h d", h=8), in0=xs[:].rearrange("p (h d) -> p h d", h=8),
                        in1=v3(SDT)[:, c, lo:lo + 8].unsqueeze(2).to_broadcast([128, 8, 64]), op=ALU.mult),
                        reads=[r_xs, r_b[8]], writes=[r_xw])
                    pb3, r_pb3 = PB[3]

                    def smm(e, pb3=pb3, bt=bt, xw=xw):
                        for g in range(2):
                            ins = e.matmul(pb3[:, g * 256:(g + 1) * 256], lhsT=bt[:, g * 128:(g + 1) * 128], rhs=xw[:, g * 256:(g + 1) * 256],
                                           start=True, stop=True)
                        return ins
                    S.op("pe", smm, reads=[r_bt, r_xw], writes=[r_pb3])
                    return pb3, r_pb3

                def scan_step(H, r_H, c, lo, pb3, r_pb3):
                    S.op("dve", lambda e, H=H, c=c, lo=lo: e.tensor_tensor(
                        out=tmpH[:].rearrange("p (h d) -> p h d", h=8), in0=H[:].rearrange("p (h d) -> p h d", h=8),
                        in1=v3(DEC)[:, c, lo:lo + 8].unsqueeze(2).to_broadcast([128, 8, 64]), op=ALU.mult),
                        reads=[r_H, r_b[9]], writes=[r_tmpH])
                    S.op("dve", lambda e, H=H, pb3=pb3: e.tensor_tensor(out=H[:], in0=pb3, in1=tmpH[:], op=ALU.add),
                         reads=[r_tmpH], writes=[r_H, r_pb3])

                nxt = conv_chunk(nch - 1, False)
                for c in range(nch - 1, -1, -1):
                    xs, r_xs, bt, r_bt, _, _ = nxt
                    S.dma("sp", self.XSS[s][c * 128:(c + 1) * 128, 0:512], xs[:], reads=[r_xs], writes=[self.res("XSS", s, c)])
                    S.dma("sp", self.XSS[s][c * 128:(c + 1) * 128, 512:768], bt[:], reads=[r_bt], writes=[self.res("XSS", s, c)])
                    if c > 0:
                        nxt = conv_chunk(c - 1, False)
                    S.op("pool", lambda e, c=c: e.tensor_copy(out=HB[:, c, :], in_=Hb[:]), reads=[r_Hb], writes=[r_HB[c]])
                    pb3, r_pb3 = state_mm(c, xs, r_xs, bt, r_bt, 8)
                    scan_step(Hb, r_Hb, c, 8, pb3, r_pb3)
                h3 = lambda ap: ap.rearrange("p (h d) -> p h d", h=8)
                a3 = lambda ap: ap.rearrange("p (c k) -> p c k", k=16)
                convs = {0: conv_chunk(0, with_out, load=True)}

                rhs_ = {}

                def build_rh(c):
                    rh, r_rh = rhr.next()
                    r_rl = Res()
                    S.op("dve", lambda e, rh=rh, c=c: e.tensor_tensor(out=rh[:, 0], in0=ULb[:], in1=a3(AHI)[:, c, :].unsqueeze(2).to_broadcast([128, 16, 128]), op=ALU.mult),
                         reads=[r_ahl, r_ul], writes=[r_rh, r_rl])
                    S.op("pool", lambda e, rh=rh, c=c: e.tensor_tensor(out=rh[:, 1], in0=ULb[:], in1=a3(ALO)[:, c, :].unsqueeze(2).to_broadcast([128, 16, 128]), op=ALU.mult),
                         reads=[r_ahl, r_ul], writes=[r_rl])
                    rhs_[c] = (rh, r_rh, r_rl)

                def head(c):
                    xs, r_xs, bt, r_bt, bct, r_bct = convs[c]
                    pG, r_pG = PB[4]

                    def gmm(e, pG=pG, bct=bct):
                        for g in range(2):
                            ins = e.matmul(pG[:, g * 128:(g + 1) * 128], lhsT=bct[:, g, :], rhs=bct[:, 2 + g, :], start=True, stop=True)
                        return ins
                    S.op("pe", gmm, reads=[r_bct], writes=[r_pG])
                    rh, r_rh, r_rl = rhs_.pop(c)
                    pY1, r_pY1 = PB[7]
                    pD_all = {}
                    for rnd in range(2):
                        pDs = []
                        pD_all[rnd] = pDs
                        for d_ in range(2):
                            pD, r_pD = PB[(5 + d_) if rnd == 0 else d_]

                            def dmm(e, pD=pD, rh=rh, d_=d_, rnd=rnd):
                                hs = slice(d_ * 8 + rnd * 4, d_ * 8 + rnd * 4 + 4)
                                e.matmul(pD, lhsT=onesb[:], rhs=rh[:, 0, hs, :], start=True, stop=False)
                                e.matmul(pD, lhsT=onesb[:], rhs=rh[:, 1, hs, :], start=False, stop=False)
                                return e.matmul(pD, lhsT=self.ident[:], rhs=MK4[:, d_, :], start=False, stop=True)
                            S.op("pe", dmm, reads=[r_rh, r_rl, r_ul, self.r_ident], writes=[r_pD])
                            pDs.append((pD, r_pD))
                    for rnd in range(2):
                        pDs = pD_all[rnd]
                        for hq in range(4):
                            h = rnd * 4 + hq
                            g = h // 4
                            ws = []
                            for d_, lo in ((0, 0), (1, 8)):
                                pD, r_pD = pDs[d_]
                                m_t, r_m = mr.next()
                                S.op("act", lambda e, m_t=m_t, pD=pD, hq=hq, c=c, lo=lo, h=h: e.activation(
                                    out=m_t[:], in_=pD[:, hq * 128:(hq + 1) * 128], func=AF.Exp, bias=MB[:, c * 16 + lo + h:c * 16 + lo + h + 1]),
                                    reads=[r_b[11]], writes=[r_m, r_pD])
                                w_t, r_w = wr.next()
                                S.op("dve", lambda e, w_t=w_t, pG=pG, g=g, m_t=m_t: e.tensor_tensor(out=w_t[:], in0=pG[:, g * 128:(g + 1) * 128], in1=m_t[:], op=ALU.mult),
                                     reads=[r_m], writes=[r_w, r_pG])
                                ws.append((w_t, r_w))

                            def ymm(e, pY1=pY1, ws=ws, xs=xs, h=h):
                                o = pY1[:, h * 64:(h + 1) * 64]
                                e.matmul(o, lhsT=ws[0][0][:], rhs=xs[:, h * 64:(h + 1) * 64], start=True, stop=False)
                                e.matmul(o, lhsT=ws[1][0][:], rhs=xs[:, h * 64:(h + 1) * 64], start=False, stop=False)
                                return e.matmul(o, lhsT=Dh[:, h, :], rhs=xs[:, h * 64:(h + 1) * 64], start=False, stop=True)
                            S.op("pe", ymm, reads=[ws[0][1], ws[1][1], r_xs, r_diag], writes=[r_pY1])
                    y1, r_y1 = y1r.next()
                    S.op("act", lambda e, y1=y1, pY1=pY1: e.copy(out=y1[:], in_=pY1), reads=[], writes=[r_y1, r_pY1])
                    return y1, r_y1

                def tail(c, y1, r_y1):
                    xs, r_xs, bt, r_bt, bct, r_bct = convs.pop(c)
                    S.op("pool", lambda e: e.tensor_copy(out=HFb[:], in_=Hf[:]), reads=[r_Hf], writes=[r_HFb])
                    pb3, r_pb3 = state_mm(c, xs, r_xs, bt, r_bt, 0)
                    scan_step(Hf, r_Hf, c, 0, pb3, r_pb3)
                    if c + 2 < nch:
                        build_rh(c + 2)
                    pY2, r_pY2 = PB[5]
                    pY3, r_pY3 = PB[6]

                    def y2mm(e, pY2=pY2, bct=bct):
                        for g in range(2):
                            ins = e.matmul(pY2[:, g * 256:(g + 1) * 256], lhsT=bct[:, 2 + g, :], rhs=HFb[:, g * 256:(g + 1) * 256], start=True, stop=True)
                        return ins
                    S.op("pe", y2mm, reads=[r_bct, r_HFb], writes=[r_pY2])

                    def y3mm(e, pY3=pY3, bct=bct, c=c):
                        for g in range(2):
                            ins = e.matmul(pY3[:, g * 256:(g + 1) * 256], lhsT=bct[:, 2 + g, :], rhs=HB[:, c, g * 256:(g + 1) * 256], start=True, stop=True)
                        return ins
                    S.op("pe", y3mm, reads=[r_bct, r_HB[c]], writes=[r_pY3])
                    t1, r_t1 = t1r.next()
                    t2, r_t2 = t2r.next()
                    S.op("dve", lambda e, t1=t1, pY2=pY2, c=c: e.tensor_tensor(out=h3(t1[:]), in0=h3(pY2), in1=v3(EE)[:, c, 0:8].unsqueeze(2).to_broadcast([128, 8, 64]), op=ALU.mult),
                         reads=[r_b[10]], writes=[r_t1, r_pY2])
                    S.op("dve", lambda e, t2=t2, pY3=pY3, c=c: e.tensor_tensor(out=h3(t2[:]), in0=h3(pY3), in1=v3(EE)[:, c, 8:16].unsqueeze(2).to_broadcast([128, 8, 64]), op=ALU.mult),
                         reads=[r_b[10]], writes=[r_t2, r_pY3])
                    S.op("pool", lambda e, t1=t1, t2=t2: e.tensor_tensor(out=t1[:], in0=t1[:], in1=t2[:], op=ALU.add), reads=[r_t2], writes=[r_t1])
                    y, r_y = yr.next()
                    S.op("dve", lambda e, y=y, y1=y1, t1=t1: e.tensor_tensor(out=y[:], in0=y1[:], in1=t1[:], op=ALU.add), reads=[r_t1, r_y1], writes=[r_y])
                    z_t, r_z = zr.next()
                    S.dma("sp", z_t[:], self.ZS[s][c * 128:(c + 1) * 128, :], writes=[r_z])
                    S.op("pool", lambda e, y=y, z_t=z_t: e.tensor_tensor(out=y[:], in0=y[:], in1=z_t[:], op=ALU.mult), reads=[r_z], writes=[r_y])
                    st_t, r_st = str_.next()
                    S.op("act", lambda e, y=y, st_t=st_t: e.activation(out=junk[:], in_=y[:], func=AF.Square, accum_out=st_t[:, 0:1]),
                         reads=[r_y], writes=[r_junk, r_st])
                    S.op("act", lambda e, st_t=st_t: e.activation(out=st_t[:, 1:2], in_=st_t[:, 0:1], func=AF.Sqrt, scale=1.0 / 512, bias=EPS),
                         reads=[r_st], writes=[r_st])
                    S.op("dve", lambda e, st_t=st_t: e.reciprocal(out=st_t[:, 2:3], in_=st_t[:, 1:2]), reads=[r_st], writes=[r_st])
                    oc, r_oc = ocr.next()
                    S.op("dve", lambda e, oc=oc, y=y, st_t=st_t: e.scalar_tensor_tensor(out=oc[:], in0=y[:], scalar=st_t[:, 2:3], in1=gn[:], op0=ALU.mult, op1=ALU.mult),
                         reads=[r_y, r_st, r_p], writes=[r_oc])
                    S.dma("sp", self.MIX[s][c * 128:(c + 1) * 128, 512:1024], oc[:], reads=[r_oc], writes=[self.res("MIX", s, c, "C")])

                if with_out:
                    build_rh(0)
                    if nch > 1:
                        build_rh(1)
                        convs[1] = conv_chunk(1, with_out, load=True)
                    hd = head(0)
                    for c in range(nch):
                        nh = head(c + 1) if c + 1 < nch else None
                        tail(c, *hd)
                        if c + 2 < nch:
                            convs[c + 2] = conv_chunk(c + 2, with_out, load=True)
                        hd = nh
                else:
                    for c in range(nch):
                        xs, r_xs, bt, r_bt, _, _ = convs.pop(c)
                        if c + 1 < nch:
                            convs[c + 1] = conv_chunk(c + 1, with_out, load=True)
                        pb3, r_pb3 = state_mm(c, xs, r_xs, bt, r_bt, 0)
                        scan_step(Hf, r_Hf, c, 0, pb3, r_pb3)

    def _p4_end(self):
        pass

    def p5(self, l, src, streams, final):
        nc, S = self.nc, self.S
        NK2 = DFF // 128
        with ExitStack() as stw:
            W1 = self.sb(stw, "p5b_W1", [128, 8, 2 * DFF], BF16)
            W2 = self.sb(stw, "p5b_W2", [128, NK2, D], BF16)
            r_W1, r_W2 = [], []
            with ExitStack() as st:
                Wo = self.sb(st, "p5a_W", [128, 8, D], BF16)
                r_W = []
                wv = self.w_out[l].rearrange("(k p) n -> p k n", p=128)
                for k in range(8):
                    r = Res()
                    r_W.append(r)
                    S.dma("pool", Wo[:, k, :], wv[:, k, :], writes=[r])
                w1v = self.w_ffn_in[l].rearrange("(k p) n -> p k n", p=128)
                w2v = self.w_ffn_out[l].rearrange("(k p) n -> p k n", p=128)
                for k in range(8):
                    for c0 in range(0, 2 * DFF, 1408):
                        r = Res()
                        r_W1.append(r)
                        S.dma("pool", W1[:, k, c0:c0 + 1408], w1v[:, k, c0:c0 + 1408], writes=[r])
                for k in range(NK2):
                    r = Res()
                    r_W2.append(r)
                    S.dma("pool", W2[:, k, :], w2v[:, k, :], writes=[r])
                gt = self.sb(st, "p5a_gt", [128, 2, D])
                r_gt = Res()
                for s in streams:
                    S.dma("sp", gt[:, s, :], self.MOD[s, 2], writes=[r_gt])
                xr = Ring([self.sb(st, "p5a_x%d" % i, [128, D]) for i in range(3)])
                mr = Ring([self.sb(st, "p5a_m%d" % i, [128, D], BF16) for i in range(3)])
                mTr = Ring([self.sb(st, "p5a_mT%d" % i, [128, 8, 128], BF16) for i in range(3)])
                tr_ = Ring([self.sb(st, "p5a_t%d" % i, [128, D]) for i in range(2)])
                orr = Ring([self.sb(st, "p5a_o%d" % i, [128, D]) for i in range(2)])
                tiles = [(s, t) for s in streams for t in range(min(NT if s == 0 else NTC, self.opts.get('p5_nt', 99)))]

                def prep(s, t):
                    rows = slice(t * 128, (t + 1) * 128)
                    x_t, r_x = xr.next()
                    S.dma("sp", x_t[:], src[s][rows, :], writes=[r_x])
                    m_t, r_m = mr.next()
                    S.dma("sp", m_t[:], self.MIX[s][rows, :], writes=[r_m])
                    mT, r_mT = mTr.next()

                    def fin(m_t=m_t, r_m=r_m, mT=mT, r_mT=r_mT):
                        pb, r_pb = self.bank()
                        pbT = pb.bitcast(BF16)

                        def tr(e, pbT=pbT, m_t=m_t):
                            for k in range(8):
                                ins = e.transpose(out=pbT[:, k * 128:(k + 1) * 128], in_=m_t[:, k * 128:(k + 1) * 128],
                                                  identity=self.ident[:])
                            return ins
                        S.op("pe", tr, reads=[r_m, self.r_ident], writes=[r_pb])
                        S.op("act", lambda e, mT=mT, pbT=pbT: e.copy(out=mT[:], in_=pbT.rearrange("p (k t) -> p k t", k=8)),
                             reads=[], writes=[r_mT, r_pb])
                    return x_t, r_x, mT, r_mT, fin

                pend = prep(*tiles[0]) if tiles else None
                if pend:
                    pend[4]()
                for i, (s, t) in enumerate(tiles):
                    rows = slice(t * 128, (t + 1) * 128)
                    x_t, r_x, mT, r_mT, _ = pend
                    pend = prep(*tiles[i + 1]) if i + 1 < len(tiles) else None
                    t_t, r_t = tr_.next()
                    for half in range(2):
                        if half == 1 and pend:
                            pend[4]()
                        po, r_po = self.bank()

                        def mm(e, po=po, mT=mT, half=half):
                            for k in range(8):
                                ins = e.matmul(po, lhsT=mT[:, k, :], rhs=Wo[:, k, half * 512:(half + 1) * 512],
                                               start=(k == 0), stop=(k == 7))
                            return ins
                        S.op("pe", mm, reads=r_W + [r_mT], writes=[r_po])
                        S.op("dve", lambda e, t_t=t_t, po=po, half=half, s=s: e.tensor_tensor(
                            out=t_t[:, half * 512:(half + 1) * 512], in0=po, in1=gt[:, s, half * 512:(half + 1) * 512], op=ALU.mult),
                            reads=[r_gt], writes=[r_t, r_po])
                    o_t, r_o = orr.next()
                    S.op("dve", lambda e, o_t=o_t, t_t=t_t, x_t=x_t: e.tensor_tensor(out=o_t[:], in0=t_t[:], in1=x_t[:], op=ALU.add),
                         reads=[r_t, r_x], writes=[r_o])
                    S.dma("sp", self.XM[s][rows, :], o_t[:], reads=[r_o], writes=[self.res("XM", s, t)])
            S.barrier_all()
            with ExitStack() as st:
                modt = self.sb(st, "p5b_mod", [128, 3, D])
                r_mod = Res()
                gfin = None
                if final:
                    gfin = self.sb(st, "p5b_gf", [128, D])
                    r_gf = Res()
                    S.dma("sp", gfin[:], self.g_final.partition_broadcast(128), writes=[r_gf])
                xr = Ring([self.sb(st, "p5b_x%d" % i, [128, 2, D]) for i in range(2)])
                junk = self.sb(st, "p5b_junk", [128, D], BF16)
                r_junk = Res()
                stat = Ring([self.sb(st, "p5b_st%d" % i, [128, 8]) for i in range(6)])
                tmpr = Ring([self.sb(st, "p5b_t%d" % i, [128, D]) for i in range(2)])
                hr = Ring([self.sb(st, "p5b_h%d" % i, [128, D], BF16) for i in range(3)])
                hTr = Ring([self.sb(st, "p5b_hT%d" % i, [128, 8, 256], BF16) for i in range(2)])
                sgr = Ring([self.sb(st, "p5b_sg%d" % i, [128, 256]) for i in range(2)])
                actr = Ring([self.sb(st, "p5b_a%d" % i, [128, NK2, 256], BF16) for i in range(1)])
                orr = Ring([self.sb(st, "p5b_o%d" % i, [128, D]) for i in range(1)])
                groups = [(s, g0) for s in streams for g0 in range(0, min(NT if s == 0 else NTC, self.opts.get('p5_nt', 99)), 2)]
                cur_mod = [None]

                def normg(s, g0):
                    if cur_mod[0] != s:
                        cur_mod[0] = s
                        for jj, j in enumerate((3, 4, 5)):
                            S.dma("sp", modt[:, jj, :], self.MOD[s, j], writes=[r_mod])
                    x_t, r_x = xr.next()
                    S.dma("sp", x_t[:], self.XM[s][g0 * 128:(g0 + 2) * 128, :].rearrange("(t p) c -> p t c", p=128), writes=[r_x])
                    hT, r_hT = hTr.next()
                    fins = []
                    for ti in range(2):
                        st_t, r_st = stat.next()
                        S.op("act", lambda e, x_t=x_t, ti=ti, st_t=st_t: e.activation(out=junk[:], in_=x_t[:, ti, :], func=AF.Square,
                                                                                       accum_out=st_t[:, 0:1]),
                             reads=[r_x], writes=[r_junk, r_st])
                        S.op("act", lambda e, st_t=st_t: e.activation(out=st_t[:, 1:2], in_=st_t[:, 0:1], func=AF.Sqrt,
                                                                      scale=1.0 / D, bias=EPS), reads=[r_st], writes=[r_st])
                        S.op("dve", lambda e, st_t=st_t: e.reciprocal(out=st_t[:, 2:3], in_=st_t[:, 1:2]), reads=[r_st], writes=[r_st])
                        tm, r_tm = tmpr.next()
                        S.op("dve", lambda e, tm=tm, x_t=x_t, ti=ti, st_t=st_t: e.scalar_tensor_tensor(
                            out=tm[:], in0=x_t[:, ti, :], scalar=st_t[:, 2:3], in1=modt[:, 1, :], op0=ALU.mult, op1=ALU.mult),
                            reads=[r_x, r_st, r_mod], writes=[r_tm])
                        h_t, r_h = hr.next()
                        S.op("pool", lambda e, h_t=h_t, tm=tm: e.tensor_tensor(out=h_t[:], in0=tm[:], in1=modt[:, 0, :], op=ALU.add),
                             reads=[r_tm, r_mod], writes=[r_h])
                        def fin(h_t=h_t, r_h=r_h, hT=hT, r_hT=r_hT, ti=ti):
                            pb, r_pb = self.bank()
                            pbT = pb.bitcast(BF16)

                            def tr(e, pbT=pbT, h_t=h_t):
                                for k in range(8):
                                    ins = e.transpose(out=pbT[:, k * 128:(k + 1) * 128], in_=h_t[:, k * 128:(k + 1) * 128],
                                                      identity=self.ident[:])
                                return ins
                            S.op("pe", tr, reads=[r_h, self.r_ident], writes=[r_pb])
                            S.op("act", lambda e, hT=hT, ti=ti, pbT=pbT: e.copy(out=hT[:, :, ti * 128:(ti + 1) * 128],
                                                                                 in_=pbT.rearrange("p (k t) -> p k t", k=8)),
                                 reads=[], writes=[r_hT, r_pb])
                        fins.append(fin)
                    return x_t, r_x, hT, r_hT, fins

                pend = normg(*groups[0]) if groups else None
                if pend:
                    for f_ in pend[4]:
                        f_()
                for gi, (s, g0) in enumerate(groups):
                    x_t, r_x, hT, r_hT, _ = pend
                    defer = []
                    if gi + 1 < len(groups) and groups[gi + 1][0] == s:
                        pend = normg(*groups[gi + 1])
                        defer = list(pend[4])
                        late = False
                    else:
                        late = True
                    a_t, r_a = actr.next()
                    for ct in range(NK2):
                        if defer and ct in (8, 15):
                            defer.pop(0)()
                        pg, r_pg = self.bank()
                        pu, r_pu = self.bank()

                        def mm1(pb_, c0, hT=hT):
                            def f(e):
                                for k in range(8):
                                    ins = e.matmul(pb_[:, 0:256], lhsT=W1[:, k, c0:c0 + 128], rhs=hT[:, k, :],
                                                   start=(k == 0), stop=(k == 7))
                                return ins
                            return f
                        S.op("pe", mm1(pg, ct * 128), reads=r_W1 + [r_hT], writes=[r_pg])
                        S.op("pe", mm1(pu, DFF + ct * 128), reads=r_W1 + [r_hT], writes=[r_pu])
                        sg, r_sg = sgr.next()
                        S.op("act", lambda e, sg=sg, pg=pg: e.activation(out=sg[:], in_=pg[:, 0:256], func=AF.Silu),
                             reads=[], writes=[r_sg, r_pg])
                        S.op("dve", lambda e, a_t=a_t, ct=ct, pu=pu, sg=sg: e.tensor_tensor(
                            out=a_t[:, ct, :], in0=pu[:, 0:256], in1=sg[:], op=ALU.mult),
                            reads=[r_sg], writes=[r_a, r_pu])
                    for ti in range(2):
                        t = g0 + ti
                        tm, r_tm = tmpr.next()
                        for half in range(2):
                            po, r_po = self.bank()

                            def mm2(e, po=po, a_t=a_t, ti=ti, half=half):
                                for k in range(NK2):
                                    ins = e.matmul(po, lhsT=a_t[:, k, ti * 128:(ti + 1) * 128], rhs=W2[:, k, half * 512:(half + 1) * 512],
                                                   start=(k == 0), stop=(k == NK2 - 1))
                                return ins
                            S.op("pe", mm2, reads=r_W2 + [r_a], writes=[r_po])
                            S.op("dve", lambda e, tm=tm, po=po, half=half: e.tensor_tensor(
                                out=tm[:, half * 512:(half + 1) * 512], in0=po, in1=modt[:, 2, half * 512:(half + 1) * 512], op=ALU.mult),
                                reads=[r_mod], writes=[r_tm, r_po])
                        o_t, r_o = orr.next()
                        S.op("pool", lambda e, o_t=o_t, tm=tm, x_t=x_t, ti=ti: e.tensor_tensor(out=o_t[:], in0=tm[:], in1=x_t[:, ti, :], op=ALU.add),
                             reads=[r_tm, r_x], writes=[r_o])
                        rows = slice(t * 128, (t + 1) * 128)
                        if not final:
                            S.dma("sp", self.XL[s][rows, :], o_t[:], reads=[r_o], writes=[self.res("XL", s, t)])
                        else:
                            st_t, r_st = stat.next()
                            S.op("act", lambda e, o_t=o_t, st_t=st_t: e.activation(out=junk[:], in_=o_t[:], func=AF.Square,
                                                                                    accum_out=st_t[:, 0:1]),
                                 reads=[r_o], writes=[r_junk, r_st])
                            S.op("act", lambda e, st_t=st_t: e.activation(out=st_t[:, 1:2], in_=st_t[:, 0:1], func=AF.Sqrt,
                                                                          scale=1.0 / D, bias=EPS), reads=[r_st], writes=[r_st])
                            S.op("dve", lambda e, st_t=st_t: e.reciprocal(out=st_t[:, 2:3], in_=st_t[:, 1:2]), reads=[r_st], writes=[r_st])
                            f_t, r_f = tmpr.next()
                            S.op("dve", lambda e, f_t=f_t, o_t=o_t, st_t=st_t: e.scalar_tensor_tensor(
                                out=f_t[:], in0=o_t[:], scalar=st_t[:, 2:3], in1=gfin[:], op0=ALU.mult, op1=ALU.mult),
                                reads=[r_o, r_st, r_gf], writes=[r_f])
                            S.dma("sp", self.out[rows, :], f_t[:], reads=[r_f], writes=[self.res("OUT", t)])
                    while defer:
                        defer.pop(0)()
                    if late and gi + 1 < len(groups):
                        pend = normg(*groups[gi + 1])
                        for f_ in pend[4]:
                            f_()

    def build(self):
        S = self.S
        self.declare()
        phases = self.opts.get("phases")
        with ExitStack() as st:
            self.setup_common(st)
            for l in range(DEPTH):
                src = [self.x_in, self.ctx_in] if l == 0 else self.XL
                last = (l == DEPTH - 1)
                streams = (0,) if last else (1, 0)

                def run(name, fn):
                    if phases is None or (name, l) in phases:
                        fn()
                        S.barrier_all()
                run("p0", lambda: self.p0(l))
                run("p1", lambda: self.p1(l, src, do_ctx_q=not last))
                run("p2", lambda: self.p2(l, streams))
                run("p3", lambda: self.p3(l, streams))
                run("p4", lambda: self.p4(l, streams))
                run("p5", lambda: self.p5(l, src, streams, final=last))
            S.final_wait("sp")
            S.emit()
        return self.nc


def _rope_tables():
    t = np.arange(SEQ)
    rows, cols = t // 64, t % 64
    inv = (10000.0 ** (-np.arange(16, dtype=np.float32) / 16)).astype(np.float32)
    C = np.zeros((64, SEQ), np.float32)
    Sg = np.zeros((64, SEQ), np.float32)
    for blk, pos in ((0, rows), (1, cols)):
        ang = pos.astype(np.float32)[None, :] * inv[:, None]
        cs, sn = np.cos(ang).astype(np.float32), np.sin(ang).astype(np.float32)
        C[blk * 32:blk * 32 + 16] = cs
        C[blk * 32 + 16:blk * 32 + 32] = cs
        Sg[blk * 32:blk * 32 + 16] = -sn
        Sg[blk * 32 + 16:blk * 32 + 32] = sn
    C2 = np.concatenate([C, C], 0)
    S2 = np.concatenate([Sg, Sg], 0)
    return np.stack([C2 * 0.125, S2 * 0.125, C2, S2]).astype(np.float32)


def _swap_idx():
    d = np.arange(64)
    return np.where(d % 32 < 16, d + 16, d - 16)


def _w_in_ext(w_in):
    qa = np.arange(0, 256)
    qb = np.arange(256, 512)
    z = np.arange(512, 1024)
    ka = np.arange(1024, 1152)
    va = np.arange(1152, 1280)
    kb = np.arange(1280, 1536)
    vb = np.arange(1536, 1792)
    xbc = np.arange(1792, 2816)
    dt = np.arange(2816, 2832)
    sw = _swap_idx()

    def heads(base, hs):
        return np.concatenate([base[h * 64:(h + 1) * 64] for h in hs])

    def heads_sw(base, hs):
        return np.concatenate([base[h * 64:(h + 1) * 64][sw] for h in hs])
    cols = [heads(qa, (0, 2)), heads_sw(qa, (0, 2)), heads(qa, (1, 3)), heads_sw(qa, (1, 3)),
            heads(ka, (0, 1)), heads_sw(ka, (0, 1)), qb, kb, xbc, z, va, vb, dt]
    idx = np.concatenate(cols)
    assert idx.shape[0] == WCOLS
    return np.ascontiguousarray(w_in[:, :, idx])


def _bm_table(rpb):
    L = rpb.shape[0]
    krl, kc = np.divmod(np.arange(128), 64)
    qrl, qc = np.divmod(np.arange(128), 64)
    cases = [(i, j) for (i, js) in ((2, range(0, 5)), (0, range(0, 4)), (1, range(0, 4)), (30, range(28, 32)), (31, range(28, 32))) for j in js]
    out = np.full((L, 4, 21, 128, 128), NEG, np.float32)
    for ci, (i, j) in enumerate(cases):
        kr = (2 * j + krl)[:, None]
        qr = (2 * i + qrl)[None, :]
        s_ = np.clip(qr - 4, 0, 56)
        vrow = (kr >= s_) & (kr <= s_ + 7)
        cst = np.clip(qc - 8, 0, 48)[None, :]
        vcol = (kc[:, None] >= cst) & (kc[:, None] < cst + 16)
        valid = vrow & vcol
        dy = np.clip(kr - qr + 7, 0, 14)
        dx = np.clip(kc[:, None] - qc[None, :] + 15, 0, 30)
        dyb, dxb = np.broadcast_arrays(dy, dx)
        g = rpb[:, :, dyb, dxb]
        out[:, :, ci] = np.where(valid[None, None], g, np.float32(NEG))
    return np.ascontiguousarray(out.transpose(0, 3, 1, 2, 4).reshape(L, 128, 84 * 128))


def prep_inputs(inputs, n_cores):
    f = lambda a: np.ascontiguousarray(np.asarray(a, dtype=np.float32))
    x, c, ctx, c_ctx = f(inputs["x"]), f(inputs["c"]), f(inputs["ctx"]), f(inputs["c_ctx"])
    shared = {
        "w_mod": f(inputs["w_mod"]), "b_mod": f(inputs["b_mod"]), "g_mix": f(inputs["g_mix"]), "g_ffn": f(inputs["g_ffn"]),
        "w_in_ext": _w_in_ext(f(inputs["w_in"])), "rope": _rope_tables(),
        "w_out": f(inputs["w_out"]), "w_ffn_in": f(inputs["w_ffn_in"]), "w_ffn_out": f(inputs["w_ffn_out"]),
        "g_final": f(inputs["g_final"]), "wa_sink": f(inputs["wa_sink"]), "bm_tab": _bm_table(f(inputs["na_rpb"])),
        "conv_w_l": np.ascontiguousarray(f(inputs["ssm_conv_w"]).reshape(DEPTH, 7, 8, 128).transpose(0, 3, 2, 1)),
        "conv_b_l": np.ascontiguousarray(f(inputs["ssm_conv_b"]).reshape(DEPTH, 8, 128).transpose(0, 2, 1)),
        "conv_brow": f(inputs["ssm_conv_b"]).reshape(DEPTH, 1, 1024),
        "dt_bias": f(inputs["ssm_dt_bias"]).reshape(DEPTH, 16), "a_log": f(inputs["ssm_a_log"]).reshape(DEPTH, 16),
        "ssm_d": f(inputs["ssm_d"]), "ssm_g": f(inputs["ssm_norm_g"]),
    }
    maps = []
    for i in range(n_cores):
        b = i % 4
        cvec = np.concatenate([c[b].reshape(8, 128).T, c_ctx.reshape(8, 128).T], 1)
        m = dict(shared)
        m.update({"x": x[b], "ctx": ctx[b], "cvec": np.ascontiguousarray(cvec)})
        maps.append(m)
    return maps


N_CORES = 4


def kernel(**inputs):
    nc = Builder().build()
    maps = prep_inputs(inputs, N_CORES)
    res = run_bass_kernel_spmd(nc, maps, core_ids=list(range(N_CORES)))
    out = np.stack([res.results[b]["out"] for b in range(4)], 0)
    return out.astype(np.float32)
```

```python
import numpy as np
from contextlib import ExitStack
import concourse.bass as bass
import concourse.mybir as mybir
from concourse.bass_utils import run_bass_kernel_spmd

F32 = mybir.dt.float32
BF16 = mybir.dt.bfloat16
AF = mybir.ActivationFunctionType
ALU = mybir.AluOpType
AX = mybir.AxisListType

D = 1024
SEQ = 4096
LC = 256
DEPTH = 2
NT = SEQ // 128
NTC = LC // 128
EPS = 1e-6
DFF = 2816
NFM = 18
NFMO = 15
TMC = 912
WCOLS = NFM * 128 + TMC
NEG = -30000.0
VS = 66


class Res:
    __slots__ = ("name", "w", "rs")

    def __init__(self, name=""):
        self.name = name
        self.w = None
        self.rs = []


class Sched:
    ENGS = ("pe", "act", "dve", "pool", "sp")
    NDMA = 40
    NSDMA = 16

    def __init__(self, nc):
        self.nc = nc
        self.prog = {e: [] for e in self.ENGS}
        self.count = {}
        self.known = {e: {} for e in self.ENGS}
        self.dma_i = 0
        self.sdma_i = 0

    def _deps(self, eng, reads, writes):
        waits = {}

        def add(sv):
            if sv is None:
                return
            s, v = sv
            if eng == "pe" and s == "pe":
                return
            if waits.get(s, 0) < v:
                waits[s] = v
        for r in reads:
            add(r.w)
        for w in writes:
            add(w.w)
            for x in w.rs:
                add(x)
        out = []
        kn = self.known[eng]
        for s, v in waits.items():
            if kn.get(s, 0) < v:
                kn[s] = v
                out.append((s, v))
        return out

    def _mark(self, tag, reads, writes):
        for r in reads:
            r.rs.append(tag)
        for w in writes:
            w.w = tag
            w.rs = []

    def op(self, eng, fn, reads=(), writes=()):
        waits = self._deps(eng, reads, writes)
        c = self.count.get(eng, 0) + 1
        self.count[eng] = c
        self.prog[eng].append((waits, fn, (eng, 1)))
        self._mark((eng, c), reads, writes)

    def dma(self, q, out, in_, reads=(), writes=(), **kw):
        if q == "pool":
            slot = "sdma%d" % (self.sdma_i % self.NSDMA)
            self.sdma_i += 1
        else:
            slot = "dma%d" % (self.dma_i % self.NDMA)
            self.dma_i += 1
        waits = self._deps(q, reads, writes)
        prev = self.count.get(slot, 0)
        kn = self.known[q]
        if prev and kn.get(slot, 0) < prev:
            kn[slot] = prev
            waits.append((slot, prev))
        c = prev + 16
        self.count[slot] = c

        def fn(e, out=out, in_=in_, kw=kw):
            return e.dma_start(out=out, in_=in_, **kw)
        self.prog[q].append((waits, fn, (slot, 16)))
        self._mark((slot, c), reads, writes)

    def barrier_all(self):
        allv = list(self.count.items())
        for e in self.ENGS:
            kn = self.known[e]
            waits = []
            for s, v in allv:
                if s == e:
                    continue
                if kn.get(s, 0) < v:
                    kn[s] = v
                    waits.append((s, v))
            if waits:
                self.prog[e].append((waits, None, None))

    def final_wait(self, eng="sp"):
        waits = []
        kn = self.known[eng]
        for s, v in self.count.items():
            if s != eng and kn.get(s, 0) < v:
                kn[s] = v
                waits.append((s, v))
        self.prog[eng].append((waits, None, None))

    def emit(self):
        nc = self.nc
        with ExitStack() as es:
            sems = {}
            for s in self.count:
                sems[s] = es.enter_context(nc.semaphore(s))
            block = es.enter_context(nc.Block())

            def replay(name, e):
                for waits, fn, inc in self.prog[name]:
                    for s, v in waits:
                        e.wait_ge(sems[s], v)
                    if fn is not None:
                        ins = fn(e)
                        if inc is not None:
                            ins.then_inc(sems[inc[0]], inc[1])

            @block.tensor
            def _(e):
                replay("pe", e)

            @block.scalar
            def _(e):
                replay("act", e)

            @block.vector
            def _(e):
                replay("dve", e)

            @block.gpsimd
            def _(e):
                replay("pool", e)

            @block.sync
            def _(e):
                replay("sp", e)


class Ring:
    def __init__(self, aps, name=""):
        self.items = [(a, Res("%s%d" % (name, i))) for i, a in enumerate(aps)]
        self.i = 0

    def next(self):
        it = self.items[self.i % len(self.items)]
        self.i += 1
        return it


class Builder:
    def __init__(self, debug=(), stop_after=None, opts=None):
        self.opts = opts or {}
        self.debug = set(debug)
        self.stop_after = stop_after
        self.nc = bass.Bass("TRN2", target_bir_lowering=False)
        self.S = Sched(self.nc)
        self.es = ExitStack()
        self.resmap = {}

    def din(self, name, shape, dt=F32):
        return self.nc.dram_tensor(name, list(shape), dt, kind="ExternalInput").ap()

    def dscr(self, name, shape, dt=F32):
        kind = "ExternalOutput" if name in self.debug else "Internal"
        if name in self.opts.get("inject", ()):
            kind = "ExternalInput"
        return self.nc.dram_tensor(name, list(shape), dt, kind=kind).ap()

    def res(self, *key):
        r = self.resmap.get(key)
        if r is None:
            r = Res(str(key))
            self.resmap[key] = r
        return r

    def sb(self, st, name, shape, dt=F32):
        self.uid = getattr(self, "uid", 0) + 1
        return st.enter_context(self.nc.sbuf_tensor("%s_%d" % (name, self.uid), list(shape), dt))

    def declare(self):
        self.x_in = self.din("x", [SEQ, D])
        self.ctx_in = self.din("ctx", [LC, D])
        self.cvec = self.din("cvec", [128, 16])
        self.w_mod = self.din("w_mod", [DEPTH, D, 6 * D])
        self.b_mod = self.din("b_mod", [DEPTH, 6 * D])
        self.g_mix = self.din("g_mix", [DEPTH, D])
        self.g_ffn = self.din("g_ffn", [DEPTH, D])
        self.w_in = self.din("w_in_ext", [DEPTH, D, WCOLS])
        self.rope = self.din("rope", [4, 128, SEQ])
        self.w_out = self.din("w_out", [DEPTH, D, D])
        self.w_ffn_in = self.din("w_ffn_in", [DEPTH, D, 2 * DFF])
        self.w_ffn_out = self.din("w_ffn_out", [DEPTH, DFF, D])
        self.g_final = self.din("g_final", [D])
        self.wa_sink = self.din("wa_sink", [DEPTH, 4])
        self.conv_w = self.din("conv_w_l", [DEPTH, 128, 8, 7])
        self.conv_b = self.din("conv_b_l", [DEPTH, 128, 8])
        self.conv_brow = self.din("conv_brow", [DEPTH, 1, 1024])
        self.dt_bias = self.din("dt_bias", [DEPTH, 16])
        self.a_log = self.din("a_log", [DEPTH, 16])
        self.ssm_d = self.din("ssm_d", [DEPTH, 8])
        self.ssm_g = self.din("ssm_g", [DEPTH, 512])
        self.bm_tab = self.din("bm_tab", [DEPTH, 128, 84 * 128])
        self.out = self.nc.dram_tensor("out", [SEQ, D], F32, kind="ExternalOutput").ap()
        self.MOD = self.dscr("MOD", [2, 6, 128, D])
        self.FM = [self.dscr("FM_l", [NFMO, 128, SEQ], BF16), self.dscr("FM_c", [NFMO, 128, LC], BF16)]
        self.ZS = [self.dscr("ZS_l", [SEQ, 512], BF16), self.dscr("ZS_c", [LC, 512], BF16)]
        self.VA = [self.dscr("VA_l", [SEQ, 2 * VS], BF16), self.dscr("VA_c", [LC, 2 * VS], BF16)]
        self.VB = [self.dscr("VB_l", [SEQ, 4 * VS], BF16), self.dscr("VB_c", [LC, 4 * VS], BF16)]
        self.DT = [self.dscr("DT_l", [SEQ, 16]), self.dscr("DT_c", [LC, 16])]
        self.MIX = [self.dscr("MIX_l", [SEQ, D], BF16), self.dscr("MIX_c", [LC, D], BF16)]
        self.XSS = [self.dscr("XSS_l", [SEQ, 768], BF16), self.dscr("XSS_c", [LC, 768], BF16)]
        self.XM = [self.dscr("XM_l", [SEQ, D]), self.dscr("XM_c", [LC, D])]
        self.XL = [self.dscr("XL_l", [SEQ, D]), self.dscr("XL_c", [LC, D])]

    def setup_common(self, st):
        nc, S = self.nc, self.S
        self.ps = st.enter_context(nc.psum_tensor("ps", [128, 4096], F32))
        self.psb = [(self.ps[:, b * 512:(b + 1) * 512], Res("bank%d" % b)) for b in range(8)]
        self.ident = self.sb(st, "ident", [128, 128], BF16)
        self.r_ident = Res("ident")
        ident = self.ident

        S.op("pool", lambda e: e.memset(ident[:], 0.0), writes=[self.r_ident])
        S.op("pool", lambda e: e.affine_select(out=ident[:], in_=ident[:], pattern=[[-1, 128]], compare_op=ALU.not_equal,
                                               fill=1.0, base=0, channel_multiplier=1),
             reads=[self.r_ident], writes=[self.r_ident])
        self.maskP = self.sb(st, "maskP", [128, 128], BF16)
        self.maskN = self.sb(st, "maskN", [128, 128], BF16)
        self.r_mask = Res("mask")
        mP, mN = self.maskP, self.maskN
        r1, r2 = Res(), Res()
        S.op("pool", lambda e: e.memset(mP[:], 0.0), writes=[r1])
        S.op("pool", lambda e: e.memset(mN[:], 0.0), writes=[r2])
        S.op("pool", lambda e: e.affine_select(out=mP[:], in_=mP[:], pattern=[[-1, 128]], compare_op=ALU.is_ge,
                                               fill=NEG, base=0, channel_multiplier=1), reads=[r1], writes=[r1])
        S.op("pool", lambda e: e.affine_select(out=mN[:], in_=mN[:], pattern=[[1, 128]], compare_op=ALU.is_ge,
                                               fill=NEG, base=0, channel_multiplier=-1), reads=[r2], writes=[r2])
        S.op("pool", lambda e: e.memset(self.ident[0:1, 0:1], 1.0), reads=[r1, r2, self.r_ident], writes=[self.r_mask, self.r_ident])
        self.U32 = self.sb(st, "U32", [128, 128])
        self.L32 = self.sb(st, "L32", [128, 128])
        self.ones32 = self.sb(st, "ones32", [128, 128])
        self.r_tri = Res("tri")
        U32, L32, ones32 = self.U32, self.L32, self.ones32
        r3, r4 = Res(), Res()
        S.op("pool", lambda e: e.memset(U32[:], 1.0), writes=[r3])
        S.op("pool", lambda e: e.memset(L32[:], 1.0), writes=[r4])
        S.op("pool", lambda e: e.memset(ones32[:], 1.0), writes=[self.r_tri])
        S.op("pool", lambda e: e.affine_select(out=U32[:], in_=U32[:], pattern=[[1, 128]], compare_op=ALU.is_ge,
                                               fill=0.0, base=0, channel_multiplier=-1), reads=[r3], writes=[r3])
        S.op("pool", lambda e: e.affine_select(out=L32[:], in_=L32[:], pattern=[[-1, 128]], compare_op=ALU.is_ge,
                                               fill=0.0, base=0, channel_multiplier=1), reads=[r4], writes=[r4])
        S.op("pool", lambda e: e.memset(ones32[0:1, 0:1], 1.0), reads=[r3, r4, self.r_tri], writes=[self.r_tri])
        self.bank_i = 0

    def bank(self):
        b = self.psb[self.bank_i % 8]
        self.bank_i += 1
        return b

    def p0(self, l):
        nc, S = self.nc, self.S
        with ExitStack() as st:
            cv = self.sb(st, "p0_cv", [128, 16])
            scv = self.sb(st, "p0_scv", [128, 16])
            scbc = self.sb(st, "p0_scbc", [128, 16, 128])
            gbc = self.sb(st, "p0_gbc", [128, 2, D])
            wb = [self.sb(st, "p0_w%d" % i, [128, 8, 512]) for i in range(2)]
            bb = [self.sb(st, "p0_b%d" % i, [128, 512]) for i in range(2)]
            mt = [self.sb(st, "p0_m%d" % i, [128, D]) for i in range(4)]
            r_cv, r_scv, r_scbc, r_g = Res(), Res(), Res(), Res()
            wring = Ring(wb, "p0w")
            bring = Ring(bb, "p0b")
            mring = Ring(mt, "p0m")
            S.dma("sp", cv[:], self.cvec, writes=[r_cv])
            S.dma("sp", gbc[:, 0, :], self.g_mix[l].partition_broadcast(128), writes=[r_g])
            S.dma("sp", gbc[:, 1, :], self.g_ffn[l].partition_broadcast(128), writes=[r_g])
            S.op("act", lambda e: e.activation(out=scv[:], in_=cv[:], func=AF.Silu), reads=[r_cv], writes=[r_scv])
            S.op("dve", lambda e: e.tensor_copy(out=scbc[:], in_=scv[:].unsqueeze(2).to_broadcast([128, 16, 128])),
                 reads=[r_scv], writes=[r_scbc])
            wv = self.w_mod[l].rearrange("(k p) n -> p k n", p=128)
            cur = {}
            for blk in range(12):
                j, half = blk // 2, blk % 2
                w_t, r_w = wring.next()
                b_t, r_b = bring.next()
                S.dma("sp", w_t[:], wv[:, :, blk * 512:(blk + 1) * 512], writes=[r_w])
                S.dma("sp", b_t[:], self.b_mod[l, blk * 512:(blk + 1) * 512].partition_broadcast(128), writes=[r_b])
                for s in range(2):
                    if half == 0:
                        cur[s] = mring.next()
                    m_t, r_m = cur[s]
                    pb, r_pb = self.bank()

                    def mm(e, pb=pb, w_t=w_t, s=s):
                        for k in range(8):
                            ins = e.matmul(pb, lhsT=scbc[:, s * 8 + k, :], rhs=w_t[:, k, :], start=(k == 0), stop=(k == 7))
                        return ins
                    S.op("pe", mm, reads=[r_scbc, r_w], writes=[r_pb])
                    dst = m_t[:, half * 512:(half + 1) * 512]
                    if j in (1, 4):
                        gsl = gbc[:, 0 if j == 1 else 1, half * 512:(half + 1) * 512]
                        tmp_r = Res()

                        def ev(e, dst=dst, pb=pb, b_t=b_t, gsl=gsl):
                            e.tensor_tensor(out=dst, in0=pb, in1=b_t[:], op=ALU.add)
                            return e.scalar_tensor_tensor(out=dst, in0=dst, scalar=1.0, in1=gsl, op0=ALU.add, op1=ALU.mult)
                        S.op("dve", lambda e, dst=dst, pb=pb, b_t=b_t: e.tensor_tensor(out=dst, in0=pb, in1=b_t[:], op=ALU.add),
                             reads=[r_pb, r_b], writes=[r_m])
                        S.op("dve", lambda e, dst=dst, gsl=gsl: e.scalar_tensor_tensor(out=dst, in0=dst, scalar=1.0, in1=gsl,
                                                                                         op0=ALU.add, op1=ALU.mult),
                             reads=[r_m, r_g], writes=[r_m])
                    else:
                        S.op("dve", lambda e, dst=dst, pb=pb, b_t=b_t: e.tensor_tensor(out=dst, in0=pb, in1=b_t[:], op=ALU.add),
                             reads=[r_pb, r_b], writes=[r_m])
                    if half == 1:
                        S.dma("sp", self.MOD[s, j], m_t[:], reads=[r_m], writes=[self.res("MOD", s, j)])

    def p1(self, l, src, do_ctx_q):
        nc, S = self.nc, self.S
        with ExitStack() as st:
            W = self.sb(st, "p1_W", [128, 8, WCOLS], BF16)
            r_Wall = []
            wv = self.w_in[l].rearrange("(k p) n -> p k n", p=128)
            for k in range(8):
                for c0 in range(0, WCOLS, 1608):
                    r = Res()
                    r_Wall.append(r)
                    S.dma("pool", W[:, k, c0:c0 + 1608], wv[:, k, c0:c0 + 1608], writes=[r])
            modt = self.sb(st, "p1_mod", [128, 2, 2, D])
            r_mod = Res("p1mod")
            for s in range(2):
                for jj, j in enumerate((0, 1)):
                    S.dma("sp", modt[:, s, jj, :], self.MOD[s, j], reads=[self.res("MOD", s, j)], writes=[r_mod])
            xr = Ring([self.sb(st, "p1_x%d" % i, [128, D]) for i in range(3)], "p1x")
            junk = self.sb(st, "p1_junk", [128, D], BF16)
            r_junk = Res()
            stat = Ring([self.sb(st, "p1_st%d" % i, [128, 4]) for i in range(3)], "p1st")
            tmpr = Ring([self.sb(st, "p1_t%d" % i, [128, D]) for i in range(2)], "p1t")
            hr = Ring([self.sb(st, "p1_h%d" % i, [128, D], BF16) for i in range(8)], "p1h")
            hTr = Ring([self.sb(st, "p1_hT%d" % i, [128, 8, 512], BF16) for i in range(2)], "p1hT")
            ropr = Ring([self.sb(st, "p1_rp%d" % i, [128, 4, 512]) for i in range(2)], "p1rp")
            rtr = Ring([self.sb(st, "p1_rt%d" % i, [128, 2, 512]) for i in range(3)], "p1rt")
            fmr = Ring([self.sb(st, "p1_fm%d" % i, [128, NFMO, 512], BF16) for i in range(2)], "p1fm")
            zr = Ring([self.sb(st, "p1_z%d" % i, [128, 4, 512], BF16) for i in range(2)], "p1z")
            var_ = [self.sb(st, "p1_va%d" % i, [128, 4, 2, VS], BF16) for i in range(2)]
            vbr_ = [self.sb(st, "p1_vb%d" % i, [128, 4, 4, VS], BF16) for i in range(2)]
            dtr = Ring([self.sb(st, "p1_dt%d" % i, [128, 4, 16]) for i in range(2)], "p1dt")
            var = Ring(var_, "p1va")
            vbr = Ring(vbr_, "p1vb")
            for (t_, r_) in var.items + vbr.items:
                S.op("pool", lambda e, t_=t_: e.memset(t_[:], 1.0), writes=[r_])

            groups = [(1, 0, NTC)] + [(0, g * 4, 4) for g in range(NT // 4)]
            lim = self.opts.get('p1_lim', 9)
            groups = groups[:self.opts.get('p1_groups', 99)]
            if lim == 0:
                groups = []
            def norm_group(grp):
                (s, t0, ntile) = grp
                TG = ntile * 128
                tok0 = t0 * 128
                hT, r_hT = hTr.next()
                fins = []
                G1 = modt[:, s, 1, :]
                SH1 = modt[:, s, 0, :]
                for ti in range(ntile):
                    x_t, r_x = xr.next()
                    S.dma("sp", x_t[:], src[s][(t0 + ti) * 128:(t0 + ti + 1) * 128, :],
                          reads=[self.res("XL", s, t0 + ti)], writes=[r_x])
                    st_t, r_st = stat.next()
                    S.op("act", lambda e, x_t=x_t, st_t=st_t: e.activation(out=junk[:], in_=x_t[:], func=AF.Square,
                                                                            accum_out=st_t[:, 0:1]),
                         reads=[r_x], writes=[r_junk, r_st])
                    S.op("act", lambda e, st_t=st_t: e.activation(out=st_t[:, 1:2], in_=st_t[:, 0:1], func=AF.Sqrt,
                                                                  scale=1.0 / D, bias=EPS),
                         reads=[r_st], writes=[r_st])
                    S.op("dve", lambda e, st_t=st_t: e.reciprocal(out=st_t[:, 2:3], in_=st_t[:, 1:2]), reads=[r_st], writes=[r_st])
                    tm, r_tm = tmpr.next()
                    S.op("dve", lambda e, tm=tm, x_t=x_t, st_t=st_t, G1=G1: e.scalar_tensor_tensor(
                        out=tm[:], in0=x_t[:], scalar=st_t[:, 2:3], in1=G1, op0=ALU.mult, op1=ALU.mult),
                        reads=[r_x, r_st, r_mod], writes=[r_tm])
                    h_t, r_h = hr.next()
                    S.op("pool", lambda e, h_t=h_t, tm=tm, SH1=SH1: e.tensor_tensor(out=h_t[:], in0=tm[:], in1=SH1, op=ALU.add),
                         reads=[r_tm, r_mod], writes=[r_h])
                    def fin(h_t=h_t, r_h=r_h, hT=hT, r_hT=r_hT, ti=ti):
                        pb, r_pb = self.bank()
                        pbT = pb.bitcast(BF16)

                        def tr(e, pbT=pbT, h_t=h_t):
                            for k in range(8):
                                ins = e.transpose(out=pbT[:, k * 128:(k + 1) * 128], in_=h_t[:, k * 128:(k + 1) * 128],
                                                  identity=self.ident[:])
                            return ins
                        S.op("pe", tr, reads=[r_h, self.r_ident], writes=[r_pb])
                        S.op("act", lambda e, hT=hT, ti=ti, pbT=pbT: e.copy(out=hT[:, :, ti * 128:(ti + 1) * 128],
                                                                             in_=pbT.rearrange("p (k t) -> p k t", k=8)),
                             reads=[], writes=[r_hT, r_pb])
                    fins.append(fin)
                return hT, r_hT, fins

            pend = norm_group(groups[0]) if groups else None
            if pend:
                for f_ in pend[2]:
                    f_()
            for gi, (s, t0, ntile) in enumerate(groups):
                TG = ntile * 128
                tok0 = t0 * 128
                hT, r_hT, _ = pend
                pend = norm_group(groups[gi + 1]) if gi + 1 < len(groups) else None
                defer = list(pend[2]) if pend else []
                if lim <= 1:
                    continue
                fm, r_fm = fmr.next()
                if s == 0:
                    rp, r_rp = ropr.next()
                    S.dma("sp", rp[:], self.rope[:, :, tok0:tok0 + TG].rearrange("c p t -> p c t"), writes=[r_rp])

                def fm_mm(ct, pb, TG=TG, hT=hT):
                    def f(e):
                        for k in range(8):
                            ins = e.matmul(pb[:, 0:TG], lhsT=W[:, k, ct * 128:(ct + 1) * 128], rhs=hT[:, k, 0:TG],
                                           start=(k == 0), stop=(k == 7))
                        return ins
                    return f
                for (ct, slot, ci) in ((0, 0, 0), (2, 1, 0), (4, 2, 2)):
                    pq, r_pq = self.bank()
                    S.op("pe", fm_mm(ct, pq), reads=r_Wall + [r_hT], writes=[r_pq])
                    if s == 0:
                        psw, r_psw = self.bank()
                        S.op("pe", fm_mm(ct + 1, psw), reads=r_Wall + [r_hT], writes=[r_psw])
                        rt, r_rt = rtr.next()
                        S.op("dve", lambda e, rt=rt, pq=pq, rp=rp, ci=ci, TG=TG: e.tensor_tensor(
                            out=rt[:, 0, 0:TG], in0=pq[:, 0:TG], in1=rp[:, ci, 0:TG], op=ALU.mult),
                            reads=[r_pq, r_rp], writes=[r_rt])
                        S.op("dve", lambda e, rt=rt, psw=psw, rp=rp, ci=ci, TG=TG: e.tensor_tensor(
                            out=rt[:, 1, 0:TG], in0=psw[:, 0:TG], in1=rp[:, ci + 1, 0:TG], op=ALU.mult),
                            reads=[r_psw, r_rp], writes=[r_rt])
                        S.op("pool", lambda e, rt=rt, fm=fm, slot=slot, TG=TG: e.tensor_tensor(
                            out=fm[:, slot, 0:TG], in0=rt[:, 0, 0:TG], in1=rt[:, 1, 0:TG], op=ALU.add),
                            reads=[r_rt], writes=[r_fm])
                    else:
                        sc_ = 0.125 if ct < 4 else 1.0
                        S.op("act", lambda e, fm=fm, slot=slot, pq=pq, TG=TG, sc_=sc_: e.activation(
                            out=fm[:, slot, 0:TG], in_=pq[:, 0:TG], func=AF.Copy, scale=sc_),
                            reads=[r_pq], writes=[r_fm])
                for ct in range(6, NFM):
                    if defer and ct in (8, 11, 14, 17):
                        defer.pop(0)()
                    slot = ct - 3
                    pq, r_pq = self.bank()
                    S.op("pe", fm_mm(ct, pq), reads=r_Wall + [r_hT], writes=[r_pq])
                    sc_ = 0.125 if ct < 8 else 1.0
                    if ct % 2 == 0:
                        S.op("act", lambda e, fm=fm, slot=slot, pq=pq, TG=TG, sc_=sc_: e.activation(
                            out=fm[:, slot, 0:TG], in_=pq[:, 0:TG], func=AF.Copy, scale=sc_),
                            reads=[r_pq], writes=[r_fm])
                    else:
                        S.op("dve", lambda e, fm=fm, slot=slot, pq=pq, TG=TG, sc_=sc_: e.tensor_scalar(
                            out=fm[:, slot, 0:TG], in0=pq[:, 0:TG], scalar1=sc_, scalar2=None, op0=ALU.mult),
                            reads=[r_pq], writes=[r_fm])
                for c0 in range(0, NFMO, 5):
                    S.dma("sp", self.FM[s][c0:c0 + 5, :, tok0:tok0 + TG].rearrange("c p t -> p c t"), fm[:, c0:c0 + 5, 0:TG],
                          reads=[r_fm], writes=[self.res("FM", s, t0 // 4, c0)])
                while defer:
                    defer.pop(0)()
                if lim <= 2:
                    continue
                z_t, r_z = zr.next()
                va_t, r_va = var.next()
                vb_t, r_vb = vbr.next()
                dt_t, r_dt = dtr.next()
                tmm = self.opts.get('tm_mask', 7)
                for ti in range(ntile):
                    def tm_mm(pb, c0, n, hT=hT, ti=ti):
                        def f(e):
                            for k in range(8):
                                ins = e.matmul(pb[:, 0:n], lhsT=hT[:, k, ti * 128:(ti + 1) * 128],
                                               rhs=W[:, k, c0:c0 + n], start=(k == 0), stop=(k == 7))
                            return ins
                        return f
                    if not (tmm & 1):
                        continue
                    pz, r_pz = self.bank()
                    S.op("pe", tm_mm(pz, NFM * 128, 512), reads=r_Wall + [r_hT], writes=[r_pz])
                    S.op("act", lambda e, z_t=z_t, ti=ti, pz=pz: e.activation(out=z_t[:, ti, :], in_=pz, func=AF.Silu),
                         reads=[r_pz], writes=[r_z])
                    if not (tmm & 2):
                        continue
                    pv, r_pv = self.bank()
                    S.op("pe", tm_mm(pv, NFM * 128 + 512, 400), reads=r_Wall + [r_hT], writes=[r_pv])
                    if not (tmm & 8):
                      S.op("dve", lambda e, va_t=va_t, ti=ti, pv=pv: e.tensor_copy(
                        out=va_t[:, ti, :, 0:64], in_=pv[:, 0:128].rearrange("p (g d) -> p g d", g=2)),
                        reads=[r_pv], writes=[r_va, r_pv])
                    if not (tmm & 16):
                      S.op("dve", lambda e, vb_t=vb_t, ti=ti, pv=pv: e.tensor_copy(
                        out=vb_t[:, ti, :, 0:64], in_=pv[:, 128:384].rearrange("p (g d) -> p g d", g=4)),
                        reads=[r_pv], writes=[r_vb, r_pv])
                    if not (tmm & 32):
                      S.op("act", lambda e, dt_t=dt_t, ti=ti, pv=pv: e.copy(out=dt_t[:, ti, :], in_=pv[:, 384:400]),
                         reads=[r_pv], writes=[r_dt, r_pv])
                if not (tmm & 4):
                    continue
                rows = slice(tok0, tok0 + TG)
                gk = t0 // 4
                S.dma("sp", self.ZS[s][rows, :].rearrange("(t p) c -> p t c", p=128), z_t[:, 0:ntile, :],
                      reads=[r_z], writes=[self.res("ZS", s, gk)])
                S.dma("sp", self.VA[s][rows, :].rearrange("(t p) c -> p t c", p=128),
                      va_t[:, 0:ntile].rearrange("p t g d -> p t (g d)"), reads=[r_va], writes=[self.res("VA", s, gk)])
                S.dma("sp", self.VB[s][rows, :].rearrange("(t p) c -> p t c", p=128),
                      vb_t[:, 0:ntile].rearrange("p t g d -> p t (g d)"), reads=[r_vb], writes=[self.res("VB", s, gk)])
                S.dma("sp", self.DT[s][rows, :].rearrange("(t p) c -> p t c", p=128), dt_t[:, 0:ntile, :],
                      reads=[r_dt], writes=[self.res("DT", s, gk)])

    @staticmethod
    def nb_window(i):
        s0 = min(max(2 * i - 4, 0), 56)
        s1 = min(max(2 * i + 1 - 4, 0), 56)
        return list(range(s0 // 2, (s1 + 7) // 2 + 1))

    @staticmethod
    def bm_index(i, j):
        if 2 <= i <= 29:
            return j - i + 2
        if i == 0:
            return 5 + j
        if i == 1:
            return 9 + j
        if i == 30:
            return 13 + (j - 28)
        return 17 + (j - 28)

    def p_attn(self, l, kind, streams):
        nc, S = self.nc, self.S
        isA = kind == "A"
        nkt = 1 if isA else 2
        ng = 2 if isA else 4
        qs0 = 0 if isA else 3
        ks0 = 2 if isA else 5
        VSRC = self.VA if isA else self.VB
        col0 = 0 if isA else 256
        with ExitStack() as st:
            QT = self.sb(st, "at_Q", [128, 2, 2, SEQ], BF16)
            QTc = self.sb(st, "at_Qc", [128, 2, 2, LC], BF16)
            KT = self.sb(st, "at_K", [128, nkt, SEQ + LC], BF16)
            V = self.sb(st, "at_V", [128, NT + NTC, ng * VS], BF16)
            r_in = Res()
            for T in range(2):
                for hh in range(2):
                    lo, zl = hh * 64, (1 - hh) * 64
                    S.op("pool", lambda e, T=T, hh=hh, zl=zl: e.memset(QT[zl:zl + 64, T, hh, :], 0.0), writes=[r_in])
                    S.op("pool", lambda e, T=T, hh=hh, zl=zl: e.memset(QTc[zl:zl + 64, T, hh, :], 0.0), writes=[r_in])
                    S.dma("sp", QT[lo:lo + 64, T, hh, :], self.FM[0][qs0 + T][lo:lo + 64, :], writes=[r_in])
                    S.dma("sp", QTc[lo:lo + 64, T, hh, :], self.FM[1][qs0 + T][lo:lo + 64, :], writes=[r_in])
            for kt in range(nkt):
                S.dma("sp", KT[:, kt, 0:SEQ], self.FM[0][ks0 + kt], writes=[r_in])
                S.dma("sp", KT[:, kt, SEQ:SEQ + LC], self.FM[1][ks0 + kt], writes=[r_in])
            for t0 in range(0, NT, 8):
                S.dma("sp", V[:, t0:t0 + 8, :], VSRC[0][t0 * 128:(t0 + 8) * 128, :].rearrange("(t p) c -> p t c", p=128), writes=[r_in])
            S.dma("sp", V[:, NT:NT + NTC, :], VSRC[1].rearrange("(t p) c -> p t c", p=128), writes=[r_in])
            r_c = Res()
            if isA:
                esk = self.sb(st, "at_esk", [128, 4])
                S.dma("sp", esk[:], self.wa_sink[l].partition_broadcast(128), writes=[r_c])
                S.op("act", lambda e: e.activation(out=esk[:], in_=esk[:], func=AF.Exp), reads=[r_c], writes=[r_c])
            else:
                BM = self.sb(st, "at_BM", [128, 84 * 128], BF16)
                for c0 in range(0, 84 * 128, 1792):
                    S.dma("pool", BM[:, c0:c0 + 1792], self.bm_tab[l][:, c0:c0 + 1792], writes=[r_c])
            Sring = Ring([self.ps[:, i * 1024:(i + 1) * 1024] for i in range(3)])
            Oring = Ring([self.ps[:, 3072 + i * 512:3072 + (i + 1) * 512] for i in range(2)])
            PTr = Ring([self.sb(st, "at_PT%d" % i, [128, 7 * 128], BF16) for i in range(3)])
            mor = Ring([self.sb(st, "at_mo%d" % i, [128, 4, 64], BF16) for i in range(3)])
            dnr = Ring([self.sb(st, "at_dn%d" % i, [128, 8]) for i in range(3)])
            for s in streams:
                ntl = min(NT if s == 0 else NTC, self.opts.get("at_nt", 99))
                for n in range(ntl):
                    O, r_O = Oring.next()
                    pend_pv = []
                    for T in range(2):
                        for hh in range(2):
                            h = (2 * hh + T) if isA else (2 * T + hh)
                            g = hh if isA else h
                            kt = 0 if isA else T
                            ps_ = slice(hh * 64, (hh + 1) * 64)
                            chunks = []
                            if s == 0:
                                if isA:
                                    if n > 0:
                                        chunks.append(((n - 1) * 128, self.maskP[:], n - 1))
                                    chunks.append((n * 128, None, n))
                                    if n < NT - 1:
                                        chunks.append(((n + 1) * 128, self.maskN[:], n + 1))
                                else:
                                    for j in self.nb_window(n):
                                        bi = h * 21 + self.bm_index(n, j)
                                        chunks.append((j * 128, BM[:, bi * 128:(bi + 1) * 128], j))
                            chunks.append((SEQ, None, NT))
                            chunks.append((SEQ + 128, None, NT + 1))
                            nch = len(chunks)
                            q_ap = (QT if s == 0 else QTc)[:, T, hh, n * 128:(n + 1) * 128]
                            Sp, r_S = Sring.next()

                            def smm(e, Sp=Sp, chunks=chunks, q_ap=q_ap, kt=kt, ps_=ps_):
                                for c, (kc, bias, vt) in enumerate(chunks):
                                    ins = e.matmul(Sp[:, c * 128:(c + 1) * 128], lhsT=KT[:, kt, kc:kc + 128], rhs=q_ap,
                                                   start=True, stop=(bias is None))
                                    if bias is not None:
                                        ins = e.matmul(Sp[:, c * 128:(c + 1) * 128], lhsT=self.ident[:], rhs=bias,
                                                       start=False, stop=True)
                                return ins
                            S.op("pe", smm, reads=[r_in, r_c, self.r_ident, self.r_mask], writes=[r_S])
                            PT, r_PT = PTr.next()
                            S.op("act", lambda e, PT=PT, Sp=Sp, nch=nch: e.activation(out=PT[:, 0:nch * 128], in_=Sp[:, 0:nch * 128], func=AF.Exp),
                                 reads=[], writes=[r_PT, r_S])

                            def pv(e, O=O, PT=PT, chunks=chunks, h=h, g=g):
                                for c, (kc, bias, vt) in enumerate(chunks):
                                    ins = e.matmul(O[:, h * VS:h * VS + 65], lhsT=PT[:, c * 128:(c + 1) * 128],
                                                   rhs=V[:, vt, g * VS:g * VS + 65], start=(c == 0), stop=(c == len(chunks) - 1))
                                return ins
                            pend_pv.append((pv, [r_PT, r_in], [r_O]))
                            if len(pend_pv) > 1:
                                f_, rd_, wr_ = pend_pv.pop(0)
                                S.op("pe", f_, reads=rd_, writes=wr_)
                    while pend_pv:
                        f_, rd_, wr_ = pend_pv.pop(0)
                        S.op("pe", f_, reads=rd_, writes=wr_)
                    dn, r_dn = dnr.next()
                    O3 = O[:, 0:4 * VS].rearrange("p (h c) -> p h c", c=VS)
                    if isA:
                        S.op("dve", lambda e, dn=dn, O3=O3: e.tensor_tensor(out=dn[:, 0:4].unsqueeze(2), in0=O3[:, :, 64:65],
                                                                             in1=esk[:].unsqueeze(2), op=ALU.add),
                             reads=[r_c], writes=[r_dn, r_O])
                    else:
                        S.op("dve", lambda e, dn=dn, O3=O3: e.tensor_copy(out=dn[:, 0:4].unsqueeze(2), in_=O3[:, :, 64:65]),
                             reads=[], writes=[r_dn, r_O])
                    S.op("dve", lambda e, dn=dn: e.reciprocal(out=dn[:, 4:8], in_=dn[:, 0:4]), reads=[r_dn], writes=[r_dn])
                    mo, r_mo = mor.next()
                    S.op("dve", lambda e, mo=mo, O3=O3, dn=dn: e.tensor_tensor(
                        out=mo[:], in0=O3[:, :, 0:64], in1=dn[:, 4:8].unsqueeze(2).to_broadcast([128, 4, 64]), op=ALU.mult),
                        reads=[r_dn], writes=[r_mo, r_O])
                    S.dma("sp", self.MIX[s][n * 128:(n + 1) * 128, col0:col0 + 256], mo[:].rearrange("p h d -> p (h d)"),
                          reads=[r_mo], writes=[self.res("MIX", s, n, kind)])

    def p2(self, l, streams):
        self.p_attn(l, "A", streams)

    def p3(self, l, streams):
        self.p_attn(l, "B", streams)

    def p4(self, l, streams):
        nc, S = self.nc, self.S
        last = (l == DEPTH - 1)
        PB = self.psb
        with ExitStack() as st:
            cw = self.sb(st, "s_cw", [128, 8, 7])
            cb = self.sb(st, "s_cb", [128, 8])
            cbrow = self.sb(st, "s_cbrow", [128, 1024], BF16)
            ones1 = self.sb(st, "s_ones1", [128, 128], BF16)
            diag = self.sb(st, "s_diag", [128, 8, 7, 128], BF16)
            Dh = self.sb(st, "s_Dh", [128, 8, 128], BF16)
            sm = self.sb(st, "s_sm", [128, 64])
            gn = self.sb(st, "s_gn", [128, 512])
            r_p = Res("ssm_params")
            r_diag = Res("diag")
            S.dma("sp", cw[:], self.conv_w[l], writes=[r_p])
            S.dma("sp", cb[:], self.conv_b[l], writes=[r_p])
            r_cb0 = Res()
            S.op("pool", lambda e: e.memset(cbrow[:], 0.0), writes=[r_cb0])
            S.dma("pool", cbrow[0:1, :], self.conv_brow[l], reads=[r_cb0], writes=[r_p, r_cb0])
            S.dma("sp", sm[:, 0:16], self.dt_bias[l].partition_broadcast(128), writes=[r_p])
            S.dma("sp", sm[:, 16:32], self.a_log[l].partition_broadcast(128), writes=[r_p])
            S.dma("sp", sm[:, 32:40], self.ssm_d[l].partition_broadcast(128), writes=[r_p])
            S.dma("sp", gn[:], self.ssm_g[l].partition_broadcast(128), writes=[r_p])
            S.op("pool", lambda e: e.memset(ones1[:], 0.0), writes=[r_diag])
            S.op("pool", lambda e: e.memset(ones1[0:1, :], 1.0), reads=[r_diag], writes=[r_diag])
            S.op("act", lambda e: e.activation(out=sm[:, 40:56], in_=sm[:, 16:32], func=AF.Exp), reads=[r_p], writes=[r_p])
            S.op("dve", lambda e: e.tensor_scalar(out=sm[:, 16:32], in0=sm[:, 40:56], scalar1=-1.0, scalar2=None, op0=ALU.mult),
                 reads=[r_p], writes=[r_p])
            for c in range(8):
                for j in range(7):
                    S.op("dve", lambda e, c=c, j=j: e.tensor_scalar(out=diag[:, c, j, :], in0=self.ident[:], scalar1=cw[:, c, j:j + 1],
                                                                    scalar2=None, op0=ALU.mult),
                         reads=[r_p, self.r_ident], writes=[r_diag])
            for h in range(8):
                S.op("dve", lambda e, h=h: e.tensor_scalar(out=Dh[:, h, :], in0=self.ident[:], scalar1=sm[:, 32 + h:33 + h],
                                                            scalar2=None, op0=ALU.mult),
                     reads=[r_p, self.r_ident], writes=[r_diag])
            ULb = self.sb(st, "s_ULb", [128, 16, 128], BF16)
            MK4 = self.sb(st, "s_MK4", [128, 2, 512], BF16)
            onesb = self.sb(st, "s_onesb", [128, 128], BF16)
            r_ul = Res("ULb")
            S.op("dve", lambda e: e.tensor_copy(out=ULb[:, 0:8, :], in_=self.U32[:].unsqueeze(1).to_broadcast([128, 8, 128])),
                 reads=[self.r_tri], writes=[r_ul])
            S.op("dve", lambda e: e.tensor_copy(out=ULb[:, 8:16, :], in_=self.L32[:].unsqueeze(1).to_broadcast([128, 8, 128])),
                 reads=[self.r_tri, r_ul], writes=[r_ul])
            S.op("dve", lambda e: e.tensor_copy(out=MK4[:, 0, :].rearrange("p (a b) -> p a b", a=4), in_=self.maskN[:].unsqueeze(1).to_broadcast([128, 4, 128])),
                 reads=[self.r_mask, r_ul], writes=[r_ul])
            S.op("dve", lambda e: e.tensor_copy(out=MK4[:, 1, :].rearrange("p (a b) -> p a b", a=4), in_=self.maskP[:].unsqueeze(1).to_broadcast([128, 4, 128])),
                 reads=[self.r_mask, r_ul], writes=[r_ul])
            S.op("dve", lambda e: e.tensor_copy(out=onesb[:], in_=self.ones32[:]), reads=[self.r_tri, r_ul], writes=[r_ul])
            Hf = self.sb(st, "s_Hf", [128, 512])
            Hb = self.sb(st, "s_Hb", [128, 512])
            HFb = self.sb(st, "s_HFb", [128, 512], BF16)
            r_Hf, r_Hb, r_HFb = Res("Hf"), Res("Hb"), Res("HFb")
            S.op("pool", lambda e: e.memset(Hf[:], 0.0), writes=[r_Hf])
            S.op("pool", lambda e: e.memset(Hb[:], 0.0), writes=[r_Hb])
            XB = self.sb(st, "s_XB", [128, 8, SEQ + 6], BF16)
            HB = self.sb(st, "s_HB", [128, NT, 512], BF16)
            r_HB = [Res("HB%d" % c) for c in range(NT)]
            NB = 9
            big = [self.sb(st, "s_big%d" % i, [128, NT * 16]) for i in range(NB)]
            xsr = Ring([self.sb(st, "s_xs%d" % i, [128, 512], BF16) for i in range(4)])
            btr = Ring([self.sb(st, "s_bt%d" % i, [128, 256], BF16) for i in range(4)])
            bcr = Ring([self.sb(st, "s_bc%d" % i, [128, 4, 128], BF16) for i in range(4)])
            xwr = Ring([self.sb(st, "s_xw%d" % i, [128, 512], BF16) for i in range(2)])
            mr = Ring([self.sb(st, "s_m%d" % i, [128, 128]) for i in range(6)])
            wr = Ring([self.sb(st, "s_w%d" % i, [128, 128], BF16) for i in range(6)])
            t1r = Ring([self.sb(st, "s_t1%d" % i, [128, 512]) for i in range(1)])
            t2r = Ring([self.sb(st, "s_t2%d" % i, [128, 512]) for i in range(1)])
            yr = Ring([self.sb(st, "s_y%d" % i, [128, 512]) for i in range(2)])
            y1r = Ring([self.sb(st, "s_y1%d" % i, [128, 512]) for i in range(2)])
            zr = Ring([self.sb(st, "s_z%d" % i, [128, 512], BF16) for i in range(2)])
            ocr = Ring([self.sb(st, "s_oc%d" % i, [128, 512], BF16) for i in range(2)])
            str_ = Ring([self.sb(st, "s_st%d" % i, [128, 4]) for i in range(3)])
            junk = self.sb(st, "s_junk", [128, 512], BF16)
            r_junk = Res()
            tmpH = self.sb(st, "s_tmpH", [128, 512])
            r_tmpH = Res()

            r_XB = Res("XB")
            _rb = [Res("big%d" % i) for i in range(NB)]
            r_b = {0: _rb[0], 1: _rb[1], 2: _rb[2], 3: _rb[3], 4: _rb[4], 5: _rb[5], 6: _rb[6], 7: _rb[6], 8: _rb[1], 9: _rb[7], 10: _rb[8], 11: _rb[3]}
            ahi = self.sb(st, "s_ahi", [128, NT * 16], BF16)
            alo = self.sb(st, "s_alo", [128, NT * 16], BF16)
            r_ahl = Res("ahl")
            rhr = Ring([self.sb(st, "s_rh%d" % i, [128, 2, 16, 128], BF16) for i in range(2)])
            for s in streams if 1 in streams else (1,) + tuple(streams):
                with_out = not (s == 1 and last)
                nch = NTC if s == 1 else NT
                nch = min(nch, self.opts.get("ssm_nch", 99))
                T = nch * 128
                W16 = nch * 16
                S.op("pool", lambda e: e.memset(XB[:, :, 0:3], 0.0), writes=[r_XB])
                S.op("pool", lambda e, T=T: e.memset(XB[:, :, 3 + T:6 + T], 0.0), writes=[r_XB])
                for c in range(8):
                    S.dma("sp", XB[:, c, 3:3 + T], self.FM[s][7 + c][:, 0:T], writes=[r_XB])
                DTr, Ev, DTv, LN, Av, AC, TOT, DEC, EE = [b[:, 0:W16] for b in big]
                DTE, SDT, MB = TOT, Ev, LN
                v3 = lambda ap: ap.rearrange("p (c k) -> p c k", k=16)
                for c0 in range(0, nch, 8):
                    c1 = min(c0 + 8, nch)
                    S.dma("sp", v3(DTr)[:, c0:c1, :], self.DT[s][c0 * 128:c1 * 128, :].rearrange("(c p) k -> p c k", p=128), writes=[r_b[0]])
                S.op("dve", lambda e, DTr=DTr, nch=nch: e.tensor_tensor(out=v3(DTr), in0=v3(DTr), in1=sm[:, 0:16].unsqueeze(1).to_broadcast([128, nch, 16]), op=ALU.add),
                     reads=[r_p], writes=[r_b[0]])
                S.op("act", lambda e, Ev=Ev, DTr=DTr: e.activation(out=Ev, in_=DTr, func=AF.Exp), reads=[r_b[0]], writes=[r_b[1]])
                S.op("act", lambda e, Ev=Ev, DTv=DTv: e.activation(out=DTv, in_=Ev, func=AF.Ln, bias=1.0), reads=[r_b[1]], writes=[r_b[2]])
                S.op("act", lambda e, LN=LN, DTv=DTv: e.activation(out=LN, in_=DTv, func=AF.Ln), reads=[r_b[2]], writes=[r_b[3]])
                S.op("dve", lambda e, Av=Av, DTv=DTv, nch=nch: e.tensor_tensor(out=v3(Av), in0=v3(DTv), in1=sm[:, 16:32].unsqueeze(1).to_broadcast([128, nch, 16]), op=ALU.mult),
                     reads=[r_b[2], r_p], writes=[r_b[4]])
                AHI = ahi[:, 0:W16]
                ALO = alo[:, 0:W16]
                S.op("dve", lambda e, AHI=AHI, Av=Av: e.tensor_copy(out=AHI, in_=Av), reads=[r_b[4]], writes=[r_ahl])
                S.op("dve", lambda e, ALO=ALO, Av=Av, AHI=AHI: e.tensor_tensor(out=ALO, in0=Av, in1=AHI, op=ALU.subtract),
                     reads=[r_b[4], r_ahl], writes=[r_ahl])
                for (mat, tgt, lo) in ((self.U32, AC, 0), (self.L32, AC, 8), (self.ones32, TOT, None)):
                    pb, r_pb = self.bank()
                    S.op("pe", lambda e, pb=pb, mat=mat, Av=Av, W16=W16: e.matmul(pb[:, 0:W16], lhsT=mat[:], rhs=Av, start=True, stop=True),
                         reads=[r_b[4], self.r_tri], writes=[r_pb])
                    if lo is None:
                        S.op("dve", lambda e, pb=pb, tgt=tgt, W16=W16: e.tensor_copy(out=tgt, in_=pb[:, 0:W16]), reads=[], writes=[r_b[6], r_pb])
                    else:
                        S.op("dve", lambda e, pb=pb, tgt=tgt, lo=lo, W16=W16: e.tensor_copy(out=v3(tgt)[:, :, lo:lo + 8], in_=v3(pb[:, 0:W16])[:, :, lo:lo + 8]),
                             reads=[], writes=[r_b[5], r_pb])
                S.op("act", lambda e, DEC=DEC, TOT=TOT: e.activation(out=DEC, in_=TOT, func=AF.Exp), reads=[r_b[6]], writes=[r_b[9]])
                S.op("dve", lambda e, DTE=DTE, TOT=TOT, AC=AC: e.tensor_tensor(out=DTE, in0=TOT, in1=AC, op=ALU.subtract), reads=[r_b[5]], writes=[r_b[7]])
                S.op("act", lambda e, DTE=DTE: e.activation(out=DTE, in_=DTE, func=AF.Exp), reads=[], writes=[r_b[7]])
                S.op("dve", lambda e, SDT=SDT, DTE=DTE, DTv=DTv: e.tensor_tensor(out=SDT, in0=DTE, in1=DTv, op=ALU.mult), reads=[r_b[7], r_b[2]], writes=[r_b[8]])
                S.op("act", lambda e, EE=EE, AC=AC: e.activation(out=EE, in_=AC, func=AF.Exp), reads=[r_b[5]], writes=[r_b[10]])
                S.op("dve", lambda e, MB=MB, LN=LN, AC=AC: e.tensor_tensor(out=MB, in0=LN, in1=AC, op=ALU.subtract), reads=[r_b[3], r_b[5]], writes=[r_b[11]])

                def conv_chunk(c, want_fm, load=False):
                    if load:
                        xs, r_xs = xsr.next()
                        bt, r_bt = btr.next()
                        rd = [self.res("XSS", s, c)]
                        S.dma("sp", xs[:], self.XSS[s][c * 128:(c + 1) * 128, 0:512], reads=rd, writes=[r_xs])
                        S.dma("sp", bt[:], self.XSS[s][c * 128:(c + 1) * 128, 512:768], reads=rd, writes=[r_bt])
                        bct, r_bct = None, None
                        if want_fm:
                            pb2, r_pb2 = PB[2]

                            def cfm(e, pb2=pb2, c=c):
                                for q, ct in enumerate((4, 5, 6, 7)):
                                    for j in range(7):
                                        ins = e.matmul(pb2[:, q * 128:(q + 1) * 128], lhsT=diag[:, ct, j, :], rhs=XB[:, ct, c * 128 + j:c * 128 + j + 128],
                                                       start=(j == 0), stop=(j == 6))
                                return ins
                            S.op("pe", cfm, reads=[r_XB, r_diag], writes=[r_pb2])
                            bct, r_bct = bcr.next()
                            for q, ct in enumerate((4, 5, 6, 7)):
                                S.op("act", lambda e, bct=bct, q=q, ct=ct, pb2=pb2: e.activation(out=bct[:, q, :], in_=pb2[:, q * 128:(q + 1) * 128], func=AF.Silu,
                                                                                                  bias=cb[:, ct:ct + 1]),
                                     reads=[r_p], writes=[r_bct, r_pb2])
                        return xs, r_xs, bt, r_bt, bct, r_bct
                    pb, r_pb = PB[0]

                    def cx(e, pb=pb, c=c):
                        for ct in range(4):
                            for j in range(7):
                                e.matmul(pb[:, ct * 128:(ct + 1) * 128], lhsT=XB[:, ct, c * 128 + j:c * 128 + j + 128], rhs=diag[:, ct, j, :],
                                         start=(j == 0), stop=False)
                            ins = e.matmul(pb[:, ct * 128:(ct + 1) * 128], lhsT=ones1[:], rhs=cbrow[:, ct * 128:(ct + 1) * 128], start=False, stop=True)
                        return ins
                    S.op("pe", cx, reads=[r_XB, r_diag, r_p], writes=[r_pb])
                    xs, r_xs = xsr.next()
                    S.op("act", lambda e, xs=xs, pb=pb: e.activation(out=xs[:], in_=pb, func=AF.Silu), reads=[], writes=[r_xs, r_pb])
                    pb1, r_pb1 = PB[1]

                    def cbt(e, pb1=pb1, c=c):
                        for ct in range(4, 6):
                            o = pb1[:, (ct - 4) * 128:(ct - 3) * 128]
                            for j in range(7):
                                e.matmul(o, lhsT=XB[:, ct, c * 128 + j:c * 128 + j + 128], rhs=diag[:, ct, j, :], start=(j == 0), stop=False)
                            ins = e.matmul(o, lhsT=ones1[:], rhs=cbrow[:, ct * 128:(ct + 1) * 128], start=False, stop=True)
                        return ins
                    S.op("pe", cbt, reads=[r_XB, r_diag, r_p], writes=[r_pb1])
                    bt, r_bt = btr.next()
                    S.op("act", lambda e, bt=bt, pb1=pb1: e.activation(out=bt[:], in_=pb1[:, 0:256], func=AF.Silu), reads=[], writes=[r_bt, r_pb1])
                    bct, r_bct = None, None
                    if want_fm:
                        pb2, r_pb2 = PB[2]

                        def cfm(e, pb2=pb2, c=c):
                            for q, ct in enumerate((4, 5, 6, 7)):
                                for j in range(7):
                                    ins = e.matmul(pb2[:, q * 128:(q + 1) * 128], lhsT=diag[:, ct, j, :], rhs=XB[:, ct, c * 128 + j:c * 128 + j + 128],
                                                   start=(j == 0), stop=(j == 6))
                            return ins
                        S.op("pe", cfm, reads=[r_XB, r_diag], writes=[r_pb2])
                        bct, r_bct = bcr.next()
                        for q, ct in enumerate((4, 5, 6, 7)):
                            S.op("act", lambda e, bct=bct, q=q, ct=ct, pb2=pb2: e.activation(out=bct[:, q, :], in_=pb2[:, q * 128:(q + 1) * 128], func=AF.Silu,
                                                                                              bias=cb[:, ct:ct + 1]),
                                 reads=[r_p], writes=[r_bct, r_pb2])
                    return xs, r_xs, bt, r_bt, bct, r_bct

                def state_mm(c, xs, r_xs, bt, r_bt, lo):
                    xw, r_xw = xwr.next()
                    S.op("dve", lambda e, xw=xw, xs=xs, c=c, lo=lo: e.tensor_tensor(
                        out=xw[:].rearrange("p (h d) -> p h d", h=8), in0=xs[:].rearrange("p (h d) -> p h d", h=8),
                        in1=v3(SDT)[:, c, lo:lo + 8].unsqueeze(2).to_broadcast([128, 8, 64]), op=ALU.mult),
                        reads=[r_xs, r_b[8]], writes=[r_xw])
                    pb3, r_pb3 = PB[3]

                    def smm(e, pb3=pb3, bt=bt, xw=xw):
                        for g in range(2):
                            ins = e.matmul(pb3[:, g * 256:(g + 1) * 256], lhsT=bt[:, g * 128:(g + 1) * 128], rhs=xw[:, g * 256:(g + 1) * 256],
                                           start=True, stop=True)
                        return ins
                    S.op("pe", smm, reads=[r_bt, r_xw], writes=[r_pb3])
                    return pb3, r_pb3

                def scan_step(H, r_H, c, lo, pb3, r_pb3):
                    S.op("dve", lambda e, H=H, c=c, lo=lo: e.tensor_tensor(
                        out=tmpH[:].rearrange("p (h d) -> p h d", h=8), in0=H[:].rearrange("p (h d) -> p h d", h=8),
                        in1=v3(DEC)[:, c, lo:lo + 8].unsqueeze(2).to_broadcast([128, 8, 64]), op=ALU.mult),
                        reads=[r_H, r_b[9]], writes=[r_tmpH])
                    S.op("dve", lambda e, H=H, pb3=pb3: e.tensor_tensor(out=H[:], in0=pb3, in1=tmpH[:], op=ALU.add),
                         reads=[r_tmpH], writes=[r_H, r_pb3])

                nxt = conv_chunk(nch - 1, False)
                for c in range(nch - 1, -1, -1):
                    xs, r_xs, bt, r_bt, _, _ = nxt
                    S.dma("sp", self.XSS[s][c * 128:(c + 1) * 128, 0:512], xs[:], reads=[r_xs], writes=[self.res("XSS", s, c)])
                    S.dma("sp", self.XSS[s][c * 128:(c + 1) * 128, 512:768], bt[:], reads=[r_bt], writes=[self.res("XSS", s, c)])
                    if c > 0:
                        nxt = conv_chunk(c - 1, False)
                    S.op("pool", lambda e, c=c: e.tensor_copy(out=HB[:, c, :], in_=Hb[:]), reads=[r_Hb], writes=[r_HB[c]])
                    pb3, r_pb3 = state_mm(c, xs, r_xs, bt, r_bt, 8)
                    scan_step(Hb, r_Hb, c, 8, pb3, r_pb3)
                h3 = lambda ap: ap.rearrange("p (h d) -> p h d", h=8)
                a3 = lambda ap: ap.rearrange("p (c k) -> p c k", k=16)
                convs = {0: conv_chunk(0, with_out, load=True)}

                rhs_ = {}

                def build_rh(c):
                    rh, r_rh = rhr.next()
                    r_rl = Res()
                    S.op("dve", lambda e, rh=rh, c=c: e.tensor_tensor(out=rh[:, 0], in0=ULb[:], in1=a3(AHI)[:, c, :].unsqueeze(2).to_broadcast([128, 16, 128]), op=ALU.mult),
                         reads=[r_ahl, r_ul], writes=[r_rh, r_rl])
                    S.op("pool", lambda e, rh=rh, c=c: e.tensor_tensor(out=rh[:, 1], in0=ULb[:], in1=a3(ALO)[:, c, :].unsqueeze(2).to_broadcast([128, 16, 128]), op=ALU.mult),
                         reads=[r_ahl, r_ul], writes=[r_rl])
                    rhs_[c] = (rh, r_rh, r_rl)

                def head(c):
                    xs, r_xs, bt, r_bt, bct, r_bct = convs[c]
                    rh, r_rh, r_rl = rhs_.pop(c)
                    pY1, r_pY1 = PB[7]
                    pD_all = {}
                    for rnd in range(2):
                        pDs = []
                        pD_all[rnd] = pDs
                        for d_ in range(2):
                            pD, r_pD = PB[(5 + d_) if rnd == 0 else d_]

                            def dmm(e, pD=pD, rh=rh, d_=d_, rnd=rnd):
                                hs = slice(d_ * 8 + rnd * 4, d_ * 8 + rnd * 4 + 4)
                                e.matmul(pD, lhsT=onesb[:], rhs=rh[:, 0, hs, :], start=True, stop=False)
                                e.matmul(pD, lhsT=onesb[:], rhs=rh[:, 1, hs, :], start=False, stop=False)
                                return e.matmul(pD, lhsT=self.ident[:], rhs=MK4[:, d_, :], start=False, stop=True)
                            S.op("pe", dmm, reads=[r_rh, r_rl, r_ul, self.r_ident], writes=[r_pD])
                            pDs.append((pD, r_pD))
                    pG, r_pG = PB[4]

                    def gmm(e, pG=pG, bct=bct):
                        for g in range(2):
                            ins = e.matmul(pG[:, g * 128:(g + 1) * 128], lhsT=bct[:, g, :], rhs=bct[:, 2 + g, :], start=True, stop=True)
                        return ins
                    S.op("pe", gmm, reads=[r_bct], writes=[r_pG])
                    for rnd in range(2):
                        pDs = pD_all[rnd]
                        for hq in range(4):
                            h = rnd * 4 + hq
                            g = h // 4
                            ws = []
                            for d_, lo in ((0, 0), (1, 8)):
                                pD, r_pD = pDs[d_]
                                m_t, r_m = mr.next()
                                S.op("act", lambda e, m_t=m_t, pD=pD, hq=hq, c=c, lo=lo, h=h: e.activation(
                                    out=m_t[:], in_=pD[:, hq * 128:(hq + 1) * 128], func=AF.Exp, bias=MB[:, c * 16 + lo + h:c * 16 + lo + h + 1]),
                                    reads=[r_b[11]], writes=[r_m, r_pD])
                                w_t, r_w = wr.next()
                                S.op("dve", lambda e, w_t=w_t, pG=pG, g=g, m_t=m_t: e.tensor_tensor(out=w_t[:], in0=pG[:, g * 128:(g + 1) * 128], in1=m_t[:], op=ALU.mult),
                                     reads=[r_m], writes=[r_w, r_pG])
                                ws.append((w_t, r_w))

                            def ymm(e, pY1=pY1, ws=ws, xs=xs, h=h):
                                o = pY1[:, h * 64:(h + 1) * 64]
                                e.matmul(o, lhsT=ws[0][0][:], rhs=xs[:, h * 64:(h + 1) * 64], start=True, stop=False)
                                e.matmul(o, lhsT=ws[1][0][:], rhs=xs[:, h * 64:(h + 1) * 64], start=False, stop=False)
                                return e.matmul(o, lhsT=Dh[:, h, :], rhs=xs[:, h * 64:(h + 1) * 64], start=False, stop=True)
                            S.op("pe", ymm, reads=[ws[0][1], ws[1][1], r_xs, r_diag], writes=[r_pY1])
                    y1, r_y1 = y1r.next()
                    S.op("act", lambda e, y1=y1, pY1=pY1: e.copy(out=y1[:], in_=pY1), reads=[], writes=[r_y1, r_pY1])
                    return y1, r_y1

                def tail(c, y1, r_y1):
                    xs, r_xs, bt, r_bt, bct, r_bct = convs.pop(c)
                    S.op("pool", lambda e: e.tensor_copy(out=HFb[:], in_=Hf[:]), reads=[r_Hf], writes=[r_HFb])
                    pb3, r_pb3 = state_mm(c, xs, r_xs, bt, r_bt, 0)
                    scan_step(Hf, r_Hf, c, 0, pb3, r_pb3)
                    if c + 2 < nch:
                        build_rh(c + 2)
                    pY2, r_pY2 = PB[5]
                    pY3, r_pY3 = PB[6]

                    def y2mm(e, pY2=pY2, bct=bct):
                        for g in range(2):
                            ins = e.matmul(pY2[:, g * 256:(g + 1) * 256], lhsT=bct[:, 2 + g, :], rhs=HFb[:, g * 256:(g + 1) * 256], start=True, stop=True)
                        return ins
                    S.op("pe", y2mm, reads=[r_bct, r_HFb], writes=[r_pY2])

                    def y3mm(e, pY3=pY3, bct=bct, c=c):
                        for g in range(2):
                            ins = e.matmul(pY3[:, g * 256:(g + 1) * 256], lhsT=bct[:, 2 + g, :], rhs=HB[:, c, g * 256:(g + 1) * 256], start=True, stop=True)
                        return ins
                    S.op("pe", y3mm, reads=[r_bct, r_HB[c]], writes=[r_pY3])
                    t1, r_t1 = t1r.next()
                    t2, r_t2 = t2r.next()
                    S.op("dve", lambda e, t1=t1, pY2=pY2, c=c: e.tensor_tensor(out=h3(t1[:]), in0=h3(pY2), in1=v3(EE)[:, c, 0:8].unsqueeze(2).to_broadcast([128, 8, 64]), op=ALU.mult),
                         reads=[r_b[10]], writes=[r_t1, r_pY2])
                    S.op("dve", lambda e, t2=t2, pY3=pY3, c=c: e.tensor_tensor(out=h3(t2[:]), in0=h3(pY3), in1=v3(EE)[:, c, 8:16].unsqueeze(2).to_broadcast([128, 8, 64]), op=ALU.mult),
                         reads=[r_b[10]], writes=[r_t2, r_pY3])
                    S.op("dve", lambda e, t1=t1, t2=t2: e.tensor_tensor(out=t1[:], in0=t1[:], in1=t2[:], op=ALU.add), reads=[r_t2], writes=[r_t1])
                    y, r_y = yr.next()
                    S.op("dve", lambda e, y=y, y1=y1, t1=t1: e.tensor_tensor(out=y[:], in0=y1[:], in1=t1[:], op=ALU.add), reads=[r_t1, r_y1], writes=[r_y])
                    z_t, r_z = zr.next()
                    S.dma("sp", z_t[:], self.ZS[s][c * 128:(c + 1) * 128, :], writes=[r_z])
                    S.op("dve", lambda e, y=y, z_t=z_t: e.tensor_tensor(out=y[:], in0=y[:], in1=z_t[:], op=ALU.mult), reads=[r_z], writes=[r_y])
                    if c + 2 < nch:
                        convs[c + 2] = conv_chunk(c + 2, with_out, load=True)
                    st_t, r_st = str_.next()
                    S.op("act", lambda e, y=y, st_t=st_t: e.activation(out=junk[:], in_=y[:], func=AF.Square, accum_out=st_t[:, 0:1]),
                         reads=[r_y], writes=[r_junk, r_st])
                    S.op("act", lambda e, st_t=st_t: e.activation(out=st_t[:, 1:2], in_=st_t[:, 0:1], func=AF.Sqrt, scale=1.0 / 512, bias=EPS),
                         reads=[r_st], writes=[r_st])
                    S.op("dve", lambda e, st_t=st_t: e.reciprocal(out=st_t[:, 2:3], in_=st_t[:, 1:2]), reads=[r_st], writes=[r_st])
                    oc, r_oc = ocr.next()
                    S.op("dve", lambda e, oc=oc, y=y, st_t=st_t: e.scalar_tensor_tensor(out=oc[:], in0=y[:], scalar=st_t[:, 2:3], in1=gn[:], op0=ALU.mult, op1=ALU.mult),
                         reads=[r_y, r_st, r_p], writes=[r_oc])
                    S.dma("sp", self.MIX[s][c * 128:(c + 1) * 128, 512:1024], oc[:], reads=[r_oc], writes=[self.res("MIX", s, c, "C")])

                if with_out:
                    build_rh(0)
                    if nch > 1:
                        build_rh(1)
                        convs[1] = conv_chunk(1, with_out, load=True)
                    hd = head(0)
                    for c in range(nch):
                        nh = head(c + 1) if c + 1 < nch else None
                        tail(c, *hd)
                        hd = nh
                else:
                    for c in range(nch):
                        xs, r_xs, bt, r_bt, _, _ = convs.pop(c)
                        if c + 1 < nch:
                            convs[c + 1] = conv_chunk(c + 1, with_out, load=True)
                        pb3, r_pb3 = state_mm(c, xs, r_xs, bt, r_bt, 0)
                        scan_step(Hf, r_Hf, c, 0, pb3, r_pb3)

    def _p4_end(self):
        pass

    def p5(self, l, src, streams, final):
        nc, S = self.nc, self.S
        NK2 = DFF // 128
        with ExitStack() as stw:
            W1 = self.sb(stw, "p5b_W1", [128, 8, 2 * DFF], BF16)
            W2 = self.sb(stw, "p5b_W2", [128, NK2, D], BF16)
            r_W1, r_W2 = [], []
            with ExitStack() as st:
                Wo = self.sb(st, "p5a_W", [128, 8, D], BF16)
                r_W = []
                wv = self.w_out[l].rearrange("(k p) n -> p k n", p=128)
                for k in range(8):
                    r = Res()
                    r_W.append(r)
                    S.dma("pool", Wo[:, k, :], wv[:, k, :], writes=[r])
                w1v = self.w_ffn_in[l].rearrange("(k p) n -> p k n", p=128)
                w2v = self.w_ffn_out[l].rearrange("(k p) n -> p k n", p=128)
                for k in range(8):
                    for c0 in range(0, 2 * DFF, 1408):
                        r = Res()
                        r_W1.append(r)
                        S.dma("pool", W1[:, k, c0:c0 + 1408], w1v[:, k, c0:c0 + 1408], writes=[r])
                for k in range(NK2):
                    r = Res()
                    r_W2.append(r)
                    S.dma("pool", W2[:, k, :], w2v[:, k, :], writes=[r])
                gt = self.sb(st, "p5a_gt", [128, 2, D])
                r_gt = Res()
                for s in streams:
                    S.dma("sp", gt[:, s, :], self.MOD[s, 2], writes=[r_gt])
                xr = Ring([self.sb(st, "p5a_x%d" % i, [128, D]) for i in range(3)])
                mr = Ring([self.sb(st, "p5a_m%d" % i, [128, D], BF16) for i in range(3)])
                mTr = Ring([self.sb(st, "p5a_mT%d" % i, [128, 8, 128], BF16) for i in range(3)])
                tr_ = Ring([self.sb(st, "p5a_t%d" % i, [128, D]) for i in range(2)])
                orr = Ring([self.sb(st, "p5a_o%d" % i, [128, D]) for i in range(2)])
                tiles = [(s, t) for s in streams for t in range(min(NT if s == 0 else NTC, self.opts.get('p5_nt', 99)))]

                def prep(s, t):
                    rows = slice(t * 128, (t + 1) * 128)
                    x_t, r_x = xr.next()
                    S.dma("sp", x_t[:], src[s][rows, :], writes=[r_x])
                    m_t, r_m = mr.next()
                    S.dma("sp", m_t[:], self.MIX[s][rows, :], writes=[r_m])
                    mT, r_mT = mTr.next()

                    def fin(m_t=m_t, r_m=r_m, mT=mT, r_mT=r_mT):
                        pb, r_pb = self.bank()
                        pbT = pb.bitcast(BF16)

                        def tr(e, pbT=pbT, m_t=m_t):
                            for k in range(8):
                                ins = e.transpose(out=pbT[:, k * 128:(k + 1) * 128], in_=m_t[:, k * 128:(k + 1) * 128],
                                                  identity=self.ident[:])
                            return ins
                        S.op("pe", tr, reads=[r_m, self.r_ident], writes=[r_pb])
                        S.op("act", lambda e, mT=mT, pbT=pbT: e.copy(out=mT[:], in_=pbT.rearrange("p (k t) -> p k t", k=8)),
                             reads=[], writes=[r_mT, r_pb])
                    return x_t, r_x, mT, r_mT, fin

                pend = prep(*tiles[0]) if tiles else None
                if pend:
                    pend[4]()
                for i, (s, t) in enumerate(tiles):
                    rows = slice(t * 128, (t + 1) * 128)
                    x_t, r_x, mT, r_mT, _ = pend
                    pend = prep(*tiles[i + 1]) if i + 1 < len(tiles) else None
                    t_t, r_t = tr_.next()
                    for half in range(2):
                        if half == 1 and pend:
                            pend[4]()
                        po, r_po = self.bank()

                        def mm(e, po=po, mT=mT, half=half):
                            for k in range(8):
                                ins = e.matmul(po, lhsT=mT[:, k, :], rhs=Wo[:, k, half * 512:(half + 1) * 512],
                                               start=(k == 0), stop=(k == 7))
                            return ins
                        S.op("pe", mm, reads=r_W + [r_mT], writes=[r_po])
                        S.op("dve", lambda e, t_t=t_t, po=po, half=half, s=s: e.tensor_tensor(
                            out=t_t[:, half * 512:(half + 1) * 512], in0=po, in1=gt[:, s, half * 512:(half + 1) * 512], op=ALU.mult),
                            reads=[r_gt], writes=[r_t, r_po])
                    o_t, r_o = orr.next()
                    S.op("dve", lambda e, o_t=o_t, t_t=t_t, x_t=x_t: e.tensor_tensor(out=o_t[:], in0=t_t[:], in1=x_t[:], op=ALU.add),
                         reads=[r_t, r_x], writes=[r_o])
                    S.dma("sp", self.XM[s][rows, :], o_t[:], reads=[r_o], writes=[self.res("XM", s, t)])
            S.barrier_all()
            with ExitStack() as st:
                modt = self.sb(st, "p5b_mod", [128, 3, D])
                r_mod = Res()
                gfin = None
                if final:
                    gfin = self.sb(st, "p5b_gf", [128, D])
                    r_gf = Res()
                    S.dma("sp", gfin[:], self.g_final.partition_broadcast(128), writes=[r_gf])
                xr = Ring([self.sb(st, "p5b_x%d" % i, [128, 2, D]) for i in range(2)])
                junk = self.sb(st, "p5b_junk", [128, D], BF16)
                r_junk = Res()
                stat = Ring([self.sb(st, "p5b_st%d" % i, [128, 8]) for i in range(6)])
                tmpr = Ring([self.sb(st, "p5b_t%d" % i, [128, D]) for i in range(2)])
                hr = Ring([self.sb(st, "p5b_h%d" % i, [128, D], BF16) for i in range(3)])
                hTr = Ring([self.sb(st, "p5b_hT%d" % i, [128, 8, 256], BF16) for i in range(2)])
                sgr = Ring([self.sb(st, "p5b_sg%d" % i, [128, 256]) for i in range(2)])
                actr = Ring([self.sb(st, "p5b_a%d" % i, [128, NK2, 256], BF16) for i in range(1)])
                orr = Ring([self.sb(st, "p5b_o%d" % i, [128, D]) for i in range(1)])
                groups = [(s, g0) for s in streams for g0 in range(0, min(NT if s == 0 else NTC, self.opts.get('p5_nt', 99)), 2)]
                cur_mod = [None]

                def normg(s, g0):
                    if cur_mod[0] != s:
                        cur_mod[0] = s
                        for jj, j in enumerate((3, 4, 5)):
                            S.dma("sp", modt[:, jj, :], self.MOD[s, j], writes=[r_mod])
                    x_t, r_x = xr.next()
                    S.dma("sp", x_t[:], self.XM[s][g0 * 128:(g0 + 2) * 128, :].rearrange("(t p) c -> p t c", p=128), writes=[r_x])
                    hT, r_hT = hTr.next()
                    fins = []
                    for ti in range(2):
                        st_t, r_st = stat.next()
                        S.op("act", lambda e, x_t=x_t, ti=ti, st_t=st_t: e.activation(out=junk[:], in_=x_t[:, ti, :], func=AF.Square,
                                                                                       accum_out=st_t[:, 0:1]),
                             reads=[r_x], writes=[r_junk, r_st])
                        S.op("act", lambda e, st_t=st_t: e.activation(out=st_t[:, 1:2], in_=st_t[:, 0:1], func=AF.Sqrt,
                                                                      scale=1.0 / D, bias=EPS), reads=[r_st], writes=[r_st])
                        S.op("dve", lambda e, st_t=st_t: e.reciprocal(out=st_t[:, 2:3], in_=st_t[:, 1:2]), reads=[r_st], writes=[r_st])
                        tm, r_tm = tmpr.next()
                        S.op("dve", lambda e, tm=tm, x_t=x_t, ti=ti, st_t=st_t: e.scalar_tensor_tensor(
                            out=tm[:], in0=x_t[:, ti, :], scalar=st_t[:, 2:3], in1=modt[:, 1, :], op0=ALU.mult, op1=ALU.mult),
                            reads=[r_x, r_st, r_mod], writes=[r_tm])
                        h_t, r_h = hr.next()
                        S.op("pool", lambda e, h_t=h_t, tm=tm: e.tensor_tensor(out=h_t[:], in0=tm[:], in1=modt[:, 0, :], op=ALU.add),
                             reads=[r_tm, r_mod], writes=[r_h])
                        def fin(h_t=h_t, r_h=r_h, hT=hT, r_hT=r_hT, ti=ti):
                            pb, r_pb = self.bank()
                            pbT = pb.bitcast(BF16)

                            def tr(e, pbT=pbT, h_t=h_t):
                                for k in range(8):
                                    ins = e.transpose(out=pbT[:, k * 128:(k + 1) * 128], in_=h_t[:, k * 128:(k + 1) * 128],
                                                      identity=self.ident[:])
                                return ins
                            S.op("pe", tr, reads=[r_h, self.r_ident], writes=[r_pb])
                            S.op("act", lambda e, hT=hT, ti=ti, pbT=pbT: e.copy(out=hT[:, :, ti * 128:(ti + 1) * 128],
                                                                                 in_=pbT.rearrange("p (k t) -> p k t", k=8)),
                                 reads=[], writes=[r_hT, r_pb])
                        fins.append(fin)
                    return x_t, r_x, hT, r_hT, fins

                pend = normg(*groups[0]) if groups else None
                if pend:
                    for f_ in pend[4]:
                        f_()
                for gi, (s, g0) in enumerate(groups):
                    x_t, r_x, hT, r_hT, _ = pend
                    defer = []
                    if gi + 1 < len(groups) and groups[gi + 1][0] == s:
                        pend = normg(*groups[gi + 1])
                        defer = list(pend[4])
                        late = False
                    else:
                        late = True
                    a_t, r_a = actr.next()
                    for ct in range(NK2):
                        if defer and ct in (8, 15):
                            defer.pop(0)()
                        pg, r_pg = self.bank()
                        pu, r_pu = self.bank()

                        def mm1(pb_, c0, hT=hT):
                            def f(e):
                                for k in range(8):
                                    ins = e.matmul(pb_[:, 0:256], lhsT=W1[:, k, c0:c0 + 128], rhs=hT[:, k, :],
                                                   start=(k == 0), stop=(k == 7))
                                return ins
                            return f
                        S.op("pe", mm1(pg, ct * 128), reads=r_W1 + [r_hT], writes=[r_pg])
                        S.op("pe", mm1(pu, DFF + ct * 128), reads=r_W1 + [r_hT], writes=[r_pu])
                        sg, r_sg = sgr.next()
                        S.op("act", lambda e, sg=sg, pg=pg: e.activation(out=sg[:], in_=pg[:, 0:256], func=AF.Silu),
                             reads=[], writes=[r_sg, r_pg])
                        S.op("dve", lambda e, a_t=a_t, ct=ct, pu=pu, sg=sg: e.tensor_tensor(
                            out=a_t[:, ct, :], in0=pu[:, 0:256], in1=sg[:], op=ALU.mult),
                            reads=[r_sg], writes=[r_a, r_pu])
                    for ti in range(2):
                        t = g0 + ti
                        tm, r_tm = tmpr.next()
                        for half in range(2):
                            po, r_po = self.bank()

                            def mm2(e, po=po, a_t=a_t, ti=ti, half=half):
                                for k in range(NK2):
                                    ins = e.matmul(po, lhsT=a_t[:, k, ti * 128:(ti + 1) * 128], rhs=W2[:, k, half * 512:(half + 1) * 512],
                                                   start=(k == 0), stop=(k == NK2 - 1))
                                return ins
                            S.op("pe", mm2, reads=r_W2 + [r_a], writes=[r_po])
                            S.op("dve", lambda e, tm=tm, po=po, half=half: e.tensor_tensor(
                                out=tm[:, half * 512:(half + 1) * 512], in0=po, in1=modt[:, 2, half * 512:(half + 1) * 512], op=ALU.mult),
                                reads=[r_mod], writes=[r_tm, r_po])
                        o_t, r_o = orr.next()
                        S.op("pool", lambda e, o_t=o_t, tm=tm, x_t=x_t, ti=ti: e.tensor_tensor(out=o_t[:], in0=tm[:], in1=x_t[:, ti, :], op=ALU.add),
                             reads=[r_tm, r_x], writes=[r_o])
                        rows = slice(t * 128, (t + 1) * 128)
                        if not final:
                            S.dma("sp", self.XL[s][rows, :], o_t[:], reads=[r_o], writes=[self.res("XL", s, t)])
                        else:
                            st_t, r_st = stat.next()
                            S.op("act", lambda e, o_t=o_t, st_t=st_t: e.activation(out=junk[:], in_=o_t[:], func=AF.Square,
                                                                                    accum_out=st_t[:, 0:1]),
                                 reads=[r_o], writes=[r_junk, r_st])
                            S.op("act", lambda e, st_t=st_t: e.activation(out=st_t[:, 1:2], in_=st_t[:, 0:1], func=AF.Sqrt,
                                                                          scale=1.0 / D, bias=EPS), reads=[r_st], writes=[r_st])
                            S.op("dve", lambda e, st_t=st_t: e.reciprocal(out=st_t[:, 2:3], in_=st_t[:, 1:2]), reads=[r_st], writes=[r_st])
                            f_t, r_f = tmpr.next()
                            S.op("dve", lambda e, f_t=f_t, o_t=o_t, st_t=st_t: e.scalar_tensor_tensor(
                                out=f_t[:], in0=o_t[:], scalar=st_t[:, 2:3], in1=gfin[:], op0=ALU.mult, op1=ALU.mult),
                                reads=[r_o, r_st, r_gf], writes=[r_f])
                            S.dma("sp", self.out[rows, :], f_t[:], reads=[r_f], writes=[self.res("OUT", t)])
                    while defer:
                        defer.pop(0)()
                    if late and gi + 1 < len(groups):
                        pend = normg(*groups[gi + 1])
                        for f_ in pend[4]:
                            f_()

    def build(self):
        S = self.S
        self.declare()
        phases = self.opts.get("phases")
        with ExitStack() as st:
            self.setup_common(st)
            for l in range(DEPTH):
                src = [self.x_in, self.ctx_in] if l == 0 else self.XL
                last = (l == DEPTH - 1)
                streams = (0,) if last else (1, 0)

                def run(name, fn):
                    if phases is None or (name, l) in phases:
                        fn()
                        S.barrier_all()
                run("p0", lambda: self.p0(l))
                run("p1", lambda: self.p1(l, src, do_ctx_q=not last))
                run("p2", lambda: self.p2(l, streams))
                run("p3", lambda: self.p3(l, streams))
                run("p4", lambda: self.p4(l, streams))
                run("p5", lambda: self.p5(l, src, streams, final=last))
            S.final_wait("sp")
            S.emit()
        return self.nc


def _rope_tables():
    t = np.arange(SEQ)
    rows, cols = t // 64, t % 64
    inv = (10000.0 ** (-np.arange(16, dtype=np.float32) / 16)).astype(np.float32)
    C = np.zeros((64, SEQ), np.float32)
    Sg = np.zeros((64, SEQ), np.float32)
    for blk, pos in ((0, rows), (1, cols)):
        ang = pos.astype(np.float32)[None, :] * inv[:, None]
        cs, sn = np.cos(ang).astype(np.float32), np.sin(ang).astype(np.float32)
        C[blk * 32:blk * 32 + 16] = cs
        C[blk * 32 + 16:blk * 32 + 32] = cs
        Sg[blk * 32:blk * 32 + 16] = -sn
        Sg[blk * 32 + 16:blk * 32 + 32] = sn
    C2 = np.concatenate([C, C], 0)
    S2 = np.concatenate([Sg, Sg], 0)
    return np.stack([C2 * 0.125, S2 * 0.125, C2, S2]).astype(np.float32)


def _swap_idx():
    d = np.arange(64)
    return np.where(d % 32 < 16, d + 16, d - 16)


def _w_in_ext(w_in):
    qa = np.arange(0, 256)
    qb = np.arange(256, 512)
    z = np.arange(512, 1024)
    ka = np.arange(1024, 1152)
    va = np.arange(1152, 1280)
    kb = np.arange(1280, 1536)
    vb = np.arange(1536, 1792)
    xbc = np.arange(1792, 2816)
    dt = np.arange(2816, 2832)
    sw = _swap_idx()

    def heads(base, hs):
        return np.concatenate([base[h * 64:(h + 1) * 64] for h in hs])

    def heads_sw(base, hs):
        return np.concatenate([base[h * 64:(h + 1) * 64][sw] for h in hs])
    cols = [heads(qa, (0, 2)), heads_sw(qa, (0, 2)), heads(qa, (1, 3)), heads_sw(qa, (1, 3)),
            heads(ka, (0, 1)), heads_sw(ka, (0, 1)), qb, kb, xbc, z, va, vb, dt]
    idx = np.concatenate(cols)
    assert idx.shape[0] == WCOLS
    return np.ascontiguousarray(w_in[:, :, idx])


def _bm_table(rpb):
    L = rpb.shape[0]
    krl, kc = np.divmod(np.arange(128), 64)
    qrl, qc = np.divmod(np.arange(128), 64)
    cases = [(i, j) for (i, js) in ((2, range(0, 5)), (0, range(0, 4)), (1, range(0, 4)), (30, range(28, 32)), (31, range(28, 32))) for j in js]
    out = np.full((L, 4, 21, 128, 128), NEG, np.float32)
    for ci, (i, j) in enumerate(cases):
        kr = (2 * j + krl)[:, None]
        qr = (2 * i + qrl)[None, :]
        s_ = np.clip(qr - 4, 0, 56)
        vrow = (kr >= s_) & (kr <= s_ + 7)
        cst = np.clip(qc - 8, 0, 48)[None, :]
        vcol = (kc[:, None] >= cst) & (kc[:, None] < cst + 16)
        valid = vrow & vcol
        dy = np.clip(kr - qr + 7, 0, 14)
        dx = np.clip(kc[:, None] - qc[None, :] + 15, 0, 30)
        dyb, dxb = np.broadcast_arrays(dy, dx)
        g = rpb[:, :, dyb, dxb]
        out[:, :, ci] = np.where(valid[None, None], g, np.float32(NEG))
    return np.ascontiguousarray(out.transpose(0, 3, 1, 2, 4).reshape(L, 128, 84 * 128))


def prep_inputs(inputs, n_cores):
    f = lambda a: np.ascontiguousarray(np.asarray(a, dtype=np.float32))
    x, c, ctx, c_ctx = f(inputs["x"]), f(inputs["c"]), f(inputs["ctx"]), f(inputs["c_ctx"])
    shared = {
        "w_mod": f(inputs["w_mod"]), "b_mod": f(inputs["b_mod"]), "g_mix": f(inputs["g_mix"]), "g_ffn": f(inputs["g_ffn"]),
        "w_in_ext": _w_in_ext(f(inputs["w_in"])), "rope": _rope_tables(),
        "w_out": f(inputs["w_out"]), "w_ffn_in": f(inputs["w_ffn_in"]), "w_ffn_out": f(inputs["w_ffn_out"]),
        "g_final": f(inputs["g_final"]), "wa_sink": f(inputs["wa_sink"]), "bm_tab": _bm_table(f(inputs["na_rpb"])),
        "conv_w_l": np.ascontiguousarray(f(inputs["ssm_conv_w"]).reshape(DEPTH, 7, 8, 128).transpose(0, 3, 2, 1)),
        "conv_b_l": np.ascontiguousarray(f(inputs["ssm_conv_b"]).reshape(DEPTH, 8, 128).transpose(0, 2, 1)),
        "conv_brow": f(inputs["ssm_conv_b"]).reshape(DEPTH, 1, 1024),
        "dt_bias": f(inputs["ssm_dt_bias"]).reshape(DEPTH, 16), "a_log": f(inputs["ssm_a_log"]).reshape(DEPTH, 16),
        "ssm_d": f(inputs["ssm_d"]), "ssm_g": f(inputs["ssm_norm_g"]),
    }
    maps = []
    for i in range(n_cores):
        b = i % 4
        cvec = np.concatenate([c[b].reshape(8, 128).T, c_ctx.reshape(8, 128).T], 1)
        m = dict(shared)
        m.update({"x": x[b], "ctx": ctx[b], "cvec": np.ascontiguousarray(cvec)})
        maps.append(m)
    return maps


N_CORES = 4


def kernel(**inputs):
    nc = Builder().build()
    maps = prep_inputs(inputs, N_CORES)
    res = run_bass_kernel_spmd(nc, maps, core_ids=list(range(N_CORES)))
    out = np.stack([res.results[b]["out"] for b in range(4)], 0)
    return out.astype(np.float32)
```

```python
import numpy as np
from contextlib import ExitStack
import concourse.bass as bass
import concourse.mybir as mybir
from concourse.bass_utils import run_bass_kernel_spmd

F32 = mybir.dt.float32
BF16 = mybir.dt.bfloat16
AF = mybir.ActivationFunctionType
ALU = mybir.AluOpType
AX = mybir.AxisListType

D = 1024
SEQ = 4096
LC = 256
DEPTH = 2
NT = SEQ // 128
NTC = LC // 128
EPS = 1e-6
DFF = 2816
NFM = 18
NFMO = 15
TMC = 912
WCOLS = NFM * 128 + TMC
NEG = -30000.0
VS = 66


class Res:
    __slots__ = ("name", "w", "rs")

    def __init__(self, name=""):
        self.name = name
        self.w = None
        self.rs = []


class Sched:
    ENGS = ("pe", "act", "dve", "pool", "sp")
    NDMA = 40
    NSDMA = 16

    def __init__(self, nc):
        self.nc = nc
        self.prog = {e: [] for e in self.ENGS}
        self.count = {}
        self.known = {e: {} for e in self.ENGS}
        self.dma_i = 0
        self.sdma_i = 0

    def _deps(self, eng, reads, writes):
        waits = {}

        def add(sv):
            if sv is None:
                return
            s, v = sv
            if eng == "pe" and s == "pe":
                return
            if waits.get(s, 0) < v:
                waits[s] = v
        for r in reads:
            add(r.w)
        for w in writes:
            add(w.w)
            for x in w.rs:
                add(x)
        out = []
        kn = self.known[eng]
        for s, v in waits.items():
            if kn.get(s, 0) < v:
                kn[s] = v
                out.append((s, v))
        return out

    def _mark(self, tag, reads, writes):
        for r in reads:
            r.rs.append(tag)
        for w in writes:
            w.w = tag
            w.rs = []

    def op(self, eng, fn, reads=(), writes=()):
        waits = self._deps(eng, reads, writes)
        c = self.count.get(eng, 0) + 1
        self.count[eng] = c
        self.prog[eng].append((waits, fn, (eng, 1)))
        self._mark((eng, c), reads, writes)

    def dma(self, q, out, in_, reads=(), writes=(), **kw):
        if q == "pool":
            slot = "sdma%d" % (self.sdma_i % self.NSDMA)
            self.sdma_i += 1
        else:
            slot = "dma%d" % (self.dma_i % self.NDMA)
            self.dma_i += 1
        waits = self._deps(q, reads, writes)
        prev = self.count.get(slot, 0)
        kn = self.known[q]
        if prev and kn.get(slot, 0) < prev:
            kn[slot] = prev
            waits.append((slot, prev))
        c = prev + 16
        self.count[slot] = c

        def fn(e, out=out, in_=in_, kw=kw):
            return e.dma_start(out=out, in_=in_, **kw)
        self.prog[q].append((waits, fn, (slot, 16)))
        self._mark((slot, c), reads, writes)

    def barrier_all(self):
        allv = list(self.count.items())
        for e in self.ENGS:
            kn = self.known[e]
            waits = []
            for s, v in allv:
                if s == e:
                    continue
                if kn.get(s, 0) < v:
                    kn[s] = v
                    waits.append((s, v))
            if waits:
                self.prog[e].append((waits, None, None))

    def final_wait(self, eng="sp"):
        waits = []
        kn = self.known[eng]
        for s, v in self.count.items():
            if s != eng and kn.get(s, 0) < v:
                kn[s] = v
                waits.append((s, v))
        self.prog[eng].append((waits, None, None))

    def emit(self):
        nc = self.nc
        with ExitStack() as es:
            sems = {}
            for s in self.count:
                sems[s] = es.enter_context(nc.semaphore(s))
            block = es.enter_context(nc.Block())

            def replay(name, e):
                for waits, fn, inc in self.prog[name]:
                    for s, v in waits:
                        e.wait_ge(sems[s], v)
                    if fn is not None:
                        ins = fn(e)
                        if inc is not None:
                            ins.then_inc(sems[inc[0]], inc[1])

            @block.tensor
            def _(e):
                replay("pe", e)

            @block.scalar
            def _(e):
                replay("act", e)

            @block.vector
            def _(e):
                replay("dve", e)

            @block.gpsimd
            def _(e):
                replay("pool", e)

            @block.sync
            def _(e):
                replay("sp", e)


class Ring:
    def __init__(self, aps, name=""):
        self.items = [(a, Res("%s%d" % (name, i))) for i, a in enumerate(aps)]
        self.i = 0

    def next(self):
        it = self.items[self.i % len(self.items)]
        self.i += 1
        return it


class Builder:
    def __init__(self, debug=(), stop_after=None, opts=None):
        self.opts = opts or {}
        self.debug = set(debug)
        self.stop_after = stop_after
        self.nc = bass.Bass("TRN2", target_bir_lowering=False)
        self.S = Sched(self.nc)
        self.es = ExitStack()
        self.resmap = {}

    def din(self, name, shape, dt=F32):
        return self.nc.dram_tensor(name, list(shape), dt, kind="ExternalInput").ap()

    def dscr(self, name, shape, dt=F32):
        kind = "ExternalOutput" if name in self.debug else "Internal"
        if name in self.opts.get("inject", ()):
            kind = "ExternalInput"
        return self.nc.dram_tensor(name, list(shape), dt, kind=kind).ap()

    def res(self, *key):
        r = self.resmap.get(key)
        if r is None:
            r = Res(str(key))
            self.resmap[key] = r
        return r

    def sb(self, st, name, shape, dt=F32):
        self.uid = getattr(self, "uid", 0) + 1
        return st.enter_context(self.nc.sbuf_tensor("%s_%d" % (name, self.uid), list(shape), dt))

    def declare(self):
        self.x_in = self.din("x", [SEQ, D])
        self.ctx_in = self.din("ctx", [LC, D])
        self.cvec = self.din("cvec", [128, 16])
        self.w_mod = self.din("w_mod", [DEPTH, D, 6 * D])
        self.b_mod = self.din("b_mod", [DEPTH, 6 * D])
        self.g_mix = self.din("g_mix", [DEPTH, D])
        self.g_ffn = self.din("g_ffn", [DEPTH, D])
        self.w_in = self.din("w_in_ext", [DEPTH, D, WCOLS])
        self.rope = self.din("rope", [4, 128, SEQ])
        self.w_out = self.din("w_out", [DEPTH, D, D])
        self.w_ffn_in = self.din("w_ffn_in", [DEPTH, D, 2 * DFF])
        self.w_ffn_out = self.din("w_ffn_out", [DEPTH, DFF, D])
        self.g_final = self.din("g_final", [D])
        self.wa_sink = self.din("wa_sink", [DEPTH, 4])
        self.conv_w = self.din("conv_w_l", [DEPTH, 128, 8, 7])
        self.conv_b = self.din("conv_b_l", [DEPTH, 128, 8])
        self.conv_brow = self.din("conv_brow", [DEPTH, 1, 1024])
        self.dt_bias = self.din("dt_bias", [DEPTH, 16])
        self.a_log = self.din("a_log", [DEPTH, 16])
        self.ssm_d = self.din("ssm_d", [DEPTH, 8])
        self.ssm_g = self.din("ssm_g", [DEPTH, 512])
        self.bm_tab = self.din("bm_tab", [DEPTH, 128, 84 * 128])
        self.out = self.nc.dram_tensor("out", [SEQ, D], F32, kind="ExternalOutput").ap()
        self.MOD = self.dscr("MOD", [2, 6, 128, D])
        self.FM = [self.dscr("FM_l", [NFMO, 128, SEQ], BF16), self.dscr("FM_c", [NFMO, 128, LC], BF16)]
        self.ZS = [self.dscr("ZS_l", [SEQ, 512], BF16), self.dscr("ZS_c", [LC, 512], BF16)]
        self.VA = [self.dscr("VA_l", [SEQ, 2 * VS], BF16), self.dscr("VA_c", [LC, 2 * VS], BF16)]
        self.VB = [self.dscr("VB_l", [SEQ, 4 * VS], BF16), self.dscr("VB_c", [LC, 4 * VS], BF16)]
        self.DT = [self.dscr("DT_l", [SEQ, 16]), self.dscr("DT_c", [LC, 16])]
        self.MIX = [self.dscr("MIX_l", [SEQ, D], BF16), self.dscr("MIX_c", [LC, D], BF16)]
        self.XSS = [self.dscr("XSS_l", [SEQ, 768], BF16), self.dscr("XSS_c", [LC, 768], BF16)]
        self.XM = [self.dscr("XM_l", [SEQ, D]), self.dscr("XM_c", [LC, D])]
        self.XL = [self.dscr("XL_l", [SEQ, D]), self.dscr("XL_c", [LC, D])]

    def setup_common(self, st):
        nc, S = self.nc, self.S
        self.ps = st.enter_context(nc.psum_tensor("ps", [128, 4096], F32))
        self.psb = [(self.ps[:, b * 512:(b + 1) * 512], Res("bank%d" % b)) for b in range(8)]
        self.ident = self.sb(st, "ident", [128, 128], BF16)
        self.r_ident = Res("ident")
        ident = self.ident

        S.op("pool", lambda e: e.memset(ident[:], 0.0), writes=[self.r_ident])
        S.op("pool", lambda e: e.affine_select(out=ident[:], in_=ident[:], pattern=[[-1, 128]], compare_op=ALU.not_equal,
                                               fill=1.0, base=0, channel_multiplier=1),
             reads=[self.r_ident], writes=[self.r_ident])
        self.maskP = self.sb(st, "maskP", [128, 128], BF16)
        self.maskN = self.sb(st, "maskN", [128, 128], BF16)
        self.r_mask = Res("mask")
        mP, mN = self.maskP, self.maskN
        r1, r2 = Res(), Res()
        S.op("pool", lambda e: e.memset(mP[:], 0.0), writes=[r1])
        S.op("pool", lambda e: e.memset(mN[:], 0.0), writes=[r2])
        S.op("pool", lambda e: e.affine_select(out=mP[:], in_=mP[:], pattern=[[-1, 128]], compare_op=ALU.is_ge,
                                               fill=NEG, base=0, channel_multiplier=1), reads=[r1], writes=[r1])
        S.op("pool", lambda e: e.affine_select(out=mN[:], in_=mN[:], pattern=[[1, 128]], compare_op=ALU.is_ge,
                                               fill=NEG, base=0, channel_multiplier=-1), reads=[r2], writes=[r2])
        S.op("pool", lambda e: e.memset(self.ident[0:1, 0:1], 1.0), reads=[r1, r2, self.r_ident], writes=[self.r_mask, self.r_ident])
        self.U32 = self.sb(st, "U32", [128, 128])
        self.L32 = self.sb(st, "L32", [128, 128])
        self.ones32 = self.sb(st, "ones32", [128, 128])
        self.r_tri = Res("tri")
        U32, L32, ones32 = self.U32, self.L32, self.ones32
        r3, r4 = Res(), Res()
        S.op("pool", lambda e: e.memset(U32[:], 1.0), writes=[r3])
        S.op("pool", lambda e: e.memset(L32[:], 1.0), writes=[r4])
        S.op("pool", lambda e: e.memset(ones32[:], 1.0), writes=[self.r_tri])
        S.op("pool", lambda e: e.affine_select(out=U32[:], in_=U32[:], pattern=[[1, 128]], compare_op=ALU.is_ge,
                                               fill=0.0, base=0, channel_multiplier=-1), reads=[r3], writes=[r3])
        S.op("pool", lambda e: e.affine_select(out=L32[:], in_=L32[:], pattern=[[-1, 128]], compare_op=ALU.is_ge,
                                               fill=0.0, base=0, channel_multiplier=1), reads=[r4], writes=[r4])
        S.op("pool", lambda e: e.memset(ones32[0:1, 0:1], 1.0), reads=[r3, r4, self.r_tri], writes=[self.r_tri])
        self.bank_i = 0

    def bank(self):
        b = self.psb[self.bank_i % 8]
        self.bank_i += 1
        return b

    def p0(self, l):
        nc, S = self.nc, self.S
        with ExitStack() as st:
            cv = self.sb(st, "p0_cv", [128, 16])
            scv = self.sb(st, "p0_scv", [128, 16])
            scbc = self.sb(st, "p0_scbc", [128, 16, 128])
            gbc = self.sb(st, "p0_gbc", [128, 2, D])
            wb = [self.sb(st, "p0_w%d" % i, [128, 8, 512]) for i in range(2)]
            bb = [self.sb(st, "p0_b%d" % i, [128, 512]) for i in range(2)]
            mt = [self.sb(st, "p0_m%d" % i, [128, D]) for i in range(4)]
            r_cv, r_scv, r_scbc, r_g = Res(), Res(), Res(), Res()
            wring = Ring(wb, "p0w")
            bring = Ring(bb, "p0b")
            mring = Ring(mt, "p0m")
            S.dma("sp", cv[:], self.cvec, writes=[r_cv])
            S.dma("sp", gbc[:, 0, :], self.g_mix[l].partition_broadcast(128), writes=[r_g])
            S.dma("sp", gbc[:, 1, :], self.g_ffn[l].partition_broadcast(128), writes=[r_g])
            S.op("act", lambda e: e.activation(out=scv[:], in_=cv[:], func=AF.Silu), reads=[r_cv], writes=[r_scv])
            S.op("dve", lambda e: e.tensor_copy(out=scbc[:], in_=scv[:].unsqueeze(2).to_broadcast([128, 16, 128])),
                 reads=[r_scv], writes=[r_scbc])
            wv = self.w_mod[l].rearrange("(k p) n -> p k n", p=128)
            cur = {}
            for blk in range(12):
                j, half = blk // 2, blk % 2
                w_t, r_w = wring.next()
                b_t, r_b = bring.next()
                S.dma("sp", w_t[:], wv[:, :, blk * 512:(blk + 1) * 512], writes=[r_w])
                S.dma("sp", b_t[:], self.b_mod[l, blk * 512:(blk + 1) * 512].partition_broadcast(128), writes=[r_b])
                for s in range(2):
                    if half == 0:
                        cur[s] = mring.next()
                    m_t, r_m = cur[s]
                    pb, r_pb = self.bank()

                    def mm(e, pb=pb, w_t=w_t, s=s):
                        for k in range(8):
                            ins = e.matmul(pb, lhsT=scbc[:, s * 8 + k, :], rhs=w_t[:, k, :], start=(k == 0), stop=(k == 7))
                        return ins
                    S.op("pe", mm, reads=[r_scbc, r_w], writes=[r_pb])
                    dst = m_t[:, half * 512:(half + 1) * 512]
                    if j in (1, 4):
                        gsl = gbc[:, 0 if j == 1 else 1, half * 512:(half + 1) * 512]
                        tmp_r = Res()

                        def ev(e, dst=dst, pb=pb, b_t=b_t, gsl=gsl):
                            e.tensor_tensor(out=dst, in0=pb, in1=b_t[:], op=ALU.add)
                            return e.scalar_tensor_tensor(out=dst, in0=dst, scalar=1.0, in1=gsl, op0=ALU.add, op1=ALU.mult)
                        S.op("dve", lambda e, dst=dst, pb=pb, b_t=b_t: e.tensor_tensor(out=dst, in0=pb, in1=b_t[:], op=ALU.add),
                             reads=[r_pb, r_b], writes=[r_m])
                        S.op("dve", lambda e, dst=dst, gsl=gsl: e.scalar_tensor_tensor(out=dst, in0=dst, scalar=1.0, in1=gsl,
                                                                                         op0=ALU.add, op1=ALU.mult),
                             reads=[r_m, r_g], writes=[r_m])
                    else:
                        S.op("dve", lambda e, dst=dst, pb=pb, b_t=b_t: e.tensor_tensor(out=dst, in0=pb, in1=b_t[:], op=ALU.add),
                             reads=[r_pb, r_b], writes=[r_m])
                    if half == 1:
                        S.dma("sp", self.MOD[s, j], m_t[:], reads=[r_m], writes=[self.res("MOD", s, j)])

    def p1(self, l, src, do_ctx_q):
        nc, S = self.nc, self.S
        with ExitStack() as st:
            W = self.sb(st, "p1_W", [128, 8, WCOLS], BF16)
            r_Wall = []
            wv = self.w_in[l].rearrange("(k p) n -> p k n", p=128)
            for k in range(8):
                for c0 in range(0, WCOLS, 1608):
                    r = Res()
                    r_Wall.append(r)
                    S.dma("pool", W[:, k, c0:c0 + 1608], wv[:, k, c0:c0 + 1608], writes=[r])
            modt = self.sb(st, "p1_mod", [128, 2, 2, D])
            r_mod = Res("p1mod")
            for s in range(2):
                for jj, j in enumerate((0, 1)):
                    S.dma("sp", modt[:, s, jj, :], self.MOD[s, j], reads=[self.res("MOD", s, j)], writes=[r_mod])
            xr = Ring([self.sb(st, "p1_x%d" % i, [128, D]) for i in range(3)], "p1x")
            junk = self.sb(st, "p1_junk", [128, D], BF16)
            r_junk = Res()
            stat = Ring([self.sb(st, "p1_st%d" % i, [128, 4]) for i in range(3)], "p1st")
            tmpr = Ring([self.sb(st, "p1_t%d" % i, [128, D]) for i in range(2)], "p1t")
            hr = Ring([self.sb(st, "p1_h%d" % i, [128, D], BF16) for i in range(8)], "p1h")
            hTr = Ring([self.sb(st, "p1_hT%d" % i, [128, 8, 512], BF16) for i in range(2)], "p1hT")
            ropr = Ring([self.sb(st, "p1_rp%d" % i, [128, 4, 512]) for i in range(2)], "p1rp")
            rtr = Ring([self.sb(st, "p1_rt%d" % i, [128, 2, 512]) for i in range(3)], "p1rt")
            fmr = Ring([self.sb(st, "p1_fm%d" % i, [128, NFMO, 512], BF16) for i in range(2)], "p1fm")
            zr = Ring([self.sb(st, "p1_z%d" % i, [128, 4, 512], BF16) for i in range(2)], "p1z")
            var_ = [self.sb(st, "p1_va%d" % i, [128, 4, 2, VS], BF16) for i in range(2)]
            vbr_ = [self.sb(st, "p1_vb%d" % i, [128, 4, 4, VS], BF16) for i in range(2)]
            dtr = Ring([self.sb(st, "p1_dt%d" % i, [128, 4, 16]) for i in range(2)], "p1dt")
            var = Ring(var_, "p1va")
            vbr = Ring(vbr_, "p1vb")
            for (t_, r_) in var.items + vbr.items:
                S.op("pool", lambda e, t_=t_: e.memset(t_[:], 1.0), writes=[r_])

            groups = [(1, 0, NTC)] + [(0, g * 4, 4) for g in range(NT // 4)]
            lim = self.opts.get('p1_lim', 9)
            groups = groups[:self.opts.get('p1_groups', 99)]
            if lim == 0:
                groups = []
            def norm_group(grp):
                (s, t0, ntile) = grp
                TG = ntile * 128
                tok0 = t0 * 128
                hT, r_hT = hTr.next()
                fins = []
                G1 = modt[:, s, 1, :]
                SH1 = modt[:, s, 0, :]
                for ti in range(ntile):
                    x_t, r_x = xr.next()
                    S.dma("sp", x_t[:], src[s][(t0 + ti) * 128:(t0 + ti + 1) * 128, :],
                          reads=[self.res("XL", s, t0 + ti)], writes=[r_x])
                    st_t, r_st = stat.next()
                    S.op("act", lambda e, x_t=x_t, st_t=st_t: e.activation(out=junk[:], in_=x_t[:], func=AF.Square,
                                                                            accum_out=st_t[:, 0:1]),
                         reads=[r_x], writes=[r_junk, r_st])
                    S.op("act", lambda e, st_t=st_t: e.activation(out=st_t[:, 1:2], in_=st_t[:, 0:1], func=AF.Sqrt,
                                                                  scale=1.0 / D, bias=EPS),
                         reads=[r_st], writes=[r_st])
                    S.op("dve", lambda e, st_t=st_t: e.reciprocal(out=st_t[:, 2:3], in_=st_t[:, 1:2]), reads=[r_st], writes=[r_st])
                    tm, r_tm = tmpr.next()
                    S.op("dve", lambda e, tm=tm, x_t=x_t, st_t=st_t, G1=G1: e.scalar_tensor_tensor(
                        out=tm[:], in0=x_t[:], scalar=st_t[:, 2:3], in1=G1, op0=ALU.mult, op1=ALU.mult),
                        reads=[r_x, r_st, r_mod], writes=[r_tm])
                    h_t, r_h = hr.next()
                    S.op("pool", lambda e, h_t=h_t, tm=tm, SH1=SH1: e.tensor_tensor(out=h_t[:], in0=tm[:], in1=SH1, op=ALU.add),
                         reads=[r_tm, r_mod], writes=[r_h])
                    def fin(h_t=h_t, r_h=r_h, hT=hT, r_hT=r_hT, ti=ti):
                        pb, r_pb = self.bank()
                        pbT = pb.bitcast(BF16)

                        def tr(e, pbT=pbT, h_t=h_t):
                            for k in range(8):
                                ins = e.transpose(out=pbT[:, k * 128:(k + 1) * 128], in_=h_t[:, k * 128:(k + 1) * 128],
                                                  identity=self.ident[:])
                            return ins
                        S.op("pe", tr, reads=[r_h, self.r_ident], writes=[r_pb])
                        S.op("act", lambda e, hT=hT, ti=ti, pbT=pbT: e.copy(out=hT[:, :, ti * 128:(ti + 1) * 128],
                                                                             in_=pbT.rearrange("p (k t) -> p k t", k=8)),
                             reads=[], writes=[r_hT, r_pb])
                    fins.append(fin)
                return hT, r_hT, fins

            pend = norm_group(groups[0]) if groups else None
            if pend:
                for f_ in pend[2]:
                    f_()
            for gi, (s, t0, ntile) in enumerate(groups):
                TG = ntile * 128
                tok0 = t0 * 128
                hT, r_hT, _ = pend
                pend = norm_group(groups[gi + 1]) if gi + 1 < len(groups) else None
                defer = list(pend[2]) if pend else []
                if lim <= 1:
                    continue
                fm, r_fm = fmr.next()
                if s == 0:
                    rp, r_rp = ropr.next()
                    S.dma("sp", rp[:], self.rope[:, :, tok0:tok0 + TG].rearrange("c p t -> p c t"), writes=[r_rp])

                def fm_mm(ct, pb, TG=TG, hT=hT):
                    def f(e):
                        for k in range(8):
                            ins = e.matmul(pb[:, 0:TG], lhsT=W[:, k, ct * 128:(ct + 1) * 128], rhs=hT[:, k, 0:TG],
                                           start=(k == 0), stop=(k == 7))
                        return ins
                    return f
                for (ct, slot, ci) in ((0, 0, 0), (2, 1, 0), (4, 2, 2)):
                    pq, r_pq = self.bank()
                    S.op("pe", fm_mm(ct, pq), reads=r_Wall + [r_hT], writes=[r_pq])
                    if s == 0:
                        psw, r_psw = self.bank()
                        S.op("pe", fm_mm(ct + 1, psw), reads=r_Wall + [r_hT], writes=[r_psw])
                        rt, r_rt = rtr.next()
                        S.op("dve", lambda e, rt=rt, pq=pq, rp=rp, ci=ci, TG=TG: e.tensor_tensor(
                            out=rt[:, 0, 0:TG], in0=pq[:, 0:TG], in1=rp[:, ci, 0:TG], op=ALU.mult),
                            reads=[r_pq, r_rp], writes=[r_rt])
                        S.op("dve", lambda e, rt=rt, psw=psw, rp=rp, ci=ci, TG=TG: e.tensor_tensor(
                            out=rt[:, 1, 0:TG], in0=psw[:, 0:TG], in1=rp[:, ci + 1, 0:TG], op=ALU.mult),
                            reads=[r_psw, r_rp], writes=[r_rt])
                        S.op("pool", lambda e, rt=rt, fm=fm, slot=slot, TG=TG: e.tensor_tensor(
                            out=fm[:, slot, 0:TG], in0=rt[:, 0, 0:TG], in1=rt[:, 1, 0:TG], op=ALU.add),
                            reads=[r_rt], writes=[r_fm])
                    else:
                        sc_ = 0.125 if ct < 4 else 1.0
                        S.op("act", lambda e, fm=fm, slot=slot, pq=pq, TG=TG, sc_=sc_: e.activation(
                            out=fm[:, slot, 0:TG], in_=pq[:, 0:TG], func=AF.Copy, scale=sc_),
                            reads=[r_pq], writes=[r_fm])
                for ct in range(6, NFM):
                    if defer and ct in (8, 11, 14, 17):
                        defer.pop(0)()
                    slot = ct - 3
                    pq, r_pq = self.bank()
                    S.op("pe", fm_mm(ct, pq), reads=r_Wall + [r_hT], writes=[r_pq])
                    sc_ = 0.125 if ct < 8 else 1.0
                    if ct % 2 == 0:
                        S.op("act", lambda e, fm=fm, slot=slot, pq=pq, TG=TG, sc_=sc_: e.activation(
                            out=fm[:, slot, 0:TG], in_=pq[:, 0:TG], func=AF.Copy, scale=sc_),
                            reads=[r_pq], writes=[r_fm])
                    else:
                        S.op("dve", lambda e, fm=fm, slot=slot, pq=pq, TG=TG, sc_=sc_: e.tensor_scalar(
                            out=fm[:, slot, 0:TG], in0=pq[:, 0:TG], scalar1=sc_, scalar2=None, op0=ALU.mult),
                            reads=[r_pq], writes=[r_fm])
                for c0 in range(0, NFMO, 5):
                    S.dma("sp", self.FM[s][c0:c0 + 5, :, tok0:tok0 + TG].rearrange("c p t -> p c t"), fm[:, c0:c0 + 5, 0:TG],
                          reads=[r_fm], writes=[self.res("FM", s, t0 // 4, c0)])
                while defer:
                    defer.pop(0)()
                if lim <= 2:
                    continue
                z_t, r_z = zr.next()
                va_t, r_va = var.next()
                vb_t, r_vb = vbr.next()
                dt_t, r_dt = dtr.next()
                tmm = self.opts.get('tm_mask', 7)
                for ti in range(ntile):
                    def tm_mm(pb, c0, n, hT=hT, ti=ti):
                        def f(e):
                            for k in range(8):
                                ins = e.matmul(pb[:, 0:n], lhsT=hT[:, k, ti * 128:(ti + 1) * 128],
                                               rhs=W[:, k, c0:c0 + n], start=(k == 0), stop=(k == 7))
                            return ins
                        return f
                    if not (tmm & 1):
                        continue
                    pz, r_pz = self.bank()
                    S.op("pe", tm_mm(pz, NFM * 128, 512), reads=r_Wall + [r_hT], writes=[r_pz])
                    S.op("act", lambda e, z_t=z_t, ti=ti, pz=pz: e.activation(out=z_t[:, ti, :], in_=pz, func=AF.Silu),
                         reads=[r_pz], writes=[r_z])
                    if not (tmm & 2):
                        continue
                    pv, r_pv = self.bank()
                    S.op("pe", tm_mm(pv, NFM * 128 + 512, 400), reads=r_Wall + [r_hT], writes=[r_pv])
                    if not (tmm & 8):
                      S.op("dve", lambda e, va_t=va_t, ti=ti, pv=pv: e.tensor_copy(
                        out=va_t[:, ti, :, 0:64], in_=pv[:, 0:128].rearrange("p (g d) -> p g d", g=2)),
                        reads=[r_pv], writes=[r_va, r_pv])
                    if not (tmm & 16):
                      S.op("dve", lambda e, vb_t=vb_t, ti=ti, pv=pv: e.tensor_copy(
                        out=vb_t[:, ti, :, 0:64], in_=pv[:, 128:384].rearrange("p (g d) -> p g d", g=4)),
                        reads=[r_pv], writes=[r_vb, r_pv])
                    if not (tmm & 32):
                      S.op("act", lambda e, dt_t=dt_t, ti=ti, pv=pv: e.copy(out=dt_t[:, ti, :], in_=pv[:, 384:400]),
                         reads=[r_pv], writes=[r_dt, r_pv])
                if not (tmm & 4):
                    continue
                rows = slice(tok0, tok0 + TG)
                gk = t0 // 4
                S.dma("sp", self.ZS[s][rows, :].rearrange("(t p) c -> p t c", p=128), z_t[:, 0:ntile, :],
                      reads=[r_z], writes=[self.res("ZS", s, gk)])
                S.dma("sp", self.VA[s][rows, :].rearrange("(t p) c -> p t c", p=128),
                      va_t[:, 0:ntile].rearrange("p t g d -> p t (g d)"), reads=[r_va], writes=[self.res("VA", s, gk)])
                S.dma("sp", self.VB[s][rows, :].rearrange("(t p) c -> p t c", p=128),
                      vb_t[:, 0:ntile].rearrange("p t g d -> p t (g d)"), reads=[r_vb], writes=[self.res("VB", s, gk)])
                S.dma("sp", self.DT[s][rows, :].rearrange("(t p) c -> p t c", p=128), dt_t[:, 0:ntile, :],
                      reads=[r_dt], writes=[self.res("DT", s, gk)])

    @staticmethod
    def nb_window(i):
        s0 = min(max(2 * i - 4, 0), 56)
        s1 = min(max(2 * i + 1 - 4, 0), 56)
        return list(range(s0 // 2, (s1 + 7) // 2 + 1))

    @staticmethod
    def bm_index(i, j):
        if 2 <= i <= 29:
            return j - i + 2
        if i == 0:
            return 5 + j
        if i == 1:
            return 9 + j
        if i == 30:
            return 13 + (j - 28)
        return 17 + (j - 28)

    def p_attn(self, l, kind, streams):
        nc, S = self.nc, self.S
        isA = kind == "A"
        nkt = 1 if isA else 2
        ng = 2 if isA else 4
        qs0 = 0 if isA else 3
        ks0 = 2 if isA else 5
        VSRC = self.VA if isA else self.VB
        col0 = 0 if isA else 256
        with ExitStack() as st:
            QT = self.sb(st, "at_Q", [128, 2, 2, SEQ], BF16)
            QTc = self.sb(st, "at_Qc", [128, 2, 2, LC], BF16)
            KT = self.sb(st, "at_K", [128, nkt, SEQ + LC], BF16)
            V = self.sb(st, "at_V", [128, NT + NTC, ng * VS], BF16)
            r_in = Res()
            for T in range(2):
                for hh in range(2):
                    lo, zl = hh * 64, (1 - hh) * 64
                    S.op("pool", lambda e, T=T, hh=hh, zl=zl: e.memset(QT[zl:zl + 64, T, hh, :], 0.0), writes=[r_in])
                    S.op("pool", lambda e, T=T, hh=hh, zl=zl: e.memset(QTc[zl:zl + 64, T, hh, :], 0.0), writes=[r_in])
                    S.dma("sp", QT[lo:lo + 64, T, hh, :], self.FM[0][qs0 + T][lo:lo + 64, :], writes=[r_in])
                    S.dma("sp", QTc[lo:lo + 64, T, hh, :], self.FM[1][qs0 + T][lo:lo + 64, :], writes=[r_in])
            for kt in range(nkt):
                S.dma("sp", KT[:, kt, 0:SEQ], self.FM[0][ks0 + kt], writes=[r_in])
                S.dma("sp", KT[:, kt, SEQ:SEQ + LC], self.FM[1][ks0 + kt], writes=[r_in])
            for t0 in range(0, NT, 8):
                S.dma("sp", V[:, t0:t0 + 8, :], VSRC[0][t0 * 128:(t0 + 8) * 128, :].rearrange("(t p) c -> p t c", p=128), writes=[r_in])
            S.dma("sp", V[:, NT:NT + NTC, :], VSRC[1].rearrange("(t p) c -> p t c", p=128), writes=[r_in])
            r_c = Res()
            if isA:
                esk = self.sb(st, "at_esk", [128, 4])
                S.dma("sp", esk[:], self.wa_sink[l].partition_broadcast(128), writes=[r_c])
                S.op("act", lambda e: e.activation(out=esk[:], in_=esk[:], func=AF.Exp), reads=[r_c], writes=[r_c])
            else:
                BM = self.sb(st, "at_BM", [128, 84 * 128], BF16)
                for c0 in range(0, 84 * 128, 1792):
                    S.dma("pool", BM[:, c0:c0 + 1792], self.bm_tab[l][:, c0:c0 + 1792], writes=[r_c])
            Sring = Ring([self.ps[:, i * 1024:(i + 1) * 1024] for i in range(3)])
            Oring = Ring([self.ps[:, 3072 + i * 512:3072 + (i + 1) * 512] for i in range(2)])
            PTr = Ring([self.sb(st, "at_PT%d" % i, [128, 7 * 128], BF16) for i in range(3)])
            mor = Ring([self.sb(st, "at_mo%d" % i, [128, 4, 64], BF16) for i in range(3)])
            dnr = Ring([self.sb(st, "at_dn%d" % i, [128, 8]) for i in range(3)])
            for s in streams:
                ntl = min(NT if s == 0 else NTC, self.opts.get("at_nt", 99))
                for n in range(ntl):
                    O, r_O = Oring.next()
                    pend_pv = []
                    for T in range(2):
                        for hh in range(2):
                            h = (2 * hh + T) if isA else (2 * T + hh)
                            g = hh if isA else h
                            kt = 0 if isA else T
                            ps_ = slice(hh * 64, (hh + 1) * 64)
                            chunks = []
                            if s == 0:
                                if isA:
                                    if n > 0:
                                        chunks.append(((n - 1) * 128, self.maskP[:], n - 1))
                                    chunks.append((n * 128, None, n))
                                    if n < NT - 1:
                                        chunks.append(((n + 1) * 128, self.maskN[:], n + 1))
                                else:
                                    for j in self.nb_window(n):
                                        bi = h * 21 + self.bm_index(n, j)
                                        chunks.append((j * 128, BM[:, bi * 128:(bi + 1) * 128], j))
                            chunks.append((SEQ, None, NT))
                            chunks.append((SEQ + 128, None, NT + 1))
                            nch = len(chunks)
                            q_ap = (QT if s == 0 else QTc)[:, T, hh, n * 128:(n + 1) * 128]
                            Sp, r_S = Sring.next()

                            def smm(e, Sp=Sp, chunks=chunks, q_ap=q_ap, kt=kt, ps_=ps_):
                                for c, (kc, bias, vt) in enumerate(chunks):
                                    ins = e.matmul(Sp[:, c * 128:(c + 1) * 128], lhsT=KT[:, kt, kc:kc + 128], rhs=q_ap,
                                                   start=True, stop=(bias is None))
                                    if bias is not None:
                                        ins = e.matmul(Sp[:, c * 128:(c + 1) * 128], lhsT=self.ident[:], rhs=bias,
                                                       start=False, stop=True)
                                return ins
                            S.op("pe", smm, reads=[r_in, r_c, self.r_ident, self.r_mask], writes=[r_S])
                            PT, r_PT = PTr.next()
                            S.op("act", lambda e, PT=PT, Sp=Sp, nch=nch: e.activation(out=PT[:, 0:nch * 128], in_=Sp[:, 0:nch * 128], func=AF.Exp),
                                 reads=[], writes=[r_PT, r_S])

                            def pv(e, O=O, PT=PT, chunks=chunks, h=h, g=g):
                                for c, (kc, bias, vt) in enumerate(chunks):
                                    ins = e.matmul(O[:, h * VS:h * VS + 65], lhsT=PT[:, c * 128:(c + 1) * 128],
                                                   rhs=V[:, vt, g * VS:g * VS + 65], start=(c == 0), stop=(c == len(chunks) - 1))
                                return ins
                            pend_pv.append((pv, [r_PT, r_in], [r_O]))
                            if len(pend_pv) > 1:
                                f_, rd_, wr_ = pend_pv.pop(0)
                                S.op("pe", f_, reads=rd_, writes=wr_)
                    while pend_pv:
                        f_, rd_, wr_ = pend_pv.pop(0)
                        S.op("pe", f_, reads=rd_, writes=wr_)
                    dn, r_dn = dnr.next()
                    O3 = O[:, 0:4 * VS].rearrange("p (h c) -> p h c", c=VS)
                    if isA:
                        S.op("dve", lambda e, dn=dn, O3=O3: e.tensor_tensor(out=dn[:, 0:4].unsqueeze(2), in0=O3[:, :, 64:65],
                                                                             in1=esk[:].unsqueeze(2), op=ALU.add),
                             reads=[r_c], writes=[r_dn, r_O])
                    else:
                        S.op("dve", lambda e, dn=dn, O3=O3: e.tensor_copy(out=dn[:, 0:4].unsqueeze(2), in_=O3[:, :, 64:65]),
                             reads=[], writes=[r_dn, r_O])
                    S.op("dve", lambda e, dn=dn: e.reciprocal(out=dn[:, 4:8], in_=dn[:, 0:4]), reads=[r_dn], writes=[r_dn])
                    mo, r_mo = mor.next()
                    S.op("dve", lambda e, mo=mo, O3=O3, dn=dn: e.tensor_tensor(
                        out=mo[:], in0=O3[:, :, 0:64], in1=dn[:, 4:8].unsqueeze(2).to_broadcast([128, 4, 64]), op=ALU.mult),
                        reads=[r_dn], writes=[r_mo, r_O])
                    S.dma("sp", self.MIX[s][n * 128:(n + 1) * 128, col0:col0 + 256], mo[:].rearrange("p h d -> p (h d)"),
                          reads=[r_mo], writes=[self.res("MIX", s, n, kind)])

    def p2(self, l, streams):
        self.p_attn(l, "A", streams)

    def p3(self, l, streams):
        self.p_attn(l, "B", streams)

    def p4(self, l, streams):
        nc, S = self.nc, self.S
        last = (l == DEPTH - 1)
        PB = self.psb
        with ExitStack() as st:
            cw = self.sb(st, "s_cw", [128, 8, 7])
            cb = self.sb(st, "s_cb", [128, 8])
            cbrow = self.sb(st, "s_cbrow", [128, 1024], BF16)
            ones1 = self.sb(st, "s_ones1", [128, 128], BF16)
            diag = self.sb(st, "s_diag", [128, 8, 7, 128], BF16)
            Dh = self.sb(st, "s_Dh", [128, 8, 128], BF16)
            sm = self.sb(st, "s_sm", [128, 64])
            gn = self.sb(st, "s_gn", [128, 512])
            r_p = Res("ssm_params")
            r_diag = Res("diag")
            S.dma("sp", cw[:], self.conv_w[l], writes=[r_p])
            S.dma("sp", cb[:], self.conv_b[l], writes=[r_p])
            r_cb0 = Res()
            S.op("pool", lambda e: e.memset(cbrow[:], 0.0), writes=[r_cb0])
            S.dma("pool", cbrow[0:1, :], self.conv_brow[l], reads=[r_cb0], writes=[r_p, r_cb0])
            S.dma("sp", sm[:, 0:16], self.dt_bias[l].partition_broadcast(128), writes=[r_p])
            S.dma("sp", sm[:, 16:32], self.a_log[l].partition_broadcast(128), writes=[r_p])
            S.dma("sp", sm[:, 32:40], self.ssm_d[l].partition_broadcast(128), writes=[r_p])
            S.dma("sp", gn[:], self.ssm_g[l].partition_broadcast(128), writes=[r_p])
            S.op("pool", lambda e: e.memset(ones1[:], 0.0), writes=[r_diag])
            S.op("pool", lambda e: e.memset(ones1[0:1, :], 1.0), reads=[r_diag], writes=[r_diag])
            S.op("act", lambda e: e.activation(out=sm[:, 40:56], in_=sm[:, 16:32], func=AF.Exp), reads=[r_p], writes=[r_p])
            S.op("dve", lambda e: e.tensor_scalar(out=sm[:, 16:32], in0=sm[:, 40:56], scalar1=-1.0, scalar2=None, op0=ALU.mult),
                 reads=[r_p], writes=[r_p])
            for c in range(8):
                for j in range(7):
                    S.op("dve", lambda e, c=c, j=j: e.tensor_scalar(out=diag[:, c, j, :], in0=self.ident[:], scalar1=cw[:, c, j:j + 1],
                                                                    scalar2=None, op0=ALU.mult),
                         reads=[r_p, self.r_ident], writes=[r_diag])
            for h in range(8):
                S.op("dve", lambda e, h=h: e.tensor_scalar(out=Dh[:, h, :], in0=self.ident[:], scalar1=sm[:, 32 + h:33 + h],
                                                            scalar2=None, op0=ALU.mult),
                     reads=[r_p, self.r_ident], writes=[r_diag])
            ULb = self.sb(st, "s_ULb", [128, 16, 128], BF16)
            MK4 = self.sb(st, "s_MK4", [128, 2, 512], BF16)
            onesb = self.sb(st, "s_onesb", [128, 128], BF16)
            r_ul = Res("ULb")
            S.op("dve", lambda e: e.tensor_copy(out=ULb[:, 0:8, :], in_=self.U32[:].unsqueeze(1).to_broadcast([128, 8, 128])),
                 reads=[self.r_tri], writes=[r_ul])
            S.op("dve", lambda e: e.tensor_copy(out=ULb[:, 8:16, :], in_=self.L32[:].unsqueeze(1).to_broadcast([128, 8, 128])),
                 reads=[self.r_tri, r_ul], writes=[r_ul])
            S.op("dve", lambda e: e.tensor_copy(out=MK4[:, 0, :].rearrange("p (a b) -> p a b", a=4), in_=self.maskN[:].unsqueeze(1).to_broadcast([128, 4, 128])),
                 reads=[self.r_mask, r_ul], writes=[r_ul])
            S.op("dve", lambda e: e.tensor_copy(out=MK4[:, 1, :].rearrange("p (a b) -> p a b", a=4), in_=self.maskP[:].unsqueeze(1).to_broadcast([128, 4, 128])),
                 reads=[self.r_mask, r_ul], writes=[r_ul])
            S.op("dve", lambda e: e.tensor_copy(out=onesb[:], in_=self.ones32[:]), reads=[self.r_tri, r_ul], writes=[r_ul])
            Hf = self.sb(st, "s_Hf", [128, 512])
            Hb = self.sb(st, "s_Hb", [128, 512])
            HFb = self.sb(st, "s_HFb", [128, 512], BF16)
            r_Hf, r_Hb, r_HFb = Res("Hf"), Res("Hb"), Res("HFb")
            S.op("pool", lambda e: e.memset(Hf[:], 0.0), writes=[r_Hf])
            S.op("pool", lambda e: e.memset(Hb[:], 0.0), writes=[r_Hb])
            XB = self.sb(st, "s_XB", [128, 8, SEQ + 6], BF16)
            HB = self.sb(st, "s_HB", [128, NT, 512], BF16)
            r_HB = [Res("HB%d" % c) for c in range(NT)]
            NB = 9
            big = [self.sb(st, "s_big%d" % i, [128, NT * 16]) for i in range(NB)]
            xsr = Ring([self.sb(st, "s_xs%d" % i, [128, 512], BF16) for i in range(4)])
            btr = Ring([self.sb(st, "s_bt%d" % i, [128, 256], BF16) for i in range(4)])
            bcr = Ring([self.sb(st, "s_bc%d" % i, [128, 4, 128], BF16) for i in range(4)])
            xwr = Ring([self.sb(st, "s_xw%d" % i, [128, 512], BF16) for i in range(2)])
            mr = Ring([self.sb(st, "s_m%d" % i, [128, 128]) for i in range(6)])
            wr = Ring([self.sb(st, "s_w%d" % i, [128, 128], BF16) for i in range(6)])
            t1r = Ring([self.sb(st, "s_t1%d" % i, [128, 512]) for i in range(1)])
            t2r = Ring([self.sb(st, "s_t2%d" % i, [128, 512]) for i in range(1)])
            yr = Ring([self.sb(st, "s_y%d" % i, [128, 512]) for i in range(2)])
            y1r = Ring([self.sb(st, "s_y1%d" % i, [128, 512]) for i in range(2)])
            zr = Ring([self.sb(st, "s_z%d" % i, [128, 512], BF16) for i in range(2)])
            ocr = Ring([self.sb(st, "s_oc%d" % i, [128, 512], BF16) for i in range(2)])
            str_ = Ring([self.sb(st, "s_st%d" % i, [128, 4]) for i in range(3)])
            junk = self.sb(st, "s_junk", [128, 512], BF16)
            r_junk = Res()
            tmpH = self.sb(st, "s_tmpH", [128, 512])
            r_tmpH = Res()

            r_XB = Res("XB")
            _rb = [Res("big%d" % i) for i in range(NB)]
            r_b = {0: _rb[0], 1: _rb[1], 2: _rb[2], 3: _rb[3], 4: _rb[4], 5: _rb[5], 6: _rb[6], 7: _rb[6], 8: _rb[1], 9: _rb[7], 10: _rb[8], 11: _rb[3]}
            ahi = self.sb(st, "s_ahi", [128, NT * 16], BF16)
            alo = self.sb(st, "s_alo", [128, NT * 16], BF16)
            r_ahl = Res("ahl")
            rhr = Ring([self.sb(st, "s_rh%d" % i, [128, 2, 16, 128], BF16) for i in range(2)])
            for s in streams if 1 in streams else (1,) + tuple(streams):
                with_out = not (s == 1 and last)
                nch = NTC if s == 1 else NT
                nch = min(nch, self.opts.get("ssm_nch", 99))
                T = nch * 128
                W16 = nch * 16
                S.op("pool", lambda e: e.memset(XB[:, :, 0:3], 0.0), writes=[r_XB])
                S.op("pool", lambda e, T=T: e.memset(XB[:, :, 3 + T:6 + T], 0.0), writes=[r_XB])
                for c in range(8):
                    S.dma("sp", XB[:, c, 3:3 + T], self.FM[s][7 + c][:, 0:T], writes=[r_XB])
                DTr, Ev, DTv, LN, Av, AC, TOT, DEC, EE = [b[:, 0:W16] for b in big]
                DTE, SDT, MB = TOT, Ev, LN
                v3 = lambda ap: ap.rearrange("p (c k) -> p c k", k=16)
                for c0 in range(0, nch, 8):
                    c1 = min(c0 + 8, nch)
                    S.dma("sp", v3(DTr)[:, c0:c1, :], self.DT[s][c0 * 128:c1 * 128, :].rearrange("(c p) k -> p c k", p=128), writes=[r_b[0]])
                S.op("dve", lambda e, DTr=DTr, nch=nch: e.tensor_tensor(out=v3(DTr), in0=v3(DTr), in1=sm[:, 0:16].unsqueeze(1).to_broadcast([128, nch, 16]), op=ALU.add),
                     reads=[r_p], writes=[r_b[0]])
                S.op("act", lambda e, Ev=Ev, DTr=DTr: e.activation(out=Ev, in_=DTr, func=AF.Exp), reads=[r_b[0]], writes=[r_b[1]])
                S.op("act", lambda e, Ev=Ev, DTv=DTv: e.activation(out=DTv, in_=Ev, func=AF.Ln, bias=1.0), reads=[r_b[1]], writes=[r_b[2]])
                S.op("act", lambda e, LN=LN, DTv=DTv: e.activation(out=LN, in_=DTv, func=AF.Ln), reads=[r_b[2]], writes=[r_b[3]])
                S.op("dve", lambda e, Av=Av, DTv=DTv, nch=nch: e.tensor_tensor(out=v3(Av), in0=v3(DTv), in1=sm[:, 16:32].unsqueeze(1).to_broadcast([128, nch, 16]), op=ALU.mult),
                     reads=[r_b[2], r_p], writes=[r_b[4]])
                AHI = ahi[:, 0:W16]
                ALO = alo[:, 0:W16]
                S.op("dve", lambda e, AHI=AHI, Av=Av: e.tensor_copy(out=AHI, in_=Av), reads=[r_b[4]], writes=[r_ahl])
                S.op("dve", lambda e, ALO=ALO, Av=Av, AHI=AHI: e.tensor_tensor(out=ALO, in0=Av, in1=AHI, op=ALU.subtract),
                     reads=[r_b[4], r_ahl], writes=[r_ahl])
                for (mat, tgt, lo) in ((self.U32, AC, 0), (self.L32, AC, 8), (self.ones32, TOT, None)):
                    pb, r_pb = self.bank()
                    S.op("pe", lambda e, pb=pb, mat=mat, Av=Av, W16=W16: e.matmul(pb[:, 0:W16], lhsT=mat[:], rhs=Av, start=True, stop=True),
                         reads=[r_b[4], self.r_tri], writes=[r_pb])
                    if lo is None:
                        S.op("dve", lambda e, pb=pb, tgt=tgt, W16=W16: e.tensor_copy(out=tgt, in_=pb[:, 0:W16]), reads=[], writes=[r_b[6], r_pb])
                    else:
                        S.op("dve", lambda e, pb=pb, tgt=tgt, lo=lo, W16=W16: e.tensor_copy(out=v3(tgt)[:, :, lo:lo + 8], in_=v3(pb[:, 0:W16])[:, :, lo:lo + 8]),
                             reads=[], writes=[r_b[5], r_pb])
                S.op("act", lambda e, DEC=DEC, TOT=TOT: e.activation(out=DEC, in_=TOT, func=AF.Exp), reads=[r_b[6]], writes=[r_b[9]])
                S.op("dve", lambda e, DTE=DTE, TOT=TOT, AC=AC: e.tensor_tensor(out=DTE, in0=TOT, in1=AC, op=ALU.subtract), reads=[r_b[5]], writes=[r_b[7]])
                S.op("act", lambda e, DTE=DTE: e.activation(out=DTE, in_=DTE, func=AF.Exp), reads=[], writes=[r_b[7]])
                S.op("dve", lambda e, SDT=SDT, DTE=DTE, DTv=DTv: e.tensor_tensor(out=SDT, in0=DTE, in1=DTv, op=ALU.mult), reads=[r_b[7], r_b[2]], writes=[r_b[8]])
                S.op("act", lambda e, EE=EE, AC=AC: e.activation(out=EE, in_=AC, func=AF.Exp), reads=[r_b[5]], writes=[r_b[10]])
                S.op("dve", lambda e, MB=MB, LN=LN, AC=AC: e.tensor_tensor(out=MB, in0=LN, in1=AC, op=ALU.subtract), reads=[r_b[3], r_b[5]], writes=[r_b[11]])

                def conv_chunk(c, want_fm, load=False):
                    if load:
                        xs, r_xs = xsr.next()
                        bt, r_bt = btr.next()
                        rd = [self.res("XSS", s, c)]
                        S.dma("sp", xs[:], self.XSS[s][c * 128:(c + 1) * 128, 0:512], reads=rd, writes=[r_xs])
                        S.dma("sp", bt[:], self.XSS[s][c * 128:(c + 1) * 128, 512:768], reads=rd, writes=[r_bt])
                        bct, r_bct = None, None
                        if want_fm:
                            pb2, r_pb2 = PB[2]

                            def cfm(e, pb2=pb2, c=c):
                                for q, ct in enumerate((4, 5, 6, 7)):
                                    for j in range(7):
                                        ins = e.matmul(pb2[:, q * 128:(q + 1) * 128], lhsT=diag[:, ct, j, :], rhs=XB[:, ct, c * 128 + j:c * 128 + j + 128],
                                                       start=(j == 0), stop=(j == 6))
                                return ins
                            S.op("pe", cfm, reads=[r_XB, r_diag], writes=[r_pb2])
                            bct, r_bct = bcr.next()
                            for q, ct in enumerate((4, 5, 6, 7)):
                                S.op("act", lambda e, bct=bct, q=q, ct=ct, pb2=pb2: e.activation(out=bct[:, q, :], in_=pb2[:, q * 128:(q + 1) * 128], func=AF.Silu,
                                                                                                  bias=cb[:, ct:ct + 1]),
                                     reads=[r_p], writes=[r_bct, r_pb2])
                        return xs, r_xs, bt, r_bt, bct, r_bct
                    pb, r_pb = PB[0]

                    def cx(e, pb=pb, c=c):
                        for ct in range(4):
                            for j in range(7):
                                e.matmul(pb[:, ct * 128:(ct + 1) * 128], lhsT=XB[:, ct, c * 128 + j:c * 128 + j + 128], rhs=diag[:, ct, j, :],
                                         start=(j == 0), stop=False)
                            ins = e.matmul(pb[:, ct * 128:(ct + 1) * 128], lhsT=ones1[:], rhs=cbrow[:, ct * 128:(ct + 1) * 128], start=False, stop=True)
                        return ins
                    S.op("pe", cx, reads=[r_XB, r_diag, r_p], writes=[r_pb])
                    xs, r_xs = xsr.next()
                    S.op("act", lambda e, xs=xs, pb=pb: e.activation(out=xs[:], in_=pb, func=AF.Silu), reads=[], writes=[r_xs, r_pb])
                    pb1, r_pb1 = PB[1]

                    def cbt(e, pb1=pb1, c=c):
                        for ct in range(4, 6):
                            o = pb1[:, (ct - 4) * 128:(ct - 3) * 128]
                            for j in range(7):
                                e.matmul(o, lhsT=XB[:, ct, c * 128 + j:c * 128 + j + 128], rhs=diag[:, ct, j, :], start=(j == 0), stop=False)
                            ins = e.matmul(o, lhsT=ones1[:], rhs=cbrow[:, ct * 128:(ct + 1) * 128], start=False, stop=True)
                        return ins
                    S.op("pe", cbt, reads=[r_XB, r_diag, r_p], writes=[r_pb1])
                    bt, r_bt = btr.next()
                    S.op("act", lambda e, bt=bt, pb1=pb1: e.activation(out=bt[:], in_=pb1[:, 0:256], func=AF.Silu), reads=[], writes=[r_bt, r_pb1])
                    bct, r_bct = None, None
                    if want_fm:
                        pb2, r_pb2 = PB[2]

                        def cfm(e, pb2=pb2, c=c):
                            for q, ct in enumerate((4, 5, 6, 7)):
                                for j in range(7):
                                    ins = e.matmul(pb2[:, q * 128:(q + 1) * 128], lhsT=diag[:, ct, j, :], rhs=XB[:, ct, c * 128 + j:c * 128 + j + 128],
                                                   start=(j == 0), stop=(j == 6))
                            return ins
                        S.op("pe", cfm, reads=[r_XB, r_diag], writes=[r_pb2])
                        bct, r_bct = bcr.next()
                        for q, ct in enumerate((4, 5, 6, 7)):
                            S.op("act", lambda e, bct=bct, q=q, ct=ct, pb2=pb2: e.activation(out=bct[:, q, :], in_=pb2[:, q * 128:(q + 1) * 128], func=AF.Silu,
                                                                                              bias=cb[:, ct:ct + 1]),
                                 reads=[r_p], writes=[r_bct, r_pb2])
                    return xs, r_xs, bt, r_bt, bct, r_bct

                def state_mm(c, xs, r_xs, bt, r_bt, lo):
                    xw, r_xw = xwr.next()
                    S.op("dve", lambda e, xw=xw, xs=xs, c=c, lo=lo: e.tensor_tensor(
                        out=xw[:].rearrange("p (h d) -> p h d", h=8), in0=xs[:].rearrange("p (h d) -> p h d", h=8),
                        in1=v3(SDT)[:, c, lo:lo + 8].unsqueeze(2).to_broadcast([128, 8, 64]), op=ALU.mult),
                        reads=[r_xs, r_b[8]], writes=[r_xw])
                    pb3, r_pb3 = PB[3]

                    def smm(e, pb3=pb3, bt=bt, xw=xw):
                        for g in range(2):
                            ins = e.matmul(pb3[:, g * 256:(g + 1) * 256], lhsT=bt[:, g * 128:(g + 1) * 128], rhs=xw[:, g * 256:(g + 1) * 256],
                                           start=True, stop=True)
                        return ins
                    S.op("pe", smm, reads=[r_bt, r_xw], writes=[r_pb3])
                    return pb3, r_pb3

                def scan_step(H, r_H, c, lo, pb3, r_pb3):
                    S.op("dve", lambda e, H=H, c=c, lo=lo: e.tensor_tensor(
                        out=tmpH[:].rearrange("p (h d) -> p h d", h=8), in0=H[:].rearrange("p (h d) -> p h d", h=8),
                        in1=v3(DEC)[:, c, lo:lo + 8].unsqueeze(2).to_broadcast([128, 8, 64]), op=ALU.mult),
                        reads=[r_H, r_b[9]], writes=[r_tmpH])
                    S.op("dve", lambda e, H=H, pb3=pb3: e.tensor_tensor(out=H[:], in0=pb3, in1=tmpH[:], op=ALU.add),
                         reads=[r_tmpH], writes=[r_H, r_pb3])

                nxt = conv_chunk(nch - 1, False)
                for c in range(nch - 1, -1, -1):
                    xs, r_xs, bt, r_bt, _, _ = nxt
                    S.dma("sp", self.XSS[s][c * 128:(c + 1) * 128, 0:512], xs[:], reads=[r_xs], writes=[self.res("XSS", s, c)])
                    S.dma("sp", self.XSS[s][c * 128:(c + 1) * 128, 512:768], bt[:], reads=[r_bt], writes=[self.res("XSS", s, c)])
                    if c > 0:
                        nxt = conv_chunk(c - 1, False)
                    S.op("pool", lambda e, c=c: e.tensor_copy(out=HB[:, c, :], in_=Hb[:]), reads=[r_Hb], writes=[r_HB[c]])
                    pb3, r_pb3 = state_mm(c, xs, r_xs, bt, r_bt, 8)
                    scan_step(Hb, r_Hb, c, 8, pb3, r_pb3)
                h3 = lambda ap: ap.rearrange("p (h d) -> p h d", h=8)
                a3 = lambda ap: ap.rearrange("p (c k) -> p c k", k=16)
                convs = {0: conv_chunk(0, with_out, load=True)}

                rhs_ = {}

                def build_rh(c):
                    rh, r_rh = rhr.next()
                    r_rl = Res()
                    S.op("dve", lambda e, rh=rh, c=c: e.tensor_tensor(out=rh[:, 0], in0=ULb[:], in1=a3(AHI)[:, c, :].unsqueeze(2).to_broadcast([128, 16, 128]), op=ALU.mult),
                         reads=[r_ahl, r_ul], writes=[r_rh, r_rl])
                    S.op("dve", lambda e, rh=rh, c=c: e.tensor_tensor(out=rh[:, 1], in0=ULb[:], in1=a3(ALO)[:, c, :].unsqueeze(2).to_broadcast([128, 16, 128]), op=ALU.mult),
                         reads=[r_ahl, r_ul], writes=[r_rl])
                    rhs_[c] = (rh, r_rh, r_rl)

                def head(c):
                    xs, r_xs, bt, r_bt, bct, r_bct = convs[c]
                    rh, r_rh, r_rl = rhs_.pop(c)
                    pY1, r_pY1 = PB[7]
                    pD_all = {}
                    for rnd in range(2):
                        pDs = []
                        pD_all[rnd] = pDs
                        for d_ in range(2):
                            pD, r_pD = PB[(5 + d_) if rnd == 0 else d_]

                            def dmm(e, pD=pD, rh=rh, d_=d_, rnd=rnd):
                                hs = slice(d_ * 8 + rnd * 4, d_ * 8 + rnd * 4 + 4)
                                e.matmul(pD, lhsT=onesb[:], rhs=rh[:, 0, hs, :], start=True, stop=False)
                                e.matmul(pD, lhsT=onesb[:], rhs=rh[:, 1, hs, :], start=False, stop=False)
                                return e.matmul(pD, lhsT=self.ident[:], rhs=MK4[:, d_, :], start=False, stop=True)
                            S.op("pe", dmm, reads=[r_rh, r_rl, r_ul, self.r_ident], writes=[r_pD])
                            pDs.append((pD, r_pD))
                    pG, r_pG = PB[4]

                    def gmm(e, pG=pG, bct=bct):
                        for g in range(2):
                            ins = e.matmul(pG[:, g * 128:(g + 1) * 128], lhsT=bct[:, g, :], rhs=bct[:, 2 + g, :], start=True, stop=True)
                        return ins
                    S.op("pe", gmm, reads=[r_bct], writes=[r_pG])
                    for rnd in range(2):
                        pDs = pD_all[rnd]
                        for hq in range(4):
                            h = rnd * 4 + hq
                            g = h // 4
                            ws = []
                            for d_, lo in ((0, 0), (1, 8)):
                                pD, r_pD = pDs[d_]
                                m_t, r_m = mr.next()
                                S.op("act", lambda e, m_t=m_t, pD=pD, hq=hq, c=c, lo=lo, h=h: e.activation(
                                    out=m_t[:], in_=pD[:, hq * 128:(hq + 1) * 128], func=AF.Exp, bias=MB[:, c * 16 + lo + h:c * 16 + lo + h + 1]),
                                    reads=[r_b[11]], writes=[r_m, r_pD])
                                w_t, r_w = wr.next()
                                S.op("dve", lambda e, w_t=w_t, pG=pG, g=g, m_t=m_t: e.tensor_tensor(out=w_t[:], in0=pG[:, g * 128:(g + 1) * 128], in1=m_t[:], op=ALU.mult),
                                     reads=[r_m], writes=[r_w, r_pG])
                                ws.append((w_t, r_w))

                            def ymm(e, pY1=pY1, ws=ws, xs=xs, h=h):
                                o = pY1[:, h * 64:(h + 1) * 64]
                                e.matmul(o, lhsT=ws[0][0][:], rhs=xs[:, h * 64:(h + 1) * 64], start=True, stop=False)
                                e.matmul(o, lhsT=ws[1][0][:], rhs=xs[:, h * 64:(h + 1) * 64], start=False, stop=False)
                                return e.matmul(o, lhsT=Dh[:, h, :], rhs=xs[:, h * 64:(h + 1) * 64], start=False, stop=True)
                            S.op("pe", ymm, reads=[ws[0][1], ws[1][1], r_xs, r_diag], writes=[r_pY1])
                    y1, r_y1 = y1r.next()
                    S.op("act", lambda e, y1=y1, pY1=pY1: e.copy(out=y1[:], in_=pY1), reads=[], writes=[r_y1, r_pY1])
                    return y1, r_y1

                def tail(c, y1, r_y1):
                    xs, r_xs, bt, r_bt, bct, r_bct = convs.pop(c)
                    S.op("act", lambda e: e.copy(out=HFb[:], in_=Hf[:]), reads=[r_Hf], writes=[r_HFb])
                    pb3, r_pb3 = state_mm(c, xs, r_xs, bt, r_bt, 0)
                    scan_step(Hf, r_Hf, c, 0, pb3, r_pb3)
                    if c + 2 < nch:
                        build_rh(c + 2)
                    pY2, r_pY2 = PB[5]
                    pY3, r_pY3 = PB[6]

                    def y2mm(e, pY2=pY2, bct=bct):
                        for g in range(2):
                            ins = e.matmul(pY2[:, g * 256:(g + 1) * 256], lhsT=bct[:, 2 + g, :], rhs=HFb[:, g * 256:(g + 1) * 256], start=True, stop=True)
                        return ins
                    S.op("pe", y2mm, reads=[r_bct, r_HFb], writes=[r_pY2])

                    def y3mm(e, pY3=pY3, bct=bct, c=c):
                        for g in range(2):
                            ins = e.matmul(pY3[:, g * 256:(g + 1) * 256], lhsT=bct[:, 2 + g, :], rhs=HB[:, c, g * 256:(g + 1) * 256], start=True, stop=True)
                        return ins
                    S.op("pe", y3mm, reads=[r_bct, r_HB[c]], writes=[r_pY3])
                    t1, r_t1 = t1r.next()
                    t2, r_t2 = t2r.next()
                    S.op("dve", lambda e, t1=t1, pY2=pY2, c=c: e.tensor_tensor(out=h3(t1[:]), in0=h3(pY2), in1=v3(EE)[:, c, 0:8].unsqueeze(2).to_broadcast([128, 8, 64]), op=ALU.mult),
                         reads=[r_b[10]], writes=[r_t1, r_pY2])
                    S.op("dve", lambda e, t2=t2, pY3=pY3, c=c: e.tensor_tensor(out=h3(t2[:]), in0=h3(pY3), in1=v3(EE)[:, c, 8:16].unsqueeze(2).to_broadcast([128, 8, 64]), op=ALU.mult),
                         reads=[r_b[10]], writes=[r_t2, r_pY3])
                    S.op("dve", lambda e, t1=t1, t2=t2: e.tensor_tensor(out=t1[:], in0=t1[:], in1=t2[:], op=ALU.add), reads=[r_t2], writes=[r_t1])
                    y, r_y = yr.next()
                    S.op("dve", lambda e, y=y, y1=y1, t1=t1: e.tensor_tensor(out=y[:], in0=y1[:], in1=t1[:], op=ALU.add), reads=[r_t1, r_y1], writes=[r_y])
                    z_t, r_z = zr.next()
                    S.dma("sp", z_t[:], self.ZS[s][c * 128:(c + 1) * 128, :], writes=[r_z])
                    S.op("dve", lambda e, y=y, z_t=z_t: e.tensor_tensor(out=y[:], in0=y[:], in1=z_t[:], op=ALU.mult), reads=[r_z], writes=[r_y])
                    if c + 2 < nch:
                        convs[c + 2] = conv_chunk(c + 2, with_out, load=True)
                    st_t, r_st = str_.next()
                    S.op("act", lambda e, y=y, st_t=st_t: e.activation(out=junk[:], in_=y[:], func=AF.Square, accum_out=st_t[:, 0:1]),
                         reads=[r_y], writes=[r_junk, r_st])
                    S.op("act", lambda e, st_t=st_t: e.activation(out=st_t[:, 1:2], in_=st_t[:, 0:1], func=AF.Sqrt, scale=1.0 / 512, bias=EPS),
                         reads=[r_st], writes=[r_st])
                    S.op("dve", lambda e, st_t=st_t: e.reciprocal(out=st_t[:, 2:3], in_=st_t[:, 1:2]), reads=[r_st], writes=[r_st])
                    oc, r_oc = ocr.next()
                    S.op("dve", lambda e, oc=oc, y=y, st_t=st_t: e.scalar_tensor_tensor(out=oc[:], in0=y[:], scalar=st_t[:, 2:3], in1=gn[:], op0=ALU.mult, op1=ALU.mult),
                         reads=[r_y, r_st, r_p], writes=[r_oc])
                    S.dma("sp", self.MIX[s][c * 128:(c + 1) * 128, 512:1024], oc[:], reads=[r_oc], writes=[self.res("MIX", s, c, "C")])

                if with_out:
                    build_rh(0)
                    if nch > 1:
                        build_rh(1)
                        convs[1] = conv_chunk(1, with_out, load=True)
                    hd = head(0)
                    for c in range(nch):
                        nh = head(c + 1) if c + 1 < nch else None
                        tail(c, *hd)
                        hd = nh
                else:
                    for c in range(nch):
                        xs, r_xs, bt, r_bt, _, _ = convs.pop(c)
                        if c + 1 < nch:
                            convs[c + 1] = conv_chunk(c + 1, with_out, load=True)
                        pb3, r_pb3 = state_mm(c, xs, r_xs, bt, r_bt, 0)
                        scan_step(Hf, r_Hf, c, 0, pb3, r_pb3)

    def _p4_end(self):
        pass

    def p5(self, l, src, streams, final):
        nc, S = self.nc, self.S
        NK2 = DFF // 128
        with ExitStack() as stw:
            W1 = self.sb(stw, "p5b_W1", [128, 8, 2 * DFF], BF16)
            W2 = self.sb(stw, "p5b_W2", [128, NK2, D], BF16)
            r_W1, r_W2 = [], []
            with ExitStack() as st:
                Wo = self.sb(st, "p5a_W", [128, 8, D], BF16)
                r_W = []
                wv = self.w_out[l].rearrange("(k p) n -> p k n", p=128)
                for k in range(8):
                    r = Res()
                    r_W.append(r)
                    S.dma("pool", Wo[:, k, :], wv[:, k, :], writes=[r])
                w1v = self.w_ffn_in[l].rearrange("(k p) n -> p k n", p=128)
                w2v = self.w_ffn_out[l].rearrange("(k p) n -> p k n", p=128)
                for k in range(8):
                    for c0 in range(0, 2 * DFF, 1408):
                        r = Res()
                        r_W1.append(r)
                        S.dma("pool", W1[:, k, c0:c0 + 1408], w1v[:, k, c0:c0 + 1408], writes=[r])
                for k in range(NK2):
                    r = Res()
                    r_W2.append(r)
                    S.dma("pool", W2[:, k, :], w2v[:, k, :], writes=[r])
                gt = self.sb(st, "p5a_gt", [128, 2, D])
                r_gt = Res()
                for s in streams:
                    S.dma("sp", gt[:, s, :], self.MOD[s, 2], writes=[r_gt])
                xr = Ring([self.sb(st, "p5a_x%d" % i, [128, D]) for i in range(3)])
                mr = Ring([self.sb(st, "p5a_m%d" % i, [128, D], BF16) for i in range(3)])
                mTr = Ring([self.sb(st, "p5a_mT%d" % i, [128, 8, 128], BF16) for i in range(3)])
                tr_ = Ring([self.sb(st, "p5a_t%d" % i, [128, D]) for i in range(2)])
                orr = Ring([self.sb(st, "p5a_o%d" % i, [128, D]) for i in range(2)])
                tiles = [(s, t) for s in streams for t in range(min(NT if s == 0 else NTC, self.opts.get('p5_nt', 99)))]

                def prep(s, t):
                    rows = slice(t * 128, (t + 1) * 128)
                    x_t, r_x = xr.next()
                    S.dma("sp", x_t[:], src[s][rows, :], writes=[r_x])
                    m_t, r_m = mr.next()
                    S.dma("sp", m_t[:], self.MIX[s][rows, :], writes=[r_m])
                    mT, r_mT = mTr.next()

                    def fin(m_t=m_t, r_m=r_m, mT=mT, r_mT=r_mT):
                        pb, r_pb = self.bank()
                        pbT = pb.bitcast(BF16)

                        def tr(e, pbT=pbT, m_t=m_t):
                            for k in range(8):
                                ins = e.transpose(out=pbT[:, k * 128:(k + 1) * 128], in_=m_t[:, k * 128:(k + 1) * 128],
                                                  identity=self.ident[:])
                            return ins
                        S.op("pe", tr, reads=[r_m, self.r_ident], writes=[r_pb])
                        S.op("act", lambda e, mT=mT, pbT=pbT: e.copy(out=mT[:], in_=pbT.rearrange("p (k t) -> p k t", k=8)),
                             reads=[], writes=[r_mT, r_pb])
                    return x_t, r_x, mT, r_mT, fin

                pend = prep(*tiles[0]) if tiles else None
                if pend:
                    pend[4]()
                for i, (s, t) in enumerate(tiles):
                    rows = slice(t * 128, (t + 1) * 128)
                    x_t, r_x, mT, r_mT, _ = pend
                    pend = prep(*tiles[i + 1]) if i + 1 < len(tiles) else None
                    t_t, r_t = tr_.next()
                    for half in range(2):
                        if half == 1 and pend:
                            pend[4]()
                        po, r_po = self.bank()

                        def mm(e, po=po, mT=mT, half=half):
                            for k in range(8):
                                ins = e.matmul(po, lhsT=mT[:, k, :], rhs=Wo[:, k, half * 512:(half + 1) * 512],
                                               start=(k == 0), stop=(k == 7))
                            return ins
                        S.op("pe", mm, reads=r_W + [r_mT], writes=[r_po])
                        S.op("dve", lambda e, t_t=t_t, po=po, half=half, s=s: e.tensor_tensor(
                            out=t_t[:, half * 512:(half + 1) * 512], in0=po, in1=gt[:, s, half * 512:(half + 1) * 512], op=ALU.mult),
                            reads=[r_gt], writes=[r_t, r_po])
                    o_t, r_o = orr.next()
                    S.op("dve", lambda e, o_t=o_t, t_t=t_t, x_t=x_t: e.tensor_tensor(out=o_t[:], in0=t_t[:], in1=x_t[:], op=ALU.add),
                         reads=[r_t, r_x], writes=[r_o])
                    S.dma("sp", self.XM[s][rows, :], o_t[:], reads=[r_o], writes=[self.res("XM", s, t)])
            S.barrier_all()
            with ExitStack() as st:
                modt = self.sb(st, "p5b_mod", [128, 3, D])
                r_mod = Res()
                gfin = None
                if final:
                    gfin = self.sb(st, "p5b_gf", [128, D])
                    r_gf = Res()
                    S.dma("sp", gfin[:], self.g_final.partition_broadcast(128), writes=[r_gf])
                xr = Ring([self.sb(st, "p5b_x%d" % i, [128, 2, D]) for i in range(2)])
                junk = self.sb(st, "p5b_junk", [128, D], BF16)
                r_junk = Res()
                stat = Ring([self.sb(st, "p5b_st%d" % i, [128, 8]) for i in range(6)])
                tmpr = Ring([self.sb(st, "p5b_t%d" % i, [128, D]) for i in range(2)])
                hr = Ring([self.sb(st, "p5b_h%d" % i, [128, D], BF16) for i in range(3)])
                hTr = Ring([self.sb(st, "p5b_hT%d" % i, [128, 8, 256], BF16) for i in range(2)])
                sgr = Ring([self.sb(st, "p5b_sg%d" % i, [128, 256]) for i in range(2)])
                actr = Ring([self.sb(st, "p5b_a%d" % i, [128, NK2, 256], BF16) for i in range(1)])
                orr = Ring([self.sb(st, "p5b_o%d" % i, [128, D]) for i in range(1)])
                groups = [(s, g0) for s in streams for g0 in range(0, min(NT if s == 0 else NTC, self.opts.get('p5_nt', 99)), 2)]
                cur_mod = [None]

                def normg(s, g0):
                    if cur_mod[0] != s:
                        cur_mod[0] = s
                        for jj, j in enumerate((3, 4, 5)):
                            S.dma("sp", modt[:, jj, :], self.MOD[s, j], writes=[r_mod])
                    x_t, r_x = xr.next()
                    S.dma("sp", x_t[:], self.XM[s][g0 * 128:(g0 + 2) * 128, :].rearrange("(t p) c -> p t c", p=128), writes=[r_x])
                    hT, r_hT = hTr.next()
                    fins = []
                    for ti in range(2):
                        st_t, r_st = stat.next()
                        S.op("act", lambda e, x_t=x_t, ti=ti, st_t=st_t: e.activation(out=junk[:], in_=x_t[:, ti, :], func=AF.Square,
                                                                                       accum_out=st_t[:, 0:1]),
                             reads=[r_x], writes=[r_junk, r_st])
                        S.op("act", lambda e, st_t=st_t: e.activation(out=st_t[:, 1:2], in_=st_t[:, 0:1], func=AF.Sqrt,
                                                                      scale=1.0 / D, bias=EPS), reads=[r_st], writes=[r_st])
                        S.op("dve", lambda e, st_t=st_t: e.reciprocal(out=st_t[:, 2:3], in_=st_t[:, 1:2]), reads=[r_st], writes=[r_st])
                        tm, r_tm = tmpr.next()
                        S.op("dve", lambda e, tm=tm, x_t=x_t, ti=ti, st_t=st_t: e.scalar_tensor_tensor(
                            out=tm[:], in0=x_t[:, ti, :], scalar=st_t[:, 2:3], in1=modt[:, 1, :], op0=ALU.mult, op1=ALU.mult),
                            reads=[r_x, r_st, r_mod], writes=[r_tm])
                        h_t, r_h = hr.next()
                        S.op("pool", lambda e, h_t=h_t, tm=tm: e.tensor_tensor(out=h_t[:], in0=tm[:], in1=modt[:, 0, :], op=ALU.add),
                             reads=[r_tm, r_mod], writes=[r_h])
                        def fin(h_t=h_t, r_h=r_h, hT=hT, r_hT=r_hT, ti=ti):
                            pb, r_pb = self.bank()
                            pbT = pb.bitcast(BF16)

                            def tr(e, pbT=pbT, h_t=h_t):
                                for k in range(8):
                                    ins = e.transpose(out=pbT[:, k * 128:(k + 1) * 128], in_=h_t[:, k * 128:(k + 1) * 128],
                                                      identity=self.ident[:])
                                return ins
                            S.op("pe", tr, reads=[r_h, self.r_ident], writes=[r_pb])
                            S.op("act", lambda e, hT=hT, ti=ti, pbT=pbT: e.copy(out=hT[:, :, ti * 128:(ti + 1) * 128],
                                                                                 in_=pbT.rearrange("p (k t) -> p k t", k=8)),
                                 reads=[], writes=[r_hT, r_pb])
                        fins.append(fin)
                    return x_t, r_x, hT, r_hT, fins

                pend = normg(*groups[0]) if groups else None
                if pend:
                    for f_ in pend[4]:
                        f_()
                for gi, (s, g0) in enumerate(groups):
                    x_t, r_x, hT, r_hT, _ = pend
                    defer = []
                    if gi + 1 < len(groups) and groups[gi + 1][0] == s:
                        pend = normg(*groups[gi + 1])
                        defer = list(pend[4])
                        late = False
                    else:
                        late = True
                    a_t, r_a = actr.next()
                    for ct in range(NK2):
                        if defer and ct in (8, 15):
                            defer.pop(0)()
                        pg, r_pg = self.bank()
                        pu, r_pu = self.bank()

                        def mm1(pb_, c0, hT=hT):
                            def f(e):
                                for k in range(8):
                                    ins = e.matmul(pb_[:, 0:256], lhsT=W1[:, k, c0:c0 + 128], rhs=hT[:, k, :],
                                                   start=(k == 0), stop=(k == 7))
                                return ins
                            return f
                        S.op("pe", mm1(pg, ct * 128), reads=r_W1 + [r_hT], writes=[r_pg])
                        S.op("pe", mm1(pu, DFF + ct * 128), reads=r_W1 + [r_hT], writes=[r_pu])
                        sg, r_sg = sgr.next()
                        S.op("act", lambda e, sg=sg, pg=pg: e.activation(out=sg[:], in_=pg[:, 0:256], func=AF.Silu),
                             reads=[], writes=[r_sg, r_pg])
                        S.op("dve", lambda e, a_t=a_t, ct=ct, pu=pu, sg=sg: e.tensor_tensor(
                            out=a_t[:, ct, :], in0=pu[:, 0:256], in1=sg[:], op=ALU.mult),
                            reads=[r_sg], writes=[r_a, r_pu])
                    for ti in range(2):
                        t = g0 + ti
                        tm, r_tm = tmpr.next()
                        for half in range(2):
                            po, r_po = self.bank()

                            def mm2(e, po=po, a_t=a_t, ti=ti, half=half):
                                for k in range(NK2):
                                    ins = e.matmul(po, lhsT=a_t[:, k, ti * 128:(ti + 1) * 128], rhs=W2[:, k, half * 512:(half + 1) * 512],
                                                   start=(k == 0), stop=(k == NK2 - 1))
                                return ins
                            S.op("pe", mm2, reads=r_W2 + [r_a], writes=[r_po])
                            S.op("dve", lambda e, tm=tm, po=po, half=half: e.tensor_tensor(
                                out=tm[:, half * 512:(half + 1) * 512], in0=po, in1=modt[:, 2, half * 512:(half + 1) * 512], op=ALU.mult),
                                reads=[r_mod], writes=[r_tm, r_po])
                        o_t, r_o = orr.next()
                        S.op("pool", lambda e, o_t=o_t, tm=tm, x_t=x_t, ti=ti: e.tensor_tensor(out=o_t[:], in0=tm[:], in1=x_t[:, ti, :], op=ALU.add),
                             reads=[r_tm, r_x], writes=[r_o])
                        rows = slice(t * 128, (t + 1) * 128)
                        if not final:
                            S.dma("sp", self.XL[s][rows, :], o_t[:], reads=[r_o], writes=[self.res("XL", s, t)])
                        else:
                            st_t, r_st = stat.next()
                            S.op("act", lambda e, o_t=o_t, st_t=st_t: e.activation(out=junk[:], in_=o_t[:], func=AF.Square,
                                                                                    accum_out=st_t[:, 0:1]),
                                 reads=[r_o], writes=[r_junk, r_st])
                            S.op("act", lambda e, st_t=st_t: e.activation(out=st_t[:, 1:2], in_=st_t[:, 0:1], func=AF.Sqrt,
                                                                          scale=1.0 / D, bias=EPS), reads=[r_st], writes=[r_st])
                            S.op("dve", lambda e, st_t=st_t: e.reciprocal(out=st_t[:, 2:3], in_=st_t[:, 1:2]), reads=[r_st], writes=[r_st])
                            f_t, r_f = tmpr.next()
                            S.op("dve", lambda e, f_t=f_t, o_t=o_t, st_t=st_t: e.scalar_tensor_tensor(
                                out=f_t[:], in0=o_t[:], scalar=st_t[:, 2:3], in1=gfin[:], op0=ALU.mult, op1=ALU.mult),
                                reads=[r_o, r_st, r_gf], writes=[r_f])
                            S.dma("sp", self.out[rows, :], f_t[:], reads=[r_f], writes=[self.res("OUT", t)])
                    while defer:
                        defer.pop(0)()
                    if late and gi + 1 < len(groups):
                        pend = normg(*groups[gi + 1])
                        for f_ in pend[4]:
                            f_()

    def build(self):
        S = self.S
        self.declare()
        phases = self.opts.get("phases")
        with ExitStack() as st:
            self.setup_common(st)
            for l in range(DEPTH):
                src = [self.x_in, self.ctx_in] if l == 0 else self.XL
                last = (l == DEPTH - 1)
                streams = (0,) if last else (1, 0)

                def run(name, fn):
                    if phases is None or (name, l) in phases:
                        fn()
                        S.barrier_all()
                run("p0", lambda: self.p0(l))
                run("p1", lambda: self.p1(l, src, do_ctx_q=not last))
                run("p2", lambda: self.p2(l, streams))
                run("p3", lambda: self.p3(l, streams))
                run("p4", lambda: self.p4(l, streams))
                run("p5", lambda: self.p5(l, src, streams, final=last))
            S.final_wait("sp")
            S.emit()
        return self.nc


def _rope_tables():
    t = np.arange(SEQ)
    rows, cols = t // 64, t % 64
    inv = (10000.0 ** (-np.arange(16, dtype=np.float32) / 16)).astype(np.float32)
    C = np.zeros((64, SEQ), np.float32)
    Sg = np.zeros((64, SEQ), np.float32)
    for blk, pos in ((0, rows), (1, cols)):
        ang = pos.astype(np.float32)[None, :] * inv[:, None]
        cs, sn = np.cos(ang).astype(np.float32), np.sin(ang).astype(np.float32)
        C[blk * 32:blk * 32 + 16] = cs
        C[blk * 32 + 16:blk * 32 + 32] = cs
        Sg[blk * 32:blk * 32 + 16] = -sn
        Sg[blk * 32 + 16:blk * 32 + 32] = sn
    C2 = np.concatenate([C, C], 0)
    S2 = np.concatenate([Sg, Sg], 0)
    return np.stack([C2 * 0.125, S2 * 0.125, C2, S2]).astype(np.float32)


def _swap_idx():
    d = np.arange(64)
    return np.where(d % 32 < 16, d + 16, d - 16)


def _w_in_ext(w_in):
    qa = np.arange(0, 256)
    qb = np.arange(256, 512)
    z = np.arange(512, 1024)
    ka = np.arange(1024, 1152)
    va = np.arange(1152, 1280)
    kb = np.arange(1280, 1536)
    vb = np.arange(1536, 1792)
    xbc = np.arange(1792, 2816)
    dt = np.arange(2816, 2832)
    sw = _swap_idx()

    def heads(base, hs):
        return np.concatenate([base[h * 64:(h + 1) * 64] for h in hs])

    def heads_sw(base, hs):
        return np.concatenate([base[h * 64:(h + 1) * 64][sw] for h in hs])
    cols = [heads(qa, (0, 2)), heads_sw(qa, (0, 2)), heads(qa, (1, 3)), heads_sw(qa, (1, 3)),
            heads(ka, (0, 1)), heads_sw(ka, (0, 1)), qb, kb, xbc, z, va, vb, dt]
    idx = np.concatenate(cols)
    assert idx.shape[0] == WCOLS
    return np.ascontiguousarray(w_in[:, :, idx])


def _bm_table(rpb):
    L = rpb.shape[0]
    krl, kc = np.divmod(np.arange(128), 64)
    qrl, qc = np.divmod(np.arange(128), 64)
    cases = [(i, j) for (i, js) in ((2, range(0, 5)), (0, range(0, 4)), (1, range(0, 4)), (30, range(28, 32)), (31, range(28, 32))) for j in js]
    out = np.full((L, 4, 21, 128, 128), NEG, np.float32)
    for ci, (i, j) in enumerate(cases):
        kr = (2 * j + krl)[:, None]
        qr = (2 * i + qrl)[None, :]
        s_ = np.clip(qr - 4, 0, 56)
        vrow = (kr >= s_) & (kr <= s_ + 7)
        cst = np.clip(qc - 8, 0, 48)[None, :]
        vcol = (kc[:, None] >= cst) & (kc[:, None] < cst + 16)
        valid = vrow & vcol
        dy = np.clip(kr - qr + 7, 0, 14)
        dx = np.clip(kc[:, None] - qc[None, :] + 15, 0, 30)
        dyb, dxb = np.broadcast_arrays(dy, dx)
        g = rpb[:, :, dyb, dxb]
        out[:, :, ci] = np.where(valid[None, None], g, np.float32(NEG))
    return np.ascontiguousarray(out.transpose(0, 3, 1, 2, 4).reshape(L, 128, 84 * 128))


def prep_inputs(inputs, n_cores):
    f = lambda a: np.ascontiguousarray(np.asarray(a, dtype=np.float32))
    x, c, ctx, c_ctx = f(inputs["x"]), f(inputs["c"]), f(inputs["ctx"]), f(inputs["c_ctx"])
    shared = {
        "w_mod": f(inputs["w_mod"]), "b_mod": f(inputs["b_mod"]), "g_mix": f(inputs["g_mix"]), "g_ffn": f(inputs["g_ffn"]),
        "w_in_ext": _w_in_ext(f(inputs["w_in"])), "rope": _rope_tables(),
        "w_out": f(inputs["w_out"]), "w_ffn_in": f(inputs["w_ffn_in"]), "w_ffn_out": f(inputs["w_ffn_out"]),
        "g_final": f(inputs["g_final"]), "wa_sink": f(inputs["wa_sink"]), "bm_tab": _bm_table(f(inputs["na_rpb"])),
        "conv_w_l": np.ascontiguousarray(f(inputs["ssm_conv_w"]).reshape(DEPTH, 7, 8, 128).transpose(0, 3, 2, 1)),
        "conv_b_l": np.ascontiguousarray(f(inputs["ssm_conv_b"]).reshape(DEPTH, 8, 128).transpose(0, 2, 1)),
        "conv_brow": f(inputs["ssm_conv_b"]).reshape(DEPTH, 1, 1024),
        "dt_bias": f(inputs["ssm_dt_bias"]).reshape(DEPTH, 16), "a_log": f(inputs["ssm_a_log"]).reshape(DEPTH, 16),
        "ssm_d": f(inputs["ssm_d"]), "ssm_g": f(inputs["ssm_norm_g"]),
    }
    maps = []
    for i in range(n_cores):
        b = i % 4
        cvec = np.concatenate([c[b].reshape(8, 128).T, c_ctx.reshape(8, 128).T], 1)
        m = dict(shared)
        m.update({"x": x[b], "ctx": ctx[b], "cvec": np.ascontiguousarray(cvec)})
        maps.append(m)
    return maps


N_CORES = 4


def kernel(**inputs):
    nc = Builder().build()
    maps = prep_inputs(inputs, N_CORES)
    res = run_bass_kernel_spmd(nc, maps, core_ids=list(range(N_CORES)))
    out = np.stack([res.results[b]["out"] for b in range(4)], 0)
    return out.astype(np.float32)
```

```python
import numpy as np
from contextlib import ExitStack
import concourse.bass as bass
import concourse.mybir as mybir
from concourse.bass_utils import run_bass_kernel_spmd

F32 = mybir.dt.float32
BF16 = mybir.dt.bfloat16
AF = mybir.ActivationFunctionType
ALU = mybir.AluOpType
AX = mybir.AxisListType

D = 1024
SEQ = 4096
LC = 256
DEPTH = 2
NT = SEQ // 128
NTC = LC // 128
EPS = 1e-6
DFF = 2816
NFM = 18
NFMO = 15
TMC = 912
WCOLS = NFM * 128 + TMC
NEG = -30000.0
VS = 66


class Res:
    __slots__ = ("name", "w", "rs")

    def __init__(self, name=""):
        self.name = name
        self.w = None
        self.rs = []


class Sched:
    ENGS = ("pe", "act", "dve", "pool", "sp")
    NDMA = 40
    NSDMA = 16

    def __init__(self, nc):
        self.nc = nc
        self.prog = {e: [] for e in self.ENGS}
        self.count = {}
        self.known = {e: {} for e in self.ENGS}
        self.dma_i = 0
        self.sdma_i = 0

    def _deps(self, eng, reads, writes):
        waits = {}

        def add(sv):
            if sv is None:
                return
            s, v = sv
            if eng == "pe" and s == "pe":
                return
            if waits.get(s, 0) < v:
                waits[s] = v
        for r in reads:
            add(r.w)
        for w in writes:
            add(w.w)
            for x in w.rs:
                add(x)
        out = []
        kn = self.known[eng]
        for s, v in waits.items():
            if kn.get(s, 0) < v:
                kn[s] = v
                out.append((s, v))
        return out

    def _mark(self, tag, reads, writes):
        for r in reads:
            r.rs.append(tag)
        for w in writes:
            w.w = tag
            w.rs = []

    def op(self, eng, fn, reads=(), writes=()):
        waits = self._deps(eng, reads, writes)
        c = self.count.get(eng, 0) + 1
        self.count[eng] = c
        self.prog[eng].append((waits, fn, (eng, 1)))
        self._mark((eng, c), reads, writes)

    def dma(self, q, out, in_, reads=(), writes=(), **kw):
        if q == "pool":
            slot = "sdma%d" % (self.sdma_i % self.NSDMA)
            self.sdma_i += 1
        else:
            slot = "dma%d" % (self.dma_i % self.NDMA)
            self.dma_i += 1
        waits = self._deps(q, reads, writes)
        prev = self.count.get(slot, 0)
        kn = self.known[q]
        if prev and kn.get(slot, 0) < prev:
            kn[slot] = prev
            waits.append((slot, prev))
        c = prev + 16
        self.count[slot] = c

        def fn(e, out=out, in_=in_, kw=kw):
            return e.dma_start(out=out, in_=in_, **kw)
        self.prog[q].append((waits, fn, (slot, 16)))
        self._mark((slot, c), reads, writes)

    def barrier_all(self):
        allv = list(self.count.items())
        for e in self.ENGS:
            kn = self.known[e]
            waits = []
            for s, v in allv:
                if s == e:
                    continue
                if kn.get(s, 0) < v:
                    kn[s] = v
                    waits.append((s, v))
            if waits:
                self.prog[e].append((waits, None, None))

    def final_wait(self, eng="sp"):
        waits = []
        kn = self.known[eng]
        for s, v in self.count.items():
            if s != eng and kn.get(s, 0) < v:
                kn[s] = v
                waits.append((s, v))
        self.prog[eng].append((waits, None, None))

    def emit(self):
        nc = self.nc
        with ExitStack() as es:
            sems = {}
            for s in self.count:
                sems[s] = es.enter_context(nc.semaphore(s))
            block = es.enter_context(nc.Block())

            def replay(name, e):
                for waits, fn, inc in self.prog[name]:
                    for s, v in waits:
                        e.wait_ge(sems[s], v)
                    if fn is not None:
                        ins = fn(e)
                        if inc is not None:
                            ins.then_inc(sems[inc[0]], inc[1])

            @block.tensor
            def _(e):
                replay("pe", e)

            @block.scalar
            def _(e):
                replay("act", e)

            @block.vector
            def _(e):
                replay("dve", e)

            @block.gpsimd
            def _(e):
                replay("pool", e)

            @block.sync
            def _(e):
                replay("sp", e)


class Ring:
    def __init__(self, aps, name=""):
        self.items = [(a, Res("%s%d" % (name, i))) for i, a in enumerate(aps)]
        self.i = 0

    def next(self):
        it = self.items[self.i % len(self.items)]
        self.i += 1
        return it


class Builder:
    def __init__(self, debug=(), stop_after=None, opts=None):
        self.opts = opts or {}
        self.debug = set(debug)
        self.stop_after = stop_after
        self.nc = bass.Bass("TRN2", target_bir_lowering=False)
        self.S = Sched(self.nc)
        self.es = ExitStack()
        self.resmap = {}

    def din(self, name, shape, dt=F32):
        return self.nc.dram_tensor(name, list(shape), dt, kind="ExternalInput").ap()

    def dscr(self, name, shape, dt=F32):
        kind = "ExternalOutput" if name in self.debug else "Internal"
        if name in self.opts.get("inject", ()):
            kind = "ExternalInput"
        return self.nc.dram_tensor(name, list(shape), dt, kind=kind).ap()

    def res(self, *key):
        r = self.resmap.get(key)
        if r is None:
            r = Res(str(key))
            self.resmap[key] = r
        return r

    def sb(self, st, name, shape, dt=F32):
        self.uid = getattr(self, "uid", 0) + 1
        return st.enter_context(self.nc.sbuf_tensor("%s_%d" % (name, self.uid), list(shape), dt))

    def declare(self):
        self.x_in = self.din("x", [SEQ, D])
        self.ctx_in = self.din("ctx", [LC, D])
        self.cvec = self.din("cvec", [128, 16])
        self.w_mod = self.din("w_mod", [DEPTH, D, 6 * D])
        self.b_mod = self.din("b_mod", [DEPTH, 6 * D])
        self.g_mix = self.din("g_mix", [DEPTH, D])
        self.g_ffn = self.din("g_ffn", [DEPTH, D])
        self.w_in = self.din("w_in_ext", [DEPTH, D, WCOLS])
        self.rope = self.din("rope", [4, 128, SEQ])
        self.w_out = self.din("w_out", [DEPTH, D, D])
        self.w_ffn_in = self.din("w_ffn_in", [DEPTH, D, 2 * DFF])
        self.w_ffn_out = self.din("w_ffn_out", [DEPTH, DFF, D])
        self.g_final = self.din("g_final", [D])
        self.wa_sink = self.din("wa_sink", [DEPTH, 4])
        self.conv_w = self.din("conv_w_l", [DEPTH, 128, 8, 7])
        self.conv_b = self.din("conv_b_l", [DEPTH, 128, 8])
        self.conv_brow = self.din("conv_brow", [DEPTH, 1, 1024])
        self.dt_bias = self.din("dt_bias", [DEPTH, 16])
        self.a_log = self.din("a_log", [DEPTH, 16])
        self.ssm_d = self.din("ssm_d", [DEPTH, 8])
        self.ssm_g = self.din("ssm_g", [DEPTH, 512])
        self.bm_tab = self.din("bm_tab", [DEPTH, 128, 84 * 128])
        self.out = self.nc.dram_tensor("out", [SEQ, D], F32, kind="ExternalOutput").ap()
        self.MOD = self.dscr("MOD", [2, 6, 128, D])
        self.FM = [self.dscr("FM_l", [NFMO, 128, SEQ], BF16), self.dscr("FM_c", [NFMO, 128, LC], BF16)]
        self.ZS = [self.dscr("ZS_l", [SEQ, 512], BF16), self.dscr("ZS_c", [LC, 512], BF16)]
        self.VA = [self.dscr("VA_l", [SEQ, 2 * VS], BF16), self.dscr("VA_c", [LC, 2 * VS], BF16)]
        self.VB = [self.dscr("VB_l", [SEQ, 4 * VS], BF16), self.dscr("VB_c", [LC, 4 * VS], BF16)]
        self.DT = [self.dscr("DT_l", [SEQ, 16]), self.dscr("DT_c", [LC, 16])]
        self.MIX = [self.dscr("MIX_l", [SEQ, D], BF16), self.dscr("MIX_c", [LC, D], BF16)]
        self.XSS = [self.dscr("XSS_l", [SEQ, 768], BF16), self.dscr("XSS_c", [LC, 768], BF16)]
        self.XM = [self.dscr("XM_l", [SEQ, D]), self.dscr("XM_c", [LC, D])]
        self.XL = [self.dscr("XL_l", [SEQ, D]), self.dscr("XL_c", [LC, D])]

    def setup_common(self, st):
        nc, S = self.nc, self.S
        self.ps = st.enter_context(nc.psum_tensor("ps", [128, 4096], F32))
        self.psb = [(self.ps[:, b * 512:(b + 1) * 512], Res("bank%d" % b)) for b in range(8)]
        self.ident = self.sb(st, "ident", [128, 128], BF16)
        self.r_ident = Res("ident")
        ident = self.ident

        S.op("pool", lambda e: e.memset(ident[:], 0.0), writes=[self.r_ident])
        S.op("pool", lambda e: e.affine_select(out=ident[:], in_=ident[:], pattern=[[-1, 128]], compare_op=ALU.not_equal,
                                               fill=1.0, base=0, channel_multiplier=1),
             reads=[self.r_ident], writes=[self.r_ident])
        self.maskP = self.sb(st, "maskP", [128, 128], BF16)
        self.maskN = self.sb(st, "maskN", [128, 128], BF16)
        self.r_mask = Res("mask")
        mP, mN = self.maskP, self.maskN
        r1, r2 = Res(), Res()
        S.op("pool", lambda e: e.memset(mP[:], 0.0), writes=[r1])
        S.op("pool", lambda e: e.memset(mN[:], 0.0), writes=[r2])
        S.op("pool", lambda e: e.affine_select(out=mP[:], in_=mP[:], pattern=[[-1, 128]], compare_op=ALU.is_ge,
                                               fill=NEG, base=0, channel_multiplier=1), reads=[r1], writes=[r1])
        S.op("pool", lambda e: e.affine_select(out=mN[:], in_=mN[:], pattern=[[1, 128]], compare_op=ALU.is_ge,
                                               fill=NEG, base=0, channel_multiplier=-1), reads=[r2], writes=[r2])
        S.op("pool", lambda e: e.memset(self.ident[0:1, 0:1], 1.0), reads=[r1, r2, self.r_ident], writes=[self.r_mask, self.r_ident])
        self.U32 = self.sb(st, "U32", [128, 128])
        self.L32 = self.sb(st, "L32", [128, 128])
        self.ones32 = self.sb(st, "ones32", [128, 128])
        self.r_tri = Res("tri")
        U32, L32, ones32 = self.U32, self.L32, self.ones32
        r3, r4 = Res(), Res()
        S.op("pool", lambda e: e.memset(U32[:], 1.0), writes=[r3])
        S.op("pool", lambda e: e.memset(L32[:], 1.0), writes=[r4])
        S.op("pool", lambda e: e.memset(ones32[:], 1.0), writes=[self.r_tri])
        S.op("pool", lambda e: e.affine_select(out=U32[:], in_=U32[:], pattern=[[1, 128]], compare_op=ALU.is_ge,
                                               fill=0.0, base=0, channel_multiplier=-1), reads=[r3], writes=[r3])
        S.op("pool", lambda e: e.affine_select(out=L32[:], in_=L32[:], pattern=[[-1, 128]], compare_op=ALU.is_ge,
                                               fill=0.0, base=0, channel_multiplier=1), reads=[r4], writes=[r4])
        S.op("pool", lambda e: e.memset(ones32[0:1, 0:1], 1.0), reads=[r3, r4, self.r_tri], writes=[self.r_tri])
        self.bank_i = 0

    def bank(self):
        b = self.psb[self.bank_i % 8]
        self.bank_i += 1
        return b

    def p0(self, l):
        nc, S = self.nc, self.S
        with ExitStack() as st:
            cv = self.sb(st, "p0_cv", [128, 16])
            scv = self.sb(st, "p0_scv", [128, 16])
            scbc = self.sb(st, "p0_scbc", [128, 16, 128])
            gbc = self.sb(st, "p0_gbc", [128, 2, D])
            wb = [self.sb(st, "p0_w%d" % i, [128, 8, 512]) for i in range(2)]
            bb = [self.sb(st, "p0_b%d" % i, [128, 512]) for i in range(2)]
            mt = [self.sb(st, "p0_m%d" % i, [128, D]) for i in range(4)]
            r_cv, r_scv, r_scbc, r_g = Res(), Res(), Res(), Res()
            wring = Ring(wb, "p0w")
            bring = Ring(bb, "p0b")
            mring = Ring(mt, "p0m")
            S.dma("sp", cv[:], self.cvec, writes=[r_cv])
            S.dma("sp", gbc[:, 0, :], self.g_mix[l].partition_broadcast(128), writes=[r_g])
            S.dma("sp", gbc[:, 1, :], self.g_ffn[l].partition_broadcast(128), writes=[r_g])
            S.op("act", lambda e: e.activation(out=scv[:], in_=cv[:], func=AF.Silu), reads=[r_cv], writes=[r_scv])
            S.op("dve", lambda e: e.tensor_copy(out=scbc[:], in_=scv[:].unsqueeze(2).to_broadcast([128, 16, 128])),
                 reads=[r_scv], writes=[r_scbc])
            wv = self.w_mod[l].rearrange("(k p) n -> p k n", p=128)
            cur = {}
            for blk in range(12):
                j, half = blk // 2, blk % 2
                w_t, r_w = wring.next()
                b_t, r_b = bring.next()
                S.dma("sp", w_t[:], wv[:, :, blk * 512:(blk + 1) * 512], writes=[r_w])
                S.dma("sp", b_t[:], self.b_mod[l, blk * 512:(blk + 1) * 512].partition_broadcast(128), writes=[r_b])
                for s in range(2):
                    if half == 0:
                        cur[s] = mring.next()
                    m_t, r_m = cur[s]
                    pb, r_pb = self.bank()

                    def mm(e, pb=pb, w_t=w_t, s=s):
                        for k in range(8):
                            ins = e.matmul(pb, lhsT=scbc[:, s * 8 + k, :], rhs=w_t[:, k, :], start=(k == 0), stop=(k == 7))
                        return ins
                    S.op("pe", mm, reads=[r_scbc, r_w], writes=[r_pb])
                    dst = m_t[:, half * 512:(half + 1) * 512]
                    if j in (1, 4):
                        gsl = gbc[:, 0 if j == 1 else 1, half * 512:(half + 1) * 512]
                        tmp_r = Res()

                        def ev(e, dst=dst, pb=pb, b_t=b_t, gsl=gsl):
                            e.tensor_tensor(out=dst, in0=pb, in1=b_t[:], op=ALU.add)
                            return e.scalar_tensor_tensor(out=dst, in0=dst, scalar=1.0, in1=gsl, op0=ALU.add, op1=ALU.mult)
                        S.op("dve", lambda e, dst=dst, pb=pb, b_t=b_t: e.tensor_tensor(out=dst, in0=pb, in1=b_t[:], op=ALU.add),
                             reads=[r_pb, r_b], writes=[r_m])
                        S.op("dve", lambda e, dst=dst, gsl=gsl: e.scalar_tensor_tensor(out=dst, in0=dst, scalar=1.0, in1=gsl,
                                                                                         op0=ALU.add, op1=ALU.mult),
                             reads=[r_m, r_g], writes=[r_m])
                    else:
                        S.op("dve", lambda e, dst=dst, pb=pb, b_t=b_t: e.tensor_tensor(out=dst, in0=pb, in1=b_t[:], op=ALU.add),
                             reads=[r_pb, r_b], writes=[r_m])
                    if half == 1:
                        S.dma("sp", self.MOD[s, j], m_t[:], reads=[r_m], writes=[self.res("MOD", s, j)])

    def p1(self, l, src, do_ctx_q):
        nc, S = self.nc, self.S
        with ExitStack() as st:
            W = self.sb(st, "p1_W", [128, 8, WCOLS], BF16)
            r_Wall = []
            wv = self.w_in[l].rearrange("(k p) n -> p k n", p=128)
            for k in range(8):
                for c0 in range(0, WCOLS, 1608):
                    r = Res()
                    r_Wall.append(r)
                    S.dma("pool", W[:, k, c0:c0 + 1608], wv[:, k, c0:c0 + 1608], writes=[r])
            modt = self.sb(st, "p1_mod", [128, 2, 2, D])
            r_mod = Res("p1mod")
            for s in range(2):
                for jj, j in enumerate((0, 1)):
                    S.dma("sp", modt[:, s, jj, :], self.MOD[s, j], reads=[self.res("MOD", s, j)], writes=[r_mod])
            xr = Ring([self.sb(st, "p1_x%d" % i, [128, D]) for i in range(3)], "p1x")
            junk = self.sb(st, "p1_junk", [128, D], BF16)
            r_junk = Res()
            stat = Ring([self.sb(st, "p1_st%d" % i, [128, 4]) for i in range(3)], "p1st")
            tmpr = Ring([self.sb(st, "p1_t%d" % i, [128, D]) for i in range(2)], "p1t")
            hr = Ring([self.sb(st, "p1_h%d" % i, [128, D], BF16) for i in range(8)], "p1h")
            hTr = Ring([self.sb(st, "p1_hT%d" % i, [128, 8, 512], BF16) for i in range(2)], "p1hT")
            ropr = Ring([self.sb(st, "p1_rp%d" % i, [128, 4, 512]) for i in range(2)], "p1rp")
            rtr = Ring([self.sb(st, "p1_rt%d" % i, [128, 2, 512]) for i in range(3)], "p1rt")
            fmr = Ring([self.sb(st, "p1_fm%d" % i, [128, NFMO, 512], BF16) for i in range(2)], "p1fm")
            zr = Ring([self.sb(st, "p1_z%d" % i, [128, 4, 512], BF16) for i in range(2)], "p1z")
            var_ = [self.sb(st, "p1_va%d" % i, [128, 4, 2, VS], BF16) for i in range(2)]
            vbr_ = [self.sb(st, "p1_vb%d" % i, [128, 4, 4, VS], BF16) for i in range(2)]
            dtr = Ring([self.sb(st, "p1_dt%d" % i, [128, 4, 16]) for i in range(2)], "p1dt")
            var = Ring(var_, "p1va")
            vbr = Ring(vbr_, "p1vb")
            for (t_, r_) in var.items + vbr.items:
                S.op("pool", lambda e, t_=t_: e.memset(t_[:], 1.0), writes=[r_])

            groups = [(1, 0, NTC)] + [(0, g * 4, 4) for g in range(NT // 4)]
            lim = self.opts.get('p1_lim', 9)
            groups = groups[:self.opts.get('p1_groups', 99)]
            if lim == 0:
                groups = []
            def norm_group(grp):
                (s, t0, ntile) = grp
                TG = ntile * 128
                tok0 = t0 * 128
                hT, r_hT = hTr.next()
                fins = []
                G1 = modt[:, s, 1, :]
                SH1 = modt[:, s, 0, :]
                for ti in range(ntile):
                    x_t, r_x = xr.next()
                    S.dma("sp", x_t[:], src[s][(t0 + ti) * 128:(t0 + ti + 1) * 128, :],
                          reads=[self.res("XL", s, t0 + ti)], writes=[r_x])
                    st_t, r_st = stat.next()
                    S.op("act", lambda e, x_t=x_t, st_t=st_t: e.activation(out=junk[:], in_=x_t[:], func=AF.Square,
                                                                            accum_out=st_t[:, 0:1]),
                         reads=[r_x], writes=[r_junk, r_st])
                    S.op("act", lambda e, st_t=st_t: e.activation(out=st_t[:, 1:2], in_=st_t[:, 0:1], func=AF.Sqrt,
                                                                  scale=1.0 / D, bias=EPS),
                         reads=[r_st], writes=[r_st])
                    S.op("dve", lambda e, st_t=st_t: e.reciprocal(out=st_t[:, 2:3], in_=st_t[:, 1:2]), reads=[r_st], writes=[r_st])
                    tm, r_tm = tmpr.next()
                    S.op("dve", lambda e, tm=tm, x_t=x_t, st_t=st_t, G1=G1: e.scalar_tensor_tensor(
                        out=tm[:], in0=x_t[:], scalar=st_t[:, 2:3], in1=G1, op0=ALU.mult, op1=ALU.mult),
                        reads=[r_x, r_st, r_mod], writes=[r_tm])
                    h_t, r_h = hr.next()
                    S.op("dve", lambda e, h_t=h_t, tm=tm, SH1=SH1: e.tensor_tensor(out=h_t[:], in0=tm[:], in1=SH1, op=ALU.add),
                         reads=[r_tm, r_mod], writes=[r_h])
                    def fin(h_t=h_t, r_h=r_h, hT=hT, r_hT=r_hT, ti=ti):
                        pb, r_pb = self.bank()
                        pbT = pb.bitcast(BF16)

                        def tr(e, pbT=pbT, h_t=h_t):
                            for k in range(8):
                                ins = e.transpose(out=pbT[:, k * 128:(k + 1) * 128], in_=h_t[:, k * 128:(k + 1) * 128],
                                                  identity=self.ident[:])
                            return ins
                        S.op("pe", tr, reads=[r_h, self.r_ident], writes=[r_pb])
                        S.op("act", lambda e, hT=hT, ti=ti, pbT=pbT: e.copy(out=hT[:, :, ti * 128:(ti + 1) * 128],
                                                                             in_=pbT.rearrange("p (k t) -> p k t", k=8)),
                             reads=[], writes=[r_hT, r_pb])
                    fins.append(fin)
                return hT, r_hT, fins

            pend = norm_group(groups[0]) if groups else None
            if pend:
                for f_ in pend[2]:
                    f_()
            for gi, (s, t0, ntile) in enumerate(groups):
                TG = ntile * 128
                tok0 = t0 * 128
                hT, r_hT, _ = pend
                pend = norm_group(groups[gi + 1]) if gi + 1 < len(groups) else None
                defer = list(pend[2]) if pend else []
                if lim <= 1:
                    continue
                fm, r_fm = fmr.next()
                if s == 0:
                    rp, r_rp = ropr.next()
                    S.dma("sp", rp[:], self.rope[:, :, tok0:tok0 + TG].rearrange("c p t -> p c t"), writes=[r_rp])

                def fm_mm(ct, pb, TG=TG, hT=hT):
                    def f(e):
                        for k in range(8):
                            ins = e.matmul(pb[:, 0:TG], lhsT=W[:, k, ct * 128:(ct + 1) * 128], rhs=hT[:, k, 0:TG],
                                           start=(k == 0), stop=(k == 7))
                        return ins
                    return f
                for (ct, slot, ci) in ((0, 0, 0), (2, 1, 0), (4, 2, 2)):
                    pq, r_pq = self.bank()
                    S.op("pe", fm_mm(ct, pq), reads=r_Wall + [r_hT], writes=[r_pq])
                    if s == 0:
                        psw, r_psw = self.bank()
                        S.op("pe", fm_mm(ct + 1, psw), reads=r_Wall + [r_hT], writes=[r_psw])
                        rt, r_rt = rtr.next()
                        S.op("dve", lambda e, rt=rt, pq=pq, rp=rp, ci=ci, TG=TG: e.tensor_tensor(
                            out=rt[:, 0, 0:TG], in0=pq[:, 0:TG], in1=rp[:, ci, 0:TG], op=ALU.mult),
                            reads=[r_pq, r_rp], writes=[r_rt])
                        S.op("dve", lambda e, rt=rt, psw=psw, rp=rp, ci=ci, TG=TG: e.tensor_tensor(
                            out=rt[:, 1, 0:TG], in0=psw[:, 0:TG], in1=rp[:, ci + 1, 0:TG], op=ALU.mult),
                            reads=[r_psw, r_rp], writes=[r_rt])
                        S.op("dve", lambda e, rt=rt, fm=fm, slot=slot, TG=TG: e.tensor_tensor(
                            out=fm[:, slot, 0:TG], in0=rt[:, 0, 0:TG], in1=rt[:, 1, 0:TG], op=ALU.add),
                            reads=[r_rt], writes=[r_fm])
                    else:
                        sc_ = 0.125 if ct < 4 else 1.0
                        S.op("act", lambda e, fm=fm, slot=slot, pq=pq, TG=TG, sc_=sc_: e.activation(
                            out=fm[:, slot, 0:TG], in_=pq[:, 0:TG], func=AF.Copy, scale=sc_),
                            reads=[r_pq], writes=[r_fm])
                for ct in range(6, NFM):
                    if defer and ct in (8, 11, 14, 17):
                        defer.pop(0)()
                    slot = ct - 3
                    pq, r_pq = self.bank()
                    S.op("pe", fm_mm(ct, pq), reads=r_Wall + [r_hT], writes=[r_pq])
                    sc_ = 0.125 if ct < 8 else 1.0
                    if ct % 2 == 0:
                        S.op("act", lambda e, fm=fm, slot=slot, pq=pq, TG=TG, sc_=sc_: e.activation(
                            out=fm[:, slot, 0:TG], in_=pq[:, 0:TG], func=AF.Copy, scale=sc_),
                            reads=[r_pq], writes=[r_fm])
                    else:
                        S.op("dve", lambda e, fm=fm, slot=slot, pq=pq, TG=TG, sc_=sc_: e.tensor_scalar(
                            out=fm[:, slot, 0:TG], in0=pq[:, 0:TG], scalar1=sc_, scalar2=None, op0=ALU.mult),
                            reads=[r_pq], writes=[r_fm])
                for c0 in range(0, NFMO, 5):
                    S.dma("sp", self.FM[s][c0:c0 + 5, :, tok0:tok0 + TG].rearrange("c p t -> p c t"), fm[:, c0:c0 + 5, 0:TG],
                          reads=[r_fm], writes=[self.res("FM", s, t0 // 4, c0)])
                while defer:
                    defer.pop(0)()
                if lim <= 2:
                    continue
                z_t, r_z = zr.next()
                va_t, r_va = var.next()
                vb_t, r_vb = vbr.next()
                dt_t, r_dt = dtr.next()
                tmm = self.opts.get('tm_mask', 7)
                for ti in range(ntile):
                    def tm_mm(pb, c0, n, hT=hT, ti=ti):
                        def f(e):
                            for k in range(8):
                                ins = e.matmul(pb[:, 0:n], lhsT=hT[:, k, ti * 128:(ti + 1) * 128],
                                               rhs=W[:, k, c0:c0 + n], start=(k == 0), stop=(k == 7))
                            return ins
                        return f
                    if not (tmm & 1):
                        continue
                    pz, r_pz = self.bank()
                    S.op("pe", tm_mm(pz, NFM * 128, 512), reads=r_Wall + [r_hT], writes=[r_pz])
                    S.op("act", lambda e, z_t=z_t, ti=ti, pz=pz: e.activation(out=z_t[:, ti, :], in_=pz, func=AF.Silu),
                         reads=[r_pz], writes=[r_z])
                    if not (tmm & 2):
                        continue
                    pv, r_pv = self.bank()
                    S.op("pe", tm_mm(pv, NFM * 128 + 512, 400), reads=r_Wall + [r_hT], writes=[r_pv])
                    if not (tmm & 8):
                      S.op("dve", lambda e, va_t=va_t, ti=ti, pv=pv: e.tensor_copy(
                        out=va_t[:, ti, :, 0:64], in_=pv[:, 0:128].rearrange("p (g d) -> p g d", g=2)),
                        reads=[r_pv], writes=[r_va, r_pv])
                    if not (tmm & 16):
                      S.op("dve", lambda e, vb_t=vb_t, ti=ti, pv=pv: e.tensor_copy(
                        out=vb_t[:, ti, :, 0:64], in_=pv[:, 128:384].rearrange("p (g d) -> p g d", g=4)),
                        reads=[r_pv], writes=[r_vb, r_pv])
                    if not (tmm & 32):
                      S.op("act", lambda e, dt_t=dt_t, ti=ti, pv=pv: e.copy(out=dt_t[:, ti, :], in_=pv[:, 384:400]),
                         reads=[r_pv], writes=[r_dt, r_pv])
                if not (tmm & 4):
                    continue
                rows = slice(tok0, tok0 + TG)
                gk = t0 // 4
                S.dma("sp", self.ZS[s][rows, :].rearrange("(t p) c -> p t c", p=128), z_t[:, 0:ntile, :],
                      reads=[r_z], writes=[self.res("ZS", s, gk)])
                S.dma("sp", self.VA[s][rows, :].rearrange("(t p) c -> p t c", p=128),
                      va_t[:, 0:ntile].rearrange("p t g d -> p t (g d)"), reads=[r_va], writes=[self.res("VA", s, gk)])
                S.dma("sp", self.VB[s][rows, :].rearrange("(t p) c -> p t c", p=128),
                      vb_t[:, 0:ntile].rearrange("p t g d -> p t (g d)"), reads=[r_vb], writes=[self.res("VB", s, gk)])
                S.dma("sp", self.DT[s][rows, :].rearrange("(t p) c -> p t c", p=128), dt_t[:, 0:ntile, :],
                      reads=[r_dt], writes=[self.res("DT", s, gk)])

    @staticmethod
    def nb_window(i):
        s0 = min(max(2 * i - 4, 0), 56)
        s1 = min(max(2 * i + 1 - 4, 0), 56)
        return list(range(s0 // 2, (s1 + 7) // 2 + 1))

    @staticmethod
    def bm_index(i, j):
        if 2 <= i <= 29:
            return j - i + 2
        if i == 0:
            return 5 + j
        if i == 1:
            return 9 + j
        if i == 30:
            return 13 + (j - 28)
        return 17 + (j - 28)

    def p_attn(self, l, kind, streams):
        nc, S = self.nc, self.S
        isA = kind == "A"
        nkt = 1 if isA else 2
        ng = 2 if isA else 4
        qs0 = 0 if isA else 3
        ks0 = 2 if isA else 5
        VSRC = self.VA if isA else self.VB
        col0 = 0 if isA else 256
        with ExitStack() as st:
            QT = self.sb(st, "at_Q", [128, 2, 2, SEQ], BF16)
            QTc = self.sb(st, "at_Qc", [128, 2, 2, LC], BF16)
            KT = self.sb(st, "at_K", [128, nkt, SEQ + LC], BF16)
            V = self.sb(st, "at_V", [128, NT + NTC, ng * VS], BF16)
            r_in = Res()
            for T in range(2):
                for hh in range(2):
                    lo, zl = hh * 64, (1 - hh) * 64
                    S.op("pool", lambda e, T=T, hh=hh, zl=zl: e.memset(QT[zl:zl + 64, T, hh, :], 0.0), writes=[r_in])
                    S.op("pool", lambda e, T=T, hh=hh, zl=zl: e.memset(QTc[zl:zl + 64, T, hh, :], 0.0), writes=[r_in])
                    S.dma("sp", QT[lo:lo + 64, T, hh, :], self.FM[0][qs0 + T][lo:lo + 64, :], writes=[r_in])
                    S.dma("sp", QTc[lo:lo + 64, T, hh, :], self.FM[1][qs0 + T][lo:lo + 64, :], writes=[r_in])
            for kt in range(nkt):
                S.dma("sp", KT[:, kt, 0:SEQ], self.FM[0][ks0 + kt], writes=[r_in])
                S.dma("sp", KT[:, kt, SEQ:SEQ + LC], self.FM[1][ks0 + kt], writes=[r_in])
            for t0 in range(0, NT, 8):
                S.dma("sp", V[:, t0:t0 + 8, :], VSRC[0][t0 * 128:(t0 + 8) * 128, :].rearrange("(t p) c -> p t c", p=128), writes=[r_in])
            S.dma("sp", V[:, NT:NT + NTC, :], VSRC[1].rearrange("(t p) c -> p t c", p=128), writes=[r_in])
            r_c = Res()
            if isA:
                esk = self.sb(st, "at_esk", [128, 4])
                S.dma("sp", esk[:], self.wa_sink[l].partition_broadcast(128), writes=[r_c])
                S.op("act", lambda e: e.activation(out=esk[:], in_=esk[:], func=AF.Exp), reads=[r_c], writes=[r_c])
            else:
                BM = self.sb(st, "at_BM", [128, 84 * 128], BF16)
                for c0 in range(0, 84 * 128, 1792):
                    S.dma("pool", BM[:, c0:c0 + 1792], self.bm_tab[l][:, c0:c0 + 1792], writes=[r_c])
            Sring = Ring([self.ps[:, i * 1024:(i + 1) * 1024] for i in range(3)])
            Oring = Ring([self.ps[:, 3072 + i * 512:3072 + (i + 1) * 512] for i in range(2)])
            PTr = Ring([self.sb(st, "at_PT%d" % i, [128, 7 * 128], BF16) for i in range(3)])
            mor = Ring([self.sb(st, "at_mo%d" % i, [128, 4, 64], BF16) for i in range(3)])
            dnr = Ring([self.sb(st, "at_dn%d" % i, [128, 8]) for i in range(3)])
            for s in streams:
                ntl = min(NT if s == 0 else NTC, self.opts.get("at_nt", 99))
                for n in range(ntl):
                    O, r_O = Oring.next()
                    pend_pv = []
                    for T in range(2):
                        for hh in range(2):
                            h = (2 * hh + T) if isA else (2 * T + hh)
                            g = hh if isA else h
                            kt = 0 if isA else T
                            ps_ = slice(hh * 64, (hh + 1) * 64)
                            chunks = []
                            if s == 0:
                                if isA:
                                    if n > 0:
                                        chunks.append(((n - 1) * 128, self.maskP[:], n - 1))
                                    chunks.append((n * 128, None, n))
                                    if n < NT - 1:
                                        chunks.append(((n + 1) * 128, self.maskN[:], n + 1))
                                else:
                                    for j in self.nb_window(n):
                                        bi = h * 21 + self.bm_index(n, j)
                                        chunks.append((j * 128, BM[:, bi * 128:(bi + 1) * 128], j))
                            chunks.append((SEQ, None, NT))
                            chunks.append((SEQ + 128, None, NT + 1))
                            nch = len(chunks)
                            q_ap = (QT if s == 0 else QTc)[:, T, hh, n * 128:(n + 1) * 128]
                            Sp, r_S = Sring.next()

                            def smm(e, Sp=Sp, chunks=chunks, q_ap=q_ap, kt=kt, ps_=ps_):
                                for c, (kc, bias, vt) in enumerate(chunks):
                                    ins = e.matmul(Sp[:, c * 128:(c + 1) * 128], lhsT=KT[:, kt, kc:kc + 128], rhs=q_ap,
                                                   start=True, stop=(bias is None))
                                    if bias is not None:
                                        ins = e.matmul(Sp[:, c * 128:(c + 1) * 128], lhsT=self.ident[:], rhs=bias,
                                                       start=False, stop=True)
                                return ins
                            S.op("pe", smm, reads=[r_in, r_c, self.r_ident, self.r_mask], writes=[r_S])
                            PT, r_PT = PTr.next()
                            S.op("act", lambda e, PT=PT, Sp=Sp, nch=nch: e.activation(out=PT[:, 0:nch * 128], in_=Sp[:, 0:nch * 128], func=AF.Exp),
                                 reads=[], writes=[r_PT, r_S])

                            def pv(e, O=O, PT=PT, chunks=chunks, h=h, g=g):
                                for c, (kc, bias, vt) in enumerate(chunks):
                                    ins = e.matmul(O[:, h * VS:h * VS + 65], lhsT=PT[:, c * 128:(c + 1) * 128],
                                                   rhs=V[:, vt, g * VS:g * VS + 65], start=(c == 0), stop=(c == len(chunks) - 1))
                                return ins
                            pend_pv.append((pv, [r_PT, r_in], [r_O]))
                            if len(pend_pv) > 1:
                                f_, rd_, wr_ = pend_pv.pop(0)
                                S.op("pe", f_, reads=rd_, writes=wr_)
                    while pend_pv:
                        f_, rd_, wr_ = pend_pv.pop(0)
                        S.op("pe", f_, reads=rd_, writes=wr_)
                    dn, r_dn = dnr.next()
                    O3 = O[:, 0:4 * VS].rearrange("p (h c) -> p h c", c=VS)
                    if isA:
                        S.op("dve", lambda e, dn=dn, O3=O3: e.tensor_tensor(out=dn[:, 0:4].unsqueeze(2), in0=O3[:, :, 64:65],
                                                                             in1=esk[:].unsqueeze(2), op=ALU.add),
                             reads=[r_c], writes=[r_dn, r_O])
                    else:
                        S.op("dve", lambda e, dn=dn, O3=O3: e.tensor_copy(out=dn[:, 0:4].unsqueeze(2), in_=O3[:, :, 64:65]),
                             reads=[], writes=[r_dn, r_O])
                    S.op("dve", lambda e, dn=dn: e.reciprocal(out=dn[:, 4:8], in_=dn[:, 0:4]), reads=[r_dn], writes=[r_dn])
                    mo, r_mo = mor.next()
                    S.op("dve", lambda e, mo=mo, O3=O3, dn=dn: e.tensor_tensor(
                        out=mo[:], in0=O3[:, :, 0:64], in1=dn[:, 4:8].unsqueeze(2).to_broadcast([128, 4, 64]), op=ALU.mult),
                        reads=[r_dn], writes=[r_mo, r_O])
                    S.dma("sp", self.MIX[s][n * 128:(n + 1) * 128, col0:col0 + 256], mo[:].rearrange("p h d -> p (h d)"),
                          reads=[r_mo], writes=[self.res("MIX", s, n, kind)])

    def p2(self, l, streams):
        self.p_attn(l, "A", streams)

    def p3(self, l, streams):
        self.p_attn(l, "B", streams)

    def p4(self, l, streams):
        nc, S = self.nc, self.S
        last = (l == DEPTH - 1)
        PB = self.psb
        with ExitStack() as st:
            cw = self.sb(st, "s_cw", [128, 8, 7])
            cb = self.sb(st, "s_cb", [128, 8])
            cbrow = self.sb(st, "s_cbrow", [128, 1024], BF16)
            ones1 = self.sb(st, "s_ones1", [128, 128], BF16)
            diag = self.sb(st, "s_diag", [128, 8, 7, 128], BF16)
            Dh = self.sb(st, "s_Dh", [128, 8, 128], BF16)
            sm = self.sb(st, "s_sm", [128, 64])
            gn = self.sb(st, "s_gn", [128, 512])
            r_p = Res("ssm_params")
            r_diag = Res("diag")
            S.dma("sp", cw[:], self.conv_w[l], writes=[r_p])
            S.dma("sp", cb[:], self.conv_b[l], writes=[r_p])
            r_cb0 = Res()
            S.op("pool", lambda e: e.memset(cbrow[:], 0.0), writes=[r_cb0])
            S.dma("pool", cbrow[0:1, :], self.conv_brow[l], reads=[r_cb0], writes=[r_p, r_cb0])
            S.dma("sp", sm[:, 0:16], self.dt_bias[l].partition_broadcast(128), writes=[r_p])
            S.dma("sp", sm[:, 16:32], self.a_log[l].partition_broadcast(128), writes=[r_p])
            S.dma("sp", sm[:, 32:40], self.ssm_d[l].partition_broadcast(128), writes=[r_p])
            S.dma("sp", gn[:], self.ssm_g[l].partition_broadcast(128), writes=[r_p])
            S.op("pool", lambda e: e.memset(ones1[:], 0.0), writes=[r_diag])
            S.op("pool", lambda e: e.memset(ones1[0:1, :], 1.0), reads=[r_diag], writes=[r_diag])
            S.op("act", lambda e: e.activation(out=sm[:, 40:56], in_=sm[:, 16:32], func=AF.Exp), reads=[r_p], writes=[r_p])
            S.op("dve", lambda e: e.tensor_scalar(out=sm[:, 16:32], in0=sm[:, 40:56], scalar1=-1.0, scalar2=None, op0=ALU.mult),
                 reads=[r_p], writes=[r_p])
            for c in range(8):
                for j in range(7):
                    S.op("dve", lambda e, c=c, j=j: e.tensor_scalar(out=diag[:, c, j, :], in0=self.ident[:], scalar1=cw[:, c, j:j + 1],
                                                                    scalar2=None, op0=ALU.mult),
                         reads=[r_p, self.r_ident], writes=[r_diag])
            for h in range(8):
                S.op("dve", lambda e, h=h: e.tensor_scalar(out=Dh[:, h, :], in0=self.ident[:], scalar1=sm[:, 32 + h:33 + h],
                                                            scalar2=None, op0=ALU.mult),
                     reads=[r_p, self.r_ident], writes=[r_diag])
            ULb = self.sb(st, "s_ULb", [128, 16, 128], BF16)
            MK4 = self.sb(st, "s_MK4", [128, 2, 512], BF16)
            onesb = self.sb(st, "s_onesb", [128, 128], BF16)
            r_ul = Res("ULb")
            S.op("dve", lambda e: e.tensor_copy(out=ULb[:, 0:8, :], in_=self.U32[:].unsqueeze(1).to_broadcast([128, 8, 128])),
                 reads=[self.r_tri], writes=[r_ul])
            S.op("dve", lambda e: e.tensor_copy(out=ULb[:, 8:16, :], in_=self.L32[:].unsqueeze(1).to_broadcast([128, 8, 128])),
                 reads=[self.r_tri, r_ul], writes=[r_ul])
            S.op("dve", lambda e: e.tensor_copy(out=MK4[:, 0, :].rearrange("p (a b) -> p a b", a=4), in_=self.maskN[:].unsqueeze(1).to_broadcast([128, 4, 128])),
                 reads=[self.r_mask, r_ul], writes=[r_ul])
            S.op("dve", lambda e: e.tensor_copy(out=MK4[:, 1, :].rearrange("p (a b) -> p a b", a=4), in_=self.maskP[:].unsqueeze(1).to_broadcast([128, 4, 128])),
                 reads=[self.r_mask, r_ul], writes=[r_ul])
            S.op("dve", lambda e: e.tensor_copy(out=onesb[:], in_=self.ones32[:]), reads=[self.r_tri, r_ul], writes=[r_ul])
            Hf = self.sb(st, "s_Hf", [128, 512])
            Hb = self.sb(st, "s_Hb", [128, 512])
            HFb = self.sb(st, "s_HFb", [128, 512], BF16)
            r_Hf, r_Hb, r_HFb = Res("Hf"), Res("Hb"), Res("HFb")
            S.op("pool", lambda e: e.memset(Hf[:], 0.0), writes=[r_Hf])
            S.op("pool", lambda e: e.memset(Hb[:], 0.0), writes=[r_Hb])
            XB = self.sb(st, "s_XB", [128, 8, SEQ + 6], BF16)
            HB = self.sb(st, "s_HB", [128, NT, 512], BF16)
            r_HB = [Res("HB%d" % c) for c in range(NT)]
            NB = 9
            big = [self.sb(st, "s_big%d" % i, [128, NT * 16]) for i in range(NB)]
            xsr = Ring([self.sb(st, "s_xs%d" % i, [128, 512], BF16) for i in range(4)])
            btr = Ring([self.sb(st, "s_bt%d" % i, [128, 256], BF16) for i in range(4)])
            bcr = Ring([self.sb(st, "s_bc%d" % i, [128, 4, 128], BF16) for i in range(4)])
            xwr = Ring([self.sb(st, "s_xw%d" % i, [128, 512], BF16) for i in range(2)])
            mr = Ring([self.sb(st, "s_m%d" % i, [128, 128]) for i in range(6)])
            wr = Ring([self.sb(st, "s_w%d" % i, [128, 128], BF16) for i in range(6)])
            t1r = Ring([self.sb(st, "s_t1%d" % i, [128, 512]) for i in range(1)])
            t2r = Ring([self.sb(st, "s_t2%d" % i, [128, 512]) for i in range(1)])
            yr = Ring([self.sb(st, "s_y%d" % i, [128, 512]) for i in range(2)])
            y1r = Ring([self.sb(st, "s_y1%d" % i, [128, 512]) for i in range(2)])
            zr = Ring([self.sb(st, "s_z%d" % i, [128, 512], BF16) for i in range(2)])
            ocr = Ring([self.sb(st, "s_oc%d" % i, [128, 512], BF16) for i in range(2)])
            str_ = Ring([self.sb(st, "s_st%d" % i, [128, 4]) for i in range(3)])
            junk = self.sb(st, "s_junk", [128, 512], BF16)
            r_junk = Res()
            tmpH = self.sb(st, "s_tmpH", [128, 512])
            r_tmpH = Res()

            r_XB = Res("XB")
            _rb = [Res("big%d" % i) for i in range(NB)]
            r_b = {0: _rb[0], 1: _rb[1], 2: _rb[2], 3: _rb[3], 4: _rb[4], 5: _rb[5], 6: _rb[6], 7: _rb[6], 8: _rb[1], 9: _rb[7], 10: _rb[8], 11: _rb[3]}
            ahi = self.sb(st, "s_ahi", [128, NT * 16], BF16)
            alo = self.sb(st, "s_alo", [128, NT * 16], BF16)
            r_ahl = Res("ahl")
            rhr = Ring([self.sb(st, "s_rh%d" % i, [128, 2, 16, 128], BF16) for i in range(2)])
            for s in streams if 1 in streams else (1,) + tuple(streams):
                with_out = not (s == 1 and last)
                nch = NTC if s == 1 else NT
                nch = min(nch, self.opts.get("ssm_nch", 99))
                T = nch * 128
                W16 = nch * 16
                S.op("pool", lambda e: e.memset(XB[:, :, 0:3], 0.0), writes=[r_XB])
                S.op("pool", lambda e, T=T: e.memset(XB[:, :, 3 + T:6 + T], 0.0), writes=[r_XB])
                for c in range(8):
                    S.dma("sp", XB[:, c, 3:3 + T], self.FM[s][7 + c][:, 0:T], writes=[r_XB])
                DTr, Ev, DTv, LN, Av, AC, TOT, DEC, EE = [b[:, 0:W16] for b in big]
                DTE, SDT, MB = TOT, Ev, LN
                v3 = lambda ap: ap.rearrange("p (c k) -> p c k", k=16)
                for c0 in range(0, nch, 8):
                    c1 = min(c0 + 8, nch)
                    S.dma("sp", v3(DTr)[:, c0:c1, :], self.DT[s][c0 * 128:c1 * 128, :].rearrange("(c p) k -> p c k", p=128), writes=[r_b[0]])
                S.op("dve", lambda e, DTr=DTr, nch=nch: e.tensor_tensor(out=v3(DTr), in0=v3(DTr), in1=sm[:, 0:16].unsqueeze(1).to_broadcast([128, nch, 16]), op=ALU.add),
                     reads=[r_p], writes=[r_b[0]])
                S.op("act", lambda e, Ev=Ev, DTr=DTr: e.activation(out=Ev, in_=DTr, func=AF.Exp), reads=[r_b[0]], writes=[r_b[1]])
                S.op("act", lambda e, Ev=Ev, DTv=DTv: e.activation(out=DTv, in_=Ev, func=AF.Ln, bias=1.0), reads=[r_b[1]], writes=[r_b[2]])
                S.op("act", lambda e, LN=LN, DTv=DTv: e.activation(out=LN, in_=DTv, func=AF.Ln), reads=[r_b[2]], writes=[r_b[3]])
                S.op("dve", lambda e, Av=Av, DTv=DTv, nch=nch: e.tensor_tensor(out=v3(Av), in0=v3(DTv), in1=sm[:, 16:32].unsqueeze(1).to_broadcast([128, nch, 16]), op=ALU.mult),
                     reads=[r_b[2], r_p], writes=[r_b[4]])
                AHI = ahi[:, 0:W16]
                ALO = alo[:, 0:W16]
                S.op("dve", lambda e, AHI=AHI, Av=Av: e.tensor_copy(out=AHI, in_=Av), reads=[r_b[4]], writes=[r_ahl])
                S.op("dve", lambda e, ALO=ALO, Av=Av, AHI=AHI: e.tensor_tensor(out=ALO, in0=Av, in1=AHI, op=ALU.subtract),
                     reads=[r_b[4], r_ahl], writes=[r_ahl])
                for (mat, tgt, lo) in ((self.U32, AC, 0), (self.L32, AC, 8), (self.ones32, TOT, None)):
                    pb, r_pb = self.bank()
                    S.op("pe", lambda e, pb=pb, mat=mat, Av=Av, W16=W16: e.matmul(pb[:, 0:W16], lhsT=mat[:], rhs=Av, start=True, stop=True),
                         reads=[r_b[4], self.r_tri], writes=[r_pb])
                    if lo is None:
                        S.op("dve", lambda e, pb=pb, tgt=tgt, W16=W16: e.tensor_copy(out=tgt, in_=pb[:, 0:W16]), reads=[], writes=[r_b[6], r_pb])
                    else:
                        S.op("dve", lambda e, pb=pb, tgt=tgt, lo=lo, W16=W16: e.tensor_copy(out=v3(tgt)[:, :, lo:lo + 8], in_=v3(pb[:, 0:W16])[:, :, lo:lo + 8]),
                             reads=[], writes=[r_b[5], r_pb])
                S.op("act", lambda e, DEC=DEC, TOT=TOT: e.activation(out=DEC, in_=TOT, func=AF.Exp), reads=[r_b[6]], writes=[r_b[9]])
                S.op("dve", lambda e, DTE=DTE, TOT=TOT, AC=AC: e.tensor_tensor(out=DTE, in0=TOT, in1=AC, op=ALU.subtract), reads=[r_b[5]], writes=[r_b[7]])
                S.op("act", lambda e, DTE=DTE: e.activation(out=DTE, in_=DTE, func=AF.Exp), reads=[], writes=[r_b[7]])
                S.op("dve", lambda e, SDT=SDT, DTE=DTE, DTv=DTv: e.tensor_tensor(out=SDT, in0=DTE, in1=DTv, op=ALU.mult), reads=[r_b[7], r_b[2]], writes=[r_b[8]])
                S.op("act", lambda e, EE=EE, AC=AC: e.activation(out=EE, in_=AC, func=AF.Exp), reads=[r_b[5]], writes=[r_b[10]])
                S.op("dve", lambda e, MB=MB, LN=LN, AC=AC: e.tensor_tensor(out=MB, in0=LN, in1=AC, op=ALU.subtract), reads=[r_b[3], r_b[5]], writes=[r_b[11]])

                def conv_chunk(c, want_fm, load=False):
                    if load:
                        xs, r_xs = xsr.next()
                        bt, r_bt = btr.next()
                        rd = [self.res("XSS", s, c)]
                        S.dma("sp", xs[:], self.XSS[s][c * 128:(c + 1) * 128, 0:512], reads=rd, writes=[r_xs])
                        S.dma("sp", bt[:], self.XSS[s][c * 128:(c + 1) * 128, 512:768], reads=rd, writes=[r_bt])
                        bct, r_bct = None, None
                        if want_fm:
                            pb2, r_pb2 = PB[2]

                            def cfm(e, pb2=pb2, c=c):
                                for q, ct in enumerate((4, 5, 6, 7)):
                                    for j in range(7):
                                        ins = e.matmul(pb2[:, q * 128:(q + 1) * 128], lhsT=diag[:, ct, j, :], rhs=XB[:, ct, c * 128 + j:c * 128 + j + 128],
                                                       start=(j == 0), stop=(j == 6))
                                return ins
                            S.op("pe", cfm, reads=[r_XB, r_diag], writes=[r_pb2])
                            bct, r_bct = bcr.next()
                            for q, ct in enumerate((4, 5, 6, 7)):
                                S.op("act", lambda e, bct=bct, q=q, ct=ct, pb2=pb2: e.activation(out=bct[:, q, :], in_=pb2[:, q * 128:(q + 1) * 128], func=AF.Silu,
                                                                                                  bias=cb[:, ct:ct + 1]),
                                     reads=[r_p], writes=[r_bct, r_pb2])
                        return xs, r_xs, bt, r_bt, bct, r_bct
                    pb, r_pb = PB[0]

                    def cx(e, pb=pb, c=c):
                        for ct in range(4):
                            for j in range(7):
                                e.matmul(pb[:, ct * 128:(ct + 1) * 128], lhsT=XB[:, ct, c * 128 + j:c * 128 + j + 128], rhs=diag[:, ct, j, :],
                                         start=(j == 0), stop=False)
                            ins = e.matmul(pb[:, ct * 128:(ct + 1) * 128], lhsT=ones1[:], rhs=cbrow[:, ct * 128:(ct + 1) * 128], start=False, stop=True)
                        return ins
                    S.op("pe", cx, reads=[r_XB, r_diag, r_p], writes=[r_pb])
                    xs, r_xs = xsr.next()
                    S.op("act", lambda e, xs=xs, pb=pb: e.activation(out=xs[:], in_=pb, func=AF.Silu), reads=[], writes=[r_xs, r_pb])
                    pb1, r_pb1 = PB[1]

                    def cbt(e, pb1=pb1, c=c):
                        for ct in range(4, 6):
                            o = pb1[:, (ct - 4) * 128:(ct - 3) * 128]
                            for j in range(7):
                                e.matmul(o, lhsT=XB[:, ct, c * 128 + j:c * 128 + j + 128], rhs=diag[:, ct, j, :], start=(j == 0), stop=False)
                            ins = e.matmul(o, lhsT=ones1[:], rhs=cbrow[:, ct * 128:(ct + 1) * 128], start=False, stop=True)
                        return ins
                    S.op("pe", cbt, reads=[r_XB, r_diag, r_p], writes=[r_pb1])
                    bt, r_bt = btr.next()
                    S.op("act", lambda e, bt=bt, pb1=pb1: e.activation(out=bt[:], in_=pb1[:, 0:256], func=AF.Silu), reads=[], writes=[r_bt, r_pb1])
                    bct, r_bct = None, None
                    if want_fm:
                        pb2, r_pb2 = PB[2]

                        def cfm(e, pb2=pb2, c=c):
                            for q, ct in enumerate((4, 5, 6, 7)):
                                for j in range(7):
                                    ins = e.matmul(pb2[:, q * 128:(q + 1) * 128], lhsT=diag[:, ct, j, :], rhs=XB[:, ct, c * 128 + j:c * 128 + j + 128],
                                                   start=(j == 0), stop=(j == 6))
                            return ins
                        S.op("pe", cfm, reads=[r_XB, r_diag], writes=[r_pb2])
                        bct, r_bct = bcr.next()
                        for q, ct in enumerate((4, 5, 6, 7)):
                            S.op("act", lambda e, bct=bct, q=q, ct=ct, pb2=pb2: e.activation(out=bct[:, q, :], in_=pb2[:, q * 128:(q + 1) * 128], func=AF.Silu,
                                                                                              bias=cb[:, ct:ct + 1]),
                                 reads=[r_p], writes=[r_bct, r_pb2])
                    return xs, r_xs, bt, r_bt, bct, r_bct

                def state_mm(c, xs, r_xs, bt, r_bt, lo):
                    xw, r_xw = xwr.next()
                    S.op("dve", lambda e, xw=xw, xs=xs, c=c, lo=lo: e.tensor_tensor(
                        out=xw[:].rearrange("p (h d) -> p h d", h=8), in0=xs[:].rearrange("p (h d) -> p h d", h=8),
                        in1=v3(SDT)[:, c, lo:lo + 8].unsqueeze(2).to_broadcast([128, 8, 64]), op=ALU.mult),
                        reads=[r_xs, r_b[8]], writes=[r_xw])
                    pb3, r_pb3 = PB[3]

                    def smm(e, pb3=pb3, bt=bt, xw=xw):
                        for g in range(2):
                            ins = e.matmul(pb3[:, g * 256:(g + 1) * 256], lhsT=bt[:, g * 128:(g + 1) * 128], rhs=xw[:, g * 256:(g + 1) * 256],
                                           start=True, stop=True)
                        return ins
                    S.op("pe", smm, reads=[r_bt, r_xw], writes=[r_pb3])
                    return pb3, r_pb3

                def scan_step(H, r_H, c, lo, pb3, r_pb3):
                    S.op("dve", lambda e, H=H, c=c, lo=lo: e.tensor_tensor(
                        out=tmpH[:].rearrange("p (h d) -> p h d", h=8), in0=H[:].rearrange("p (h d) -> p h d", h=8),
                        in1=v3(DEC)[:, c, lo:lo + 8].unsqueeze(2).to_broadcast([128, 8, 64]), op=ALU.mult),
                        reads=[r_H, r_b[9]], writes=[r_tmpH])
                    S.op("dve", lambda e, H=H, pb3=pb3: e.tensor_tensor(out=H[:], in0=pb3, in1=tmpH[:], op=ALU.add),
                         reads=[r_tmpH], writes=[r_H, r_pb3])

                nxt = conv_chunk(nch - 1, False)
                for c in range(nch - 1, -1, -1):
                    xs, r_xs, bt, r_bt, _, _ = nxt
                    S.dma("sp", self.XSS[s][c * 128:(c + 1) * 128, 0:512], xs[:], reads=[r_xs], writes=[self.res("XSS", s, c)])
                    S.dma("sp", self.XSS[s][c * 128:(c + 1) * 128, 512:768], bt[:], reads=[r_bt], writes=[self.res("XSS", s, c)])
                    if c > 0:
                        nxt = conv_chunk(c - 1, False)
                    S.op("pool", lambda e, c=c: e.tensor_copy(out=HB[:, c, :], in_=Hb[:]), reads=[r_Hb], writes=[r_HB[c]])
                    pb3, r_pb3 = state_mm(c, xs, r_xs, bt, r_bt, 8)
                    scan_step(Hb, r_Hb, c, 8, pb3, r_pb3)
                h3 = lambda ap: ap.rearrange("p (h d) -> p h d", h=8)
                a3 = lambda ap: ap.rearrange("p (c k) -> p c k", k=16)
                convs = {0: conv_chunk(0, with_out, load=True)}

                rhs_ = {}

                def build_rh(c):
                    rh, r_rh = rhr.next()
                    r_rl = Res()
                    S.op("dve", lambda e, rh=rh, c=c: e.tensor_tensor(out=rh[:, 0], in0=ULb[:], in1=a3(AHI)[:, c, :].unsqueeze(2).to_broadcast([128, 16, 128]), op=ALU.mult),
                         reads=[r_ahl, r_ul], writes=[r_rh, r_rl])
                    S.op("dve", lambda e, rh=rh, c=c: e.tensor_tensor(out=rh[:, 1], in0=ULb[:], in1=a3(ALO)[:, c, :].unsqueeze(2).to_broadcast([128, 16, 128]), op=ALU.mult),
                         reads=[r_ahl, r_ul], writes=[r_rl])
                    rhs_[c] = (rh, r_rh, r_rl)

                def head(c):
                    xs, r_xs, bt, r_bt, bct, r_bct = convs[c]
                    rh, r_rh, r_rl = rhs_.pop(c)
                    pY1, r_pY1 = PB[7]
                    pD_all = {}
                    for rnd in range(2):
                        pDs = []
                        pD_all[rnd] = pDs
                        for d_ in range(2):
                            pD, r_pD = PB[(5 + d_) if rnd == 0 else d_]

                            def dmm(e, pD=pD, rh=rh, d_=d_, rnd=rnd):
                                hs = slice(d_ * 8 + rnd * 4, d_ * 8 + rnd * 4 + 4)
                                e.matmul(pD, lhsT=onesb[:], rhs=rh[:, 0, hs, :], start=True, stop=False)
                                e.matmul(pD, lhsT=onesb[:], rhs=rh[:, 1, hs, :], start=False, stop=False)
                                return e.matmul(pD, lhsT=self.ident[:], rhs=MK4[:, d_, :], start=False, stop=True)
                            S.op("pe", dmm, reads=[r_rh, r_rl, r_ul, self.r_ident], writes=[r_pD])
                            pDs.append((pD, r_pD))
                    pG, r_pG = PB[4]

                    def gmm(e, pG=pG, bct=bct):
                        for g in range(2):
                            ins = e.matmul(pG[:, g * 128:(g + 1) * 128], lhsT=bct[:, g, :], rhs=bct[:, 2 + g, :], start=True, stop=True)
                        return ins
                    S.op("pe", gmm, reads=[r_bct], writes=[r_pG])
                    for rnd in range(2):
                        pDs = pD_all[rnd]
                        for hq in range(4):
                            h = rnd * 4 + hq
                            g = h // 4
                            ws = []
                            for d_, lo in ((0, 0), (1, 8)):
                                pD, r_pD = pDs[d_]
                                m_t, r_m = mr.next()
                                S.op("act", lambda e, m_t=m_t, pD=pD, hq=hq, c=c, lo=lo, h=h: e.activation(
                                    out=m_t[:], in_=pD[:, hq * 128:(hq + 1) * 128], func=AF.Exp, bias=MB[:, c * 16 + lo + h:c * 16 + lo + h + 1]),
                                    reads=[r_b[11]], writes=[r_m, r_pD])
                                w_t, r_w = wr.next()
                                S.op("dve", lambda e, w_t=w_t, pG=pG, g=g, m_t=m_t: e.tensor_tensor(out=w_t[:], in0=pG[:, g * 128:(g + 1) * 128], in1=m_t[:], op=ALU.mult),
                                     reads=[r_m], writes=[r_w, r_pG])
                                ws.append((w_t, r_w))

                            def ymm(e, pY1=pY1, ws=ws, xs=xs, h=h):
                                o = pY1[:, h * 64:(h + 1) * 64]
                                e.matmul(o, lhsT=ws[0][0][:], rhs=xs[:, h * 64:(h + 1) * 64], start=True, stop=False)
                                e.matmul(o, lhsT=ws[1][0][:], rhs=xs[:, h * 64:(h + 1) * 64], start=False, stop=False)
                                return e.matmul(o, lhsT=Dh[:, h, :], rhs=xs[:, h * 64:(h + 1) * 64], start=False, stop=True)
                            S.op("pe", ymm, reads=[ws[0][1], ws[1][1], r_xs, r_diag], writes=[r_pY1])
                    y1, r_y1 = y1r.next()
                    S.op("act", lambda e, y1=y1, pY1=pY1: e.copy(out=y1[:], in_=pY1), reads=[], writes=[r_y1, r_pY1])
                    return y1, r_y1

                def tail(c, y1, r_y1):
                    xs, r_xs, bt, r_bt, bct, r_bct = convs.pop(c)
                    S.op("act", lambda e: e.copy(out=HFb[:], in_=Hf[:]), reads=[r_Hf], writes=[r_HFb])
                    pb3, r_pb3 = state_mm(c, xs, r_xs, bt, r_bt, 0)
                    scan_step(Hf, r_Hf, c, 0, pb3, r_pb3)
                    if c + 2 < nch:
                        build_rh(c + 2)
                    pY2, r_pY2 = PB[5]
                    pY3, r_pY3 = PB[6]

                    def y2mm(e, pY2=pY2, bct=bct):
                        for g in range(2):
                            ins = e.matmul(pY2[:, g * 256:(g + 1) * 256], lhsT=bct[:, 2 + g, :], rhs=HFb[:, g * 256:(g + 1) * 256], start=True, stop=True)
                        return ins
                    S.op("pe", y2mm, reads=[r_bct, r_HFb], writes=[r_pY2])

                    def y3mm(e, pY3=pY3, bct=bct, c=c):
                        for g in range(2):
                            ins = e.matmul(pY3[:, g * 256:(g + 1) * 256], lhsT=bct[:, 2 + g, :], rhs=HB[:, c, g * 256:(g + 1) * 256], start=True, stop=True)
                        return ins
                    S.op("pe", y3mm, reads=[r_bct, r_HB[c]], writes=[r_pY3])
                    t1, r_t1 = t1r.next()
                    t2, r_t2 = t2r.next()
                    S.op("dve", lambda e, t1=t1, pY2=pY2, c=c: e.tensor_tensor(out=h3(t1[:]), in0=h3(pY2), in1=v3(EE)[:, c, 0:8].unsqueeze(2).to_broadcast([128, 8, 64]), op=ALU.mult),
                         reads=[r_b[10]], writes=[r_t1, r_pY2])
                    S.op("dve", lambda e, t2=t2, pY3=pY3, c=c: e.tensor_tensor(out=h3(t2[:]), in0=h3(pY3), in1=v3(EE)[:, c, 8:16].unsqueeze(2).to_broadcast([128, 8, 64]), op=ALU.mult),
                         reads=[r_b[10]], writes=[r_t2, r_pY3])
                    S.op("dve", lambda e, t1=t1, t2=t2: e.tensor_tensor(out=t1[:], in0=t1[:], in1=t2[:], op=ALU.add), reads=[r_t2], writes=[r_t1])
                    y, r_y = yr.next()
                    S.op("dve", lambda e, y=y, y1=y1, t1=t1: e.tensor_tensor(out=y[:], in0=y1[:], in1=t1[:], op=ALU.add), reads=[r_t1, r_y1], writes=[r_y])
                    z_t, r_z = zr.next()
                    S.dma("sp", z_t[:], self.ZS[s][c * 128:(c + 1) * 128, :], writes=[r_z])
                    S.op("dve", lambda e, y=y, z_t=z_t: e.tensor_tensor(out=y[:], in0=y[:], in1=z_t[:], op=ALU.mult), reads=[r_z], writes=[r_y])
                    if c + 2 < nch:
                        convs[c + 2] = conv_chunk(c + 2, with_out, load=True)
                    st_t, r_st = str_.next()
                    S.op("act", lambda e, y=y, st_t=st_t: e.activation(out=junk[:], in_=y[:], func=AF.Square, accum_out=st_t[:, 0:1]),
                         reads=[r_y], writes=[r_junk, r_st])
                    S.op("act", lambda e, st_t=st_t: e.activation(out=st_t[:, 1:2], in_=st_t[:, 0:1], func=AF.Sqrt, scale=1.0 / 512, bias=EPS),
                         reads=[r_st], writes=[r_st])
                    S.op("dve", lambda e, st_t=st_t: e.reciprocal(out=st_t[:, 2:3], in_=st_t[:, 1:2]), reads=[r_st], writes=[r_st])
                    oc, r_oc = ocr.next()
                    S.op("dve", lambda e, oc=oc, y=y, st_t=st_t: e.scalar_tensor_tensor(out=oc[:], in0=y[:], scalar=st_t[:, 2:3], in1=gn[:], op0=ALU.mult, op1=ALU.mult),
                         reads=[r_y, r_st, r_p], writes=[r_oc])
                    S.dma("sp", self.MIX[s][c * 128:(c + 1) * 128, 512:1024], oc[:], reads=[r_oc], writes=[self.res("MIX", s, c, "C")])

                if with_out:
                    build_rh(0)
                    if nch > 1:
                        build_rh(1)
                        convs[1] = conv_chunk(1, with_out, load=True)
                    hd = head(0)
                    for c in range(nch):
                        nh = head(c + 1) if c + 1 < nch else None
                        tail(c, *hd)
                        hd = nh
                else:
                    for c in range(nch):
                        xs, r_xs, bt, r_bt, _, _ = convs.pop(c)
                        if c + 1 < nch:
                            convs[c + 1] = conv_chunk(c + 1, with_out, load=True)
                        pb3, r_pb3 = state_mm(c, xs, r_xs, bt, r_bt, 0)
                        scan_step(Hf, r_Hf, c, 0, pb3, r_pb3)

    def _p4_end(self):
        pass

    def p5(self, l, src, streams, final):
        nc, S = self.nc, self.S
        NK2 = DFF // 128
        with ExitStack() as stw:
            W1 = self.sb(stw, "p5b_W1", [128, 8, 2 * DFF], BF16)
            W2 = self.sb(stw, "p5b_W2", [128, NK2, D], BF16)
            r_W1, r_W2 = [], []
            with ExitStack() as st:
                Wo = self.sb(st, "p5a_W", [128, 8, D], BF16)
                r_W = []
                wv = self.w_out[l].rearrange("(k p) n -> p k n", p=128)
                for k in range(8):
                    r = Res()
                    r_W.append(r)
                    S.dma("pool", Wo[:, k, :], wv[:, k, :], writes=[r])
                w1v = self.w_ffn_in[l].rearrange("(k p) n -> p k n", p=128)
                w2v = self.w_ffn_out[l].rearrange("(k p) n -> p k n", p=128)
                for k in range(8):
                    for c0 in range(0, 2 * DFF, 1408):
                        r = Res()
                        r_W1.append(r)
                        S.dma("pool", W1[:, k, c0:c0 + 1408], w1v[:, k, c0:c0 + 1408], writes=[r])
                for k in range(NK2):
                    r = Res()
                    r_W2.append(r)
                    S.dma("pool", W2[:, k, :], w2v[:, k, :], writes=[r])
                gt = self.sb(st, "p5a_gt", [128, 2, D])
                r_gt = Res()
                for s in streams:
                    S.dma("sp", gt[:, s, :], self.MOD[s, 2], writes=[r_gt])
                xr = Ring([self.sb(st, "p5a_x%d" % i, [128, D]) for i in range(3)])
                mr = Ring([self.sb(st, "p5a_m%d" % i, [128, D], BF16) for i in range(3)])
                mTr = Ring([self.sb(st, "p5a_mT%d" % i, [128, 8, 128], BF16) for i in range(3)])
                tr_ = Ring([self.sb(st, "p5a_t%d" % i, [128, D]) for i in range(2)])
                orr = Ring([self.sb(st, "p5a_o%d" % i, [128, D]) for i in range(2)])
                tiles = [(s, t) for s in streams for t in range(min(NT if s == 0 else NTC, self.opts.get('p5_nt', 99)))]

                def prep(s, t):
                    rows = slice(t * 128, (t + 1) * 128)
                    x_t, r_x = xr.next()
                    S.dma("sp", x_t[:], src[s][rows, :], writes=[r_x])
                    m_t, r_m = mr.next()
                    S.dma("sp", m_t[:], self.MIX[s][rows, :], writes=[r_m])
                    mT, r_mT = mTr.next()

                    def fin(m_t=m_t, r_m=r_m, mT=mT, r_mT=r_mT):
                        pb, r_pb = self.bank()
                        pbT = pb.bitcast(BF16)

                        def tr(e, pbT=pbT, m_t=m_t):
                            for k in range(8):
                                ins = e.transpose(out=pbT[:, k * 128:(k + 1) * 128], in_=m_t[:, k * 128:(k + 1) * 128],
                                                  identity=self.ident[:])
                            return ins
                        S.op("pe", tr, reads=[r_m, self.r_ident], writes=[r_pb])
                        S.op("act", lambda e, mT=mT, pbT=pbT: e.copy(out=mT[:], in_=pbT.rearrange("p (k t) -> p k t", k=8)),
                             reads=[], writes=[r_mT, r_pb])
                    return x_t, r_x, mT, r_mT, fin

                pend = prep(*tiles[0]) if tiles else None
                if pend:
                    pend[4]()
                for i, (s, t) in enumerate(tiles):
                    rows = slice(t * 128, (t + 1) * 128)
                    x_t, r_x, mT, r_mT, _ = pend
                    pend = prep(*tiles[i + 1]) if i + 1 < len(tiles) else None
                    t_t, r_t = tr_.next()
                    for half in range(2):
                        if half == 1 and pend:
                            pend[4]()
                        po, r_po = self.bank()

                        def mm(e, po=po, mT=mT, half=half):
                            for k in range(8):
                                ins = e.matmul(po, lhsT=mT[:, k, :], rhs=Wo[:, k, half * 512:(half + 1) * 512],
                                               start=(k == 0), stop=(k == 7))
                            return ins
                        S.op("pe", mm, reads=r_W + [r_mT], writes=[r_po])
                        S.op("dve", lambda e, t_t=t_t, po=po, half=half, s=s: e.tensor_tensor(
                            out=t_t[:, half * 512:(half + 1) * 512], in0=po, in1=gt[:, s, half * 512:(half + 1) * 512], op=ALU.mult),
                            reads=[r_gt], writes=[r_t, r_po])
                    o_t, r_o = orr.next()
                    S.op("dve", lambda e, o_t=o_t, t_t=t_t, x_t=x_t: e.tensor_tensor(out=o_t[:], in0=t_t[:], in1=x_t[:], op=ALU.add),
                         reads=[r_t, r_x], writes=[r_o])
                    S.dma("sp", self.XM[s][rows, :], o_t[:], reads=[r_o], writes=[self.res("XM", s, t)])
            S.barrier_all()
            with ExitStack() as st:
                modt = self.sb(st, "p5b_mod", [128, 3, D])
                r_mod = Res()
                gfin = None
                if final:
                    gfin = self.sb(st, "p5b_gf", [128, D])
                    r_gf = Res()
                    S.dma("sp", gfin[:], self.g_final.partition_broadcast(128), writes=[r_gf])
                xr = Ring([self.sb(st, "p5b_x%d" % i, [128, 2, D]) for i in range(2)])
                junk = self.sb(st, "p5b_junk", [128, D], BF16)
                r_junk = Res()
                stat = Ring([self.sb(st, "p5b_st%d" % i, [128, 8]) for i in range(6)])
                tmpr = Ring([self.sb(st, "p5b_t%d" % i, [128, D]) for i in range(2)])
                hr = Ring([self.sb(st, "p5b_h%d" % i, [128, D], BF16) for i in range(3)])
                hTr = Ring([self.sb(st, "p5b_hT%d" % i, [128, 8, 256], BF16) for i in range(2)])
                sgr = Ring([self.sb(st, "p5b_sg%d" % i, [128, 256]) for i in range(2)])
                actr = Ring([self.sb(st, "p5b_a%d" % i, [128, NK2, 256], BF16) for i in range(1)])
                orr = Ring([self.sb(st, "p5b_o%d" % i, [128, D]) for i in range(1)])
                groups = [(s, g0) for s in streams for g0 in range(0, min(NT if s == 0 else NTC, self.opts.get('p5_nt', 99)), 2)]
                cur_mod = [None]

                def normg(s, g0):
                    if cur_mod[0] != s:
                        cur_mod[0] = s
                        for jj, j in enumerate((3, 4, 5)):
                            S.dma("sp", modt[:, jj, :], self.MOD[s, j], writes=[r_mod])
                    x_t, r_x = xr.next()
                    S.dma("sp", x_t[:], self.XM[s][g0 * 128:(g0 + 2) * 128, :].rearrange("(t p) c -> p t c", p=128), writes=[r_x])
                    hT, r_hT = hTr.next()
                    fins = []
                    for ti in range(2):
                        st_t, r_st = stat.next()
                        S.op("act", lambda e, x_t=x_t, ti=ti, st_t=st_t: e.activation(out=junk[:], in_=x_t[:, ti, :], func=AF.Square,
                                                                                       accum_out=st_t[:, 0:1]),
                             reads=[r_x], writes=[r_junk, r_st])
                        S.op("act", lambda e, st_t=st_t: e.activation(out=st_t[:, 1:2], in_=st_t[:, 0:1], func=AF.Sqrt,
                                                                      scale=1.0 / D, bias=EPS), reads=[r_st], writes=[r_st])
                        S.op("dve", lambda e, st_t=st_t: e.reciprocal(out=st_t[:, 2:3], in_=st_t[:, 1:2]), reads=[r_st], writes=[r_st])
                        tm, r_tm = tmpr.next()
                        S.op("dve", lambda e, tm=tm, x_t=x_t, ti=ti, st_t=st_t: e.scalar_tensor_tensor(
                            out=tm[:], in0=x_t[:, ti, :], scalar=st_t[:, 2:3], in1=modt[:, 1, :], op0=ALU.mult, op1=ALU.mult),
                            reads=[r_x, r_st, r_mod], writes=[r_tm])
                        h_t, r_h = hr.next()
                        S.op("dve", lambda e, h_t=h_t, tm=tm: e.tensor_tensor(out=h_t[:], in0=tm[:], in1=modt[:, 0, :], op=ALU.add),
                             reads=[r_tm, r_mod], writes=[r_h])
                        def fin(h_t=h_t, r_h=r_h, hT=hT, r_hT=r_hT, ti=ti):
                            pb, r_pb = self.bank()
                            pbT = pb.bitcast(BF16)

                            def tr(e, pbT=pbT, h_t=h_t):
                                for k in range(8):
                                    ins = e.transpose(out=pbT[:, k * 128:(k + 1) * 128], in_=h_t[:, k * 128:(k + 1) * 128],
                                                      identity=self.ident[:])
                                return ins
                            S.op("pe", tr, reads=[r_h, self.r_ident], writes=[r_pb])
                            S.op("act", lambda e, hT=hT, ti=ti, pbT=pbT: e.copy(out=hT[:, :, ti * 128:(ti + 1) * 128],
                                                                                 in_=pbT.rearrange("p (k t) -> p k t", k=8)),
                                 reads=[], writes=[r_hT, r_pb])
                        fins.append(fin)
                    return x_t, r_x, hT, r_hT, fins

                pend = normg(*groups[0]) if groups else None
                if pend:
                    for f_ in pend[4]:
                        f_()
                for gi, (s, g0) in enumerate(groups):
                    x_t, r_x, hT, r_hT, _ = pend
                    defer = []
                    if gi + 1 < len(groups) and groups[gi + 1][0] == s:
                        pend = normg(*groups[gi + 1])
                        defer = list(pend[4])
                        late = False
                    else:
                        late = True
                    a_t, r_a = actr.next()
                    for ct in range(NK2):
                        if defer and ct in (8, 15):
                            defer.pop(0)()
                        pg, r_pg = self.bank()
                        pu, r_pu = self.bank()

                        def mm1(pb_, c0, hT=hT):
                            def f(e):
                                for k in range(8):
                                    ins = e.matmul(pb_[:, 0:256], lhsT=W1[:, k, c0:c0 + 128], rhs=hT[:, k, :],
                                                   start=(k == 0), stop=(k == 7))
                                return ins
                            return f
                        S.op("pe", mm1(pg, ct * 128), reads=r_W1 + [r_hT], writes=[r_pg])
                        S.op("pe", mm1(pu, DFF + ct * 128), reads=r_W1 + [r_hT], writes=[r_pu])
                        sg, r_sg = sgr.next()
                        S.op("act", lambda e, sg=sg, pg=pg: e.activation(out=sg[:], in_=pg[:, 0:256], func=AF.Silu),
                             reads=[], writes=[r_sg, r_pg])
                        S.op("dve", lambda e, a_t=a_t, ct=ct, pu=pu, sg=sg: e.tensor_tensor(
                            out=a_t[:, ct, :], in0=pu[:, 0:256], in1=sg[:], op=ALU.mult),
                            reads=[r_sg], writes=[r_a, r_pu])
                    for ti in range(2):
                        t = g0 + ti
                        tm, r_tm = tmpr.next()
                        for half in range(2):
                            po, r_po = self.bank()

                            def mm2(e, po=po, a_t=a_t, ti=ti, half=half):
                                for k in range(NK2):
                                    ins = e.matmul(po, lhsT=a_t[:, k, ti * 128:(ti + 1) * 128], rhs=W2[:, k, half * 512:(half + 1) * 512],
                                                   start=(k == 0), stop=(k == NK2 - 1))
                                return ins
                            S.op("pe", mm2, reads=r_W2 + [r_a], writes=[r_po])
                            S.op("dve", lambda e, tm=tm, po=po, half=half: e.tensor_tensor(
                                out=tm[:, half * 512:(half + 1) * 512], in0=po, in1=modt[:, 2, half * 512:(half + 1) * 512], op=ALU.mult),
                                reads=[r_mod], writes=[r_tm, r_po])
                        o_t, r_o = orr.next()
                        S.op("dve", lambda e, o_t=o_t, tm=tm, x_t=x_t, ti=ti: e.tensor_tensor(out=o_t[:], in0=tm[:], in1=x_t[:, ti, :], op=ALU.add),
                             reads=[r_tm, r_x], writes=[r_o])
                        rows = slice(t * 128, (t + 1) * 128)
                        if not final:
                            S.dma("sp", self.XL[s][rows, :], o_t[:], reads=[r_o], writes=[self.res("XL", s, t)])
                        else:
                            st_t, r_st = stat.next()
                            S.op("act", lambda e, o_t=o_t, st_t=st_t: e.activation(out=junk[:], in_=o_t[:], func=AF.Square,
                                                                                    accum_out=st_t[:, 0:1]),
                                 reads=[r_o], writes=[r_junk, r_st])
                            S.op("act", lambda e, st_t=st_t: e.activation(out=st_t[:, 1:2], in_=st_t[:, 0:1], func=AF.Sqrt,
                                                                          scale=1.0 / D, bias=EPS), reads=[r_st], writes=[r_st])
                            S.op("dve", lambda e, st_t=st_t: e.reciprocal(out=st_t[:, 2:3], in_=st_t[:, 1:2]), reads=[r_st], writes=[r_st])
                            f_t, r_f = tmpr.next()
                            S.op("dve", lambda e, f_t=f_t, o_t=o_t, st_t=st_t: e.scalar_tensor_tensor(
                                out=f_t[:], in0=o_t[:], scalar=st_t[:, 2:3], in1=gfin[:], op0=ALU.mult, op1=ALU.mult),
                                reads=[r_o, r_st, r_gf], writes=[r_f])
                            S.dma("sp", self.out[rows, :], f_t[:], reads=[r_f], writes=[self.res("OUT", t)])
                    while defer:
                        defer.pop(0)()
                    if late and gi + 1 < len(groups):
                        pend = normg(*groups[gi + 1])
                        for f_ in pend[4]:
                            f_()

    def build(self):
        S = self.S
        self.declare()
        phases = self.opts.get("phases")
        with ExitStack() as st:
            self.setup_common(st)
            for l in range(DEPTH):
                src = [self.x_in, self.ctx_in] if l == 0 else self.XL
                last = (l == DEPTH - 1)
                streams = (0,) if last else (1, 0)

                def run(name, fn):
                    if phases is None or (name, l) in phases:
                        fn()
                        S.barrier_all()
                run("p0", lambda: self.p0(l))
                run("p1", lambda: self.p1(l, src, do_ctx_q=not last))
                run("p2", lambda: self.p2(l, streams))
                run("p3", lambda: self.p3(l, streams))
                run("p4", lambda: self.p4(l, streams))
                run("p5", lambda: self.p5(l, src, streams, final=last))
            S.final_wait("sp")
            S.emit()
        return self.nc


def _rope_tables():
    t = np.arange(SEQ)
    rows, cols = t // 64, t % 64
    inv = (10000.0 ** (-np.arange(16, dtype=np.float32) / 16)).astype(np.float32)
    C = np.zeros((64, SEQ), np.float32)
    Sg = np.zeros((64, SEQ), np.float32)
    for blk, pos in ((0, rows), (1, cols)):
        ang = pos.astype(np.float32)[None, :] * inv[:, None]
        cs, sn = np.cos(ang).astype(np.float32), np.sin(ang).astype(np.float32)
        C[blk * 32:blk * 32 + 16] = cs
        C[blk * 32 + 16:blk * 32 + 32] = cs
        Sg[blk * 32:blk * 32 + 16] = -sn
        Sg[blk * 32 + 16:blk * 32 + 32] = sn
    C2 = np.concatenate([C, C], 0)
    S2 = np.concatenate([Sg, Sg], 0)
    return np.stack([C2 * 0.125, S2 * 0.125, C2, S2]).astype(np.float32)


def _swap_idx():
    d = np.arange(64)
    return np.where(d % 32 < 16, d + 16, d - 16)


def _w_in_ext(w_in):
    qa = np.arange(0, 256)
    qb = np.arange(256, 512)
    z = np.arange(512, 1024)
    ka = np.arange(1024, 1152)
    va = np.arange(1152, 1280)
    kb = np.arange(1280, 1536)
    vb = np.arange(1536, 1792)
    xbc = np.arange(1792, 2816)
    dt = np.arange(2816, 2832)
    sw = _swap_idx()

    def heads(base, hs):
        return np.concatenate([base[h * 64:(h + 1) * 64] for h in hs])

    def heads_sw(base, hs):
        return np.concatenate([base[h * 64:(h + 1) * 64][sw] for h in hs])
    cols = [heads(qa, (0, 2)), heads_sw(qa, (0, 2)), heads(qa, (1, 3)), heads_sw(qa, (1, 3)),
            heads(ka, (0, 1)), heads_sw(ka, (0, 1)), qb, kb, xbc, z, va, vb, dt]
    idx = np.concatenate(cols)
    assert idx.shape[0] == WCOLS
    return np.ascontiguousarray(w_in[:, :, idx])


def _bm_table(rpb):
    L = rpb.shape[0]
    krl, kc = np.divmod(np.arange(128), 64)
    qrl, qc = np.divmod(np.arange(128), 64)
    cases = [(i, j) for (i, js) in ((2, range(0, 5)), (0, range(0, 4)), (1, range(0, 4)), (30, range(28, 32)), (31, range(28, 32))) for j in js]
    out = np.full((L, 4, 21, 128, 128), NEG, np.float32)
    for ci, (i, j) in enumerate(cases):
        kr = (2 * j + krl)[:, None]
        qr = (2 * i + qrl)[None, :]
        s_ = np.clip(qr - 4, 0, 56)
        vrow = (kr >= s_) & (kr <= s_ + 7)
        cst = np.clip(qc - 8, 0, 48)[None, :]
        vcol = (kc[:, None] >= cst) & (kc[:, None] < cst + 16)
        valid = vrow & vcol
        dy = np.clip(kr - qr + 7, 0, 14)
        dx = np.clip(kc[:, None] - qc[None, :] + 15, 0, 30)
        dyb, dxb = np.broadcast_arrays(dy, dx)
        g = rpb[:, :, dyb, dxb]
        out[:, :, ci] = np.where(valid[None, None], g, np.float32(NEG))
    return np.ascontiguousarray(out.transpose(0, 3, 1, 2, 4).reshape(L, 128, 84 * 128))


def prep_inputs(inputs, n_cores):
    f = lambda a: np.ascontiguousarray(np.asarray(a, dtype=np.float32))
    x, c, ctx, c_ctx = f(inputs["x"]), f(inputs["c"]), f(inputs["ctx"]), f(inputs["c_ctx"])
    shared = {
        "w_mod": f(inputs["w_mod"]), "b_mod": f(inputs["b_mod"]), "g_mix": f(inputs["g_mix"]), "g_ffn": f(inputs["g_ffn"]),
        "w_in_ext": _w_in_ext(f(inputs["w_in"])), "rope": _rope_tables(),
        "w_out": f(inputs["w_out"]), "w_ffn_in": f(inputs["w_ffn_in"]), "w_ffn_out": f(inputs["w_ffn_out"]),
        "g_final": f(inputs["g_final"]), "wa_sink": f(inputs["wa_sink"]), "bm_tab": _bm_table(f(inputs["na_rpb"])),
        "conv_w_l": np.ascontiguousarray(f(inputs["ssm_conv_w"]).reshape(DEPTH, 7, 8, 128).transpose(0, 3, 2, 1)),
        "conv_b_l": np.ascontiguousarray(f(inputs["ssm_conv_b"]).reshape(DEPTH, 8, 128).transpose(0, 2, 1)),
        "conv_brow": f(inputs["ssm_conv_b"]).reshape(DEPTH, 1, 1024),
        "dt_bias": f(inputs["ssm_dt_bias"]).reshape(DEPTH, 16), "a_log": f(inputs["ssm_a_log"]).reshape(DEPTH, 16),
        "ssm_d": f(inputs["ssm_d"]), "ssm_g": f(inputs["ssm_norm_g"]),
    }
    maps = []
    for i in range(n_cores):
        b = i % 4
        cvec = np.concatenate([c[b].reshape(8, 128).T, c_ctx.reshape(8, 128).T], 1)
        m = dict(shared)
        m.update({"x": x[b], "ctx": ctx[b], "cvec": np.ascontiguousarray(cvec)})
        maps.append(m)
    return maps


N_CORES = 4


def kernel(**inputs):
    nc = Builder().build()
    maps = prep_inputs(inputs, N_CORES)
    res = run_bass_kernel_spmd(nc, maps, core_ids=list(range(N_CORES)))
    out = np.stack([res.results[b]["out"] for b in range(4)], 0)
    return out.astype(np.float32)
```
